# Optimizing a Trainium2 kernel written in Bass

```python
import jax, jax.numpy as jnp
from jax import lax
import numpy as np

D_MODEL = 1024
BATCH = 8
SEQ = 2048
DEPTH = 1
DEC_BATCH = 128
DEC_SEQ = 4
PAST_LEN = 16384
PAGE_SIZE = 128

H_A = 4
DK_A = 128
DV_A = 128
H_B = 4
DK_B = 128
DV_B = 128
CONV_W = 4
CHUNK = 64
PLE_DIM = 256
N_KEYS = 128
N_EXPERTS = N_KEYS * N_KEYS
PEER_HEADS = 8
PEER_QDIM = 256
PEER_HALF = PEER_QDIM // 2
PEER_TOPK = 16
PEER_BLOCK = 256
EPS = 1e-6

W_A = H_A * DK_A
W_AV = H_A * DV_A
W_BK = H_B * DK_B
W_BV = H_B * DV_B
CONV_CH = 2 * W_BK + W_BV
SPLIT_SIZES = (W_A, W_A, W_AV, W_AV, W_BK, W_BK, W_BV, W_BV, H_B, H_B, D_MODEL, D_MODEL)
IN_COLS = int(sum(SPLIT_SIZES))
SPLIT_POINTS = tuple(int(v) for v in np.cumsum(SPLIT_SIZES)[:-1])
F32 = jnp.float32

kernel_name = 'hgrn2_gdn_peer_hybrid_step'


def rmsnorm(x, g):
    xf = x.astype(F32)
    y = xf * lax.rsqrt(jnp.mean(xf * xf, axis=-1, keepdims=True) + EPS)
    return (y * g.astype(F32)).astype(x.dtype)


def head_rmsnorm(o, g):
    return o * lax.rsqrt(jnp.mean(o * o, axis=-1, keepdims=True) + EPS) * g.astype(F32)


def l2norm(x):
    return x * lax.rsqrt(jnp.sum(x * x, axis=-1, keepdims=True) + EPS)


def _to_chunks(a, c, n):
    t = a.shape[1]
    pad = [(0, 0)] * a.ndim
    pad[1] = (0, n * c - t)
    a = jnp.pad(a, pad)
    a = a.reshape(a.shape[0], n, c, *a.shape[2:])
    return jnp.moveaxis(a, 1, 0)


def _from_chunks(a, t):
    a = jnp.moveaxis(a, 0, 1)
    a = a.reshape(a.shape[0], -1, *a.shape[3:])
    return a[:, :t]


def hgrn2_recurrence(q, k, v, logf, s0):
    t = q.shape[1]
    c = min(CHUNK, t)
    n = -(-t // c)
    mask = jnp.tril(jnp.ones((c, c), bool))[None, :, :, None, None]

    def step(s, inp):
        qc, kc, vc, gc = inp
        b = jnp.cumsum(gc, axis=1)
        diff = b[:, :, None] - b[:, None, :]
        dec = jnp.where(mask, jnp.exp(jnp.minimum(diff, 0.0)), 0.0)
        att = jnp.einsum('bthd,bshd,btshd->bhts', qc, kc, dec)
        o = (jnp.einsum('bhts,bshv->bthv', att, vc)
             + jnp.einsum('bthd,bhdv->bthv', qc * jnp.exp(b), s))
        bl = b[:, -1]
        s = (jnp.exp(bl)[..., None] * s
             + jnp.einsum('bshd,bshv->bhdv', kc * jnp.exp(bl[:, None] - b), vc))
        return s, o

    xs = tuple(_to_chunks(a, c, n) for a in (q, k, v, logf))
    s, o = lax.scan(step, s0, xs)
    return _from_chunks(o, t), s


def gated_delta_recurrence(q, k, v, g, beta, s0):
    t = q.shape[1]
    c = min(CHUNK, t)
    n = -(-t // c)
    incl = jnp.tril(jnp.ones((c, c), bool))
    strict = jnp.tril(jnp.ones((c, c), bool), -1)
    eye = jnp.eye(c, dtype=F32)

    def step(s, inp):
        qc, kc, vc, gc, bc = inp
        gcum = jnp.cumsum(gc, axis=1).transpose(0, 2, 1)
        diff = gcum[..., :, None] - gcum[..., None, :]
        gam = jnp.where(incl, jnp.exp(jnp.minimum(diff, 0.0)), 0.0)
        bt = bc.transpose(0, 2, 1)
        kk = jnp.einsum('bthd,bshd->bhts', kc, kc)
        lmat = jnp.where(strict, bt[..., :, None] * gam * kk, 0.0)
        eg = jnp.exp(gcum)
        ks = jnp.einsum('bthd,bhdv->bhtv', kc, s)
        rhs = bt[..., None] * (vc.transpose(0, 2, 1, 3) - eg[..., None] * ks)
        u = lax.linalg.triangular_solve(eye + lmat, rhs, left_side=True, lower=True,
                                        unit_diagonal=True)
        qk = jnp.einsum('bthd,bshd->bhts', qc, kc) * gam
        o = (jnp.einsum('bhts,bhsv->bhtv', qk, u)
             + eg[..., None] * jnp.einsum('bthd,bhdv->bhtv', qc, s))
        gl = gcum[..., -1]
        kdec = kc * jnp.exp(gl[..., None] - gcum).transpose(0, 2, 1)[..., None]
        s = jnp.exp(gl)[..., None, None] * s + jnp.einsum('bshd,bhsv->bhdv', kdec, u)
        return s, o.transpose(0, 2, 1, 3)

    xs = tuple(_to_chunks(a, c, n) for a in (q, k, v, g, beta))
    s, o = lax.scan(step, s0, xs)
    return _from_chunks(o, t), s


def peer_ffn(h, peer_wq, peer_keys, expert_u, expert_v):
    bsz, t, d = h.shape
    n_tok = bsz * t
    blk = min(PEER_BLOCK, n_tok)
    n_blk = -(-n_tok // blk)
    hf = jnp.pad(h.reshape(n_tok, d), ((0, n_blk * blk - n_tok), (0, 0))).reshape(n_blk, blk, d)

    def block(hb):
        q = (hb @ peer_wq).astype(F32).reshape(blk, PEER_HEADS, 2, PEER_HALF)
        sc = jnp.einsum('nhcd,hckd->nhck', q, peer_keys.astype(F32))
        s1, i1 = lax.top_k(sc[:, :, 0], PEER_TOPK)
        s2, i2 = lax.top_k(sc[:, :, 1], PEER_TOPK)
        cand = (s1[..., :, None] + s2[..., None, :]).reshape(blk, PEER_HEADS, PEER_TOPK * PEER_TOPK)
        s_top, ci = lax.top_k(cand, PEER_TOPK)
        ia = jnp.take_along_axis(i1, ci // PEER_TOPK, axis=-1)
        ib = jnp.take_along_axis(i2, ci % PEER_TOPK, axis=-1)
        idx = ia * N_KEYS + ib
        gate = jax.nn.softmax(s_top, axis=-1)
        u = expert_u[idx]
        act = jax.nn.gelu(jnp.einsum('nd,nhkd->nhk', hb, u).astype(F32))
        wgt = (gate * act).astype(hb.dtype)
        return jnp.einsum('nhk,nhkd->nd', wgt, expert_v[idx])

    out = lax.map(block, hf)
    return out.reshape(n_blk * blk, d)[:n_tok].reshape(bsz, t, d)


def decoder_layer(x, p, s_a, s_b, conv_buf, lb, g_mix, w_in, conv_w, a_log, dt_bias,
                  g_norm_a, g_norm_b, w_br_a, w_br_b, w_out, g_ffn, peer_wq, peer_keys,
                  expert_u, expert_v, g_ple, w_ple, w_ple_gate):
    bsz, t, _ = x.shape
    h = rmsnorm(x, g_mix)
    proj = h @ w_in
    (qa, fa, ia, oga, qb, kb, vb, zb, bb, ab, gate_a, gate_b) = jnp.split(proj, SPLIT_POINTS, axis=-1)

    lbh = lb.reshape(H_A, DK_A)
    q_a = jax.nn.silu(qa.astype(F32)).reshape(bsz, t, H_A, DK_A)
    f_a = lbh + (1.0 - lbh) * jax.nn.sigmoid(fa.astype(F32).reshape(bsz, t, H_A, DK_A))
    k_a = 1.0 - f_a
    v_a = ia.astype(F32).reshape(bsz, t, H_A, DV_A)
    o_a, s_a_new = hgrn2_recurrence(q_a, k_a, v_a, jnp.log(f_a), s_a.astype(F32))
    o_a = head_rmsnorm(o_a, g_norm_a) * jax.nn.silu(oga.astype(F32)).reshape(bsz, t, H_A, DV_A)
    y_a = o_a.reshape(bsz, t, W_AV).astype(x.dtype) @ w_br_a

    raw = jnp.concatenate([qb, kb, vb], axis=-1)
    xcat = jnp.concatenate([conv_buf.astype(raw.dtype), raw], axis=1)
    conv = xcat[:, 0:t] * conv_w[0]
    for j in range(1, CONV_W):
        conv = conv + xcat[:, j:j + t] * conv_w[j]
    conv = jax.nn.silu(conv.astype(F32))
    new_buf = xcat[:, t:]
    qc, kc, vc = jnp.split(conv, (W_BK, 2 * W_BK), axis=-1)
    q_b = l2norm(qc.reshape(bsz, t, H_B, DK_B)) * (DK_B ** -0.5)
    k_b = l2norm(kc.reshape(bsz, t, H_B, DK_B))
    v_b = vc.reshape(bsz, t, H_B, DV_B)
    g_b = -jnp.exp(a_log.astype(F32)) * jax.nn.softplus(ab.astype(F32) + dt_bias.astype(F32))
    beta_b = jax.nn.sigmoid(bb.astype(F32))
    o_b, s_b_new = gated_delta_recurrence(q_b, k_b, v_b, g_b, beta_b, s_b.astype(F32))
    o_b = head_rmsnorm(o_b, g_norm_b) * jax.nn.silu(zb.astype(F32)).reshape(bsz, t, H_B, DV_B)
    y_b = o_b.reshape(bsz, t, W_BV).astype(x.dtype) @ w_br_b

    merged = jax.nn.sigmoid(gate_a) * y_a + jax.nn.sigmoid(gate_b) * y_b
    x = x + merged @ w_out

    x = x + peer_ffn(rmsnorm(x, g_ffn), peer_wq, peer_keys, expert_u, expert_v)

    ple_gate = jax.nn.sigmoid(rmsnorm(x, g_ple) @ w_ple_gate)
    x = x + (p.astype(x.dtype) @ w_ple) * ple_gate
    return x, s_a_new, s_b_new, new_buf


def setup_inputs(seed: int = 0) -> dict:
    key = jax.random.key(seed)
    ks = list(jax.random.split(key, 40))

    def nrm(i, shape, scale):
        return jax.random.normal(ks[i], shape, F32) * scale

    dt = jnp.exp(jax.random.uniform(ks[30], (DEPTH, H_B), F32, np.log(1e-3), np.log(1e-1)))
    return {
        'x_prompt': nrm(0, (BATCH, SEQ, D_MODEL), 1.0),
        'x_sample': nrm(1, (DEC_BATCH, DEC_SEQ, D_MODEL), 1.0),
        'state_hgrn': nrm(2, (DEPTH, DEC_BATCH, H_A, DK_A, DV_A), 0.5),
        'state_delta': nrm(3, (DEPTH, DEC_BATCH, H_B, DK_B, DV_B), 0.2),
        'state_conv': nrm(4, (DEPTH, DEC_BATCH, CONV_W - 1, CONV_CH), 1.0),
        'p_prompt': nrm(5, (DEPTH, BATCH, SEQ, PLE_DIM), 1.0),
        'p_sample': nrm(6, (DEPTH, DEC_BATCH, DEC_SEQ, PLE_DIM), 1.0),
        'lb_param': nrm(7, (DEPTH + 1, W_A), 1.0),
        'g_mix': 1.0 + nrm(8, (DEPTH, D_MODEL), 0.01),
        'w_in': nrm(9, (DEPTH, D_MODEL, IN_COLS), D_MODEL ** -0.5),
        'conv_w': nrm(10, (DEPTH, CONV_W, CONV_CH), CONV_W ** -0.5),
        'a_log': jnp.log(jax.random.uniform(ks[11], (DEPTH, H_B), F32, 1.0, 16.0)),
        'dt_bias': jnp.log(jnp.expm1(dt)),
        'g_norm_a': 1.0 + nrm(12, (DEPTH, DV_A), 0.01),
        'g_norm_b': 1.0 + nrm(13, (DEPTH, DV_B), 0.01),
        'w_br_a': nrm(14, (DEPTH, W_AV, D_MODEL), W_AV ** -0.5),
        'w_br_b': nrm(15, (DEPTH, W_BV, D_MODEL), W_BV ** -0.5),
        'w_out': nrm(16, (DEPTH, D_MODEL, D_MODEL), D_MODEL ** -0.5),
        'g_ffn': 1.0 + nrm(17, (DEPTH, D_MODEL), 0.01),
        'peer_wq': nrm(18, (DEPTH, D_MODEL, PEER_HEADS * PEER_QDIM), D_MODEL ** -0.5),
        'peer_keys': nrm(19, (DEPTH, PEER_HEADS, 2, N_KEYS, PEER_HALF), PEER_HALF ** -0.5),
        'expert_u': nrm(20, (DEPTH, N_EXPERTS, D_MODEL), D_MODEL ** -0.5),
        'expert_v': nrm(21, (DEPTH, N_EXPERTS, D_MODEL), (PEER_HEADS * PEER_TOPK) ** -0.5),
        'g_ple': 1.0 + nrm(22, (DEPTH, D_MODEL), 0.01),
        'w_ple': nrm(23, (DEPTH, PLE_DIM, D_MODEL), 0.5 * PLE_DIM ** -0.5),
        'w_ple_gate': nrm(24, (DEPTH, D_MODEL, D_MODEL), D_MODEL ** -0.5),
        'g_final': 1.0 + nrm(25, (D_MODEL,), 0.01),
    }


def reference(x_prompt, x_sample, state_hgrn, state_delta, state_conv, p_prompt, p_sample,
              lb_param, g_mix, w_in, conv_w, a_log, dt_bias, g_norm_a, g_norm_b, w_br_a, w_br_b,
              w_out, g_ffn, peer_wq, peer_keys, expert_u, expert_v, g_ple, w_ple, w_ple_gate,
              g_final):
    lb_all = jnp.cumsum(jax.nn.softmax(lb_param.astype(F32), axis=0), axis=0)
    xp, xs = x_prompt, x_sample
    hp, dp, cp, hs, ds, cs = [], [], [], [], [], []
    for i in range(DEPTH):
        lw = (g_mix[i], w_in[i], conv_w[i], a_log[i], dt_bias[i], g_norm_a[i], g_norm_b[i],
              w_br_a[i], w_br_b[i], w_out[i], g_ffn[i], peer_wq[i], peer_keys[i], expert_u[i],
              expert_v[i], g_ple[i], w_ple[i], w_ple_gate[i])
        za = jnp.zeros((BATCH, H_A, DK_A, DV_A), F32)
        zb = jnp.zeros((BATCH, H_B, DK_B, DV_B), F32)
        zc = jnp.zeros((BATCH, CONV_W - 1, CONV_CH), xp.dtype)
        xp, sa, sb, sc = decoder_layer(xp, p_prompt[i], za, zb, zc, lb_all[i], *lw)
        hp.append(sa.astype(state_hgrn.dtype))
        dp.append(sb.astype(state_delta.dtype))
        cp.append(sc.astype(state_conv.dtype))
        xs, sa2, sb2, sc2 = decoder_layer(xs, p_sample[i], state_hgrn[i], state_delta[i],
                                          state_conv[i], lb_all[i], *lw)
        hs.append(sa2.astype(state_hgrn.dtype))
        ds.append(sb2.astype(state_delta.dtype))
        cs.append(sc2.astype(state_conv.dtype))
    y_prompt = rmsnorm(xp, g_final)
    y_sample = rmsnorm(xs, g_final)
    return (y_prompt, y_sample, jnp.stack(hp), jnp.stack(dp), jnp.stack(cp),
            jnp.stack(hs), jnp.stack(ds), jnp.stack(cs))
```

```python
import numpy as np
from contextlib import ExitStack
import concourse.bass as bass
import concourse.mybir as mybir
from concourse.bass_utils import run_bass_kernel_spmd

F32 = mybir.dt.float32
BF16 = mybir.dt.bfloat16
I32 = mybir.dt.int32
U32 = mybir.dt.uint32
AF = mybir.ActivationFunctionType
ALU = mybir.AluOpType
AX = mybir.AxisListType

EPS = 1e-6
NPT = 32
NTOK = 2112
EPOCH = 12000
DMA_POOL = 8
DMA_EPOCH = 700
ARENA_COLS = 105984
NBUF = 6
NEG = -30000.0
DBG = set()
DBGT = 0
PHASES = ('A1', 'A2', 'B')
TILES = None


class Sched:
    def __init__(self, nc, es):
        self.nc = nc
        self.es = es
        self.eng = {'pe': nc.tensor, 'act': nc.scalar, 'dve': nc.vector,
                    'pool': nc.gpsimd, 'sp': nc.sync}
        self.prog = {e: [] for e in self.eng}
        self.cnt = {e: 0 for e in self.eng}
        self.sem = {}
        self.nsem = 0
        for e in self.eng:
            self.sem[e] = self._newsem(e)
        self.waited = {e: {} for e in self.eng}
        self.dpool = {}
        self.res_w = {}
        self.res_r = {}
        self.final_tokens = []
        self.pending = {e: [] for e in self.eng}

    def _newsem(self, name):
        self.nsem += 1
        return self.es.enter_context(self.nc.semaphore(f"s{self.nsem}_{name}"))

    def _need(self, e, tok, waits):
        if tok is None:
            return
        sem, val = tok[0], tok[1]
        if e == 'pe' and tok[2] == 'pe':
            return
        w = self.waited[e]
        if w.get(id(sem), 0) >= val:
            return
        w[id(sem)] = val
        waits.append((sem, val))

    def _deps(self, e, reads, writes, waits):
        for t in self.pending[e]:
            self._need(e, t, waits)
        self.pending[e] = []
        for k in reads:
            self._need(e, self.res_w.get(k), waits)
        for k in writes:
            self._need(e, self.res_w.get(k), waits)
            for t in self.res_r.get(k, ()):
                self._need(e, t, waits)

    def _commit(self, tok, reads, writes):
        for k in reads:
            self.res_r.setdefault(k, []).append(tok)
        for k in writes:
            self.res_w[k] = tok
            self.res_r[k] = []

    def op(self, e, fn, reads=(), writes=()):
        waits = []
        self._deps(e, reads, writes, waits)
        if self.cnt[e] >= EPOCH:
            self.sem[e] = self._newsem(e)
            self.cnt[e] = 0
        self.cnt[e] += 1
        tok = (self.sem[e], self.cnt[e], e)
        self.prog[e].append((waits, fn, self.sem[e], 1))
        self._commit(tok, reads, writes)
        return tok

    def dma(self, e, fn, reads=(), writes=(), final=False):
        waits = []
        self._deps(e, reads, writes, waits)
        pool = self.dpool.setdefault(e, {'sems': [], 'uses': [], 'i': 0})
        i = pool['i'] % DMA_POOL
        pool['i'] += 1
        if len(pool['sems']) <= i:
            pool['sems'].append(self._newsem(e + 'd'))
            pool['uses'].append(0)
        if pool['uses'][i] >= DMA_EPOCH:
            pool['sems'][i] = self._newsem(e + 'd')
            pool['uses'][i] = 0
        sem = pool['sems'][i]
        if pool['uses'][i] > 0:
            self._need(e, (sem, 16 * pool['uses'][i], 'dma'), waits)
        pool['uses'][i] += 1
        tok = (sem, 16 * pool['uses'][i], 'dma')
        self.prog[e].append((waits, fn, sem, 16))
        self._commit(tok, reads, writes)
        if final:
            self.final_tokens.append(tok)
        return tok

    def barrier(self):
        toks = []
        for e in self.eng:
            if self.cnt[e] > 0:
                toks.append((self.sem[e], self.cnt[e], e + '_bar'))
        for e, pool in self.dpool.items():
            for sem, u in zip(pool['sems'], pool['uses']):
                if u > 0:
                    toks.append((sem, 16 * u, 'dma'))
        for e in self.eng:
            self.pending[e] = list(toks)

    def emit(self):
        nc = self.nc
        fw = []
        for t in self.final_tokens:
            self._need('sp', t, fw)
        with nc.Block() as block:
            def run(e, engine):
                for waits, fn, sem, inc in self.prog[e]:
                    for (s, v) in waits:
                        engine.wait_ge(s, v)
                    fn(engine).then_inc(sem, inc)

            @block.tensor
            def _(eng):
                run('pe', eng)

            @block.scalar
            def _(eng):
                run('act', eng)

            @block.vector
            def _(eng):
                run('dve', eng)

            @block.gpsimd
            def _(eng):
                run('pool', eng)

            @block.sync
            def _(eng):
                run('sp', eng)
                for (s, v) in fw:
                    eng.wait_ge(s, v)


class Alloc:
    def __init__(self, arena, ncols):
        self.a = arena
        self.n = ncols
        self.off = 0

    def get(self, parts, free, dt):
        if isinstance(free, int):
            free = [free]
        nel = int(np.prod(free))
        cols = nel * (1 if dt == BF16 else 2)
        cols = (cols + 1) // 2 * 2
        o = self.off
        self.off += cols
        assert self.off <= self.n, f"arena overflow {self.off} > {self.n}"
        ap = self.a[0:parts, o:o + cols]
        if dt != BF16:
            ap = ap.bitcast(dt)
        if len(free) > 1:
            ds = [f"d{i}" for i in range(len(free))]
            kw = {ds[i]: free[i] for i in range(1, len(free))}
            ap = ap.rearrange(f"p ({' '.join(ds)}) -> p {' '.join(ds)}", **kw)
        return ap


def _tile_consts(nch, C):
    t = np.arange(64)
    ch = t // C
    same = ch[:, None] == ch[None, :]
    tri = (same & (t[:, None] <= t[None, :])).astype(np.float32)
    blk = same.astype(np.float32)
    nmT = np.where(tri > 0, 0.0, NEG).astype(np.float32)
    strict = same & (t[None, :] < t[:, None])
    pmS = np.where(strict, 0.0, -NEG).astype(np.float32)
    cind = np.zeros((64, 16), np.float32)
    cind[t, ch] = 1.0
    return tri, blk, np.tile(nmT, (1, 4)), np.tile(pmS, (1, 4)), cind


CST_COLS = {}


def _build_consts():
    cols = []
    off = [0]

    def add(name, arr):
        a = np.zeros((128, arr.shape[1]), np.float32)
        a[:arr.shape[0]] = arr
        CST_COLS[name] = (off[0], off[0] + arr.shape[1], arr.shape[0])
        off[0] += arr.shape[1]
        cols.append(a)

    add('ident', np.eye(128, dtype=np.float32))
    add('ones', np.ones((128, 128), np.float32))
    for nm, (nch, C) in (('p', (1, 64)), ('s', (16, 4))):
        tri, blk, nmT, pmS, cind = _tile_consts(nch, C)
        add('tri_' + nm, tri)
        add('blk_' + nm, blk)
        add('nmT_' + nm, nmT)
        add('pmS_' + nm, pmS)
        add('cind_' + nm, cind)
    add('iota16', np.tile(np.arange(16, dtype=np.float32)[None, :], (64, 1)))
    add('eps', np.full((128, 1), EPS, np.float32))
    add('one', np.ones((128, 1), np.float32))
    cst = np.concatenate(cols, axis=1)
    t = np.arange(64)
    cm = (t[None, :] // 4 == np.arange(16)[:, None]).astype(np.float32)
    cmf = np.tile(cm.reshape(1, 16 * 64), (128, 1))
    z = np.zeros((64, 64, 128), np.float32)
    z[t, t, :] = 1.0
    z = z.reshape(64, 64 * 128)
    dl = np.tile(np.eye(64, dtype=np.float32).reshape(1, 64 * 64), (128, 1))
    return cst, cmf, z, dl


def build_program():
    cst_np, _, _, _ = _build_consts()
    NCST = cst_np.shape[1]
    nc = bass.Bass("TRN2", target_bir_lowering=False)

    def din(name, shape, dt=F32):
        return nc.dram_tensor(name, shape, dt, kind="ExternalInput").ap()

    def dout(name, shape, dt=F32):
        return nc.dram_tensor(name, shape, dt, kind="ExternalOutput").ap()

    x_d = din("x", [NTOK, 1024])
    p_d = din("p", [NTOK, 256])
    sh_d = din("sh", [16, 4, 128, 128])
    sd_d = din("sd", [16, 4, 128, 128])
    scv_d = din("scv", [48, 1536])
    lbp_d = din("lbp", [2, 512])
    gmix_d = din("gmix", [1, 1024])
    win_d = din("w_in", [1024, 6152])
    convw_d = din("convw", [4, 1536])
    alog_d = din("alog", [1, 4])
    dtb_d = din("dtb", [1, 4])
    gna_d = din("gna", [1, 128])
    gnb_d = din("gnb", [1, 128])
    wbra_d = din("wbra", [512, 1024])
    wbrb_d = din("wbrb", [512, 1024])
    wout_d = din("wout", [1024, 1024])
    gffn_d = din("gffn", [1, 1024])
    wq_d = din("wq", [1024, 2048])
    keysT_d = din("keysT", [16, 128, 128])
    eu_d = din("eu", [16384, 1024])
    ev_d = din("ev", [16384, 1024])
    gple_d = din("gple", [1, 1024])
    wple_d = din("wple", [256, 1024])
    wpg_d = din("wpg", [1024, 1024])
    gfin_d = din("gfin", [1, 1024])
    cst_d = din("cst", [128, NCST])
    cmf_d = din("cmf", [128, 1024])
    zsel_d = din("zsel", [64, 8192])
    dlt_d = din("dlt", [128, 4096])

    y_d = dout("y", [NTOK, 1024])
    hp_d = dout("hp", [4, 128, 128])
    dp_d = dout("dp", [4, 128, 128])
    cp_d = dout("cp", [3, 1536])
    hs_d = dout("hs", [16, 4, 128, 128])
    ds_d = dout("ds", [16, 4, 128, 128])
    cs_d = dout("cs", [48, 1536])
    m1_d = dout("m1s", [NTOK, 1024])
    x1_d = dout("x1s", [NTOK, 1024])

    es = ExitStack()
    with es:
        S = Sched(nc, es)

        def dbg(name, ap, key, ti=0, want=0):
            if name not in DBG or ti != want:
                return
            shp = list(ap.shape)
            dd = nc.dram_tensor("dbg_" + name, shp, ap.dtype, kind="ExternalOutput").ap()
            S.dma('sp', lambda e: e.dma_start(out=dd, in_=ap), reads=[key] if not isinstance(key, list) else key,
                  final=True)
        ARENA = es.enter_context(nc.sbuf_tensor("arena", [128, ARENA_COLS], BF16))
        PB = es.enter_context(nc.psum_tensor("pb", [128, 7 * 512], F32))
        PTt = es.enter_context(nc.psum_tensor("pt", [128, 1024], BF16))
        AL = Alloc(ARENA, ARENA_COLS)

        def bank(j, parts=128, n=512, off=0):
            return PB[0:parts, j * 512 + off:j * 512 + off + n]

        def bk(*js):
            return ['b%d' % j for j in js]

        CST = AL.get(128, NCST, F32)
        S.dma('sp', lambda e: e.dma_start(out=CST, in_=cst_d), writes=['cst'])

        def C_(name, parts=None):
            a, b, r = CST_COLS[name]
            return CST[0:(parts or r), a:b]

        identf = C_('ident')
        onesf = C_('ones')
        epsc = C_('eps')
        identb = AL.get(128, 128, BF16)
        S.op('dve', lambda e: e.tensor_copy(out=identb, in_=identf), reads=['cst'], writes=['identb'])
        TP = dict(nch=1, C=64, tri=C_('tri_p'), blk=C_('blk_p'), nmT=C_('nmT_p'), pmS=C_('pmS_p'),
                  cind=C_('cind_p'))
        TS = dict(nch=16, C=4, tri=C_('tri_s'), blk=C_('blk_s'), nmT=C_('nmT_s'), pmS=C_('pmS_s'),
                  cind=C_('cind_s'))
        cmb = AL.get(128, [16, 64], BF16)
        S.dma('pool', lambda e: e.dma_start(out=cmb, in_=cmf_d.rearrange("p (c t) -> p c t", c=16)),
              writes=['cmb'])
        pers_mark = AL.off

        def load_w_bf16(dst3, src, r0, nk, c0, ncols, key):
            for k in range(nk):
                for cc in range(0, ncols, 2048):
                    w = min(2048, ncols - cc)
                    S.dma('pool', lambda e, k=k, cc=cc, w=w: e.dma_start(
                        out=dst3[:, k, cc:cc + w],
                        in_=src[r0 + k * 128:r0 + (k + 1) * 128, c0 + cc:c0 + cc + w]), writes=[key])

        def bcast_load(dst, src_row, parts, n, key):
            S.dma('sp', lambda e: e.dma_start(out=dst, in_=src_row.to_broadcast([parts, n])), writes=[key])

        def rmsnorm_to_bf16(xt, xkey, gb, gkey, hb, hkey, ss, rs):
            S.op('act', lambda e: e.activation(out=hb, in_=xt, func=AF.Square, accum_out=ss),
                 reads=[xkey], writes=[hkey, 'ss'])
            S.op('act', lambda e: e.activation(out=rs, in_=ss, func=AF.Sqrt, scale=1.0 / 1024, bias=epsc[0:64, :]),
                 reads=['ss', 'cst'], writes=['rs'])
            S.op('dve', lambda e: e.reciprocal(out=rs, in_=rs), reads=['rs'], writes=['rs'])
            S.op('dve', lambda e: e.scalar_tensor_tensor(out=hb, in0=xt, scalar=rs[:, 0:1], in1=gb,
                                                         op0=ALU.mult, op1=ALU.mult),
                 reads=[xkey, 'rs', gkey], writes=[hkey])

        def transpose_tm(src, skey, nblk, dst, dkey, eng='act'):
            for k in range(nblk):
                S.op('pe', lambda e, k=k: e.transpose(out=PTt[:, k * 64:(k + 1) * 64],
                                                      in_=src[:, k * 128:(k + 1) * 128],
                                                      identity=identb[0:64, 0:64]),
                     reads=[skey, 'identb'], writes=['pt'])
            pv = PTt[:, 0:nblk * 64].rearrange("p (k t) -> p k t", k=nblk)
            if eng == 'act':
                S.op('act', lambda e: e.copy(out=dst, in_=pv), reads=['pt'], writes=[dkey])
            else:
                S.op('dve', lambda e: e.tensor_copy(out=dst, in_=pv), reads=['pt'], writes=[dkey])

        def proj_tm(pout, pkeys, hT, hTkey, w3, wkey, c0, ncols, nk=8):
            for k in range(nk):
                S.op('pe', lambda e, k=k: e.matmul(pout, lhsT=hT[:, k, :], rhs=w3[:, k, c0:c0 + ncols],
                                                   start=(k == 0), stop=(k == nk - 1)),
                     reads=[hTkey, wkey], writes=pkeys)

        def head_norm_gate_keys(o_sb, okey, gnb_, gkey, gate_sb, gatekey, sq, sqkey, on, onkey, ogt_, ogtkey, ss4, rs4):
            o3 = o_sb.rearrange("p (h v) -> p h v", h=4)
            S.op('dve', lambda e: e.tensor_tensor(out=sq, in0=o_sb, in1=o_sb, op=ALU.mult),
                 reads=[okey], writes=[sqkey])
            S.op('dve', lambda e: e.reduce_sum(out=ss4, in_=sq.rearrange("p (h v) -> p h v", h=4), axis=AX.X),
                 reads=[sqkey], writes=['ss4'])
            S.op('act', lambda e: e.activation(out=rs4, in_=ss4, func=AF.Sqrt, scale=1.0 / 128, bias=epsc[0:64, :]),
                 reads=['ss4', 'cst'], writes=['rs4'])
            S.op('dve', lambda e: e.reciprocal(out=rs4, in_=rs4), reads=['rs4'], writes=['rs4'])
            on3 = on.rearrange("p (h v) -> p h v", h=4)
            S.op('dve', lambda e: e.tensor_tensor(out=on3, in0=o3, in1=rs4.unsqueeze(2).to_broadcast([64, 4, 128]),
                                                  op=ALU.mult), reads=[okey, 'rs4'], writes=[onkey])
            S.op('dve', lambda e: e.tensor_tensor(out=on3, in0=on3, in1=gnb_.unsqueeze(1).to_broadcast([64, 4, 128]),
                                                  op=ALU.mult), reads=[onkey, gkey], writes=[onkey])
            S.op('dve', lambda e: e.tensor_tensor(out=ogt_, in0=on, in1=gate_sb, op=ALU.mult),
                 reads=[onkey, gatekey], writes=[ogtkey])

        def phase_A1():
            AL.off = pers_mark
            winA = AL.get(128, [8, 2048], BF16)
            wgA = AL.get(128, [8, 1024], BF16)
            wbrA = AL.get(128, [4, 1024], BF16)
            load_w_bf16(winA, win_d, 0, 8, 0, 2048, 'winA')
            load_w_bf16(wgA, win_d, 0, 8, 4104, 1024, 'wgA')
            load_w_bf16(wbrA, wbra_d, 0, 4, 0, 1024, 'wbrA')
            gmixb = AL.get(64, 1024, F32)
            bcast_load(gmixb, gmix_d[0:1, :], 64, 1024, 'gmixb')
            lbb = AL.get(64, 512, F32)
            omlb = AL.get(64, 512, F32)
            lb1 = AL.get(64, 512, F32)
            bcast_load(lbb, lbp_d[0:1, :], 64, 512, 'lbb')
            bcast_load(lb1, lbp_d[1:2, :], 64, 512, 'lb1')
            S.op('dve', lambda e: e.tensor_tensor(out=lbb, in0=lbb, in1=lb1, op=ALU.subtract),
                 reads=['lbb', 'lb1'], writes=['lbb'])
            S.op('act', lambda e: e.activation(out=lbb, in_=lbb, func=AF.Sigmoid), reads=['lbb'], writes=['lbb'])
            S.op('dve', lambda e: e.tensor_scalar(out=omlb, in0=lbb, scalar1=-1.0, scalar2=1.0, op0=ALU.mult,
                                                  op1=ALU.add), reads=['lbb'], writes=['omlb'])
            gnab = AL.get(64, 128, F32)
            bcast_load(gnab, gna_d[0:1, :], 64, 128, 'gnab')
            xt = AL.get(64, 1024, F32)
            hb = AL.get(64, 1024, BF16)
            hT = AL.get(128, [8, 64], BF16)
            ss = AL.get(64, 1, F32)
            rs = AL.get(64, 1, F32)
            F = [AL.get(64, 512, F32) for _ in range(7)]
            qt = AL.get(64, 512, BF16)
            kt = AL.get(64, 512, BF16)
            va = AL.get(64, 512, BF16)
            km = AL.get(64, 512, BF16)
            qkT = AL.get(128, [8, 64], BF16)
            qm = AL.get(128, [4, 64], BF16)
            attm = AL.get(64, [4, 64], BF16)
            ebL = AL.get(128, [4, 16], F32)
            Sa = AL.get(128, [4, 128], F32)
            Sab = AL.get(128, [4, 128], BF16)
            ss4 = AL.get(64, 4, F32)
            rs4 = AL.get(64, 4, F32)
            ogt = AL.get(64, 512, BF16)
            oT = AL.get(128, [4, 64], BF16)
            m1t = AL.get(64, 1024, F32)
            S.op('pool', lambda e: e.memset(Sa, 0.0), writes=['Sa'])
            S.op('pool', lambda e: e.memset(Sab, 0.0), writes=['Sab'])

            def tileA1(ti, T):
                nch = T['nch']
                r0 = ti * 64
                S.dma('sp', lambda e: e.dma_start(out=xt, in_=x_d[r0:r0 + 64, :]), writes=['xt'])
                rmsnorm_to_bf16(xt, 'xt', gmixb, 'gmixb', hb, 'hb', ss, rs)
                dbg('xt', xt, 'xt', ti)
                dbg('rs', rs, 'rs', ti)
                dbg('hb', hb, 'hb', ti)
                transpose_tm(hb, 'hb', 8, hT, 'hT')
                dbg('hT', hT, 'hT', ti)
                dbg('winA', winA[:, :, 0:512], 'winA', ti)
                sig, q, kk, logf, eb, enb, og = F
                proj_tm(bank(0, 64), bk(0), hT, 'hT', winA, 'winA', 512, 512)
                S.op('act', lambda e: e.activation(out=sig, in_=bank(0, 64), func=AF.Sigmoid), reads=bk(0), writes=['F0'])
                proj_tm(bank(1, 64), bk(1), hT, 'hT', winA, 'winA', 0, 512)
                S.op('act', lambda e: e.activation(out=q, in_=bank(1, 64), func=AF.Silu), reads=bk(1), writes=['F1'])
                S.op('dve', lambda e: e.tensor_tensor(out=sig, in0=sig, in1=omlb, op=ALU.mult),
                     reads=['F0', 'omlb'], writes=['F0'])
                S.op('dve', lambda e: e.tensor_tensor(out=sig, in0=sig, in1=lbb, op=ALU.add),
                     reads=['F0', 'lbb'], writes=['F0'])
                S.op('dve', lambda e: e.tensor_scalar(out=kk, in0=sig, scalar1=-1.0, scalar2=1.0, op0=ALU.mult,
                                                      op1=ALU.add), reads=['F0'], writes=['F2'])
                S.op('act', lambda e: e.activation(out=logf, in_=sig, func=AF.Ln), reads=['F0'], writes=['F3'])
                dbg('q', q, 'F1', ti)
                dbg('f', sig, 'F0', ti)
                dbg('logf', logf, 'F3', ti)
                S.op('pe', lambda e: e.matmul(bank(2, 64), lhsT=T['tri'], rhs=logf, start=True, stop=True),
                     reads=['cst', 'F3'], writes=bk(2))
                S.op('act', lambda e: e.activation(out=eb, in_=bank(2, 64), func=AF.Exp), reads=bk(2), writes=['F4'])
                S.op('act', lambda e: e.activation(out=enb, in_=bank(2, 64), func=AF.Exp, scale=-1.0),
                     reads=bk(2), writes=['F5'])
                S.op('dve', lambda e: e.tensor_tensor(out=qt, in0=q, in1=eb, op=ALU.mult),
                     reads=['F1', 'F4'], writes=['qt'])
                S.op('dve', lambda e: e.tensor_tensor(out=kt, in0=kk, in1=enb, op=ALU.mult),
                     reads=['F2', 'F5'], writes=['kt'])
                for h in range(4):
                    S.op('pe', lambda e, h=h: e.matmul(bank(0, 128, nch, h * 16), lhsT=logf[:, h * 128:(h + 1) * 128],
                                                       rhs=T['cind'][:, 0:nch], start=True, stop=True),
                         reads=['F3', 'cst'], writes=bk(0))
                S.op('act', lambda e: e.activation(out=ebL[:, :, 0:nch],
                                                   in_=bank(0, 128, 64).rearrange("p (h c) -> p h c", h=4)[:, :, 0:nch],
                                                   func=AF.Exp), reads=bk(0), writes=['ebL'])
                proj_tm(bank(1, 64), bk(1), hT, 'hT', winA, 'winA', 1024, 512)
                S.op('act', lambda e: e.copy(out=va, in_=bank(1, 64)), reads=bk(1), writes=['va'])
                proj_tm(bank(2, 64), bk(2), hT, 'hT', winA, 'winA', 1536, 512)
                S.op('act', lambda e: e.activation(out=og, in_=bank(2, 64), func=AF.Silu), reads=bk(2), writes=['F6'])
                for h in range(4):
                    S.op('pe', lambda e, h=h: e.transpose(out=PTt[:, h * 64:(h + 1) * 64], in_=qt[:, h * 128:(h + 1) * 128],
                                                          identity=identb[0:64, 0:64]),
                         reads=['qt', 'identb'], writes=['pt'])
                for h in range(4):
                    S.op('pe', lambda e, h=h: e.transpose(out=PTt[:, (4 + h) * 64:(5 + h) * 64],
                                                          in_=kt[:, h * 128:(h + 1) * 128], identity=identb[0:64, 0:64]),
                         reads=['kt', 'identb'], writes=['pt'])
                S.op('act', lambda e: e.copy(out=qkT, in_=PTt[:, 0:512].rearrange("p (k t) -> p k t", k=8)),
                     reads=['pt'], writes=['qkT'])
                for h in range(4):
                    S.op('pe', lambda e, h=h: e.matmul(bank(0, 64, 64, h * 64), lhsT=qkT[:, 4 + h, :], rhs=qkT[:, h, :],
                                                       start=True, stop=True), reads=['qkT'], writes=bk(0))
                S.op('dve', lambda e: e.tensor_tensor(out=attm, in0=bank(0, 64, 256).rearrange("p (h t) -> p h t", h=4),
                                                      in1=T['tri'].unsqueeze(1).to_broadcast([64, 4, 64]), op=ALU.mult),
                     reads=bk(0) + ['cst'], writes=['attm'])
                for h in range(4):
                    S.op('pe', lambda e, h=h: e.matmul(bank(5, 64, 128, h * 128), lhsT=attm[:, h, :],
                                                       rhs=va[:, h * 128:(h + 1) * 128], start=(h == 0), stop=False,
                                                       skip_group_check=True),
                         reads=['attm', 'va'], writes=bk(5))
                for c in range(nch):
                    if nch > 1:
                        S.dma('sp', lambda e, c=c: e.dma_start(out=Sa, in_=sh_d[c].rearrange("h d v -> d h v")),
                              writes=['Sa'])
                        S.op('act', lambda e: e.copy(out=Sab, in_=Sa), reads=['Sa'], writes=['Sab'])
                        S.op('dve', lambda e, c=c: e.tensor_tensor(
                            out=qm, in0=qkT[:, 0:4, :], in1=cmb[:, c, :].unsqueeze(1).to_broadcast([128, 4, 64]),
                            op=ALU.mult), reads=['qkT', 'cmb'], writes=['qm'])
                        S.op('dve', lambda e, c=c: e.tensor_scalar(out=km, in0=kt, scalar1=T['cind'][:, c:c + 1],
                                                                   scalar2=None, op0=ALU.mult),
                             reads=['kt', 'cst'], writes=['km'])
                        qsrc, qkey, ksrc, kkey = qm, 'qm', km, 'km'
                    else:
                        qsrc, qkey, ksrc, kkey = qkT, 'qkT', kt, 'kt'
                    for h in range(4):
                        S.op('pe', lambda e, h=h, qsrc=qsrc, c=c: e.matmul(
                            bank(5, 64, 128, h * 128), lhsT=qsrc[:, h, :], rhs=Sab[:, h, :],
                            start=False, stop=(c == nch - 1), skip_group_check=True), reads=[qkey, 'Sab'], writes=bk(5))
                    for h in range(4):
                        S.op('pe', lambda e, h=h, ksrc=ksrc: e.matmul(
                            bank(6, 128, 128, h * 128), lhsT=ksrc[:, h * 128:(h + 1) * 128],
                            rhs=va[:, h * 128:(h + 1) * 128], start=True, stop=True),
                            reads=[kkey, 'va'], writes=bk(6))
                    S.op('dve', lambda e: e.tensor_tensor(out=Sa, in0=bank(6).rearrange("p (h v) -> p h v", h=4),
                                                          in1=Sa, op=ALU.add), reads=bk(6) + ['Sa'], writes=['Sa'])
                    S.op('dve', lambda e, c=c: e.tensor_tensor(
                        out=Sa, in0=Sa, in1=ebL[:, :, c:c + 1].to_broadcast([128, 4, 128]), op=ALU.mult),
                        reads=['Sa', 'ebL'], writes=['Sa'])
                    if nch > 1:
                        S.dma('sp', lambda e, c=c: e.dma_start(out=hs_d[c].rearrange("h d v -> d h v"), in_=Sa),
                              reads=['Sa'], final=True)
                    else:
                        S.op('act', lambda e: e.copy(out=Sab, in_=Sa), reads=['Sa'], writes=['Sab'])
                if nch == 1 and ti == NPT - 1:
                    S.dma('sp', lambda e: e.dma_start(out=hp_d.rearrange("h d v -> d h v"), in_=Sa),
                          reads=['Sa'], final=True)
                osb, sq, on = F[0], F[1], F[2]
                S.op('act', lambda e: e.copy(out=osb, in_=bank(5, 64)), reads=bk(5), writes=['F0'])
                dbg('osb', osb, 'F0', ti, DBGT)
                dbg('og', og, 'F6', ti, DBGT)
                head_norm_gate_keys(osb, 'F0', gnab, 'gnab', og, 'F6', sq, 'F1', on, 'F2', ogt, 'ogt', ss4, rs4)
                dbg('on', on, 'F2', ti, DBGT)
                transpose_tm(ogt, 'ogt', 4, oT, 'oT')
                for half in range(2):
                    for k in range(4):
                        S.op('pe', lambda e, k=k, half=half: e.matmul(
                            bank(3 + half, 64), lhsT=oT[:, k, :], rhs=wbrA[:, k, half * 512:(half + 1) * 512],
                            start=(k == 0), stop=(k == 3)), reads=['oT', 'wbrA'], writes=bk(3 + half))
                    proj_tm(bank(half, 64), bk(half), hT, 'hT', wgA, 'wgA', half * 512, 512)
                    sg = F[3 + half]
                    S.op('act', lambda e, half=half, sg=sg: e.activation(out=sg, in_=bank(half, 64), func=AF.Sigmoid),
                         reads=bk(half), writes=['F%d' % (3 + half)])
                    S.op('dve', lambda e, half=half, sg=sg: e.tensor_tensor(
                        out=m1t[:, half * 512:(half + 1) * 512], in0=bank(3 + half, 64), in1=sg, op=ALU.mult),
                        reads=bk(3 + half) + ['F%d' % (3 + half)], writes=['m1t'])
                S.dma('sp', lambda e: e.dma_start(out=m1_d[r0:r0 + 64, :], in_=m1t), reads=['m1t'],
                      writes=[('m1', ti)])

            if 'A1' in PHASES:
                for ti in (range(NPT) if TILES is None else TILES):
                    tileA1(ti, TP)
                tileA1(NPT, TS)
        phase_A1()
        S.barrier()

        def phase_A2():
            AL.off = pers_mark
            winB = AL.get(128, [8, 2056], BF16)
            wgB = AL.get(128, [8, 1024], BF16)
            wbrB = AL.get(128, [4, 1024], BF16)
            woutb = AL.get(128, [8, 1024], BF16)
            load_w_bf16(winB, win_d, 0, 8, 2048, 2056, 'winB')
            load_w_bf16(wgB, win_d, 0, 8, 5128, 1024, 'wgB')
            load_w_bf16(wbrB, wbrb_d, 0, 4, 0, 1024, 'wbrB')
            load_w_bf16(woutb, wout_d, 0, 8, 0, 1024, 'woutb')
            gmixb = AL.get(64, 1024, F32)
            bcast_load(gmixb, gmix_d[0:1, :], 64, 1024, 'gmixb')
            gnbb = AL.get(64, 128, F32)
            bcast_load(gnbb, gnb_d[0:1, :], 64, 128, 'gnbb')
            negA = AL.get(64, 4, F32)
            dtbb = AL.get(64, 4, F32)
            bcast_load(negA, alog_d[0:1, :], 64, 4, 'negA')
            bcast_load(dtbb, dtb_d[0:1, :], 64, 4, 'dtbb')
            S.op('act', lambda e: e.activation(out=negA, in_=negA, func=AF.Exp), reads=['negA'], writes=['negA'])
            S.op('dve', lambda e: e.tensor_scalar(out=negA, in0=negA, scalar1=-1.0, scalar2=None, op0=ALU.mult),
                 reads=['negA'], writes=['negA'])
            cwin = AL.get(4, 1536, F32)
            cw = AL.get(128, [12, 4], F32)
            S.dma('sp', lambda e: e.dma_start(out=cwin, in_=convw_d), writes=['cwin'])
            for g in range(12):
                S.op('pe', lambda e, g=g: e.transpose(out=bank(0, 128, 4, g * 4), in_=cwin[0:4, g * 128:(g + 1) * 128],
                                                      identity=identf[0:4, 0:4]), reads=['cwin', 'cst'], writes=bk(0))
            S.op('dve', lambda e: e.tensor_copy(out=cw, in_=bank(0, 128, 48).rearrange("p (g j) -> p g j", g=12)),
                 reads=bk(0), writes=['cw'])
            xt = AL.get(64, 1024, F32)
            hb = AL.get(64, 1024, BF16)
            hT = AL.get(128, [8, 64], BF16)
            ss = AL.get(64, 1, F32)
            rs = AL.get(64, 1, F32)
            F = [AL.get(64, 512, F32) for _ in range(6)]
            rawext = AL.get(128, 12 * 112, F32)
            acc = AL.get(128, [12, 64], F32)
            sqn = AL.get(128, 512, F32)
            qkn = AL.get(128, [8, 64], BF16)
            qknm = AL.get(128, [8, 64], BF16)
            vcb = AL.get(128, [4, 64], BF16)
            vk = AL.get(64, [8, 128], BF16)
            sm = AL.get(64, 64, F32)
            beta, zz, ez, spl, gg, gcum, gLb, eg, ngcum, dkk, kdec, nbeta, nbe = [sm[:, i * 4:(i + 1) * 4] for i in range(13)]
            gc = AL.get(64, [4, 16], F32)
            egLT = AL.get(128, [4, 16], F32)
            gd = AL.get(64, [4, 64], F32)
            gam = AL.get(64, [4, 64], F32)
            gamT = AL.get(64, [4, 64], F32)
            Nb = [AL.get(64, [4, 64], BF16) for _ in range(2)]
            Mb = [AL.get(64, [4, 64], BF16) for _ in range(2)]
            Qb = AL.get(64, [4, 64], BF16)
            Qf = AL.get(64, [4, 64], F32)
            rb = AL.get(64, [4, 128], BF16)
            ub = AL.get(64, [4, 128], BF16)
            ATb = AL.get(64, [4, 64], BF16)
            khat = AL.get(64, [4, 128], BF16)
            khm = AL.get(64, [4, 128], BF16)
            Sd = AL.get(128, [4, 128], F32)
            Sdb = AL.get(128, [4, 128], BF16)
            ss4 = AL.get(64, 4, F32)
            rs4 = AL.get(64, 4, F32)
            ogt = AL.get(64, 512, BF16)
            oT = AL.get(128, [4, 64], BF16)
            m1t = AL.get(64, 1024, F32)
            mb = AL.get(64, 1024, BF16)
            mT = AL.get(128, [8, 64], BF16)
            S.op('pool', lambda e: e.memset(Sd, 0.0), writes=['Sd'])
            S.op('pool', lambda e: e.memset(Sdb, 0.0), writes=['Sdb'])
            S.op('pool', lambda e: e.memset(rawext, 0.0), writes=['rawext'])

            def tileA2(ti, T):
                nch, C = T['nch'], T['C']
                r0 = ti * 64
                W = 3 + C
                rx = rawext[:, 0:12 * nch * W].rearrange("p (g s w) -> p g s w", g=12, s=nch)
                S.dma('sp', lambda e: e.dma_start(out=xt, in_=x_d[r0:r0 + 64, :]), writes=['xt'])
                S.dma('sp', lambda e: e.dma_start(out=m1t, in_=m1_d[r0:r0 + 64, :]), reads=[('m1', ti)], writes=['m1t'])
                rmsnorm_to_bf16(xt, 'xt', gmixb, 'gmixb', hb, 'hb', ss, rs)
                transpose_tm(hb, 'hb', 8, hT, 'hT')
                praw = PB[:, 3 * 512:3 * 512 + 768].rearrange("p (g t) -> p g t", g=12)
                for g in range(12):
                    for k in range(8):
                        S.op('pe', lambda e, g=g, k=k: e.matmul(praw[:, g, :], lhsT=winB[:, k, g * 128:(g + 1) * 128],
                                                                rhs=hT[:, k, :], start=(k == 0), stop=(k == 7)),
                             reads=['winB', 'hT'], writes=bk(3, 4))
                if nch == 1:
                    if ti > 0:
                        S.op('pool', lambda e: e.tensor_copy(out=rx[:, :, 0, 0:3], in_=rx[:, :, 0, 64:67]),
                             reads=['rawext'], writes=['rawext'])
                    S.op('act', lambda e: e.copy(out=rx[:, :, 0, 3:67], in_=praw), reads=bk(3, 4), writes=['rawext'])
                else:
                    cvin = F[0:3]
                    for i3 in range(3):
                        S.dma('sp', lambda e, i3=i3: e.dma_start(out=cvin[i3][0:48, :],
                                                                 in_=scv_d[:, i3 * 512:(i3 + 1) * 512]),
                              writes=['F%d' % i3])
                    for g in range(12):
                        S.op('pe', lambda e, g=g: e.transpose(
                            out=bank(0, 128, 48, g * 64), in_=cvin[g // 4][0:48, (g % 4) * 128:(g % 4 + 1) * 128],
                            identity=identf[0:48, 0:48]), reads=['F%d' % (g // 4), 'cst'], writes=bk(0, 1))
                    S.op('dve', lambda e: e.tensor_copy(
                        out=rx[:, :, :, 0:3],
                        in_=PB[:, 0:768].rearrange("p (g x) -> p g x", g=12)[:, :, 0:48].rearrange(
                            "p g (s j) -> p g s j", s=16)),
                        reads=bk(0, 1), writes=['rawext'])
                    S.op('act', lambda e: e.copy(out=rx[:, :, :, 3:7],
                                                 in_=praw.rearrange("p g (s t) -> p g s t", s=16)),
                         reads=bk(3, 4), writes=['rawext'])
                accv = acc.rearrange("p g (s t) -> p g s t", s=nch)
                for g in range(12):
                    S.op('dve', lambda e, g=g: e.tensor_scalar(out=accv[:, g], in0=rx[:, g, :, 0:C],
                                                               scalar1=cw[:, g, 0:1], scalar2=None, op0=ALU.mult),
                         reads=['rawext', 'cw'], writes=[('acc', g)])
                    for j in range(1, 4):
                        S.op('dve', lambda e, g=g, j=j: e.scalar_tensor_tensor(
                            out=accv[:, g], in0=rx[:, g, :, j:j + C], scalar=cw[:, g, j:j + 1], in1=accv[:, g],
                            op0=ALU.mult, op1=ALU.add), reads=['rawext', 'cw', ('acc', g)], writes=[('acc', g)])
                acck = [('acc', g) for g in range(12)]
                S.op('act', lambda e: e.activation(out=acc, in_=acc, func=AF.Silu), reads=acck, writes=acck)
                S.op('act', lambda e: e.activation(out=sqn, in_=acc[:, 0:8, :].rearrange("p g t -> p (g t)"),
                                                   func=AF.Square), reads=acck, writes=['sqn'])
                S.op('pe', lambda e: e.matmul(bank(0), lhsT=onesf, rhs=sqn, start=True, stop=True),
                     reads=['cst', 'sqn'], writes=bk(0))
                S.op('act', lambda e: e.activation(out=sqn, in_=bank(0), func=AF.Sqrt, bias=epsc), reads=bk(0) + ['cst'],
                     writes=['sqn'])
                S.op('dve', lambda e: e.reciprocal(out=sqn, in_=sqn), reads=['sqn'], writes=['sqn'])
                sq3 = sqn.rearrange("p (g t) -> p g t", g=8)
                S.op('dve', lambda e: e.scalar_tensor_tensor(out=qkn[:, 0:4, :], in0=acc[:, 0:4, :], scalar=128.0 ** -0.5,
                                                             in1=sq3[:, 0:4, :], op0=ALU.mult, op1=ALU.mult),
                     reads=acck + ['sqn'], writes=['qkn'])
                S.op('dve', lambda e: e.tensor_tensor(out=qkn[:, 4:8, :], in0=acc[:, 4:8, :], in1=sq3[:, 4:8, :],
                                                      op=ALU.mult), reads=acck + ['sqn'], writes=['qkn'])
                S.op('act', lambda e: e.copy(out=vcb, in_=acc[:, 8:12, :]), reads=acck, writes=['vcb'])
                for h in range(4):
                    S.op('pe', lambda e, h=h: e.transpose(out=PTt[0:64, h * 128:(h + 1) * 128], in_=vcb[:, h, :],
                                                          identity=identb), reads=['vcb', 'identb'], writes=['pt'])
                for h in range(4):
                    S.op('pe', lambda e, h=h: e.transpose(out=PTt[0:64, (4 + h) * 128:(5 + h) * 128], in_=qkn[:, 4 + h, :],
                                                          identity=identb), reads=['qkn', 'identb'], writes=['pt'])
                S.op('act', lambda e: e.copy(out=vk, in_=PTt[0:64, :].rearrange("p (k v) -> p k v", k=8)),
                     reads=['pt'], writes=['vk'])
                proj_tm(bank(1, 64, 8), bk(1), hT, 'hT', winB, 'winB', 2048, 8)
                S.op('act', lambda e: e.activation(out=beta, in_=bank(1, 64, 4), func=AF.Sigmoid), reads=bk(1), writes=['sm'])
                S.op('dve', lambda e: e.tensor_tensor(out=zz, in0=bank(1, 64, 4, 4), in1=dtbb, op=ALU.add),
                     reads=bk(1) + ['dtbb'], writes=['sm'])
                S.op('act', lambda e: e.activation(out=ez, in_=zz, func=AF.Exp), reads=['sm'], writes=['sm'])
                S.op('act', lambda e: e.activation(out=spl, in_=ez, func=AF.Ln, bias=C_('one', 64)), reads=['sm', 'cst'],
                     writes=['sm'])
                S.op('dve', lambda e: e.tensor_tensor(out=gg, in0=spl, in1=negA, op=ALU.mult), reads=['sm', 'negA'],
                     writes=['sm'])
                S.op('pe', lambda e: e.matmul(bank(2, 64, 4), lhsT=T['tri'], rhs=gg, start=True, stop=True),
                     reads=['cst', 'sm'], writes=bk(2))
                S.op('pe', lambda e: e.matmul(bank(2, 64, 4, 4), lhsT=T['blk'], rhs=gg, start=True, stop=True),
                     reads=['cst', 'sm'], writes=bk(2))
                S.op('dve', lambda e: e.tensor_copy(out=sm[:, 20:28], in_=bank(2, 64, 8)), reads=bk(2), writes=['sm'])
                S.op('act', lambda e: e.activation(out=eg, in_=gcum, func=AF.Exp), reads=['sm'], writes=['sm'])
                S.op('dve', lambda e: e.tensor_scalar(out=ngcum, in0=gcum, scalar1=-1.0, scalar2=None, op0=ALU.mult),
                     reads=['sm'], writes=['sm'])
                S.op('dve', lambda e: e.tensor_tensor(out=dkk, in0=gLb, in1=gcum, op=ALU.subtract), reads=['sm'],
                     writes=['sm'])
                S.op('act', lambda e: e.activation(out=kdec, in_=dkk, func=AF.Exp), reads=['sm'], writes=['sm'])
                S.op('dve', lambda e: e.tensor_scalar(out=nbeta, in0=beta, scalar1=-1.0, scalar2=None, op0=ALU.mult),
                     reads=['sm'], writes=['sm'])
                S.op('dve', lambda e: e.tensor_tensor(out=nbe, in0=nbeta, in1=eg, op=ALU.mult), reads=['sm'],
                     writes=['sm'])
                S.op('dve', lambda e: e.tensor_tensor(out=gc[:, :, 0:nch], in0=gg.unsqueeze(2).to_broadcast([64, 4, nch]),
                                                      in1=T['cind'][:, 0:nch].unsqueeze(1).to_broadcast([64, 4, nch]),
                                                      op=ALU.mult), reads=['sm', 'cst'], writes=['gc'])
                for h in range(4):
                    S.op('pe', lambda e, h=h: e.matmul(bank(1, 128, nch, 64 + h * 16), lhsT=onesf[0:64, :],
                                                       rhs=gc[:, h, 0:nch], start=True, stop=True),
                         reads=['cst', 'gc'], writes=bk(1))
                S.op('act', lambda e: e.activation(
                    out=egLT[:, :, 0:nch], in_=bank(1, 128, 64, 64).rearrange("p (h c) -> p h c", h=4)[:, :, 0:nch],
                    func=AF.Exp), reads=bk(1), writes=['egLT'])
                S.op('dve', lambda e: e.tensor_tensor(out=gd, in0=gcum.unsqueeze(2).to_broadcast([64, 4, 64]),
                                                      in1=identf[0:64, 0:64].unsqueeze(1).to_broadcast([64, 4, 64]),
                                                      op=ALU.mult), reads=['sm', 'cst'], writes=['gd'])
                gd2 = gd.rearrange("p h t -> p (h t)")
                S.op('pe', lambda e: e.matmul(bank(2, 64, 256), lhsT=onesf[0:64, 0:64], rhs=gd2, start=True, stop=False),
                     reads=['cst', 'gd'], writes=bk(2))
                S.op('pe', lambda e: e.matmul(bank(2, 64, 256), lhsT=identf[0:64, 0:64], rhs=T['pmS'], start=False,
                                              stop=True), reads=['cst'], writes=bk(2))
                S.op('pe', lambda e: e.matmul(bank(2, 64, 256, 256), lhsT=onesf[0:64, 0:64], rhs=gd2, start=True,
                                              stop=False), reads=['cst', 'gd'], writes=bk(2))
                S.op('pe', lambda e: e.matmul(bank(2, 64, 256, 256), lhsT=identf[0:64, 0:64], rhs=T['nmT'], start=False,
                                              stop=True), reads=['cst'], writes=bk(2))
                for h in range(4):
                    S.op('act', lambda e, h=h: e.activation(out=gam[:, h, :], in_=bank(2, 64, 64, h * 64), func=AF.Exp,
                                                            scale=-1.0, bias=gcum[:, h:h + 1]),
                         reads=bk(2) + ['sm'], writes=['gam'])
                for h in range(4):
                    S.op('act', lambda e, h=h: e.activation(out=gamT[:, h, :], in_=bank(2, 64, 64, 256 + h * 64),
                                                            func=AF.Exp, bias=ngcum[:, h:h + 1]),
                         reads=bk(2) + ['sm'], writes=['gamT'])
                for h in range(4):
                    S.op('pe', lambda e, h=h: e.matmul(bank(0, 64, 64, h * 64), lhsT=qkn[:, 4 + h, :], rhs=qkn[:, 4 + h, :],
                                                       start=True, stop=True), reads=['qkn'], writes=bk(0))
                for h in range(4):
                    S.op('dve', lambda e, h=h: e.scalar_tensor_tensor(
                        out=Nb[0][:, h, :], in0=bank(0, 64, 64, h * 64), scalar=nbeta[:, h:h + 1], in1=gam[:, h, :],
                        op0=ALU.mult, op1=ALU.mult), reads=bk(0) + ['sm', 'gam'], writes=['Nb0'])
                for h in range(4):
                    S.op('pe', lambda e, h=h: e.transpose(out=PTt[0:64, h * 64:(h + 1) * 64], in_=Nb[0][:, h, :],
                                                          identity=identb[0:64, 0:64]),
                         reads=['Nb0', 'identb'], writes=['pt'])
                ptv = PTt[0:64, 0:256].rearrange("p (h t) -> p h t", h=4)
                S.op('dve', lambda e: e.tensor_copy(out=Mb[0], in_=ptv), reads=['pt'], writes=['Mb0'])
                S.op('dve', lambda e: e.tensor_tensor(out=Qf, in0=ptv,
                                                      in1=identf[0:64, 0:64].unsqueeze(1).to_broadcast([64, 4, 64]),
                                                      op=ALU.add), reads=['pt', 'cst'], writes=['Qf'])
                S.op('act', lambda e: e.copy(out=Qb, in_=Qf), reads=['Qf'], writes=['Qb'])
                nsteps = {64: 5, 4: 1}[C]
                cur = 0
                for i in range(nsteps):
                    last = (i == nsteps - 1)
                    nxt = 1 - cur
                    for h in range(4):
                        S.op('pe', lambda e, h=h, cur=cur: e.matmul(bank(0, 64, 64, h * 64), lhsT=Mb[cur][:, h, :],
                                                                    rhs=Nb[cur][:, h, :], start=True, stop=True),
                             reads=['Mb%d' % cur, 'Nb%d' % cur], writes=bk(0))
                    if not last:
                        for h in range(4):
                            S.op('pe', lambda e, h=h, cur=cur: e.matmul(bank(1, 64, 64, h * 64), lhsT=Nb[cur][:, h, :],
                                                                        rhs=Mb[cur][:, h, :], start=True, stop=True),
                                 reads=['Mb%d' % cur, 'Nb%d' % cur], writes=bk(1))
                    S.op('act', lambda e, nxt=nxt: e.copy(out=Nb[nxt],
                                                          in_=bank(0, 64, 256).rearrange("p (h t) -> p h t", h=4)),
                         reads=bk(0), writes=['Nb%d' % nxt])
                    if not last:
                        S.op('dve', lambda e, nxt=nxt: e.tensor_copy(
                            out=Mb[nxt], in_=bank(1, 64, 256).rearrange("p (h t) -> p h t", h=4)),
                            reads=bk(1), writes=['Mb%d' % nxt])
                    for h in range(4):
                        S.op('pe', lambda e, h=h, nxt=nxt: e.matmul(bank(2, 64, 64, h * 64), lhsT=Nb[nxt][:, h, :],
                                                                    rhs=Qb[:, h, :], start=True, stop=True),
                             reads=['Nb%d' % nxt, 'Qb'], writes=bk(2))
                    S.op('dve', lambda e: e.tensor_tensor(out=Qf, in0=Qf,
                                                          in1=bank(2, 64, 256).rearrange("p (h t) -> p h t", h=4),
                                                          op=ALU.add), reads=['Qf'] + bk(2), writes=['Qf'])
                    S.op('act', lambda e: e.copy(out=Qb, in_=Qf), reads=['Qf'], writes=['Qb'])
                    cur = nxt
                for h in range(4):
                    S.op('pe', lambda e, h=h: e.matmul(bank(0, 64, 64, h * 64), lhsT=qkn[:, 4 + h, :], rhs=qkn[:, h, :],
                                                       start=True, stop=True), reads=['qkn'], writes=bk(0))
                S.op('dve', lambda e: e.tensor_tensor(out=ATb, in0=bank(0, 64, 256).rearrange("p (h t) -> p h t", h=4),
                                                      in1=gamT, op=ALU.mult), reads=bk(0) + ['gamT'], writes=['ATb'])
                for c in range(nch):
                    if nch > 1:
                        S.dma('sp', lambda e, c=c: e.dma_start(out=Sd, in_=sd_d[c].rearrange("h d v -> d h v")),
                              writes=['Sd'])
                        S.op('act', lambda e: e.copy(out=Sdb, in_=Sd), reads=['Sd'], writes=['Sdb'])
                        S.op('dve', lambda e, c=c: e.tensor_tensor(
                            out=qknm, in0=qkn, in1=cmb[:, c, :].unsqueeze(1).to_broadcast([128, 8, 64]), op=ALU.mult),
                            reads=['qkn', 'cmb'], writes=['qknm'])
                        src, skey = qknm, 'qknm'
                    else:
                        src, skey = qkn, 'qkn'
                    for h in range(4):
                        S.op('pe', lambda e, h=h, src=src, c=c: e.matmul(
                            bank(5, 64, 128, h * 128), lhsT=src[:, 4 + h, :], rhs=Sdb[:, h, :], start=(c == 0 and h == 0),
                            stop=(c == nch - 1), skip_group_check=True), reads=[skey, 'Sdb'], writes=bk(5))
                        S.op('pe', lambda e, h=h, src=src, c=c: e.matmul(
                            bank(6, 64, 128, h * 128), lhsT=src[:, h, :], rhs=Sdb[:, h, :], start=(c == 0 and h == 0),
                            stop=(c == nch - 1), skip_group_check=True), reads=[skey, 'Sdb'], writes=bk(6))
                t1, bv, t2, ob = F[3], F[4], F[3], F[4]
                S.op('dve', lambda e: e.tensor_tensor(out=t1.rearrange("p (h v) -> p h v", h=4),
                                                      in0=bank(5, 64).rearrange("p (h v) -> p h v", h=4),
                                                      in1=nbe.unsqueeze(2).to_broadcast([64, 4, 128]), op=ALU.mult),
                     reads=bk(5) + ['sm'], writes=['F3'])
                S.op('dve', lambda e: e.tensor_tensor(out=bv.rearrange("p (h v) -> p h v", h=4), in0=vk[:, 0:4, :],
                                                      in1=beta.unsqueeze(2).to_broadcast([64, 4, 128]), op=ALU.mult),
                     reads=['vk', 'sm'], writes=['F4'])
                S.op('dve', lambda e: e.tensor_tensor(out=rb.rearrange("p h v -> p (h v)"), in0=t1, in1=bv, op=ALU.add),
                     reads=['F3', 'F4'], writes=['rb'])
                for h in range(4):
                    S.op('pe', lambda e, h=h: e.matmul(bank(1, 64, 128, h * 128), lhsT=Qb[:, h, :], rhs=rb[:, h, :],
                                                       start=True, stop=True), reads=['Qb', 'rb'], writes=bk(1))
                S.op('act', lambda e: e.copy(out=ub, in_=bank(1, 64).rearrange("p (h v) -> p h v", h=4)),
                     reads=bk(1), writes=['ub'])
                for h in range(4):
                    S.op('pe', lambda e, h=h: e.matmul(bank(2, 64, 128, h * 128), lhsT=ATb[:, h, :], rhs=ub[:, h, :],
                                                       start=True, stop=True), reads=['ATb', 'ub'], writes=bk(2))
                S.op('dve', lambda e: e.tensor_tensor(out=t2.rearrange("p (h v) -> p h v", h=4),
                                                      in0=bank(6, 64).rearrange("p (h v) -> p h v", h=4),
                                                      in1=eg.unsqueeze(2).to_broadcast([64, 4, 128]), op=ALU.mult),
                     reads=bk(6) + ['sm'], writes=['F3'])
                S.op('dve', lambda e: e.tensor_tensor(out=ob, in0=t2, in1=bank(2, 64), op=ALU.add),
                     reads=['F3'] + bk(2), writes=['F4'])
                S.op('dve', lambda e: e.tensor_tensor(out=khat, in0=vk[:, 4:8, :],
                                                      in1=kdec.unsqueeze(2).to_broadcast([64, 4, 128]), op=ALU.mult),
                     reads=['vk', 'sm'], writes=['khat'])
                for c in range(nch):
                    if nch > 1:
                        S.dma('sp', lambda e, c=c: e.dma_start(out=Sd, in_=sd_d[c].rearrange("h d v -> d h v")),
                              writes=['Sd'])
                        S.op('dve', lambda e, c=c: e.tensor_scalar(out=khm, in0=khat, scalar1=T['cind'][:, c:c + 1],
                                                                   scalar2=None, op0=ALU.mult),
                             reads=['khat', 'cst'], writes=['khm'])
                        ks_, kkey = khm, 'khm'
                    else:
                        ks_, kkey = khat, 'khat'
                    for h in range(4):
                        S.op('pe', lambda e, h=h, ks_=ks_: e.matmul(bank(0, 128, 128, h * 128), lhsT=ks_[:, h, :],
                                                                    rhs=ub[:, h, :], start=True, stop=True),
                             reads=[kkey, 'ub'], writes=bk(0))
                    S.op('dve', lambda e, c=c: e.tensor_tensor(
                        out=Sd, in0=Sd, in1=egLT[:, :, c:c + 1].to_broadcast([128, 4, 128]), op=ALU.mult),
                        reads=['Sd', 'egLT'], writes=['Sd'])
                    S.op('dve', lambda e: e.tensor_tensor(out=Sd, in0=Sd, in1=bank(0).rearrange("p (h v) -> p h v", h=4),
                                                          op=ALU.add), reads=['Sd'] + bk(0), writes=['Sd'])
                    if nch > 1:
                        S.dma('sp', lambda e, c=c: e.dma_start(out=ds_d[c].rearrange("h d v -> d h v"), in_=Sd),
                              reads=['Sd'], final=True)
                    else:
                        S.op('act', lambda e: e.copy(out=Sdb, in_=Sd), reads=['Sd'], writes=['Sdb'])
                if nch == 1 and ti == NPT - 1:
                    S.dma('sp', lambda e: e.dma_start(out=dp_d.rearrange("h d v -> d h v"), in_=Sd),
                          reads=['Sd'], final=True)
                if nch > 1 or ti == NPT - 1:
                    for i3 in range(3):
                        proj_tm(bank(0, 64), bk(0), hT, 'hT', winB, 'winB', i3 * 512, 512)
                        S.op('act', lambda e: e.copy(out=F[0], in_=bank(0, 64)), reads=bk(0), writes=['F0'])
                        if nch == 1:
                            S.dma('sp', lambda e, i3=i3: e.dma_start(out=cp_d[:, i3 * 512:(i3 + 1) * 512],
                                                                     in_=F[0][61:64, :]), reads=['F0'], final=True)
                        else:
                            for s in range(16):
                                S.dma('sp', lambda e, i3=i3, s=s: e.dma_start(
                                    out=cs_d[3 * s:3 * s + 3, i3 * 512:(i3 + 1) * 512], in_=F[0][4 * s + 1:4 * s + 4, :]),
                                    reads=['F0'], final=True)
                proj_tm(bank(0, 64), bk(0), hT, 'hT', winB, 'winB', 1536, 512)
                S.op('act', lambda e: e.activation(out=F[5], in_=bank(0, 64), func=AF.Silu), reads=bk(0), writes=['F5'])
                head_norm_gate_keys(ob, 'F4', gnbb, 'gnbb', F[5], 'F5', F[0], 'F0', F[1], 'F1', ogt, 'ogt', ss4, rs4)
                transpose_tm(ogt, 'ogt', 4, oT, 'oT')
                for half in range(2):
                    for k in range(4):
                        S.op('pe', lambda e, k=k, half=half: e.matmul(
                            bank(3 + half, 64), lhsT=oT[:, k, :], rhs=wbrB[:, k, half * 512:(half + 1) * 512],
                            start=(k == 0), stop=(k == 3)), reads=['oT', 'wbrB'], writes=bk(3 + half))
                    proj_tm(bank(half, 64), bk(half), hT, 'hT', wgB, 'wgB', half * 512, 512)
                    sg = F[2 + half]
                    S.op('act', lambda e, half=half, sg=sg: e.activation(out=sg, in_=bank(half, 64), func=AF.Sigmoid),
                         reads=bk(half), writes=['F%d' % (2 + half)])
                    S.op('dve', lambda e, half=half, sg=sg: e.tensor_tensor(out=sg, in0=bank(3 + half, 64), in1=sg,
                                                                            op=ALU.mult),
                         reads=bk(3 + half) + ['F%d' % (2 + half)], writes=['F%d' % (2 + half)])
                    S.op('dve', lambda e, half=half, sg=sg: e.tensor_tensor(
                        out=mb[:, half * 512:(half + 1) * 512], in0=sg, in1=m1t[:, half * 512:(half + 1) * 512],
                        op=ALU.add), reads=['F%d' % (2 + half), 'm1t'], writes=['mb'])
                transpose_tm(mb, 'mb', 8, mT, 'mT')
                for half in range(2):
                    proj_tm(bank(3 + half, 64), bk(3 + half), mT, 'mT', woutb, 'woutb', half * 512, 512)
                    S.op('dve', lambda e, half=half: e.tensor_tensor(
                        out=m1t[:, half * 512:(half + 1) * 512], in0=bank(3 + half, 64),
                        in1=xt[:, half * 512:(half + 1) * 512], op=ALU.add), reads=bk(3 + half) + ['xt'], writes=['m1t'])
                S.dma('sp', lambda e: e.dma_start(out=x1_d[r0:r0 + 64, :], in_=m1t), reads=['m1t'],
                      writes=[('x1', ti)])

            if 'A2' in PHASES:
                for ti in (range(NPT) if TILES is None else TILES):
                    tileA2(ti, TP)
                tileA2(NPT, TS)
        phase_A2()
        S.barrier()

        def phase_B():
            AL.off = pers_mark
            wqb = AL.get(128, [8, 2048], BF16)
            keysTb = AL.get(128, [16, 128], BF16)
            wpgb = AL.get(128, [8, 1024], BF16)
            wpleb = AL.get(128, [2, 1024], BF16)
            Zb = AL.get(64, 8192, BF16)
            dltb = AL.get(128, [64, 64], BF16)
            WM = AL.get(128, [64, 64], BF16)
            Ug = [AL.get(128, 1024, BF16) for _ in range(NBUF)]
            Vg = [AL.get(128, 1024, BF16) for _ in range(NBUF)]
            junkb = AL.get(128, 1024, BF16)
            load_w_bf16(wqb, wq_d, 0, 8, 0, 2048, 'wqb')
            load_w_bf16(wpgb, wpg_d, 0, 8, 0, 1024, 'wpgb')
            load_w_bf16(wpleb, wple_d, 0, 2, 0, 1024, 'wpleb')
            S.dma('pool', lambda e: e.dma_start(out=keysTb, in_=keysT_d.rearrange("c d k -> d c k")), writes=['keysTb'])
            for cc in range(0, 8192, 2048):
                S.dma('pool', lambda e, cc=cc: e.dma_start(out=Zb[:, cc:cc + 2048], in_=zsel_d[:, cc:cc + 2048]),
                      writes=['Zb'])
            for cc in range(0, 4096, 2048):
                S.dma('pool', lambda e, cc=cc: e.dma_start(
                    out=dltb.rearrange("p n m -> p (n m)")[:, cc:cc + 2048], in_=dlt_d[:, cc:cc + 2048]), writes=['dltb'])
            gffnb = AL.get(64, 1024, F32)
            gpleb = AL.get(64, 1024, F32)
            gfinb = AL.get(64, 1024, F32)
            bcast_load(gffnb, gffn_d[0:1, :], 64, 1024, 'gffnb')
            bcast_load(gpleb, gple_d[0:1, :], 64, 1024, 'gpleb')
            bcast_load(gfinb, gfin_d[0:1, :], 64, 1024, 'gfinb')
            xt = AL.get(64, 1024, F32)
            hb = AL.get(64, 1024, BF16)
            hT = AL.get(128, [8, 64], BF16)
            ss = AL.get(64, 1, F32)
            rs = AL.get(64, 1, F32)
            qTb = AL.get(128, [16, 64], BF16)
            sc = AL.get(64, [16, 128], F32)
            v1 = AL.get(64, [16, 16], F32)
            i1 = AL.get(64, [16, 16], U32)
            i1f = AL.get(64, [16, 16], F32)
            wk = AL.get(64, 256, F32)
            cand = AL.get(64, [8, 256], F32)
            eq = AL.get(64, [8, 16, 16], F32)
            v2 = AL.get(64, [8, 16], F32)
            ci = AL.get(64, [8, 16], U32)
            cih = AL.get(64, [8, 16], U32)
            cil = AL.get(64, [8, 16], U32)
            cihf = AL.get(64, [8, 16], F32)
            cilf = AL.get(64, [8, 16], F32)
            iaf = AL.get(64, 128, F32)
            ibf = AL.get(64, 128, F32)
            idxf = AL.get(64, 128, F32)
            gte = AL.get(64, [8, 16], F32)
            gsum = AL.get(64, 8, F32)
            IDXT = AL.get(128, 64, I32)
            gateT = AL.get(128, 64, F32)
            ACTT = AL.get(128, 64, F32)
            g1 = AL.get(128, 64, F32)
            g2 = AL.get(128, 64, F32)
            WT = AL.get(128, 64, BF16)
            x2 = AL.get(64, 1024, F32)
            gsig = AL.get(64, 1024, F32)
            ptl = AL.get(64, 256, F32)
            ptb = AL.get(64, 256, BF16)
            pTt = AL.get(128, [2, 64], BF16)
            yt = AL.get(64, 1024, F32)
            iota16 = C_('iota16')

            def topk16(src, srckey, width, vals, vkey, idxs, ikey):
                S.op('dve', lambda e: e.max(out=vals[:, 0:8], in_=src), reads=[srckey], writes=[vkey])
                S.op('dve', lambda e: e.max_index(out=idxs[:, 0:8], in_max=vals[:, 0:8], in_values=src),
                     reads=[srckey, vkey], writes=[ikey])
                S.op('dve', lambda e: e.match_replace(out=wk[:, 0:width], in_to_replace=vals[:, 0:8], in_values=src,
                                                      imm_value=-1e30), reads=[srckey, vkey], writes=['wk'])
                S.op('dve', lambda e: e.max(out=vals[:, 8:16], in_=wk[:, 0:width]), reads=['wk'], writes=[vkey])
                S.op('dve', lambda e: e.max_index(out=idxs[:, 8:16], in_max=vals[:, 8:16], in_values=wk[:, 0:width]),
                     reads=['wk', vkey], writes=[ikey])

            def tileB(ti):
                r0 = ti * 64
                S.dma('sp', lambda e: e.dma_start(out=xt, in_=x1_d[r0:r0 + 64, :]), reads=[('x1', ti)], writes=['xt'])
                S.dma('sp', lambda e: e.dma_start(out=ptl, in_=p_d[r0:r0 + 64, :]), writes=['ptl'])
                rmsnorm_to_bf16(xt, 'xt', gffnb, 'gffnb', hb, 'hb', ss, rs)
                transpose_tm(hb, 'hb', 8, hT, 'hT')
                pq = PB[:, 0:1024].rearrange("p (c t) -> p c t", c=16)
                for hc in range(16):
                    for k in range(8):
                        S.op('pe', lambda e, hc=hc, k=k: e.matmul(pq[:, hc, :], lhsT=wqb[:, k, hc * 128:(hc + 1) * 128],
                                                                  rhs=hT[:, k, :], start=(k == 0), stop=(k == 7)),
                             reads=['wqb', 'hT'], writes=bk(0, 1))
                S.op('act', lambda e: e.copy(out=qTb, in_=pq), reads=bk(0, 1), writes=['qTb'])
                for hc in range(16):
                    S.op('pe', lambda e, hc=hc: e.matmul(PB[0:64, 1024 + hc * 128:1024 + (hc + 1) * 128],
                                                         lhsT=qTb[:, hc, :], rhs=keysTb[:, hc, :], start=True, stop=True),
                         reads=['qTb', 'keysTb'], writes=bk(2, 3, 4, 5))
                S.op('act', lambda e: e.copy(out=sc, in_=PB[0:64, 1024:3072].rearrange("p (c k) -> p c k", c=16)),
                     reads=bk(2, 3, 4, 5), writes=['sc'])
                for hc in range(16):
                    topk16(sc[:, hc, :], 'sc', 128, v1[:, hc, :], 'v1', i1[:, hc, :], 'i1')
                S.op('dve', lambda e: e.tensor_copy(out=i1f, in_=i1), reads=['i1'], writes=['i1f'])
                for h in range(8):
                    S.op('dve', lambda e, h=h: e.tensor_tensor(
                        out=cand[:, h, :].rearrange("p (i j) -> p i j", i=16),
                        in0=v1[:, 2 * h, :].unsqueeze(2).to_broadcast([64, 16, 16]),
                        in1=v1[:, 2 * h + 1, :].unsqueeze(1).to_broadcast([64, 16, 16]), op=ALU.add),
                        reads=['v1'], writes=['cand'])
                for h in range(8):
                    topk16(cand[:, h, :], 'cand', 256, v2[:, h, :], 'v2', ci[:, h, :], 'ci')
                S.op('dve', lambda e: e.tensor_scalar(out=cih, in0=ci, scalar1=4, scalar2=None,
                                                      op0=ALU.logical_shift_right), reads=['ci'], writes=['cih'])
                S.op('dve', lambda e: e.tensor_scalar(out=cil, in0=ci, scalar1=15, scalar2=None, op0=ALU.bitwise_and),
                     reads=['ci'], writes=['cil'])
                S.op('dve', lambda e: e.tensor_copy(out=cihf, in_=cih), reads=['cih'], writes=['cihf'])
                S.op('dve', lambda e: e.tensor_copy(out=cilf, in_=cil), reads=['cil'], writes=['cilf'])
                i1v = i1f.rearrange("p (h c) i -> p h c i", c=2)
                for (cf, ckey, cpos, dst, dkey) in ((cihf, 'cihf', 0, iaf, 'iaf'), (cilf, 'cilf', 1, ibf, 'ibf')):
                    S.op('dve', lambda e, cf=cf: e.tensor_tensor(
                        out=eq, in0=cf.unsqueeze(3).to_broadcast([64, 8, 16, 16]),
                        in1=iota16.unsqueeze(1).unsqueeze(1).to_broadcast([64, 8, 16, 16]), op=ALU.is_equal),
                        reads=[ckey, 'cst'], writes=['eq'])
                    S.op('dve', lambda e, cpos=cpos: e.tensor_tensor(
                        out=eq, in0=eq, in1=i1v[:, :, cpos, :].unsqueeze(2).to_broadcast([64, 8, 16, 16]), op=ALU.mult),
                        reads=['eq', 'i1f'], writes=['eq'])
                    S.op('dve', lambda e, dst=dst: e.reduce_sum(out=dst, in_=eq.rearrange("p h k i -> p (h k) i"),
                                                                axis=AX.X), reads=['eq'], writes=[dkey])
                S.op('dve', lambda e: e.scalar_tensor_tensor(out=idxf, in0=iaf, scalar=128.0, in1=ibf, op0=ALU.mult,
                                                             op1=ALU.add), reads=['iaf', 'ibf'], writes=['idxf'])
                S.op('dve', lambda e: e.tensor_tensor(out=gte, in0=v2, in1=v2[:, :, 0:1].to_broadcast([64, 8, 16]),
                                                      op=ALU.subtract), reads=['v2'], writes=['gte'])
                S.op('act', lambda e: e.activation(out=gte, in_=gte, func=AF.Exp), reads=['gte'], writes=['gte'])
                S.op('dve', lambda e: e.reduce_sum(out=gsum, in_=gte, axis=AX.X), reads=['gte'], writes=['gsum'])
                S.op('dve', lambda e: e.reciprocal(out=gsum, in_=gsum), reads=['gsum'], writes=['gsum'])
                S.op('dve', lambda e: e.tensor_tensor(out=gte, in0=gte, in1=gsum.unsqueeze(2).to_broadcast([64, 8, 16]),
                                                      op=ALU.mult), reads=['gte', 'gsum'], writes=['gte'])
                S.op('pe', lambda e: e.transpose(out=bank(6, 128, 64), in_=idxf, identity=identf[0:64, 0:64]),
                     reads=['idxf', 'cst'], writes=bk(6))
                S.op('pe', lambda e: e.transpose(out=bank(6, 128, 64, 64), in_=gte.rearrange("p h k -> p (h k)"),
                                                 identity=identf[0:64, 0:64]), reads=['gte', 'cst'], writes=bk(6))
                S.op('dve', lambda e: e.tensor_copy(out=IDXT, in_=bank(6, 128, 64)), reads=bk(6), writes=['IDXT'])
                S.op('dve', lambda e: e.tensor_copy(out=gateT, in_=bank(6, 128, 64, 64)), reads=bk(6), writes=['gateT'])
                for n in range(64):
                    ub_ = Ug[n % NBUF]
                    ukey = 'Ug%d' % (n % NBUF)
                    S.dma('pool', lambda e, n=n, ub_=ub_: e.indirect_dma_start(
                        out=ub_, out_offset=None, in_=eu_d,
                        in_offset=bass.IndirectOffsetOnAxis(ap=IDXT[:, n:n + 1], axis=0)),
                        reads=['IDXT'], writes=[ukey])
                    hbk = (0, 1) if n % 2 == 0 else (2, 3)
                    for half in range(2):
                        S.op('pe', lambda e, n=n, half=half, hbk=hbk: e.matmul(
                            bank(hbk[half]), lhsT=Zb[:, n * 128:(n + 1) * 128], rhs=hb[:, half * 512:(half + 1) * 512],
                            start=True, stop=True), reads=['Zb', 'hb'], writes=bk(hbk[half]))
                    S.op('dve', lambda e, n=n, ub_=ub_, hbk=hbk: e.scalar_tensor_tensor(
                        out=junkb, in0=ub_, scalar=1.0, in1=PB[:, hbk[0] * 512:hbk[0] * 512 + 1024], op0=ALU.mult,
                        op1=ALU.mult, accum_out=ACTT[:, n:n + 1]), reads=[ukey] + bk(*hbk), writes=[('ACTT', n)])
                actk = [('ACTT', n) for n in range(64)]
                S.op('dve', lambda e: e.tensor_tensor(out=g1, in0=ACTT, in1=ACTT, op=ALU.mult), reads=actk, writes=['g1'])
                S.op('dve', lambda e: e.tensor_scalar(out=g1, in0=g1, scalar1=0.044715, scalar2=1.0, op0=ALU.mult,
                                                      op1=ALU.add), reads=['g1'], writes=['g1'])
                S.op('dve', lambda e: e.tensor_tensor(out=g1, in0=g1, in1=ACTT, op=ALU.mult), reads=['g1'] + actk,
                     writes=['g1'])
                S.op('act', lambda e: e.activation(out=g2, in_=g1, func=AF.Sigmoid, scale=1.5957691216057308),
                     reads=['g1'], writes=['g2'])
                S.op('dve', lambda e: e.tensor_tensor(out=g2, in0=g2, in1=ACTT, op=ALU.mult), reads=['g2'] + actk,
                     writes=['g2'])
                S.op('dve', lambda e: e.tensor_tensor(out=WT, in0=g2, in1=gateT, op=ALU.mult), reads=['g2', 'gateT'],
                     writes=['WT'])
                S.op('dve', lambda e: e.tensor_tensor(out=WM, in0=WT.unsqueeze(2).to_broadcast([128, 64, 64]), in1=dltb,
                                                      op=ALU.mult), reads=['WT', 'dltb'], writes=['WM'])
                for n in range(64):
                    vb_ = Vg[n % NBUF]
                    vkey = 'Vg%d' % (n % NBUF)
                    S.dma('pool', lambda e, n=n, vb_=vb_: e.indirect_dma_start(
                        out=vb_, out_offset=None, in_=ev_d,
                        in_offset=bass.IndirectOffsetOnAxis(ap=IDXT[:, n:n + 1], axis=0)),
                        reads=['IDXT'], writes=[vkey])
                    for half in range(2):
                        S.op('pe', lambda e, n=n, half=half, vb_=vb_: e.matmul(
                            bank(4 + half, 64), lhsT=WM[:, n, :], rhs=vb_[:, half * 512:(half + 1) * 512],
                            start=(n == 0), stop=(n == 63)), reads=['WM', vkey], writes=bk(4 + half))
                S.op('dve', lambda e: e.tensor_tensor(out=x2, in0=xt, in1=PB[0:64, 2048:3072], op=ALU.add),
                     reads=['xt'] + bk(4, 5), writes=['x2'])
                rmsnorm_to_bf16(x2, 'x2', gpleb, 'gpleb', hb, 'hb', ss, rs)
                transpose_tm(hb, 'hb', 8, hT, 'hT')
                for half in range(2):
                    proj_tm(bank(half, 64), bk(half), hT, 'hT', wpgb, 'wpgb', half * 512, 512)
                S.op('act', lambda e: e.activation(out=gsig, in_=PB[0:64, 0:1024], func=AF.Sigmoid), reads=bk(0, 1),
                     writes=['gsig'])
                S.op('dve', lambda e: e.tensor_copy(out=ptb, in_=ptl), reads=['ptl'], writes=['ptb'])
                transpose_tm(ptb, 'ptb', 2, pTt, 'pTt')
                for half in range(2):
                    proj_tm(bank(2 + half, 64), bk(2 + half), pTt, 'pTt', wpleb, 'wpleb', half * 512, 512, nk=2)
                S.op('dve', lambda e: e.tensor_tensor(out=gsig, in0=gsig, in1=PB[0:64, 1024:2048], op=ALU.mult),
                     reads=['gsig'] + bk(2, 3), writes=['gsig'])
                S.op('dve', lambda e: e.tensor_tensor(out=x2, in0=x2, in1=gsig, op=ALU.add), reads=['x2', 'gsig'],
                     writes=['x2'])
                S.op('act', lambda e: e.activation(out=yt, in_=x2, func=AF.Square, accum_out=ss), reads=['x2'],
                     writes=['yt', 'ss'])
                S.op('act', lambda e: e.activation(out=rs, in_=ss, func=AF.Sqrt, scale=1.0 / 1024, bias=epsc[0:64, :]),
                     reads=['ss', 'cst'], writes=['rs'])
                S.op('dve', lambda e: e.reciprocal(out=rs, in_=rs), reads=['rs'], writes=['rs'])
                S.op('dve', lambda e: e.scalar_tensor_tensor(out=yt, in0=x2, scalar=rs[:, 0:1], in1=gfinb, op0=ALU.mult,
                                                             op1=ALU.mult), reads=['x2', 'rs', 'gfinb'], writes=['yt'])
                S.dma('sp', lambda e: e.dma_start(out=y_d[r0:r0 + 64, :], in_=yt), reads=['yt'], final=True)

            if 'B' in PHASES:
                for ti in (range(NPT + 1) if TILES is None else list(TILES) + [NPT]):
                    tileB(ti)
        phase_B()
        print('ops', {e: len(v) for e, v in S.prog.items()}, 'nsem', S.nsem, 'arena', AL.off)
        S.emit()
    return nc


_CACHE = {}


def kernel(x_prompt, x_sample, state_hgrn, state_delta, state_conv, p_prompt, p_sample,
           lb_param, g_mix, w_in, conv_w, a_log, dt_bias, g_norm_a, g_norm_b, w_br_a, w_br_b,
           w_out, g_ffn, peer_wq, peer_keys, expert_u, expert_v, g_ple, w_ple, w_ple_gate,
           g_final):
    f = lambda a: np.ascontiguousarray(np.asarray(a, dtype=np.float32))
    if 'nc' not in _CACHE:
        _CACHE['nc'] = build_program()
        _CACHE['consts'] = _build_consts()
    nc = _CACHE['nc']
    cst, cmf, zsel, dlt = _CACHE['consts']
    x_prompt, x_sample = f(x_prompt), f(x_sample)
    p_prompt, p_sample = f(p_prompt), f(p_sample)
    state_hgrn, state_delta, state_conv = f(state_hgrn), f(state_delta), f(state_conv)
    keysT = np.ascontiguousarray(np.transpose(f(peer_keys)[0], (0, 1, 3, 2)).reshape(16, 128, 128))
    shared = dict(
        lbp=f(lb_param), gmix=f(g_mix), w_in=f(w_in)[0], convw=f(conv_w)[0], alog=f(a_log), dtb=f(dt_bias),
        gna=f(g_norm_a), gnb=f(g_norm_b), wbra=f(w_br_a)[0], wbrb=f(w_br_b)[0], wout=f(w_out)[0], gffn=f(g_ffn),
        wq=f(peer_wq)[0], keysT=keysT, eu=f(expert_u)[0], ev=f(expert_v)[0], gple=f(g_ple), wple=f(w_ple)[0],
        wpg=f(w_ple_gate)[0], gfin=f(g_final).reshape(1, 1024), cst=cst, cmf=cmf, zsel=zsel, dlt=dlt)
    in_maps = []
    for b in range(8):
        m = dict(shared)
        m['x'] = np.ascontiguousarray(np.concatenate([x_prompt[b], x_sample[16 * b:16 * b + 16].reshape(64, 1024)], 0))
        m['p'] = np.ascontiguousarray(np.concatenate([p_prompt[0, b], p_sample[0, 16 * b:16 * b + 16].reshape(64, 256)], 0))
        m['sh'] = np.ascontiguousarray(state_hgrn[0, 16 * b:16 * b + 16])
        m['sd'] = np.ascontiguousarray(state_delta[0, 16 * b:16 * b + 16])
        m['scv'] = np.ascontiguousarray(state_conv[0, 16 * b:16 * b + 16].reshape(48, 1536))
        in_maps.append(m)
    res = run_bass_kernel_spmd(nc, in_maps, core_ids=list(range(8)))
    R = res.results
    y_prompt = np.stack([R[b]['y'][0:2048] for b in range(8)], 0)
    y_sample = np.concatenate([R[b]['y'][2048:2112].reshape(16, 4, 1024) for b in range(8)], 0)
    hp = np.stack([R[b]['hp'] for b in range(8)], 0)[None]
    dp = np.stack([R[b]['dp'] for b in range(8)], 0)[None]
    cp = np.stack([R[b]['cp'] for b in range(8)], 0)[None]
    hs = np.concatenate([R[b]['hs'] for b in range(8)], 0)[None]
    ds = np.concatenate([R[b]['ds'] for b in range(8)], 0)[None]
    cs = np.concatenate([R[b]['cs'].reshape(16, 3, 1536) for b in range(8)], 0)[None]
    _CACHE['dbg'] = R
    return tuple(np.ascontiguousarray(a.astype(np.float32)) for a in (y_prompt, y_sample, hp, dp, cp, hs, ds, cs))
```

```python
import numpy as np
from contextlib import ExitStack
import concourse.bass as bass
import concourse.mybir as mybir
from concourse.bass_utils import run_bass_kernel_spmd

F32 = mybir.dt.float32
BF16 = mybir.dt.bfloat16
I32 = mybir.dt.int32
U32 = mybir.dt.uint32
AF = mybir.ActivationFunctionType
ALU = mybir.AluOpType
AX = mybir.AxisListType

EPS = 1e-6
NPT = 32
NTOK = 2112
EPOCH = 12000
DMA_POOL = 8
DMA_EPOCH = 700
ARENA_COLS = 105984
NBUF = 6
NHB = 6
NEG = -30000.0
DBG = set()
DBGT = 0
PHASES = ('A1', 'A2', 'B')
TILES = None


class Sched:
    def __init__(self, nc, es):
        self.nc = nc
        self.es = es
        self.eng = {'pe': nc.tensor, 'act': nc.scalar, 'dve': nc.vector,
                    'pool': nc.gpsimd, 'sp': nc.sync}
        self.prog = {e: [] for e in self.eng}
        self.cnt = {e: 0 for e in self.eng}
        self.sem = {}
        self.nsem = 0
        for e in self.eng:
            self.sem[e] = self._newsem(e)
        self.waited = {e: {} for e in self.eng}
        self.dpool = {}
        self.res_w = {}
        self.res_r = {}
        self.final_tokens = []
        self.pending = {e: [] for e in self.eng}
        self.cap = None

    def begin(self):
        self.cap = []

    def end(self):
        L = self.cap
        self.cap = None
        return L

    def run(self, L):
        for it in L:
            if it[0] == 'op':
                self.op(it[1], it[2], it[3], it[4])
            else:
                self.dma(it[1], it[2], it[3], it[4], it[5])

    def _newsem(self, name):
        self.nsem += 1
        return self.es.enter_context(self.nc.semaphore(f"s{self.nsem}_{name}"))

    def _need(self, e, tok, waits):
        if tok is None:
            return
        sem, val = tok[0], tok[1]
        if e == 'pe' and tok[2] == 'pe':
            return
        w = self.waited[e]
        if w.get(id(sem), 0) >= val:
            return
        w[id(sem)] = val
        waits.append((sem, val))

    def _deps(self, e, reads, writes, waits):
        for t in self.pending[e]:
            self._need(e, t, waits)
        self.pending[e] = []
        for k in reads:
            self._need(e, self.res_w.get(k), waits)
        for k in writes:
            self._need(e, self.res_w.get(k), waits)
            for t in self.res_r.get(k, ()):
                self._need(e, t, waits)

    def _commit(self, tok, reads, writes):
        for k in reads:
            self.res_r.setdefault(k, []).append(tok)
        for k in writes:
            self.res_w[k] = tok
            self.res_r[k] = []

    def op(self, e, fn, reads=(), writes=()):
        if self.cap is not None:
            self.cap.append(('op', e, fn, tuple(reads), tuple(writes)))
            return None
        waits = []
        self._deps(e, reads, writes, waits)
        if self.cnt[e] >= EPOCH:
            self.sem[e] = self._newsem(e)
            self.cnt[e] = 0
        self.cnt[e] += 1
        tok = (self.sem[e], self.cnt[e], e)
        self.prog[e].append((waits, fn, self.sem[e], 1))
        self._commit(tok, reads, writes)
        return tok

    def dma(self, e, fn, reads=(), writes=(), final=False):
        if self.cap is not None:
            self.cap.append(('dma', e, fn, tuple(reads), tuple(writes), final))
            return None
        waits = []
        self._deps(e, reads, writes, waits)
        pool = self.dpool.setdefault(e, {'sems': [], 'uses': [], 'i': 0})
        i = pool['i'] % DMA_POOL
        pool['i'] += 1
        if len(pool['sems']) <= i:
            pool['sems'].append(self._newsem(e + 'd'))
            pool['uses'].append(0)
        if pool['uses'][i] >= DMA_EPOCH:
            pool['sems'][i] = self._newsem(e + 'd')
            pool['uses'][i] = 0
        sem = pool['sems'][i]
        if pool['uses'][i] > 0:
            self._need(e, (sem, 16 * pool['uses'][i], 'dma'), waits)
        pool['uses'][i] += 1
        tok = (sem, 16 * pool['uses'][i], 'dma')
        self.prog[e].append((waits, fn, sem, 16))
        self._commit(tok, reads, writes)
        if final:
            self.final_tokens.append(tok)
        return tok

    def barrier(self):
        toks = []
        for e in self.eng:
            if self.cnt[e] > 0:
                toks.append((self.sem[e], self.cnt[e], e + '_bar'))
        for e, pool in self.dpool.items():
            for sem, u in zip(pool['sems'], pool['uses']):
                if u > 0:
                    toks.append((sem, 16 * u, 'dma'))
        for e in self.eng:
            self.pending[e] = list(toks)

    def emit(self):
        nc = self.nc
        fw = []
        for t in self.final_tokens:
            self._need('sp', t, fw)
        with nc.Block() as block:
            def run(e, engine):
                for waits, fn, sem, inc in self.prog[e]:
                    for (s, v) in waits:
                        engine.wait_ge(s, v)
                    fn(engine).then_inc(sem, inc)

            @block.tensor
            def _(eng):
                run('pe', eng)

            @block.scalar
            def _(eng):
                run('act', eng)

            @block.vector
            def _(eng):
                run('dve', eng)

            @block.gpsimd
            def _(eng):
                run('pool', eng)

            @block.sync
            def _(eng):
                run('sp', eng)
                for (s, v) in fw:
                    eng.wait_ge(s, v)


class Alloc:
    def __init__(self, arena, ncols):
        self.a = arena
        self.n = ncols
        self.off = 0

    def get(self, parts, free, dt):
        if isinstance(free, int):
            free = [free]
        nel = int(np.prod(free))
        cols = nel * (1 if dt == BF16 else 2)
        cols = (cols + 1) // 2 * 2
        o = self.off
        self.off += cols
        assert self.off <= self.n, f"arena overflow {self.off} > {self.n}"
        ap = self.a[0:parts, o:o + cols]
        if dt != BF16:
            ap = ap.bitcast(dt)
        if len(free) > 1:
            ds = [f"d{i}" for i in range(len(free))]
            kw = {ds[i]: free[i] for i in range(1, len(free))}
            ap = ap.rearrange(f"p ({' '.join(ds)}) -> p {' '.join(ds)}", **kw)
        return ap


def _tile_consts(nch, C):
    t = np.arange(64)
    ch = t // C
    same = ch[:, None] == ch[None, :]
    tri = (same & (t[:, None] <= t[None, :])).astype(np.float32)
    blk = same.astype(np.float32)
    nmT = np.where(tri > 0, 0.0, NEG).astype(np.float32)
    strict = same & (t[None, :] < t[:, None])
    pmS = np.where(strict, 0.0, -NEG).astype(np.float32)
    cind = np.zeros((64, 16), np.float32)
    cind[t, ch] = 1.0
    return tri, blk, np.tile(nmT, (1, 4)), np.tile(pmS, (1, 4)), cind


CST_COLS = {}


def _build_consts():
    cols = []
    off = [0]

    def add(name, arr):
        a = np.zeros((128, arr.shape[1]), np.float32)
        a[:arr.shape[0]] = arr
        CST_COLS[name] = (off[0], off[0] + arr.shape[1], arr.shape[0])
        off[0] += arr.shape[1]
        cols.append(a)

    add('ident', np.eye(128, dtype=np.float32))
    add('ones', np.ones((128, 128), np.float32))
    for nm, (nch, C) in (('p', (1, 64)), ('s', (16, 4))):
        tri, blk, nmT, pmS, cind = _tile_consts(nch, C)
        add('tri_' + nm, tri)
        add('blk_' + nm, blk)
        add('nmT_' + nm, nmT)
        add('pmS_' + nm, pmS)
        add('cind_' + nm, cind)
    add('iota16', np.tile(np.arange(16, dtype=np.float32)[None, :], (64, 1)))
    add('eps', np.full((128, 1), EPS, np.float32))
    add('one', np.ones((128, 1), np.float32))
    cst = np.concatenate(cols, axis=1)
    t = np.arange(64)
    cm = (t[None, :] // 4 == np.arange(16)[:, None]).astype(np.float32)
    cmf = np.tile(cm.reshape(1, 16 * 64), (128, 1))
    z = np.zeros((64, 64, 128), np.float32)
    z[t, t, :] = 1.0
    z = z.reshape(64, 64 * 128)
    dl = np.tile(np.eye(64, dtype=np.float32).reshape(1, 64 * 64), (128, 1))
    return cst, cmf, z, dl


def build_program():
    cst_np, _, _, _ = _build_consts()
    NCST = cst_np.shape[1]
    nc = bass.Bass("TRN2", target_bir_lowering=False)

    def din(name, shape, dt=F32):
        return nc.dram_tensor(name, shape, dt, kind="ExternalInput").ap()

    def dout(name, shape, dt=F32):
        return nc.dram_tensor(name, shape, dt, kind="ExternalOutput").ap()

    x_d = din("x", [NTOK, 1024])
    p_d = din("p", [NTOK, 256])
    sh_d = din("sh", [16, 4, 128, 128])
    sd_d = din("sd", [16, 4, 128, 128])
    scv_d = din("scv", [48, 1536])
    lbp_d = din("lbp", [2, 512])
    gmix_d = din("gmix", [1, 1024])
    win_d = din("w_in", [1024, 6152])
    convw_d = din("convw", [4, 1536])
    alog_d = din("alog", [1, 4])
    dtb_d = din("dtb", [1, 4])
    gna_d = din("gna", [1, 128])
    gnb_d = din("gnb", [1, 128])
    wbra_d = din("wbra", [512, 1024])
    wbrb_d = din("wbrb", [512, 1024])
    wout_d = din("wout", [1024, 1024])
    gffn_d = din("gffn", [1, 1024])
    wq_d = din("wq", [1024, 2048])
    keysT_d = din("keysT", [16, 128, 128])
    eu_d = din("eu", [16384, 1024])
    ev_d = din("ev", [16384, 1024])
    gple_d = din("gple", [1, 1024])
    wple_d = din("wple", [256, 1024])
    wpg_d = din("wpg", [1024, 1024])
    gfin_d = din("gfin", [1, 1024])
    cst_d = din("cst", [128, NCST])
    cmf_d = din("cmf", [128, 1024])
    zsel_d = din("zsel", [64, 8192])
    dlt_d = din("dlt", [128, 4096])

    y_d = dout("y", [NTOK, 1024])
    hp_d = dout("hp", [4, 128, 128])
    dp_d = dout("dp", [4, 128, 128])
    cp_d = dout("cp", [3, 1536])
    hs_d = dout("hs", [16, 4, 128, 128])
    ds_d = dout("ds", [16, 4, 128, 128])
    cs_d = dout("cs", [48, 1536])
    m1_d = dout("m1s", [NTOK, 1024])
    x1_d = dout("x1s", [NTOK, 1024])
    eub_d = nc.dram_tensor("eub", [16384, 1024], BF16, kind="Internal").ap()
    evb_d = nc.dram_tensor("evb", [16384, 1024], BF16, kind="Internal").ap()
    h2_d = nc.dram_tensor("h2s", [NTOK, 1024], BF16, kind="Internal").ap()

    es = ExitStack()
    with es:
        S = Sched(nc, es)

        def dbg(name, ap, key, ti=0, want=0):
            if name not in DBG or ti != want:
                return
            shp = list(ap.shape)
            dd = nc.dram_tensor("dbg_" + name, shp, ap.dtype, kind="ExternalOutput").ap()
            S.dma('sp', lambda e: e.dma_start(out=dd, in_=ap), reads=[key] if not isinstance(key, list) else key,
                  final=True)
        ARENA = es.enter_context(nc.sbuf_tensor("arena", [128, ARENA_COLS], BF16))
        PB = es.enter_context(nc.psum_tensor("pb", [128, 7 * 512], F32))
        PTt = es.enter_context(nc.psum_tensor("pt", [128, 1024], BF16))
        AL = Alloc(ARENA, ARENA_COLS)

        def bank(j, parts=128, n=512, off=0):
            return PB[0:parts, j * 512 + off:j * 512 + off + n]

        def bk(*js):
            return ['b%d' % j for j in js]

        CST = AL.get(128, NCST, F32)
        S.dma('sp', lambda e: e.dma_start(out=CST, in_=cst_d), writes=['cst'])

        def C_(name, parts=None):
            a, b, r = CST_COLS[name]
            return CST[0:(parts or r), a:b]

        identf = C_('ident')
        onesf = C_('ones')
        epsc = C_('eps')
        identb = AL.get(128, 128, BF16)
        S.op('dve', lambda e: e.tensor_copy(out=identb, in_=identf), reads=['cst'], writes=['identb'])
        TP = dict(nch=1, C=64, tri=C_('tri_p'), blk=C_('blk_p'), nmT=C_('nmT_p'), pmS=C_('pmS_p'),
                  cind=C_('cind_p'))
        TS = dict(nch=16, C=4, tri=C_('tri_s'), blk=C_('blk_s'), nmT=C_('nmT_s'), pmS=C_('pmS_s'),
                  cind=C_('cind_s'))
        cmb = AL.get(128, [16, 64], BF16)
        S.dma('pool', lambda e: e.dma_start(out=cmb, in_=cmf_d.rearrange("p (c t) -> p c t", c=16)),
              writes=['cmb'])
        pers_mark = AL.off

        def load_w_bf16(dst3, src, r0, nk, c0, ncols, key):
            for k in range(nk):
                for cc in range(0, ncols, 2048):
                    w = min(2048, ncols - cc)
                    S.dma('pool', lambda e, k=k, cc=cc, w=w: e.dma_start(
                        out=dst3[:, k, cc:cc + w],
                        in_=src[r0 + k * 128:r0 + (k + 1) * 128, c0 + cc:c0 + cc + w]), writes=[key])

        def bcast_load(dst, src_row, parts, n, key):
            S.dma('sp', lambda e: e.dma_start(out=dst, in_=src_row.to_broadcast([parts, n])), writes=[key])

        def rmsnorm_to_bf16(xt, xkey, gb, gkey, hb, hkey, ss, rs):
            S.op('act', lambda e: e.activation(out=hb, in_=xt, func=AF.Square, accum_out=ss),
                 reads=[xkey], writes=[hkey, 'ss'])
            S.op('act', lambda e: e.activation(out=rs, in_=ss, func=AF.Sqrt, scale=1.0 / 1024, bias=epsc[0:64, :]),
                 reads=['ss', 'cst'], writes=['rs'])
            S.op('dve', lambda e: e.reciprocal(out=rs, in_=rs), reads=['rs'], writes=['rs'])
            S.op('dve', lambda e: e.scalar_tensor_tensor(out=hb, in0=xt, scalar=rs[:, 0:1], in1=gb,
                                                         op0=ALU.mult, op1=ALU.mult),
                 reads=[xkey, 'rs', gkey], writes=[hkey])

        def transpose_tm(src, skey, nblk, dst, dkey, eng='act', ptgt=None, pkey='pt'):
            if ptgt is None:
                ptgt = PTt
            for k in range(nblk):
                S.op('pe', lambda e, k=k: e.transpose(out=ptgt[:, k * 64:(k + 1) * 64],
                                                      in_=src[:, k * 128:(k + 1) * 128],
                                                      identity=identb[0:64, 0:64]),
                     reads=[skey, 'identb'], writes=[pkey])
            pv = ptgt[:, 0:nblk * 64].rearrange("p (k t) -> p k t", k=nblk)
            if eng == 'act':
                S.op('act', lambda e: e.copy(out=dst, in_=pv), reads=[pkey], writes=[dkey])
            else:
                S.op('dve', lambda e: e.tensor_copy(out=dst, in_=pv), reads=[pkey], writes=[dkey])

        def proj_tm(pout, pkeys, hT, hTkey, w3, wkey, c0, ncols, nk=8):
            for k in range(nk):
                S.op('pe', lambda e, k=k: e.matmul(pout, lhsT=hT[:, k, :], rhs=w3[:, k, c0:c0 + ncols],
                                                   start=(k == 0), stop=(k == nk - 1)),
                     reads=[hTkey, wkey], writes=pkeys)

        def head_norm_gate_keys(o_sb, okey, gnb_, gkey, gate_sb, gatekey, sq, sqkey, on, onkey, ogt_, ogtkey, ss4, rs4):
            o3 = o_sb.rearrange("p (h v) -> p h v", h=4)
            S.op('dve', lambda e: e.tensor_tensor(out=sq, in0=o_sb, in1=o_sb, op=ALU.mult),
                 reads=[okey], writes=[sqkey])
            S.op('dve', lambda e: e.reduce_sum(out=ss4, in_=sq.rearrange("p (h v) -> p h v", h=4), axis=AX.X),
                 reads=[sqkey], writes=['ss4'])
            S.op('act', lambda e: e.activation(out=rs4, in_=ss4, func=AF.Sqrt, scale=1.0 / 128, bias=epsc[0:64, :]),
                 reads=['ss4', 'cst'], writes=['rs4'])
            S.op('dve', lambda e: e.reciprocal(out=rs4, in_=rs4), reads=['rs4'], writes=['rs4'])
            on3 = on.rearrange("p (h v) -> p h v", h=4)
            S.op('dve', lambda e: e.tensor_tensor(out=on3, in0=o3, in1=rs4.unsqueeze(2).to_broadcast([64, 4, 128]),
                                                  op=ALU.mult), reads=[okey, 'rs4'], writes=[onkey])
            S.op('dve', lambda e: e.tensor_tensor(out=on3, in0=on3, in1=gnb_.unsqueeze(1).to_broadcast([64, 4, 128]),
                                                  op=ALU.mult), reads=[onkey, gkey], writes=[onkey])
            S.op('dve', lambda e: e.tensor_tensor(out=ogt_, in0=on, in1=gate_sb, op=ALU.mult),
                 reads=[onkey, gatekey], writes=[ogtkey])

        def phase_A1():
            AL.off = pers_mark
            winA = AL.get(128, [8, 2048], BF16)
            wgA = AL.get(128, [8, 1024], BF16)
            wbrA = AL.get(128, [4, 1024], BF16)
            load_w_bf16(winA, win_d, 0, 8, 0, 2048, 'winA')
            load_w_bf16(wgA, win_d, 0, 8, 4104, 1024, 'wgA')
            load_w_bf16(wbrA, wbra_d, 0, 4, 0, 1024, 'wbrA')
            for (src_t, dst_t, key) in ((eu_d, eub_d, 'eub'), (ev_d, evb_d, 'evb')):
                for r in range(0, 16384, 512):
                    S.dma('pool', lambda e, r=r, src_t=src_t, dst_t=dst_t: e.dma_start(
                        out=dst_t[r:r + 512, :], in_=src_t[r:r + 512, :]), writes=[key])
            gmixb = AL.get(64, 1024, F32)
            bcast_load(gmixb, gmix_d[0:1, :], 64, 1024, 'gmixb')
            lbb = AL.get(64, 512, F32)
            omlb = AL.get(64, 512, F32)
            lb1 = AL.get(64, 512, F32)
            bcast_load(lbb, lbp_d[0:1, :], 64, 512, 'lbb')
            bcast_load(lb1, lbp_d[1:2, :], 64, 512, 'lb1')
            S.op('dve', lambda e: e.tensor_tensor(out=lbb, in0=lbb, in1=lb1, op=ALU.subtract),
                 reads=['lbb', 'lb1'], writes=['lbb'])
            S.op('act', lambda e: e.activation(out=lbb, in_=lbb, func=AF.Sigmoid), reads=['lbb'], writes=['lbb'])
            S.op('dve', lambda e: e.tensor_scalar(out=omlb, in0=lbb, scalar1=-1.0, scalar2=1.0, op0=ALU.mult,
                                                  op1=ALU.add), reads=['lbb'], writes=['omlb'])
            gnab = AL.get(64, 128, F32)
            bcast_load(gnab, gna_d[0:1, :], 64, 128, 'gnab')
            xt = AL.get(64, 1024, F32)
            hb = AL.get(64, 1024, BF16)
            hT = AL.get(128, [8, 64], BF16)
            ss = AL.get(64, 1, F32)
            rs = AL.get(64, 1, F32)
            F = [AL.get(64, 512, F32) for _ in range(7)]
            qt = AL.get(64, 512, BF16)
            kt = AL.get(64, 512, BF16)
            va = AL.get(64, 512, BF16)
            km = AL.get(64, 512, BF16)
            qkT = AL.get(128, [8, 64], BF16)
            qm = AL.get(128, [4, 64], BF16)
            attm = AL.get(64, [4, 64], BF16)
            ebL = AL.get(128, [4, 16], F32)
            Sa = AL.get(128, [4, 128], F32)
            Sab = AL.get(128, [4, 128], BF16)
            ss4 = AL.get(64, 4, F32)
            rs4 = AL.get(64, 4, F32)
            ogt = AL.get(64, 512, BF16)
            oT = AL.get(128, [4, 64], BF16)
            m1t = AL.get(64, 1024, F32)
            S.op('pool', lambda e: e.memset(Sa, 0.0), writes=['Sa'])
            S.op('pool', lambda e: e.memset(Sab, 0.0), writes=['Sab'])

            def tileA1(ti, T):
                nch = T['nch']
                r0 = ti * 64
                S.dma('sp', lambda e: e.dma_start(out=xt, in_=x_d[r0:r0 + 64, :]), writes=['xt'])
                rmsnorm_to_bf16(xt, 'xt', gmixb, 'gmixb', hb, 'hb', ss, rs)
                dbg('xt', xt, 'xt', ti)
                dbg('rs', rs, 'rs', ti)
                dbg('hb', hb, 'hb', ti)
                transpose_tm(hb, 'hb', 8, hT, 'hT')
                dbg('hT', hT, 'hT', ti)
                dbg('winA', winA[:, :, 0:512], 'winA', ti)
                sig, q, kk, logf, eb, enb, og = F
                proj_tm(bank(0, 64), bk(0), hT, 'hT', winA, 'winA', 512, 512)
                S.op('act', lambda e: e.activation(out=sig, in_=bank(0, 64), func=AF.Sigmoid), reads=bk(0), writes=['F0'])
                proj_tm(bank(1, 64), bk(1), hT, 'hT', winA, 'winA', 0, 512)
                S.op('act', lambda e: e.activation(out=q, in_=bank(1, 64), func=AF.Silu), reads=bk(1), writes=['F1'])
                S.op('dve', lambda e: e.tensor_tensor(out=sig, in0=sig, in1=omlb, op=ALU.mult),
                     reads=['F0', 'omlb'], writes=['F0'])
                S.op('dve', lambda e: e.tensor_tensor(out=sig, in0=sig, in1=lbb, op=ALU.add),
                     reads=['F0', 'lbb'], writes=['F0'])
                S.op('dve', lambda e: e.tensor_scalar(out=kk, in0=sig, scalar1=-1.0, scalar2=1.0, op0=ALU.mult,
                                                      op1=ALU.add), reads=['F0'], writes=['F2'])
                S.op('act', lambda e: e.activation(out=logf, in_=sig, func=AF.Ln), reads=['F0'], writes=['F3'])
                dbg('q', q, 'F1', ti)
                dbg('f', sig, 'F0', ti)
                dbg('logf', logf, 'F3', ti)
                S.op('pe', lambda e: e.matmul(bank(2, 64), lhsT=T['tri'], rhs=logf, start=True, stop=True),
                     reads=['cst', 'F3'], writes=bk(2))
                S.op('act', lambda e: e.activation(out=eb, in_=bank(2, 64), func=AF.Exp), reads=bk(2), writes=['F4'])
                S.op('act', lambda e: e.activation(out=enb, in_=bank(2, 64), func=AF.Exp, scale=-1.0),
                     reads=bk(2), writes=['F5'])
                S.op('dve', lambda e: e.tensor_tensor(out=qt, in0=q, in1=eb, op=ALU.mult),
                     reads=['F1', 'F4'], writes=['qt'])
                S.op('dve', lambda e: e.tensor_tensor(out=kt, in0=kk, in1=enb, op=ALU.mult),
                     reads=['F2', 'F5'], writes=['kt'])
                for h in range(4):
                    S.op('pe', lambda e, h=h: e.matmul(bank(0, 128, nch, h * 16), lhsT=logf[:, h * 128:(h + 1) * 128],
                                                       rhs=T['cind'][:, 0:nch], start=True, stop=True),
                         reads=['F3', 'cst'], writes=bk(0))
                S.op('act', lambda e: e.activation(out=ebL[:, :, 0:nch],
                                                   in_=bank(0, 128, 64).rearrange("p (h c) -> p h c", h=4)[:, :, 0:nch],
                                                   func=AF.Exp), reads=bk(0), writes=['ebL'])
                proj_tm(bank(1, 64), bk(1), hT, 'hT', winA, 'winA', 1024, 512)
                S.op('act', lambda e: e.copy(out=va, in_=bank(1, 64)), reads=bk(1), writes=['va'])
                proj_tm(bank(2, 64), bk(2), hT, 'hT', winA, 'winA', 1536, 512)
                S.op('act', lambda e: e.activation(out=og, in_=bank(2, 64), func=AF.Silu), reads=bk(2), writes=['F6'])
                for h in range(4):
                    S.op('pe', lambda e, h=h: e.transpose(out=PTt[:, h * 64:(h + 1) * 64], in_=qt[:, h * 128:(h + 1) * 128],
                                                          identity=identb[0:64, 0:64]),
                         reads=['qt', 'identb'], writes=['pt'])
                for h in range(4):
                    S.op('pe', lambda e, h=h: e.transpose(out=PTt[:, (4 + h) * 64:(5 + h) * 64],
                                                          in_=kt[:, h * 128:(h + 1) * 128], identity=identb[0:64, 0:64]),
                         reads=['kt', 'identb'], writes=['pt'])
                S.op('act', lambda e: e.copy(out=qkT, in_=PTt[:, 0:512].rearrange("p (k t) -> p k t", k=8)),
                     reads=['pt'], writes=['qkT'])
                for h in range(4):
                    S.op('pe', lambda e, h=h: e.matmul(bank(0, 64, 64, h * 64), lhsT=qkT[:, 4 + h, :], rhs=qkT[:, h, :],
                                                       start=True, stop=True), reads=['qkT'], writes=bk(0))
                S.op('dve', lambda e: e.tensor_tensor(out=attm, in0=bank(0, 64, 256).rearrange("p (h t) -> p h t", h=4),
                                                      in1=T['tri'].unsqueeze(1).to_broadcast([64, 4, 64]), op=ALU.mult),
                     reads=bk(0) + ['cst'], writes=['attm'])
                for h in range(4):
                    S.op('pe', lambda e, h=h: e.matmul(bank(5, 64, 128, h * 128), lhsT=attm[:, h, :],
                                                       rhs=va[:, h * 128:(h + 1) * 128], start=(h == 0), stop=False,
                                                       skip_group_check=True),
                         reads=['attm', 'va'], writes=bk(5))
                for c in range(nch):
                    if nch > 1:
                        S.dma('sp', lambda e, c=c: e.dma_start(out=Sa, in_=sh_d[c].rearrange("h d v -> d h v")),
                              writes=['Sa'])
                        S.op('act', lambda e: e.copy(out=Sab, in_=Sa), reads=['Sa'], writes=['Sab'])
                        S.op('dve', lambda e, c=c: e.tensor_tensor(
                            out=qm, in0=qkT[:, 0:4, :], in1=cmb[:, c, :].unsqueeze(1).to_broadcast([128, 4, 64]),
                            op=ALU.mult), reads=['qkT', 'cmb'], writes=['qm'])
                        S.op('dve', lambda e, c=c: e.tensor_scalar(out=km, in0=kt, scalar1=T['cind'][:, c:c + 1],
                                                                   scalar2=None, op0=ALU.mult),
                             reads=['kt', 'cst'], writes=['km'])
                        qsrc, qkey, ksrc, kkey = qm, 'qm', km, 'km'
                    else:
                        qsrc, qkey, ksrc, kkey = qkT, 'qkT', kt, 'kt'
                    for h in range(4):
                        S.op('pe', lambda e, h=h, qsrc=qsrc, c=c: e.matmul(
                            bank(5, 64, 128, h * 128), lhsT=qsrc[:, h, :], rhs=Sab[:, h, :],
                            start=False, stop=(c == nch - 1), skip_group_check=True), reads=[qkey, 'Sab'], writes=bk(5))
                    for h in range(4):
                        S.op('pe', lambda e, h=h, ksrc=ksrc: e.matmul(
                            bank(6, 128, 128, h * 128), lhsT=ksrc[:, h * 128:(h + 1) * 128],
                            rhs=va[:, h * 128:(h + 1) * 128], start=True, stop=True),
                            reads=[kkey, 'va'], writes=bk(6))
                    S.op('dve', lambda e: e.tensor_tensor(out=Sa, in0=bank(6).rearrange("p (h v) -> p h v", h=4),
                                                          in1=Sa, op=ALU.add), reads=bk(6) + ['Sa'], writes=['Sa'])
                    S.op('dve', lambda e, c=c: e.tensor_tensor(
                        out=Sa, in0=Sa, in1=ebL[:, :, c:c + 1].to_broadcast([128, 4, 128]), op=ALU.mult),
                        reads=['Sa', 'ebL'], writes=['Sa'])
                    if nch > 1:
                        S.dma('sp', lambda e, c=c: e.dma_start(out=hs_d[c].rearrange("h d v -> d h v"), in_=Sa),
                              reads=['Sa'], final=True)
                    else:
                        S.op('act', lambda e: e.copy(out=Sab, in_=Sa), reads=['Sa'], writes=['Sab'])
                if nch == 1 and ti == NPT - 1:
                    S.dma('sp', lambda e: e.dma_start(out=hp_d.rearrange("h d v -> d h v"), in_=Sa),
                          reads=['Sa'], final=True)
                osb, sq, on = F[0], F[1], F[2]
                S.op('act', lambda e: e.copy(out=osb, in_=bank(5, 64)), reads=bk(5), writes=['F0'])
                dbg('osb', osb, 'F0', ti, DBGT)
                dbg('og', og, 'F6', ti, DBGT)
                head_norm_gate_keys(osb, 'F0', gnab, 'gnab', og, 'F6', sq, 'F1', on, 'F2', ogt, 'ogt', ss4, rs4)
                dbg('on', on, 'F2', ti, DBGT)
                transpose_tm(ogt, 'ogt', 4, oT, 'oT')
                for half in range(2):
                    for k in range(4):
                        S.op('pe', lambda e, k=k, half=half: e.matmul(
                            bank(3 + half, 64), lhsT=oT[:, k, :], rhs=wbrA[:, k, half * 512:(half + 1) * 512],
                            start=(k == 0), stop=(k == 3)), reads=['oT', 'wbrA'], writes=bk(3 + half))
                    proj_tm(bank(half, 64), bk(half), hT, 'hT', wgA, 'wgA', half * 512, 512)
                    sg = F[3 + half]
                    S.op('act', lambda e, half=half, sg=sg: e.activation(out=sg, in_=bank(half, 64), func=AF.Sigmoid),
                         reads=bk(half), writes=['F%d' % (3 + half)])
                    S.op('dve', lambda e, half=half, sg=sg: e.tensor_tensor(
                        out=m1t[:, half * 512:(half + 1) * 512], in0=bank(3 + half, 64), in1=sg, op=ALU.mult),
                        reads=bk(3 + half) + ['F%d' % (3 + half)], writes=['m1t'])
                S.dma('sp', lambda e: e.dma_start(out=m1_d[r0:r0 + 64, :], in_=m1t), reads=['m1t'],
                      writes=[('m1', ti)])

            if 'A1' in PHASES:
                for ti in (range(NPT) if TILES is None else TILES):
                    tileA1(ti, TP)
                tileA1(NPT, TS)
        phase_A1()
        S.barrier()

        def phase_A2():
            AL.off = pers_mark
            winB = AL.get(128, [8, 2056], BF16)
            wgB = AL.get(128, [8, 1024], BF16)
            wbrB = AL.get(128, [4, 1024], BF16)
            woutb = AL.get(128, [8, 1024], BF16)
            load_w_bf16(winB, win_d, 0, 8, 2048, 2056, 'winB')
            load_w_bf16(wgB, win_d, 0, 8, 5128, 1024, 'wgB')
            load_w_bf16(wbrB, wbrb_d, 0, 4, 0, 1024, 'wbrB')
            load_w_bf16(woutb, wout_d, 0, 8, 0, 1024, 'woutb')
            gmixb = AL.get(64, 1024, F32)
            bcast_load(gmixb, gmix_d[0:1, :], 64, 1024, 'gmixb')
            gnbb = AL.get(64, 128, F32)
            bcast_load(gnbb, gnb_d[0:1, :], 64, 128, 'gnbb')
            negA = AL.get(64, 4, F32)
            dtbb = AL.get(64, 4, F32)
            bcast_load(negA, alog_d[0:1, :], 64, 4, 'negA')
            bcast_load(dtbb, dtb_d[0:1, :], 64, 4, 'dtbb')
            S.op('act', lambda e: e.activation(out=negA, in_=negA, func=AF.Exp), reads=['negA'], writes=['negA'])
            S.op('dve', lambda e: e.tensor_scalar(out=negA, in0=negA, scalar1=-1.0, scalar2=None, op0=ALU.mult),
                 reads=['negA'], writes=['negA'])
            cwin = AL.get(4, 1536, F32)
            cw = AL.get(128, [12, 4], F32)
            S.dma('sp', lambda e: e.dma_start(out=cwin, in_=convw_d), writes=['cwin'])
            for g in range(12):
                S.op('pe', lambda e, g=g: e.transpose(out=bank(0, 128, 4, g * 4), in_=cwin[0:4, g * 128:(g + 1) * 128],
                                                      identity=identf[0:4, 0:4]), reads=['cwin', 'cst'], writes=bk(0))
            S.op('dve', lambda e: e.tensor_copy(out=cw, in_=bank(0, 128, 48).rearrange("p (g j) -> p g j", g=12)),
                 reads=bk(0), writes=['cw'])
            xt = AL.get(64, 1024, F32)
            hb = AL.get(64, 1024, BF16)
            hT = AL.get(128, [8, 64], BF16)
            ss = AL.get(64, 1, F32)
            rs = AL.get(64, 1, F32)
            F = [AL.get(64, 512, F32) for _ in range(6)]
            rawext = AL.get(128, 12 * 112, F32)
            acc = AL.get(128, [12, 64], F32)
            sqn = AL.get(128, 512, F32)
            qkn = AL.get(128, [8, 64], BF16)
            qknm = AL.get(128, [8, 64], BF16)
            vcb = AL.get(128, [4, 64], BF16)
            vk = AL.get(64, [8, 128], BF16)
            sm = AL.get(64, 64, F32)
            beta, zz, ez, spl, gg, gcum, gLb, eg, ngcum, dkk, kdec, nbeta, nbe = [sm[:, i * 4:(i + 1) * 4] for i in range(13)]
            gc = AL.get(64, [4, 16], F32)
            egLT = AL.get(128, [4, 16], F32)
            gd = AL.get(64, [4, 64], F32)
            gam = AL.get(64, [4, 64], F32)
            gamT = AL.get(64, [4, 64], F32)
            Nb = [AL.get(64, [4, 64], BF16) for _ in range(2)]
            Mb = [AL.get(64, [4, 64], BF16) for _ in range(2)]
            Qb = AL.get(64, [4, 64], BF16)
            Qf = AL.get(64, [4, 64], F32)
            rb = AL.get(64, [4, 128], BF16)
            ub = AL.get(64, [4, 128], BF16)
            ATb = AL.get(64, [4, 64], BF16)
            khat = AL.get(64, [4, 128], BF16)
            khm = AL.get(64, [4, 128], BF16)
            Sd = AL.get(128, [4, 128], F32)
            Sdb = AL.get(128, [4, 128], BF16)
            ss4 = AL.get(64, 4, F32)
            rs4 = AL.get(64, 4, F32)
            ogt = AL.get(64, 512, BF16)
            oT = AL.get(128, [4, 64], BF16)
            m1t = AL.get(64, 1024, F32)
            mb = AL.get(64, 1024, BF16)
            mT = AL.get(128, [8, 64], BF16)
            S.op('pool', lambda e: e.memset(Sd, 0.0), writes=['Sd'])
            S.op('pool', lambda e: e.memset(Sdb, 0.0), writes=['Sdb'])
            S.op('pool', lambda e: e.memset(rawext, 0.0), writes=['rawext'])

            def tileA2(ti, T):
                nch, C = T['nch'], T['C']
                r0 = ti * 64
                W = 3 + C
                rx = rawext[:, 0:12 * nch * W].rearrange("p (g s w) -> p g s w", g=12, s=nch)
                S.dma('sp', lambda e: e.dma_start(out=xt, in_=x_d[r0:r0 + 64, :]), writes=['xt'])
                S.dma('sp', lambda e: e.dma_start(out=m1t, in_=m1_d[r0:r0 + 64, :]), reads=[('m1', ti)], writes=['m1t'])
                rmsnorm_to_bf16(xt, 'xt', gmixb, 'gmixb', hb, 'hb', ss, rs)
                transpose_tm(hb, 'hb', 8, hT, 'hT')
                praw = PB[:, 3 * 512:3 * 512 + 768].rearrange("p (g t) -> p g t", g=12)
                for g in range(12):
                    for k in range(8):
                        S.op('pe', lambda e, g=g, k=k: e.matmul(praw[:, g, :], lhsT=winB[:, k, g * 128:(g + 1) * 128],
                                                                rhs=hT[:, k, :], start=(k == 0), stop=(k == 7)),
                             reads=['winB', 'hT'], writes=bk(3, 4))
                if nch == 1:
                    if ti > 0:
                        S.op('pool', lambda e: e.tensor_copy(out=rx[:, :, 0, 0:3], in_=rx[:, :, 0, 64:67]),
                             reads=['rawext'], writes=['rawext'])
                    S.op('act', lambda e: e.copy(out=rx[:, :, 0, 3:67], in_=praw), reads=bk(3, 4), writes=['rawext'])
                else:
                    cvin = F[0:3]
                    for i3 in range(3):
                        S.dma('sp', lambda e, i3=i3: e.dma_start(out=cvin[i3][0:48, :],
                                                                 in_=scv_d[:, i3 * 512:(i3 + 1) * 512]),
                              writes=['F%d' % i3])
                    for g in range(12):
                        S.op('pe', lambda e, g=g: e.transpose(
                            out=bank(0, 128, 48, g * 64), in_=cvin[g // 4][0:48, (g % 4) * 128:(g % 4 + 1) * 128],
                            identity=identf[0:48, 0:48]), reads=['F%d' % (g // 4), 'cst'], writes=bk(0, 1))
                    S.op('dve', lambda e: e.tensor_copy(
                        out=rx[:, :, :, 0:3],
                        in_=PB[:, 0:768].rearrange("p (g x) -> p g x", g=12)[:, :, 0:48].rearrange(
                            "p g (s j) -> p g s j", s=16)),
                        reads=bk(0, 1), writes=['rawext'])
                    S.op('act', lambda e: e.copy(out=rx[:, :, :, 3:7],
                                                 in_=praw.rearrange("p g (s t) -> p g s t", s=16)),
                         reads=bk(3, 4), writes=['rawext'])
                accv = acc.rearrange("p g (s t) -> p g s t", s=nch)
                for g in range(12):
                    S.op('dve', lambda e, g=g: e.tensor_scalar(out=accv[:, g], in0=rx[:, g, :, 0:C],
                                                               scalar1=cw[:, g, 0:1], scalar2=None, op0=ALU.mult),
                         reads=['rawext', 'cw'], writes=[('acc', g)])
                    for j in range(1, 4):
                        S.op('dve', lambda e, g=g, j=j: e.scalar_tensor_tensor(
                            out=accv[:, g], in0=rx[:, g, :, j:j + C], scalar=cw[:, g, j:j + 1], in1=accv[:, g],
                            op0=ALU.mult, op1=ALU.add), reads=['rawext', 'cw', ('acc', g)], writes=[('acc', g)])
                acck = [('acc', g) for g in range(12)]
                S.op('act', lambda e: e.activation(out=acc, in_=acc, func=AF.Silu), reads=acck, writes=acck)
                S.op('act', lambda e: e.activation(out=sqn, in_=acc[:, 0:8, :].rearrange("p g t -> p (g t)"),
                                                   func=AF.Square), reads=acck, writes=['sqn'])
                S.op('pe', lambda e: e.matmul(bank(0), lhsT=onesf, rhs=sqn, start=True, stop=True),
                     reads=['cst', 'sqn'], writes=bk(0))
                S.op('act', lambda e: e.activation(out=sqn, in_=bank(0), func=AF.Sqrt, bias=epsc), reads=bk(0) + ['cst'],
                     writes=['sqn'])
                S.op('dve', lambda e: e.reciprocal(out=sqn, in_=sqn), reads=['sqn'], writes=['sqn'])
                sq3 = sqn.rearrange("p (g t) -> p g t", g=8)
                S.op('dve', lambda e: e.scalar_tensor_tensor(out=qkn[:, 0:4, :], in0=acc[:, 0:4, :], scalar=128.0 ** -0.5,
                                                             in1=sq3[:, 0:4, :], op0=ALU.mult, op1=ALU.mult),
                     reads=acck + ['sqn'], writes=['qkn'])
                S.op('dve', lambda e: e.tensor_tensor(out=qkn[:, 4:8, :], in0=acc[:, 4:8, :], in1=sq3[:, 4:8, :],
                                                      op=ALU.mult), reads=acck + ['sqn'], writes=['qkn'])
                S.op('act', lambda e: e.copy(out=vcb, in_=acc[:, 8:12, :]), reads=acck, writes=['vcb'])
                for h in range(4):
                    S.op('pe', lambda e, h=h: e.transpose(out=PTt[0:64, h * 128:(h + 1) * 128], in_=vcb[:, h, :],
                                                          identity=identb), reads=['vcb', 'identb'], writes=['pt'])
                for h in range(4):
                    S.op('pe', lambda e, h=h: e.transpose(out=PTt[0:64, (4 + h) * 128:(5 + h) * 128], in_=qkn[:, 4 + h, :],
                                                          identity=identb), reads=['qkn', 'identb'], writes=['pt'])
                S.op('act', lambda e: e.copy(out=vk, in_=PTt[0:64, :].rearrange("p (k v) -> p k v", k=8)),
                     reads=['pt'], writes=['vk'])
                proj_tm(bank(1, 64, 8), bk(1), hT, 'hT', winB, 'winB', 2048, 8)
                S.op('act', lambda e: e.activation(out=beta, in_=bank(1, 64, 4), func=AF.Sigmoid), reads=bk(1), writes=['sm'])
                S.op('dve', lambda e: e.tensor_tensor(out=zz, in0=bank(1, 64, 4, 4), in1=dtbb, op=ALU.add),
                     reads=bk(1) + ['dtbb'], writes=['sm'])
                S.op('act', lambda e: e.activation(out=ez, in_=zz, func=AF.Exp), reads=['sm'], writes=['sm'])
                S.op('act', lambda e: e.activation(out=spl, in_=ez, func=AF.Ln, bias=C_('one', 64)), reads=['sm', 'cst'],
                     writes=['sm'])
                S.op('dve', lambda e: e.tensor_tensor(out=gg, in0=spl, in1=negA, op=ALU.mult), reads=['sm', 'negA'],
                     writes=['sm'])
                S.op('pe', lambda e: e.matmul(bank(2, 64, 4), lhsT=T['tri'], rhs=gg, start=True, stop=True),
                     reads=['cst', 'sm'], writes=bk(2))
                S.op('pe', lambda e: e.matmul(bank(2, 64, 4, 4), lhsT=T['blk'], rhs=gg, start=True, stop=True),
                     reads=['cst', 'sm'], writes=bk(2))
                S.op('dve', lambda e: e.tensor_copy(out=sm[:, 20:28], in_=bank(2, 64, 8)), reads=bk(2), writes=['sm'])
                S.op('act', lambda e: e.activation(out=eg, in_=gcum, func=AF.Exp), reads=['sm'], writes=['sm'])
                S.op('dve', lambda e: e.tensor_scalar(out=ngcum, in0=gcum, scalar1=-1.0, scalar2=None, op0=ALU.mult),
                     reads=['sm'], writes=['sm'])
                S.op('dve', lambda e: e.tensor_tensor(out=dkk, in0=gLb, in1=gcum, op=ALU.subtract), reads=['sm'],
                     writes=['sm'])
                S.op('act', lambda e: e.activation(out=kdec, in_=dkk, func=AF.Exp), reads=['sm'], writes=['sm'])
                S.op('dve', lambda e: e.tensor_scalar(out=nbeta, in0=beta, scalar1=-1.0, scalar2=None, op0=ALU.mult),
                     reads=['sm'], writes=['sm'])
                S.op('dve', lambda e: e.tensor_tensor(out=nbe, in0=nbeta, in1=eg, op=ALU.mult), reads=['sm'],
                     writes=['sm'])
                S.op('dve', lambda e: e.tensor_tensor(out=gc[:, :, 0:nch], in0=gg.unsqueeze(2).to_broadcast([64, 4, nch]),
                                                      in1=T['cind'][:, 0:nch].unsqueeze(1).to_broadcast([64, 4, nch]),
                                                      op=ALU.mult), reads=['sm', 'cst'], writes=['gc'])
                for h in range(4):
                    S.op('pe', lambda e, h=h: e.matmul(bank(1, 128, nch, 64 + h * 16), lhsT=onesf[0:64, :],
                                                       rhs=gc[:, h, 0:nch], start=True, stop=True),
                         reads=['cst', 'gc'], writes=bk(1))
                S.op('act', lambda e: e.activation(
                    out=egLT[:, :, 0:nch], in_=bank(1, 128, 64, 64).rearrange("p (h c) -> p h c", h=4)[:, :, 0:nch],
                    func=AF.Exp), reads=bk(1), writes=['egLT'])
                S.op('dve', lambda e: e.tensor_tensor(out=gd, in0=gcum.unsqueeze(2).to_broadcast([64, 4, 64]),
                                                      in1=identf[0:64, 0:64].unsqueeze(1).to_broadcast([64, 4, 64]),
                                                      op=ALU.mult), reads=['sm', 'cst'], writes=['gd'])
                gd2 = gd.rearrange("p h t -> p (h t)")
                S.op('pe', lambda e: e.matmul(bank(2, 64, 256), lhsT=onesf[0:64, 0:64], rhs=gd2, start=True, stop=False),
                     reads=['cst', 'gd'], writes=bk(2))
                S.op('pe', lambda e: e.matmul(bank(2, 64, 256), lhsT=identf[0:64, 0:64], rhs=T['pmS'], start=False,
                                              stop=True), reads=['cst'], writes=bk(2))
                S.op('pe', lambda e: e.matmul(bank(2, 64, 256, 256), lhsT=onesf[0:64, 0:64], rhs=gd2, start=True,
                                              stop=False), reads=['cst', 'gd'], writes=bk(2))
                S.op('pe', lambda e: e.matmul(bank(2, 64, 256, 256), lhsT=identf[0:64, 0:64], rhs=T['nmT'], start=False,
                                              stop=True), reads=['cst'], writes=bk(2))
                for h in range(4):
                    S.op('act', lambda e, h=h: e.activation(out=gam[:, h, :], in_=bank(2, 64, 64, h * 64), func=AF.Exp,
                                                            scale=-1.0, bias=gcum[:, h:h + 1]),
                         reads=bk(2) + ['sm'], writes=['gam'])
                for h in range(4):
                    S.op('act', lambda e, h=h: e.activation(out=gamT[:, h, :], in_=bank(2, 64, 64, 256 + h * 64),
                                                            func=AF.Exp, bias=ngcum[:, h:h + 1]),
                         reads=bk(2) + ['sm'], writes=['gamT'])
                for h in range(4):
                    S.op('pe', lambda e, h=h: e.matmul(bank(0, 64, 64, h * 64), lhsT=qkn[:, 4 + h, :], rhs=qkn[:, 4 + h, :],
                                                       start=True, stop=True), reads=['qkn'], writes=bk(0))
                for h in range(4):
                    S.op('dve', lambda e, h=h: e.scalar_tensor_tensor(
                        out=Nb[0][:, h, :], in0=bank(0, 64, 64, h * 64), scalar=nbeta[:, h:h + 1], in1=gam[:, h, :],
                        op0=ALU.mult, op1=ALU.mult), reads=bk(0) + ['sm', 'gam'], writes=['Nb0'])
                for h in range(4):
                    S.op('pe', lambda e, h=h: e.transpose(out=PTt[0:64, h * 64:(h + 1) * 64], in_=Nb[0][:, h, :],
                                                          identity=identb[0:64, 0:64]),
                         reads=['Nb0', 'identb'], writes=['pt'])
                ptv = PTt[0:64, 0:256].rearrange("p (h t) -> p h t", h=4)
                S.op('dve', lambda e: e.tensor_copy(out=Mb[0], in_=ptv), reads=['pt'], writes=['Mb0'])
                S.op('dve', lambda e: e.tensor_tensor(out=Qf, in0=ptv,
                                                      in1=identf[0:64, 0:64].unsqueeze(1).to_broadcast([64, 4, 64]),
                                                      op=ALU.add), reads=['pt', 'cst'], writes=['Qf'])
                S.op('act', lambda e: e.copy(out=Qb, in_=Qf), reads=['Qf'], writes=['Qb'])
                nsteps = {64: 5, 4: 1}[C]
                cur = 0
                for i in range(nsteps):
                    last = (i == nsteps - 1)
                    nxt = 1 - cur
                    for h in range(4):
                        S.op('pe', lambda e, h=h, cur=cur: e.matmul(bank(0, 64, 64, h * 64), lhsT=Mb[cur][:, h, :],
                                                                    rhs=Nb[cur][:, h, :], start=True, stop=True),
                             reads=['Mb%d' % cur, 'Nb%d' % cur], writes=bk(0))
                    if not last:
                        for h in range(4):
                            S.op('pe', lambda e, h=h, cur=cur: e.matmul(bank(1, 64, 64, h * 64), lhsT=Nb[cur][:, h, :],
                                                                        rhs=Mb[cur][:, h, :], start=True, stop=True),
                                 reads=['Mb%d' % cur, 'Nb%d' % cur], writes=bk(1))
                    S.op('act', lambda e, nxt=nxt: e.copy(out=Nb[nxt],
                                                          in_=bank(0, 64, 256).rearrange("p (h t) -> p h t", h=4)),
                         reads=bk(0), writes=['Nb%d' % nxt])
                    if not last:
                        S.op('dve', lambda e, nxt=nxt: e.tensor_copy(
                            out=Mb[nxt], in_=bank(1, 64, 256).rearrange("p (h t) -> p h t", h=4)),
                            reads=bk(1), writes=['Mb%d' % nxt])
                    for h in range(4):
                        S.op('pe', lambda e, h=h, nxt=nxt: e.matmul(bank(2, 64, 64, h * 64), lhsT=Nb[nxt][:, h, :],
                                                                    rhs=Qb[:, h, :], start=True, stop=True),
                             reads=['Nb%d' % nxt, 'Qb'], writes=bk(2))
                    S.op('dve', lambda e: e.tensor_tensor(out=Qf, in0=Qf,
                                                          in1=bank(2, 64, 256).rearrange("p (h t) -> p h t", h=4),
                                                          op=ALU.add), reads=['Qf'] + bk(2), writes=['Qf'])
                    S.op('act', lambda e: e.copy(out=Qb, in_=Qf), reads=['Qf'], writes=['Qb'])
                    cur = nxt
                for h in range(4):
                    S.op('pe', lambda e, h=h: e.matmul(bank(0, 64, 64, h * 64), lhsT=qkn[:, 4 + h, :], rhs=qkn[:, h, :],
                                                       start=True, stop=True), reads=['qkn'], writes=bk(0))
                S.op('dve', lambda e: e.tensor_tensor(out=ATb, in0=bank(0, 64, 256).rearrange("p (h t) -> p h t", h=4),
                                                      in1=gamT, op=ALU.mult), reads=bk(0) + ['gamT'], writes=['ATb'])
                for c in range(nch):
                    if nch > 1:
                        S.dma('sp', lambda e, c=c: e.dma_start(out=Sd, in_=sd_d[c].rearrange("h d v -> d h v")),
                              writes=['Sd'])
                        S.op('act', lambda e: e.copy(out=Sdb, in_=Sd), reads=['Sd'], writes=['Sdb'])
                        S.op('dve', lambda e, c=c: e.tensor_tensor(
                            out=qknm, in0=qkn, in1=cmb[:, c, :].unsqueeze(1).to_broadcast([128, 8, 64]), op=ALU.mult),
                            reads=['qkn', 'cmb'], writes=['qknm'])
                        src, skey = qknm, 'qknm'
                    else:
                        src, skey = qkn, 'qkn'
                    for h in range(4):
                        S.op('pe', lambda e, h=h, src=src, c=c: e.matmul(
                            bank(5, 64, 128, h * 128), lhsT=src[:, 4 + h, :], rhs=Sdb[:, h, :], start=(c == 0 and h == 0),
                            stop=(c == nch - 1), skip_group_check=True), reads=[skey, 'Sdb'], writes=bk(5))
                        S.op('pe', lambda e, h=h, src=src, c=c: e.matmul(
                            bank(6, 64, 128, h * 128), lhsT=src[:, h, :], rhs=Sdb[:, h, :], start=(c == 0 and h == 0),
                            stop=(c == nch - 1), skip_group_check=True), reads=[skey, 'Sdb'], writes=bk(6))
                t1, bv, t2, ob = F[3], F[4], F[3], F[4]
                S.op('dve', lambda e: e.tensor_tensor(out=t1.rearrange("p (h v) -> p h v", h=4),
                                                      in0=bank(5, 64).rearrange("p (h v) -> p h v", h=4),
                                                      in1=nbe.unsqueeze(2).to_broadcast([64, 4, 128]), op=ALU.mult),
                     reads=bk(5) + ['sm'], writes=['F3'])
                S.op('dve', lambda e: e.tensor_tensor(out=bv.rearrange("p (h v) -> p h v", h=4), in0=vk[:, 0:4, :],
                                                      in1=beta.unsqueeze(2).to_broadcast([64, 4, 128]), op=ALU.mult),
                     reads=['vk', 'sm'], writes=['F4'])
                S.op('dve', lambda e: e.tensor_tensor(out=rb.rearrange("p h v -> p (h v)"), in0=t1, in1=bv, op=ALU.add),
                     reads=['F3', 'F4'], writes=['rb'])
                for h in range(4):
                    S.op('pe', lambda e, h=h: e.matmul(bank(1, 64, 128, h * 128), lhsT=Qb[:, h, :], rhs=rb[:, h, :],
                                                       start=True, stop=True), reads=['Qb', 'rb'], writes=bk(1))
                S.op('act', lambda e: e.copy(out=ub, in_=bank(1, 64).rearrange("p (h v) -> p h v", h=4)),
                     reads=bk(1), writes=['ub'])
                for h in range(4):
                    S.op('pe', lambda e, h=h: e.matmul(bank(2, 64, 128, h * 128), lhsT=ATb[:, h, :], rhs=ub[:, h, :],
                                                       start=True, stop=True), reads=['ATb', 'ub'], writes=bk(2))
                S.op('dve', lambda e: e.tensor_tensor(out=t2.rearrange("p (h v) -> p h v", h=4),
                                                      in0=bank(6, 64).rearrange("p (h v) -> p h v", h=4),
                                                      in1=eg.unsqueeze(2).to_broadcast([64, 4, 128]), op=ALU.mult),
                     reads=bk(6) + ['sm'], writes=['F3'])
                S.op('dve', lambda e: e.tensor_tensor(out=ob, in0=t2, in1=bank(2, 64), op=ALU.add),
                     reads=['F3'] + bk(2), writes=['F4'])
                S.op('dve', lambda e: e.tensor_tensor(out=khat, in0=vk[:, 4:8, :],
                                                      in1=kdec.unsqueeze(2).to_broadcast([64, 4, 128]), op=ALU.mult),
                     reads=['vk', 'sm'], writes=['khat'])
                for c in range(nch):
                    if nch > 1:
                        S.dma('sp', lambda e, c=c: e.dma_start(out=Sd, in_=sd_d[c].rearrange("h d v -> d h v")),
                              writes=['Sd'])
                        S.op('dve', lambda e, c=c: e.tensor_scalar(out=khm, in0=khat, scalar1=T['cind'][:, c:c + 1],
                                                                   scalar2=None, op0=ALU.mult),
                             reads=['khat', 'cst'], writes=['khm'])
                        ks_, kkey = khm, 'khm'
                    else:
                        ks_, kkey = khat, 'khat'
                    for h in range(4):
                        S.op('pe', lambda e, h=h, ks_=ks_: e.matmul(bank(0, 128, 128, h * 128), lhsT=ks_[:, h, :],
                                                                    rhs=ub[:, h, :], start=True, stop=True),
                             reads=[kkey, 'ub'], writes=bk(0))
                    S.op('dve', lambda e, c=c: e.tensor_tensor(
                        out=Sd, in0=Sd, in1=egLT[:, :, c:c + 1].to_broadcast([128, 4, 128]), op=ALU.mult),
                        reads=['Sd', 'egLT'], writes=['Sd'])
                    S.op('dve', lambda e: e.tensor_tensor(out=Sd, in0=Sd, in1=bank(0).rearrange("p (h v) -> p h v", h=4),
                                                          op=ALU.add), reads=['Sd'] + bk(0), writes=['Sd'])
                    if nch > 1:
                        S.dma('sp', lambda e, c=c: e.dma_start(out=ds_d[c].rearrange("h d v -> d h v"), in_=Sd),
                              reads=['Sd'], final=True)
                    else:
                        S.op('act', lambda e: e.copy(out=Sdb, in_=Sd), reads=['Sd'], writes=['Sdb'])
                if nch == 1 and ti == NPT - 1:
                    S.dma('sp', lambda e: e.dma_start(out=dp_d.rearrange("h d v -> d h v"), in_=Sd),
                          reads=['Sd'], final=True)
                if nch > 1 or ti == NPT - 1:
                    for i3 in range(3):
                        proj_tm(bank(0, 64), bk(0), hT, 'hT', winB, 'winB', i3 * 512, 512)
                        S.op('act', lambda e: e.copy(out=F[0], in_=bank(0, 64)), reads=bk(0), writes=['F0'])
                        if nch == 1:
                            S.dma('sp', lambda e, i3=i3: e.dma_start(out=cp_d[:, i3 * 512:(i3 + 1) * 512],
                                                                     in_=F[0][61:64, :]), reads=['F0'], final=True)
                        else:
                            for s in range(16):
                                S.dma('sp', lambda e, i3=i3, s=s: e.dma_start(
                                    out=cs_d[3 * s:3 * s + 3, i3 * 512:(i3 + 1) * 512], in_=F[0][4 * s + 1:4 * s + 4, :]),
                                    reads=['F0'], final=True)
                proj_tm(bank(0, 64), bk(0), hT, 'hT', winB, 'winB', 1536, 512)
                S.op('act', lambda e: e.activation(out=F[5], in_=bank(0, 64), func=AF.Silu), reads=bk(0), writes=['F5'])
                head_norm_gate_keys(ob, 'F4', gnbb, 'gnbb', F[5], 'F5', F[0], 'F0', F[1], 'F1', ogt, 'ogt', ss4, rs4)
                transpose_tm(ogt, 'ogt', 4, oT, 'oT')
                for half in range(2):
                    for k in range(4):
                        S.op('pe', lambda e, k=k, half=half: e.matmul(
                            bank(3 + half, 64), lhsT=oT[:, k, :], rhs=wbrB[:, k, half * 512:(half + 1) * 512],
                            start=(k == 0), stop=(k == 3)), reads=['oT', 'wbrB'], writes=bk(3 + half))
                    proj_tm(bank(half, 64), bk(half), hT, 'hT', wgB, 'wgB', half * 512, 512)
                    sg = F[2 + half]
                    S.op('act', lambda e, half=half, sg=sg: e.activation(out=sg, in_=bank(half, 64), func=AF.Sigmoid),
                         reads=bk(half), writes=['F%d' % (2 + half)])
                    S.op('dve', lambda e, half=half, sg=sg: e.tensor_tensor(out=sg, in0=bank(3 + half, 64), in1=sg,
                                                                            op=ALU.mult),
                         reads=bk(3 + half) + ['F%d' % (2 + half)], writes=['F%d' % (2 + half)])
                    S.op('dve', lambda e, half=half, sg=sg: e.tensor_tensor(
                        out=mb[:, half * 512:(half + 1) * 512], in0=sg, in1=m1t[:, half * 512:(half + 1) * 512],
                        op=ALU.add), reads=['F%d' % (2 + half), 'm1t'], writes=['mb'])
                transpose_tm(mb, 'mb', 8, mT, 'mT')
                for half in range(2):
                    proj_tm(bank(3 + half, 64), bk(3 + half), mT, 'mT', woutb, 'woutb', half * 512, 512)
                    S.op('dve', lambda e, half=half: e.tensor_tensor(
                        out=m1t[:, half * 512:(half + 1) * 512], in0=bank(3 + half, 64),
                        in1=xt[:, half * 512:(half + 1) * 512], op=ALU.add), reads=bk(3 + half) + ['xt'], writes=['m1t'])
                S.dma('sp', lambda e: e.dma_start(out=x1_d[r0:r0 + 64, :], in_=m1t), reads=['m1t'],
                      writes=[('x1', ti)])

            if 'A2' in PHASES:
                for ti in (range(NPT) if TILES is None else TILES):
                    tileA2(ti, TP)
                tileA2(NPT, TS)
        phase_A2()
        S.barrier()

        def phase_B():
            AL.off = pers_mark
            wqb = AL.get(128, [8, 2048], BF16)
            keysTb = AL.get(128, [16, 128], BF16)
            wpgb = AL.get(128, [8, 1024], BF16)
            wpleb = AL.get(128, [2, 1024], BF16)
            HB = [AL.get(128, 1024, BF16) for _ in range(NHB)]
            dltb = AL.get(128, [64, 64], BF16)
            WM = [AL.get(128, [64, 64], BF16) for _ in range(2)]
            Ug = [AL.get(128, 1024, BF16) for _ in range(NBUF)]
            Vg = [AL.get(128, 1024, BF16) for _ in range(NBUF)]
            junkb = AL.get(128, 1024, BF16)
            load_w_bf16(wqb, wq_d, 0, 8, 0, 2048, 'wqb')
            load_w_bf16(wpgb, wpg_d, 0, 8, 0, 1024, 'wpgb')
            load_w_bf16(wpleb, wple_d, 0, 2, 0, 1024, 'wpleb')
            S.dma('pool', lambda e: e.dma_start(out=keysTb, in_=keysT_d.rearrange("c d k -> d c k")), writes=['keysTb'])
            for cc in range(0, 4096, 2048):
                S.dma('pool', lambda e, cc=cc: e.dma_start(
                    out=dltb.rearrange("p n m -> p (n m)")[:, cc:cc + 2048], in_=dlt_d[:, cc:cc + 2048]), writes=['dltb'])
            gffnb = AL.get(64, 1024, F32)
            gpleb = AL.get(64, 1024, F32)
            gfinb = AL.get(64, 1024, F32)
            bcast_load(gffnb, gffn_d[0:1, :], 64, 1024, 'gffnb')
            bcast_load(gpleb, gple_d[0:1, :], 64, 1024, 'gpleb')
            bcast_load(gfinb, gfin_d[0:1, :], 64, 1024, 'gfinb')
            xtF = AL.get(64, 1024, F32)
            xtV2 = [AL.get(64, 1024, F32) for _ in range(2)]
            hbF = [AL.get(64, 1024, BF16) for _ in range(2)]
            hT = AL.get(128, [8, 64], BF16)
            hb2 = AL.get(64, 1024, BF16)
            hT2 = AL.get(128, [8, 64], BF16)
            ss = AL.get(64, 1, F32)
            rs = AL.get(64, 1, F32)
            ss2 = AL.get(64, 1, F32)
            rs2 = AL.get(64, 1, F32)
            qTb = AL.get(128, [16, 64], BF16)
            sc = AL.get(64, [16, 128], F32)
            v1 = AL.get(64, [16, 16], F32)
            i1 = AL.get(64, [16, 16], U32)
            i1f = AL.get(64, [16, 16], F32)
            wk = AL.get(64, 256, F32)
            cand = AL.get(64, [8, 256], F32)
            eq = AL.get(64, [8, 16, 16], F32)
            v2 = AL.get(64, [8, 16], F32)
            ci = AL.get(64, [8, 16], U32)
            cih = AL.get(64, [8, 16], U32)
            cil = AL.get(64, [8, 16], U32)
            cihf = AL.get(64, [8, 16], F32)
            cilf = AL.get(64, [8, 16], F32)
            iaf = AL.get(64, 128, F32)
            ibf = AL.get(64, 128, F32)
            idxf = AL.get(64, 128, F32)
            gte = AL.get(64, [8, 16], F32)
            gsum = AL.get(64, 8, F32)
            IDXT = [AL.get(128, 64, I32) for _ in range(3)]
            gateT = [AL.get(128, 64, F32) for _ in range(2)]
            ACT0 = AL.get(128, 64, F32)
            ACT1 = AL.get(128, 64, F32)
            ACTT = AL.get(128, 64, F32)
            g1 = AL.get(128, 64, F32)
            g2 = AL.get(128, 64, F32)
            WT = AL.get(128, 64, BF16)
            gsig = AL.get(64, 1024, F32)
            ptl2 = [AL.get(64, 256, F32) for _ in range(2)]
            ptb = AL.get(64, 256, BF16)
            pTt = AL.get(128, [2, 64], BF16)
            yt = AL.get(64, 1024, F32)
            iota16 = C_('iota16')

            def topk16(src, srckey, width, vals, vkey, idxs, ikey):
                S.op('dve', lambda e: e.max(out=vals[:, 0:8], in_=src), reads=[srckey], writes=[vkey])
                S.op('dve', lambda e: e.max_index(out=idxs[:, 0:8], in_max=vals[:, 0:8], in_values=src),
                     reads=[srckey, vkey], writes=[ikey])
                S.op('dve', lambda e: e.match_replace(out=wk[:, 0:width], in_to_replace=vals[:, 0:8], in_values=src,
                                                      imm_value=-1e30), reads=[srckey, vkey], writes=['wk'])
                S.op('dve', lambda e: e.max(out=vals[:, 8:16], in_=wk[:, 0:width]), reads=['wk'], writes=[vkey])
                S.op('dve', lambda e: e.max_index(out=idxs[:, 8:16], in_max=vals[:, 8:16], in_values=wk[:, 0:width]),
                     reads=['wk', vkey], writes=[ikey])

            def rms_bf16(xin, xkey, gb, gkey, hout, hkey, ss_, sskey, rs_, rskey):
                S.op('act', lambda e: e.activation(out=hout, in_=xin, func=AF.Square, accum_out=ss_),
                     reads=[xkey], writes=[hkey, sskey])
                S.op('act', lambda e: e.activation(out=rs_, in_=ss_, func=AF.Sqrt, scale=1.0 / 1024,
                                                   bias=epsc[0:64, :]), reads=[sskey, 'cst'], writes=[rskey])
                S.op('dve', lambda e: e.reciprocal(out=rs_, in_=rs_), reads=[rskey], writes=[rskey])
                S.op('dve', lambda e: e.scalar_tensor_tensor(out=hout, in0=xin, scalar=rs_[:, 0:1], in1=gb,
                                                             op0=ALU.mult, op1=ALU.mult),
                     reads=[xkey, rskey, gkey], writes=[hkey])

            def front(ti, pos):
                par = pos % 2
                ip = pos % 3
                r0 = ti * 64
                hb = hbF[par]
                hkey = 'hbF%d' % par
                S.dma('sp', lambda e: e.dma_start(out=xtF, in_=x1_d[r0:r0 + 64, :]), reads=[('x1', ti)], writes=['xtF'])
                rms_bf16(xtF, 'xtF', gffnb, 'gffnb', hb, hkey, ss, 'ss', rs, 'rs')
                S.dma('sp', lambda e: e.dma_start(out=h2_d[r0:r0 + 64, :], in_=hb), reads=[hkey], writes=[('h2', ti)])
                transpose_tm(hb, hkey, 8, hT, 'hT')
                pq = PB[:, 1024:2048].rearrange("p (c t) -> p c t", c=16)
                for hc in range(16):
                    for k in range(8):
                        S.op('pe', lambda e, hc=hc, k=k: e.matmul(pq[:, hc, :], lhsT=wqb[:, k, hc * 128:(hc + 1) * 128],
                                                                  rhs=hT[:, k, :], start=(k == 0), stop=(k == 7)),
                             reads=['wqb', 'hT'], writes=bk(2, 3))
                S.op('act', lambda e: e.copy(out=qTb, in_=pq), reads=bk(2, 3), writes=['qTb'])
                for half in range(2):
                    for j in range(8):
                        hc = half * 8 + j
                        S.op('pe', lambda e, hc=hc, j=j: e.matmul(PB[0:64, 1024 + j * 128:1024 + (j + 1) * 128],
                                                                  lhsT=qTb[:, hc, :], rhs=keysTb[:, hc, :],
                                                                  start=True, stop=True),
                             reads=['qTb', 'keysTb'], writes=bk(2, 3))
                    S.op('act', lambda e, half=half: e.copy(
                        out=sc[:, half * 8:(half + 1) * 8, :],
                        in_=PB[0:64, 1024:2048].rearrange("p (c k) -> p c k", c=8)), reads=bk(2, 3), writes=['sc'])
                for hc in range(16):
                    topk16(sc[:, hc, :], 'sc', 128, v1[:, hc, :], 'v1', i1[:, hc, :], 'i1')
                S.op('dve', lambda e: e.tensor_copy(out=i1f, in_=i1), reads=['i1'], writes=['i1f'])
                for h in range(8):
                    S.op('dve', lambda e, h=h: e.tensor_tensor(
                        out=cand[:, h, :].rearrange("p (i j) -> p i j", i=16),
                        in0=v1[:, 2 * h, :].unsqueeze(2).to_broadcast([64, 16, 16]),
                        in1=v1[:, 2 * h + 1, :].unsqueeze(1).to_broadcast([64, 16, 16]), op=ALU.add),
                        reads=['v1'], writes=['cand'])
                for h in range(8):
                    topk16(cand[:, h, :], 'cand', 256, v2[:, h, :], 'v2', ci[:, h, :], 'ci')
                S.op('dve', lambda e: e.tensor_scalar(out=cih, in0=ci, scalar1=4, scalar2=None,
                                                      op0=ALU.logical_shift_right), reads=['ci'], writes=['cih'])
                S.op('dve', lambda e: e.tensor_scalar(out=cil, in0=ci, scalar1=15, scalar2=None, op0=ALU.bitwise_and),
                     reads=['ci'], writes=['cil'])
                S.op('dve', lambda e: e.tensor_copy(out=cihf, in_=cih), reads=['cih'], writes=['cihf'])
                S.op('dve', lambda e: e.tensor_copy(out=cilf, in_=cil), reads=['cil'], writes=['cilf'])
                i1v = i1f.rearrange("p (h c) i -> p h c i", c=2)
                for (cf, ckey, cpos, dst, dkey) in ((cihf, 'cihf', 0, iaf, 'iaf'), (cilf, 'cilf', 1, ibf, 'ibf')):
                    S.op('dve', lambda e, cf=cf: e.tensor_tensor(
                        out=eq, in0=cf.unsqueeze(3).to_broadcast([64, 8, 16, 16]),
                        in1=iota16.unsqueeze(1).unsqueeze(1).to_broadcast([64, 8, 16, 16]), op=ALU.is_equal),
                        reads=[ckey, 'cst'], writes=['eq'])
                    S.op('dve', lambda e, cpos=cpos: e.tensor_tensor(
                        out=eq, in0=eq, in1=i1v[:, :, cpos, :].unsqueeze(2).to_broadcast([64, 8, 16, 16]), op=ALU.mult),
                        reads=['eq', 'i1f'], writes=['eq'])
                    S.op('dve', lambda e, dst=dst: e.reduce_sum(out=dst, in_=eq.rearrange("p h k i -> p (h k) i"),
                                                                axis=AX.X), reads=['eq'], writes=[dkey])
                S.op('dve', lambda e: e.scalar_tensor_tensor(out=idxf, in0=iaf, scalar=128.0, in1=ibf, op0=ALU.mult,
                                                             op1=ALU.add), reads=['iaf', 'ibf'], writes=['idxf'])
                S.op('dve', lambda e: e.tensor_tensor(out=gte, in0=v2, in1=v2[:, :, 0:1].to_broadcast([64, 8, 16]),
                                                      op=ALU.subtract), reads=['v2'], writes=['gte'])
                S.op('act', lambda e: e.activation(out=gte, in_=gte, func=AF.Exp), reads=['gte'], writes=['gte'])
                S.op('dve', lambda e: e.reduce_sum(out=gsum, in_=gte, axis=AX.X), reads=['gte'], writes=['gsum'])
                S.op('dve', lambda e: e.reciprocal(out=gsum, in_=gsum), reads=['gsum'], writes=['gsum'])
                S.op('dve', lambda e: e.tensor_tensor(out=gte, in0=gte, in1=gsum.unsqueeze(2).to_broadcast([64, 8, 16]),
                                                      op=ALU.mult), reads=['gte', 'gsum'], writes=['gte'])
                S.op('pe', lambda e: e.transpose(out=bank(6, 128, 64), in_=idxf, identity=identf[0:64, 0:64]),
                     reads=['idxf', 'cst'], writes=bk(6))
                S.op('pe', lambda e: e.transpose(out=bank(6, 128, 64, 64), in_=gte.rearrange("p h k -> p (h k)"),
                                                 identity=identf[0:64, 0:64]), reads=['gte', 'cst'], writes=bk(6))
                S.op('dve', lambda e: e.tensor_copy(out=IDXT[ip], in_=bank(6, 128, 64)), reads=bk(6),
                     writes=['IDXT%d' % ip])
                S.op('dve', lambda e: e.tensor_copy(out=gateT[par], in_=bank(6, 128, 64, 64)), reads=bk(6),
                     writes=['gateT%d' % par])

            def ustage(ti, pos):
                par = pos % 2
                ip = pos % 3
                hb = hbF[par]
                hkey = 'hbF%d' % par
                segs = []
                for n in range(64):
                    S.begin()
                    g = (pos * 64 + n) % NBUF
                    ub_ = Ug[g]
                    ukey = 'Ug%d' % g
                    S.dma('pool', lambda e, n=n, ub_=ub_: e.indirect_dma_start(
                        out=ub_, out_offset=None, in_=eub_d,
                        in_offset=bass.IndirectOffsetOnAxis(ap=IDXT[ip][:, n:n + 1], axis=0)),
                        reads=['IDXT%d' % ip, 'eub'], writes=[ukey])
                    gh = (pos * 64 + n) % NHB
                    hbb = HB[gh]
                    hbkey = 'HB%d' % gh
                    S.dma('sp', lambda e, n=n, hbb=hbb: e.dma_start(
                        out=hbb, in_=h2_d[ti * 64 + n:ti * 64 + n + 1, :].to_broadcast([128, 1024])),
                        reads=[('h2', ti)], writes=[hbkey])
                    S.op('dve', lambda e, n=n, ub_=ub_, hbb=hbb: e.scalar_tensor_tensor(
                        out=junkb, in0=ub_, scalar=1.0, in1=hbb, op0=ALU.mult, op1=ALU.mult,
                        accum_out=ACTT[:, n:n + 1]), reads=[ukey, hbkey], writes=[('ACT0', n)])
                    segs.append(S.end())
                S.begin()
                actk = [('ACT0', n) for n in range(64)]
                S.op('dve', lambda e: e.tensor_copy(out=ACT1, in_=ACTT), reads=actk, writes=['ACTT'])
                S.op('dve', lambda e: e.tensor_tensor(out=g1, in0=ACTT, in1=ACTT, op=ALU.mult), reads=['ACTT'],
                     writes=['g1'])
                S.op('dve', lambda e: e.tensor_scalar(out=g1, in0=g1, scalar1=0.044715, scalar2=1.0, op0=ALU.mult,
                                                      op1=ALU.add), reads=['g1'], writes=['g1'])
                S.op('dve', lambda e: e.tensor_tensor(out=g1, in0=g1, in1=ACTT, op=ALU.mult), reads=['g1', 'ACTT'],
                     writes=['g1'])
                S.op('act', lambda e: e.activation(out=g2, in_=g1, func=AF.Sigmoid, scale=1.5957691216057308),
                     reads=['g1'], writes=['g2'])
                S.op('dve', lambda e: e.tensor_tensor(out=g2, in0=g2, in1=ACTT, op=ALU.mult), reads=['g2', 'ACTT'],
                     writes=['g2'])
                S.op('dve', lambda e: e.tensor_tensor(out=WT, in0=g2, in1=gateT[par], op=ALU.mult),
                     reads=['g2', 'gateT%d' % par], writes=['WT'])
                S.op('dve', lambda e: e.tensor_tensor(out=WM[par], in0=WT.unsqueeze(2).to_broadcast([128, 64, 64]),
                                                      in1=dltb, op=ALU.mult), reads=['WT', 'dltb'],
                     writes=['WM%d' % par])
                return segs, S.end()

            def vstage(ti, pos):
                par = pos % 2
                ip = pos % 3
                r0 = ti * 64
                xtV = xtV2[par]
                ptl = ptl2[par]
                xk = 'xtV%d' % par
                pk = 'ptl%d' % par
                S.begin()
                S.dma('sp', lambda e: e.dma_start(out=xtV, in_=x1_d[r0:r0 + 64, :]), reads=[('x1', ti)], writes=[xk])
                S.dma('sp', lambda e: e.dma_start(out=ptl, in_=p_d[r0:r0 + 64, :]), writes=[pk])
                head = S.end()
                segs = []
                for n in range(64):
                    S.begin()
                    g = (pos * 64 + n) % NBUF
                    vb_ = Vg[g]
                    vkey = 'Vg%d' % g
                    S.dma('pool', lambda e, n=n, vb_=vb_: e.indirect_dma_start(
                        out=vb_, out_offset=None, in_=evb_d,
                        in_offset=bass.IndirectOffsetOnAxis(ap=IDXT[ip][:, n:n + 1], axis=0)),
                        reads=['IDXT%d' % ip, 'evb'], writes=[vkey])
                    for half in range(2):
                        S.op('pe', lambda e, n=n, half=half, vb_=vb_: e.matmul(
                            bank(4 + half, 64), lhsT=WM[par][:, n, :], rhs=vb_[:, half * 512:(half + 1) * 512],
                            start=(n == 0), stop=(n == 63)), reads=['WM%d' % par, vkey], writes=bk(4 + half))
                    segs.append(S.end())
                S.begin()
                S.op('dve', lambda e: e.tensor_tensor(out=xtV, in0=xtV, in1=PB[0:64, 2048:3072], op=ALU.add),
                     reads=[xk] + bk(4, 5), writes=[xk])
                tail_a = S.end()
                S.begin()
                pt6 = PB[:, 6 * 512 + 256:7 * 512].bitcast(BF16)
                rms_bf16(xtV, xk, gpleb, 'gpleb', hb2, 'hb2', ss2, 'ss2', rs2, 'rs2')
                transpose_tm(hb2, 'hb2', 8, hT2, 'hT2', ptgt=pt6, pkey='b6u')
                S.op('dve', lambda e: e.tensor_copy(out=ptb, in_=ptl), reads=[pk], writes=['ptb'])
                transpose_tm(ptb, 'ptb', 2, pTt, 'pTt', ptgt=pt6, pkey='b6u')
                for half in range(2):
                    hs = slice(half * 512, (half + 1) * 512)
                    proj_tm(bank(0, 64), bk(0), hT2, 'hT2', wpgb, 'wpgb', half * 512, 512)
                    proj_tm(bank(1, 64), bk(1), pTt, 'pTt', wpleb, 'wpleb', half * 512, 512, nk=2)
                    S.op('act', lambda e, hs=hs: e.activation(out=gsig[:, hs], in_=bank(0, 64), func=AF.Sigmoid),
                         reads=bk(0), writes=['gsig'])
                    S.op('dve', lambda e, hs=hs: e.tensor_tensor(out=gsig[:, hs], in0=gsig[:, hs], in1=bank(1, 64),
                                                                  op=ALU.mult), reads=['gsig'] + bk(1), writes=['gsig'])
                    S.op('dve', lambda e, hs=hs: e.tensor_tensor(out=xtV[:, hs], in0=xtV[:, hs], in1=gsig[:, hs],
                                                                  op=ALU.add), reads=[xk, 'gsig'], writes=[xk])
                S.op('act', lambda e: e.activation(out=yt, in_=xtV, func=AF.Square, accum_out=ss2), reads=[xk],
                     writes=['yt', 'ss2'])
                S.op('act', lambda e: e.activation(out=rs2, in_=ss2, func=AF.Sqrt, scale=1.0 / 1024,
                                                   bias=epsc[0:64, :]), reads=['ss2', 'cst'], writes=['rs2'])
                S.op('dve', lambda e: e.reciprocal(out=rs2, in_=rs2), reads=['rs2'], writes=['rs2'])
                S.op('dve', lambda e: e.scalar_tensor_tensor(out=yt, in0=xtV, scalar=rs2[:, 0:1], in1=gfinb,
                                                             op0=ALU.mult, op1=ALU.mult),
                     reads=[xk, 'rs2', 'gfinb'], writes=['yt'])
                S.dma('sp', lambda e: e.dma_start(out=y_d[r0:r0 + 64, :], in_=yt), reads=['yt'], final=True)
                return head, segs, tail_a, S.end()

            if 'B' in PHASES:
                TL = list(range(NPT + 1)) if TILES is None else list(TILES) + [NPT]
                nT = len(TL)
                front(TL[0], 0)
                LP = []
                for j in range(nT + 1):
                    usegs, utail = ustage(TL[j], j) if j < nT else ([[]] * 64, [])
                    vhead, vsegs, vtail_a, vtail_b = vstage(TL[j - 1], j - 1) if j >= 1 else ([], [[]] * 64, [], [])
                    if j + 1 < nT:
                        S.begin()
                        front(TL[j + 1], j + 1)
                        LF = S.end()
                    else:
                        LF = []
                    per = (len(LF) + 63) // 64
                    perp = (len(LP) + 63) // 64
                    S.run(vhead)
                    for n in range(64):
                        S.run(usegs[n])
                        S.run(vsegs[n])
                        S.run(LF[n * per:(n + 1) * per])
                        S.run(LP[n * perp:(n + 1) * perp])
                    S.run(LF[64 * per:])
                    S.run(LP[64 * perp:])
                    S.run(utail)
                    S.run(vtail_a)
                    LP = vtail_b
                S.run(LP)
        phase_B()
        print('ops', {e: len(v) for e, v in S.prog.items()}, 'nsem', S.nsem, 'arena', AL.off)
        S.emit()
    return nc


_CACHE = {}


def kernel(x_prompt, x_sample, state_hgrn, state_delta, state_conv, p_prompt, p_sample,
           lb_param, g_mix, w_in, conv_w, a_log, dt_bias, g_norm_a, g_norm_b, w_br_a, w_br_b,
           w_out, g_ffn, peer_wq, peer_keys, expert_u, expert_v, g_ple, w_ple, w_ple_gate,
           g_final):
    f = lambda a: np.ascontiguousarray(np.asarray(a, dtype=np.float32))
    if 'nc' not in _CACHE:
        _CACHE['nc'] = build_program()
        _CACHE['consts'] = _build_consts()
    nc = _CACHE['nc']
    cst, cmf, zsel, dlt = _CACHE['consts']
    x_prompt, x_sample = f(x_prompt), f(x_sample)
    p_prompt, p_sample = f(p_prompt), f(p_sample)
    state_hgrn, state_delta, state_conv = f(state_hgrn), f(state_delta), f(state_conv)
    keysT = np.ascontiguousarray(np.transpose(f(peer_keys)[0], (0, 1, 3, 2)).reshape(16, 128, 128))
    shared = dict(
        lbp=f(lb_param), gmix=f(g_mix), w_in=f(w_in)[0], convw=f(conv_w)[0], alog=f(a_log), dtb=f(dt_bias),
        gna=f(g_norm_a), gnb=f(g_norm_b), wbra=f(w_br_a)[0], wbrb=f(w_br_b)[0], wout=f(w_out)[0], gffn=f(g_ffn),
        wq=f(peer_wq)[0], keysT=keysT, eu=f(expert_u)[0], ev=f(expert_v)[0], gple=f(g_ple), wple=f(w_ple)[0],
        wpg=f(w_ple_gate)[0], gfin=f(g_final).reshape(1, 1024), cst=cst, cmf=cmf, zsel=zsel, dlt=dlt)
    in_maps = []
    for b in range(8):
        m = dict(shared)
        m['x'] = np.ascontiguousarray(np.concatenate([x_prompt[b], x_sample[16 * b:16 * b + 16].reshape(64, 1024)], 0))
        m['p'] = np.ascontiguousarray(np.concatenate([p_prompt[0, b], p_sample[0, 16 * b:16 * b + 16].reshape(64, 256)], 0))
        m['sh'] = np.ascontiguousarray(state_hgrn[0, 16 * b:16 * b + 16])
        m['sd'] = np.ascontiguousarray(state_delta[0, 16 * b:16 * b + 16])
        m['scv'] = np.ascontiguousarray(state_conv[0, 16 * b:16 * b + 16].reshape(48, 1536))
        in_maps.append(m)
    res = run_bass_kernel_spmd(nc, in_maps, core_ids=list(range(8)))
    R = res.results
    y_prompt = np.stack([R[b]['y'][0:2048] for b in range(8)], 0)
    y_sample = np.concatenate([R[b]['y'][2048:2112].reshape(16, 4, 1024) for b in range(8)], 0)
    hp = np.stack([R[b]['hp'] for b in range(8)], 0)[None]
    dp = np.stack([R[b]['dp'] for b in range(8)], 0)[None]
    cp = np.stack([R[b]['cp'] for b in range(8)], 0)[None]
    hs = np.concatenate([R[b]['hs'] for b in range(8)], 0)[None]
    ds = np.concatenate([R[b]['ds'] for b in range(8)], 0)[None]
    cs = np.concatenate([R[b]['cs'].reshape(16, 3, 1536) for b in range(8)], 0)[None]
    _CACHE['dbg'] = R
    return tuple(np.ascontiguousarray(a.astype(np.float32)) for a in (y_prompt, y_sample, hp, dp, cp, hs, ds, cs))
```

```python
import numpy as np
from contextlib import ExitStack
import concourse.bass as bass
import concourse.mybir as mybir
from concourse.bass_utils import run_bass_kernel_spmd

F32 = mybir.dt.float32
BF16 = mybir.dt.bfloat16
I32 = mybir.dt.int32
U32 = mybir.dt.uint32
AF = mybir.ActivationFunctionType
ALU = mybir.AluOpType
AX = mybir.AxisListType

EPS = 1e-6
NPT = 32
NTOK = 2112
EPOCH = 12000
DMA_POOL = 8
DMA_EPOCH = 700
ARENA_COLS = 105984
NBUF = 6
NHB = 6
NUV = 8
NEG = -30000.0
DBG = set()
DBGT = 0
PHASES = ('A1', 'A2', 'B')
TILES = None


class Sched:
    def __init__(self, nc, es):
        self.nc = nc
        self.es = es
        self.eng = {'pe': nc.tensor, 'act': nc.scalar, 'dve': nc.vector,
                    'pool': nc.gpsimd, 'sp': nc.sync}
        self.prog = {e: [] for e in self.eng}
        self.cnt = {e: 0 for e in self.eng}
        self.sem = {}
        self.nsem = 0
        for e in self.eng:
            self.sem[e] = self._newsem(e)
        self.waited = {e: {} for e in self.eng}
        self.dpool = {}
        self.res_w = {}
        self.res_r = {}
        self.final_tokens = []
        self.pending = {e: [] for e in self.eng}
        self.cap = None

    def begin(self):
        self.cap = []

    def end(self):
        L = self.cap
        self.cap = None
        return L

    def run(self, L):
        for it in L:
            if it[0] == 'op':
                self.op(it[1], it[2], it[3], it[4])
            else:
                self.dma(it[1], it[2], it[3], it[4], it[5])

    def _newsem(self, name):
        self.nsem += 1
        return self.es.enter_context(self.nc.semaphore(f"s{self.nsem}_{name}"))

    def _need(self, e, tok, waits):
        if tok is None:
            return
        sem, val = tok[0], tok[1]
        if e == 'pe' and tok[2] == 'pe':
            return
        w = self.waited[e]
        if w.get(id(sem), 0) >= val:
            return
        w[id(sem)] = val
        waits.append((sem, val))

    def _deps(self, e, reads, writes, waits):
        for t in self.pending[e]:
            self._need(e, t, waits)
        self.pending[e] = []
        for k in reads:
            self._need(e, self.res_w.get(k), waits)
        for k in writes:
            self._need(e, self.res_w.get(k), waits)
            for t in self.res_r.get(k, ()):
                self._need(e, t, waits)

    def _commit(self, tok, reads, writes):
        for k in reads:
            self.res_r.setdefault(k, []).append(tok)
        for k in writes:
            self.res_w[k] = tok
            self.res_r[k] = []

    def op(self, e, fn, reads=(), writes=()):
        if self.cap is not None:
            self.cap.append(('op', e, fn, tuple(reads), tuple(writes)))
            return None
        waits = []
        self._deps(e, reads, writes, waits)
        if self.cnt[e] >= EPOCH:
            self.sem[e] = self._newsem(e)
            self.cnt[e] = 0
        self.cnt[e] += 1
        tok = (self.sem[e], self.cnt[e], e)
        self.prog[e].append((waits, fn, self.sem[e], 1))
        self._commit(tok, reads, writes)
        return tok

    def dma(self, e, fn, reads=(), writes=(), final=False):
        if self.cap is not None:
            self.cap.append(('dma', e, fn, tuple(reads), tuple(writes), final))
            return None
        waits = []
        self._deps(e, reads, writes, waits)
        pool = self.dpool.setdefault(e, {'sems': [], 'uses': [], 'i': 0})
        i = pool['i'] % DMA_POOL
        pool['i'] += 1
        if len(pool['sems']) <= i:
            pool['sems'].append(self._newsem(e + 'd'))
            pool['uses'].append(0)
        if pool['uses'][i] >= DMA_EPOCH:
            pool['sems'][i] = self._newsem(e + 'd')
            pool['uses'][i] = 0
        sem = pool['sems'][i]
        if pool['uses'][i] > 0:
            self._need(e, (sem, 16 * pool['uses'][i], 'dma'), waits)
        pool['uses'][i] += 1
        tok = (sem, 16 * pool['uses'][i], 'dma')
        self.prog[e].append((waits, fn, sem, 16))
        self._commit(tok, reads, writes)
        if final:
            self.final_tokens.append(tok)
        return tok

    def barrier(self):
        toks = []
        for e in self.eng:
            if self.cnt[e] > 0:
                toks.append((self.sem[e], self.cnt[e], e + '_bar'))
        for e, pool in self.dpool.items():
            for sem, u in zip(pool['sems'], pool['uses']):
                if u > 0:
                    toks.append((sem, 16 * u, 'dma'))
        for e in self.eng:
            self.pending[e] = list(toks)

    def emit(self):
        nc = self.nc
        fw = []
        for t in self.final_tokens:
            self._need('sp', t, fw)
        with nc.Block() as block:
            def run(e, engine):
                for waits, fn, sem, inc in self.prog[e]:
                    for (s, v) in waits:
                        engine.wait_ge(s, v)
                    fn(engine).then_inc(sem, inc)

            @block.tensor
            def _(eng):
                run('pe', eng)

            @block.scalar
            def _(eng):
                run('act', eng)

            @block.vector
            def _(eng):
                run('dve', eng)

            @block.gpsimd
            def _(eng):
                run('pool', eng)

            @block.sync
            def _(eng):
                run('sp', eng)
                for (s, v) in fw:
                    eng.wait_ge(s, v)


class Alloc:
    def __init__(self, arena, ncols):
        self.a = arena
        self.n = ncols
        self.off = 0

    def get(self, parts, free, dt):
        if isinstance(free, int):
            free = [free]
        nel = int(np.prod(free))
        cols = nel * (1 if dt == BF16 else 2)
        cols = (cols + 1) // 2 * 2
        o = self.off
        self.off += cols
        assert self.off <= self.n, f"arena overflow {self.off} > {self.n}"
        ap = self.a[0:parts, o:o + cols]
        if dt != BF16:
            ap = ap.bitcast(dt)
        if len(free) > 1:
            ds = [f"d{i}" for i in range(len(free))]
            kw = {ds[i]: free[i] for i in range(1, len(free))}
            ap = ap.rearrange(f"p ({' '.join(ds)}) -> p {' '.join(ds)}", **kw)
        return ap


def _tile_consts(nch, C):
    t = np.arange(64)
    ch = t // C
    same = ch[:, None] == ch[None, :]
    tri = (same & (t[:, None] <= t[None, :])).astype(np.float32)
    blk = same.astype(np.float32)
    nmT = np.where(tri > 0, 0.0, NEG).astype(np.float32)
    strict = same & (t[None, :] < t[:, None])
    pmS = np.where(strict, 0.0, -NEG).astype(np.float32)
    cind = np.zeros((64, 16), np.float32)
    cind[t, ch] = 1.0
    return tri, blk, np.tile(nmT, (1, 4)), np.tile(pmS, (1, 4)), cind


CST_COLS = {}


def _build_consts():
    cols = []
    off = [0]

    def add(name, arr):
        a = np.zeros((128, arr.shape[1]), np.float32)
        a[:arr.shape[0]] = arr
        CST_COLS[name] = (off[0], off[0] + arr.shape[1], arr.shape[0])
        off[0] += arr.shape[1]
        cols.append(a)

    add('ident', np.eye(128, dtype=np.float32))
    add('ones', np.ones((128, 128), np.float32))
    for nm, (nch, C) in (('p', (1, 64)), ('s', (16, 4))):
        tri, blk, nmT, pmS, cind = _tile_consts(nch, C)
        add('tri_' + nm, tri)
        add('blk_' + nm, blk)
        add('nmT_' + nm, nmT)
        add('pmS_' + nm, pmS)
        add('cind_' + nm, cind)
    add('iota16', np.tile(np.arange(16, dtype=np.float32)[None, :], (64, 1)))
    add('eps', np.full((128, 1), EPS, np.float32))
    add('one', np.ones((128, 1), np.float32))
    cst = np.concatenate(cols, axis=1)
    t = np.arange(64)
    cm = (t[None, :] // 4 == np.arange(16)[:, None]).astype(np.float32)
    cmf = np.tile(cm.reshape(1, 16 * 64), (128, 1))
    z = np.zeros((64, 64, 128), np.float32)
    z[t, t, :] = 1.0
    z = z.reshape(64, 64 * 128)
    dl = np.tile(np.eye(64, dtype=np.float32).reshape(1, 64 * 64), (128, 1))
    return cst, cmf, z, dl


def build_program():
    cst_np, _, _, _ = _build_consts()
    NCST = cst_np.shape[1]
    nc = bass.Bass("TRN2", target_bir_lowering=False)

    def din(name, shape, dt=F32):
        return nc.dram_tensor(name, shape, dt, kind="ExternalInput").ap()

    def dout(name, shape, dt=F32):
        return nc.dram_tensor(name, shape, dt, kind="ExternalOutput").ap()

    x_d = din("x", [NTOK, 1024])
    p_d = din("p", [NTOK, 256])
    sh_d = din("sh", [16, 4, 128, 128])
    sd_d = din("sd", [16, 4, 128, 128])
    scv_d = din("scv", [48, 1536])
    lbp_d = din("lbp", [2, 512])
    gmix_d = din("gmix", [1, 1024])
    win_d = din("w_in", [1024, 6152])
    convw_d = din("convw", [4, 1536])
    alog_d = din("alog", [1, 4])
    dtb_d = din("dtb", [1, 4])
    gna_d = din("gna", [1, 128])
    gnb_d = din("gnb", [1, 128])
    wbra_d = din("wbra", [512, 1024])
    wbrb_d = din("wbrb", [512, 1024])
    wout_d = din("wout", [1024, 1024])
    gffn_d = din("gffn", [1, 1024])
    wq_d = din("wq", [1024, 2048])
    keysT_d = din("keysT", [16, 128, 128])
    eu_d = din("eu", [16384, 1024])
    ev_d = din("ev", [16384, 1024])
    gple_d = din("gple", [1, 1024])
    wple_d = din("wple", [256, 1024])
    wpg_d = din("wpg", [1024, 1024])
    gfin_d = din("gfin", [1, 1024])
    cst_d = din("cst", [128, NCST])
    cmf_d = din("cmf", [128, 1024])
    zsel_d = din("zsel", [64, 8192])
    dlt_d = din("dlt", [128, 4096])

    y_d = dout("y", [NTOK, 1024])
    hp_d = dout("hp", [4, 128, 128])
    dp_d = dout("dp", [4, 128, 128])
    cp_d = dout("cp", [3, 1536])
    hs_d = dout("hs", [16, 4, 128, 128])
    ds_d = dout("ds", [16, 4, 128, 128])
    cs_d = dout("cs", [48, 1536])
    m1_d = dout("m1s", [NTOK, 1024])
    x1_d = dout("x1s", [NTOK, 1024])
    uvb_d = nc.dram_tensor("uvb", [16384, 2048], BF16, kind="Internal").ap()
    h2_d = nc.dram_tensor("h2s", [NTOK, 1024], BF16, kind="Internal").ap()

    es = ExitStack()
    with es:
        S = Sched(nc, es)

        def dbg(name, ap, key, ti=0, want=0):
            if name not in DBG or ti != want:
                return
            shp = list(ap.shape)
            dd = nc.dram_tensor("dbg_" + name, shp, ap.dtype, kind="ExternalOutput").ap()
            S.dma('sp', lambda e: e.dma_start(out=dd, in_=ap), reads=[key] if not isinstance(key, list) else key,
                  final=True)
        ARENA = es.enter_context(nc.sbuf_tensor("arena", [128, ARENA_COLS], BF16))
        PB = es.enter_context(nc.psum_tensor("pb", [128, 7 * 512], F32))
        PTt = es.enter_context(nc.psum_tensor("pt", [128, 1024], BF16))
        AL = Alloc(ARENA, ARENA_COLS)

        def bank(j, parts=128, n=512, off=0):
            return PB[0:parts, j * 512 + off:j * 512 + off + n]

        def bk(*js):
            return ['b%d' % j for j in js]

        CST = AL.get(128, NCST, F32)
        S.dma('sp', lambda e: e.dma_start(out=CST, in_=cst_d), writes=['cst'])

        def C_(name, parts=None):
            a, b, r = CST_COLS[name]
            return CST[0:(parts or r), a:b]

        identf = C_('ident')
        onesf = C_('ones')
        epsc = C_('eps')
        identb = AL.get(128, 128, BF16)
        S.op('dve', lambda e: e.tensor_copy(out=identb, in_=identf), reads=['cst'], writes=['identb'])
        TP = dict(nch=1, C=64, tri=C_('tri_p'), blk=C_('blk_p'), nmT=C_('nmT_p'), pmS=C_('pmS_p'),
                  cind=C_('cind_p'))
        TS = dict(nch=16, C=4, tri=C_('tri_s'), blk=C_('blk_s'), nmT=C_('nmT_s'), pmS=C_('pmS_s'),
                  cind=C_('cind_s'))
        cmb = AL.get(128, [16, 64], BF16)
        S.dma('pool', lambda e: e.dma_start(out=cmb, in_=cmf_d.rearrange("p (c t) -> p c t", c=16)),
              writes=['cmb'])
        pers_mark = AL.off

        def load_w_bf16(dst3, src, r0, nk, c0, ncols, key):
            for k in range(nk):
                for cc in range(0, ncols, 2048):
                    w = min(2048, ncols - cc)
                    S.dma('pool', lambda e, k=k, cc=cc, w=w: e.dma_start(
                        out=dst3[:, k, cc:cc + w],
                        in_=src[r0 + k * 128:r0 + (k + 1) * 128, c0 + cc:c0 + cc + w]), writes=[key])

        def bcast_load(dst, src_row, parts, n, key):
            S.dma('sp', lambda e: e.dma_start(out=dst, in_=src_row.to_broadcast([parts, n])), writes=[key])

        def rmsnorm_to_bf16(xt, xkey, gb, gkey, hb, hkey, ss, rs):
            S.op('act', lambda e: e.activation(out=hb, in_=xt, func=AF.Square, accum_out=ss),
                 reads=[xkey], writes=[hkey, 'ss'])
            S.op('act', lambda e: e.activation(out=rs, in_=ss, func=AF.Sqrt, scale=1.0 / 1024, bias=epsc[0:64, :]),
                 reads=['ss', 'cst'], writes=['rs'])
            S.op('dve', lambda e: e.reciprocal(out=rs, in_=rs), reads=['rs'], writes=['rs'])
            S.op('dve', lambda e: e.scalar_tensor_tensor(out=hb, in0=xt, scalar=rs[:, 0:1], in1=gb,
                                                         op0=ALU.mult, op1=ALU.mult),
                 reads=[xkey, 'rs', gkey], writes=[hkey])

        def transpose_tm(src, skey, nblk, dst, dkey, eng='act', ptgt=None, pkey='pt'):
            if ptgt is None:
                ptgt = PTt
            for k in range(nblk):
                S.op('pe', lambda e, k=k: e.transpose(out=ptgt[:, k * 64:(k + 1) * 64],
                                                      in_=src[:, k * 128:(k + 1) * 128],
                                                      identity=identb[0:64, 0:64]),
                     reads=[skey, 'identb'], writes=[pkey])
            pv = ptgt[:, 0:nblk * 64].rearrange("p (k t) -> p k t", k=nblk)
            if eng == 'act':
                S.op('act', lambda e: e.copy(out=dst, in_=pv), reads=[pkey], writes=[dkey])
            else:
                S.op('dve', lambda e: e.tensor_copy(out=dst, in_=pv), reads=[pkey], writes=[dkey])

        def proj_tm(pout, pkeys, hT, hTkey, w3, wkey, c0, ncols, nk=8):
            for k in range(nk):
                S.op('pe', lambda e, k=k: e.matmul(pout, lhsT=hT[:, k, :], rhs=w3[:, k, c0:c0 + ncols],
                                                   start=(k == 0), stop=(k == nk - 1)),
                     reads=[hTkey, wkey], writes=pkeys)

        def head_norm_gate_keys(o_sb, okey, gnb_, gkey, gate_sb, gatekey, sq, sqkey, on, onkey, ogt_, ogtkey, ss4, rs4):
            o3 = o_sb.rearrange("p (h v) -> p h v", h=4)
            S.op('dve', lambda e: e.tensor_tensor(out=sq, in0=o_sb, in1=o_sb, op=ALU.mult),
                 reads=[okey], writes=[sqkey])
            S.op('dve', lambda e: e.reduce_sum(out=ss4, in_=sq.rearrange("p (h v) -> p h v", h=4), axis=AX.X),
                 reads=[sqkey], writes=['ss4'])
            S.op('act', lambda e: e.activation(out=rs4, in_=ss4, func=AF.Sqrt, scale=1.0 / 128, bias=epsc[0:64, :]),
                 reads=['ss4', 'cst'], writes=['rs4'])
            S.op('dve', lambda e: e.reciprocal(out=rs4, in_=rs4), reads=['rs4'], writes=['rs4'])
            on3 = on.rearrange("p (h v) -> p h v", h=4)
            S.op('dve', lambda e: e.tensor_tensor(out=on3, in0=o3, in1=rs4.unsqueeze(2).to_broadcast([64, 4, 128]),
                                                  op=ALU.mult), reads=[okey, 'rs4'], writes=[onkey])
            S.op('dve', lambda e: e.tensor_tensor(out=on3, in0=on3, in1=gnb_.unsqueeze(1).to_broadcast([64, 4, 128]),
                                                  op=ALU.mult), reads=[onkey, gkey], writes=[onkey])
            S.op('dve', lambda e: e.tensor_tensor(out=ogt_, in0=on, in1=gate_sb, op=ALU.mult),
                 reads=[onkey, gatekey], writes=[ogtkey])

        def phase_A1():
            AL.off = pers_mark
            winA = AL.get(128, [8, 2048], BF16)
            wgA = AL.get(128, [8, 1024], BF16)
            wbrA = AL.get(128, [4, 1024], BF16)
            load_w_bf16(winA, win_d, 0, 8, 0, 2048, 'winA')
            load_w_bf16(wgA, win_d, 0, 8, 4104, 1024, 'wgA')
            load_w_bf16(wbrA, wbra_d, 0, 4, 0, 1024, 'wbrA')
            for (src_t, c0) in ((eu_d, 0), (ev_d, 1024)):
                for r in range(0, 16384, 512):
                    S.dma('pool', lambda e, r=r, src_t=src_t, c0=c0: e.dma_start(
                        out=uvb_d[r:r + 512, c0:c0 + 1024], in_=src_t[r:r + 512, :]), writes=['uvb'])
            gmixb = AL.get(64, 1024, F32)
            bcast_load(gmixb, gmix_d[0:1, :], 64, 1024, 'gmixb')
            lbb = AL.get(64, 512, F32)
            omlb = AL.get(64, 512, F32)
            lb1 = AL.get(64, 512, F32)
            bcast_load(lbb, lbp_d[0:1, :], 64, 512, 'lbb')
            bcast_load(lb1, lbp_d[1:2, :], 64, 512, 'lb1')
            S.op('dve', lambda e: e.tensor_tensor(out=lbb, in0=lbb, in1=lb1, op=ALU.subtract),
                 reads=['lbb', 'lb1'], writes=['lbb'])
            S.op('act', lambda e: e.activation(out=lbb, in_=lbb, func=AF.Sigmoid), reads=['lbb'], writes=['lbb'])
            S.op('dve', lambda e: e.tensor_scalar(out=omlb, in0=lbb, scalar1=-1.0, scalar2=1.0, op0=ALU.mult,
                                                  op1=ALU.add), reads=['lbb'], writes=['omlb'])
            gnab = AL.get(64, 128, F32)
            bcast_load(gnab, gna_d[0:1, :], 64, 128, 'gnab')
            xt = AL.get(64, 1024, F32)
            hb = AL.get(64, 1024, BF16)
            hT = AL.get(128, [8, 64], BF16)
            ss = AL.get(64, 1, F32)
            rs = AL.get(64, 1, F32)
            F = [AL.get(64, 512, F32) for _ in range(7)]
            qt = AL.get(64, 512, BF16)
            kt = AL.get(64, 512, BF16)
            va = AL.get(64, 512, BF16)
            km = AL.get(64, 512, BF16)
            qkT = AL.get(128, [8, 64], BF16)
            qm = AL.get(128, [4, 64], BF16)
            attm = AL.get(64, [4, 64], BF16)
            ebL = AL.get(128, [4, 16], F32)
            Sa = AL.get(128, [4, 128], F32)
            Sab = AL.get(128, [4, 128], BF16)
            ss4 = AL.get(64, 4, F32)
            rs4 = AL.get(64, 4, F32)
            ogt = AL.get(64, 512, BF16)
            oT = AL.get(128, [4, 64], BF16)
            m1t = AL.get(64, 1024, F32)
            S.op('pool', lambda e: e.memset(Sa, 0.0), writes=['Sa'])
            S.op('pool', lambda e: e.memset(Sab, 0.0), writes=['Sab'])

            def tileA1(ti, T):
                nch = T['nch']
                r0 = ti * 64
                S.dma('sp', lambda e: e.dma_start(out=xt, in_=x_d[r0:r0 + 64, :]), writes=['xt'])
                rmsnorm_to_bf16(xt, 'xt', gmixb, 'gmixb', hb, 'hb', ss, rs)
                dbg('xt', xt, 'xt', ti)
                dbg('rs', rs, 'rs', ti)
                dbg('hb', hb, 'hb', ti)
                transpose_tm(hb, 'hb', 8, hT, 'hT')
                dbg('hT', hT, 'hT', ti)
                dbg('winA', winA[:, :, 0:512], 'winA', ti)
                sig, q, kk, logf, eb, enb, og = F
                proj_tm(bank(0, 64), bk(0), hT, 'hT', winA, 'winA', 512, 512)
                S.op('act', lambda e: e.activation(out=sig, in_=bank(0, 64), func=AF.Sigmoid), reads=bk(0), writes=['F0'])
                proj_tm(bank(1, 64), bk(1), hT, 'hT', winA, 'winA', 0, 512)
                S.op('act', lambda e: e.activation(out=q, in_=bank(1, 64), func=AF.Silu), reads=bk(1), writes=['F1'])
                S.op('dve', lambda e: e.tensor_tensor(out=sig, in0=sig, in1=omlb, op=ALU.mult),
                     reads=['F0', 'omlb'], writes=['F0'])
                S.op('dve', lambda e: e.tensor_tensor(out=sig, in0=sig, in1=lbb, op=ALU.add),
                     reads=['F0', 'lbb'], writes=['F0'])
                S.op('dve', lambda e: e.tensor_scalar(out=kk, in0=sig, scalar1=-1.0, scalar2=1.0, op0=ALU.mult,
                                                      op1=ALU.add), reads=['F0'], writes=['F2'])
                S.op('act', lambda e: e.activation(out=logf, in_=sig, func=AF.Ln), reads=['F0'], writes=['F3'])
                dbg('q', q, 'F1', ti)
                dbg('f', sig, 'F0', ti)
                dbg('logf', logf, 'F3', ti)
                S.op('pe', lambda e: e.matmul(bank(2, 64), lhsT=T['tri'], rhs=logf, start=True, stop=True),
                     reads=['cst', 'F3'], writes=bk(2))
                S.op('act', lambda e: e.activation(out=eb, in_=bank(2, 64), func=AF.Exp), reads=bk(2), writes=['F4'])
                S.op('act', lambda e: e.activation(out=enb, in_=bank(2, 64), func=AF.Exp, scale=-1.0),
                     reads=bk(2), writes=['F5'])
                S.op('dve', lambda e: e.tensor_tensor(out=qt, in0=q, in1=eb, op=ALU.mult),
                     reads=['F1', 'F4'], writes=['qt'])
                S.op('dve', lambda e: e.tensor_tensor(out=kt, in0=kk, in1=enb, op=ALU.mult),
                     reads=['F2', 'F5'], writes=['kt'])
                for h in range(4):
                    S.op('pe', lambda e, h=h: e.matmul(bank(0, 128, nch, h * 16), lhsT=logf[:, h * 128:(h + 1) * 128],
                                                       rhs=T['cind'][:, 0:nch], start=True, stop=True),
                         reads=['F3', 'cst'], writes=bk(0))
                S.op('act', lambda e: e.activation(out=ebL[:, :, 0:nch],
                                                   in_=bank(0, 128, 64).rearrange("p (h c) -> p h c", h=4)[:, :, 0:nch],
                                                   func=AF.Exp), reads=bk(0), writes=['ebL'])
                proj_tm(bank(1, 64), bk(1), hT, 'hT', winA, 'winA', 1024, 512)
                S.op('act', lambda e: e.copy(out=va, in_=bank(1, 64)), reads=bk(1), writes=['va'])
                proj_tm(bank(2, 64), bk(2), hT, 'hT', winA, 'winA', 1536, 512)
                S.op('act', lambda e: e.activation(out=og, in_=bank(2, 64), func=AF.Silu), reads=bk(2), writes=['F6'])
                for h in range(4):
                    S.op('pe', lambda e, h=h: e.transpose(out=PTt[:, h * 64:(h + 1) * 64], in_=qt[:, h * 128:(h + 1) * 128],
                                                          identity=identb[0:64, 0:64]),
                         reads=['qt', 'identb'], writes=['pt'])
                for h in range(4):
                    S.op('pe', lambda e, h=h: e.transpose(out=PTt[:, (4 + h) * 64:(5 + h) * 64],
                                                          in_=kt[:, h * 128:(h + 1) * 128], identity=identb[0:64, 0:64]),
                         reads=['kt', 'identb'], writes=['pt'])
                S.op('act', lambda e: e.copy(out=qkT, in_=PTt[:, 0:512].rearrange("p (k t) -> p k t", k=8)),
                     reads=['pt'], writes=['qkT'])
                for h in range(4):
                    S.op('pe', lambda e, h=h: e.matmul(bank(0, 64, 64, h * 64), lhsT=qkT[:, 4 + h, :], rhs=qkT[:, h, :],
                                                       start=True, stop=True), reads=['qkT'], writes=bk(0))
                S.op('dve', lambda e: e.tensor_tensor(out=attm, in0=bank(0, 64, 256).rearrange("p (h t) -> p h t", h=4),
                                                      in1=T['tri'].unsqueeze(1).to_broadcast([64, 4, 64]), op=ALU.mult),
                     reads=bk(0) + ['cst'], writes=['attm'])
                for h in range(4):
                    S.op('pe', lambda e, h=h: e.matmul(bank(5, 64, 128, h * 128), lhsT=attm[:, h, :],
                                                       rhs=va[:, h * 128:(h + 1) * 128], start=(h == 0), stop=False,
                                                       skip_group_check=True),
                         reads=['attm', 'va'], writes=bk(5))
                for c in range(nch):
                    if nch > 1:
                        S.dma('sp', lambda e, c=c: e.dma_start(out=Sa, in_=sh_d[c].rearrange("h d v -> d h v")),
                              writes=['Sa'])
                        S.op('act', lambda e: e.copy(out=Sab, in_=Sa), reads=['Sa'], writes=['Sab'])
                        S.op('dve', lambda e, c=c: e.tensor_tensor(
                            out=qm, in0=qkT[:, 0:4, :], in1=cmb[:, c, :].unsqueeze(1).to_broadcast([128, 4, 64]),
                            op=ALU.mult), reads=['qkT', 'cmb'], writes=['qm'])
                        S.op('dve', lambda e, c=c: e.tensor_scalar(out=km, in0=kt, scalar1=T['cind'][:, c:c + 1],
                                                                   scalar2=None, op0=ALU.mult),
                             reads=['kt', 'cst'], writes=['km'])
                        qsrc, qkey, ksrc, kkey = qm, 'qm', km, 'km'
                    else:
                        qsrc, qkey, ksrc, kkey = qkT, 'qkT', kt, 'kt'
                    for h in range(4):
                        S.op('pe', lambda e, h=h, qsrc=qsrc, c=c: e.matmul(
                            bank(5, 64, 128, h * 128), lhsT=qsrc[:, h, :], rhs=Sab[:, h, :],
                            start=False, stop=(c == nch - 1), skip_group_check=True), reads=[qkey, 'Sab'], writes=bk(5))
                    for h in range(4):
                        S.op('pe', lambda e, h=h, ksrc=ksrc: e.matmul(
                            bank(6, 128, 128, h * 128), lhsT=ksrc[:, h * 128:(h + 1) * 128],
                            rhs=va[:, h * 128:(h + 1) * 128], start=True, stop=True),
                            reads=[kkey, 'va'], writes=bk(6))
                    S.op('dve', lambda e: e.tensor_tensor(out=Sa, in0=bank(6).rearrange("p (h v) -> p h v", h=4),
                                                          in1=Sa, op=ALU.add), reads=bk(6) + ['Sa'], writes=['Sa'])
                    S.op('dve', lambda e, c=c: e.tensor_tensor(
                        out=Sa, in0=Sa, in1=ebL[:, :, c:c + 1].to_broadcast([128, 4, 128]), op=ALU.mult),
                        reads=['Sa', 'ebL'], writes=['Sa'])
                    if nch > 1:
                        S.dma('sp', lambda e, c=c: e.dma_start(out=hs_d[c].rearrange("h d v -> d h v"), in_=Sa),
                              reads=['Sa'], final=True)
                    else:
                        S.op('act', lambda e: e.copy(out=Sab, in_=Sa), reads=['Sa'], writes=['Sab'])
                if nch == 1 and ti == NPT - 1:
                    S.dma('sp', lambda e: e.dma_start(out=hp_d.rearrange("h d v -> d h v"), in_=Sa),
                          reads=['Sa'], final=True)
                osb, sq, on = F[0], F[1], F[2]
                S.op('act', lambda e: e.copy(out=osb, in_=bank(5, 64)), reads=bk(5), writes=['F0'])
                dbg('osb', osb, 'F0', ti, DBGT)
                dbg('og', og, 'F6', ti, DBGT)
                head_norm_gate_keys(osb, 'F0', gnab, 'gnab', og, 'F6', sq, 'F1', on, 'F2', ogt, 'ogt', ss4, rs4)
                dbg('on', on, 'F2', ti, DBGT)
                transpose_tm(ogt, 'ogt', 4, oT, 'oT')
                for half in range(2):
                    for k in range(4):
                        S.op('pe', lambda e, k=k, half=half: e.matmul(
                            bank(3 + half, 64), lhsT=oT[:, k, :], rhs=wbrA[:, k, half * 512:(half + 1) * 512],
                            start=(k == 0), stop=(k == 3)), reads=['oT', 'wbrA'], writes=bk(3 + half))
                    proj_tm(bank(half, 64), bk(half), hT, 'hT', wgA, 'wgA', half * 512, 512)
                    sg = F[3 + half]
                    S.op('act', lambda e, half=half, sg=sg: e.activation(out=sg, in_=bank(half, 64), func=AF.Sigmoid),
                         reads=bk(half), writes=['F%d' % (3 + half)])
                    S.op('dve', lambda e, half=half, sg=sg: e.tensor_tensor(
                        out=m1t[:, half * 512:(half + 1) * 512], in0=bank(3 + half, 64), in1=sg, op=ALU.mult),
                        reads=bk(3 + half) + ['F%d' % (3 + half)], writes=['m1t'])
                S.dma('sp', lambda e: e.dma_start(out=m1_d[r0:r0 + 64, :], in_=m1t), reads=['m1t'],
                      writes=[('m1', ti)])

            if 'A1' in PHASES:
                for ti in (range(NPT) if TILES is None else TILES):
                    tileA1(ti, TP)
                tileA1(NPT, TS)
        phase_A1()
        S.barrier()

        def phase_A2():
            AL.off = pers_mark
            winB = AL.get(128, [8, 2056], BF16)
            wgB = AL.get(128, [8, 1024], BF16)
            wbrB = AL.get(128, [4, 1024], BF16)
            woutb = AL.get(128, [8, 1024], BF16)
            load_w_bf16(winB, win_d, 0, 8, 2048, 2056, 'winB')
            load_w_bf16(wgB, win_d, 0, 8, 5128, 1024, 'wgB')
            load_w_bf16(wbrB, wbrb_d, 0, 4, 0, 1024, 'wbrB')
            load_w_bf16(woutb, wout_d, 0, 8, 0, 1024, 'woutb')
            gmixb = AL.get(64, 1024, F32)
            bcast_load(gmixb, gmix_d[0:1, :], 64, 1024, 'gmixb')
            gnbb = AL.get(64, 128, F32)
            bcast_load(gnbb, gnb_d[0:1, :], 64, 128, 'gnbb')
            negA = AL.get(64, 4, F32)
            dtbb = AL.get(64, 4, F32)
            bcast_load(negA, alog_d[0:1, :], 64, 4, 'negA')
            bcast_load(dtbb, dtb_d[0:1, :], 64, 4, 'dtbb')
            S.op('act', lambda e: e.activation(out=negA, in_=negA, func=AF.Exp), reads=['negA'], writes=['negA'])
            S.op('dve', lambda e: e.tensor_scalar(out=negA, in0=negA, scalar1=-1.0, scalar2=None, op0=ALU.mult),
                 reads=['negA'], writes=['negA'])
            cwin = AL.get(4, 1536, F32)
            cw = AL.get(128, [12, 4], F32)
            S.dma('sp', lambda e: e.dma_start(out=cwin, in_=convw_d), writes=['cwin'])
            for g in range(12):
                S.op('pe', lambda e, g=g: e.transpose(out=bank(0, 128, 4, g * 4), in_=cwin[0:4, g * 128:(g + 1) * 128],
                                                      identity=identf[0:4, 0:4]), reads=['cwin', 'cst'], writes=bk(0))
            S.op('dve', lambda e: e.tensor_copy(out=cw, in_=bank(0, 128, 48).rearrange("p (g j) -> p g j", g=12)),
                 reads=bk(0), writes=['cw'])
            xt = AL.get(64, 1024, F32)
            hb = AL.get(64, 1024, BF16)
            hT = AL.get(128, [8, 64], BF16)
            ss = AL.get(64, 1, F32)
            rs = AL.get(64, 1, F32)
            F = [AL.get(64, 512, F32) for _ in range(6)]
            rawext = AL.get(128, 12 * 112, F32)
            acc = AL.get(128, [12, 64], F32)
            sqn = AL.get(128, 512, F32)
            qkn = AL.get(128, [8, 64], BF16)
            qknm = AL.get(128, [8, 64], BF16)
            vcb = AL.get(128, [4, 64], BF16)
            vk = AL.get(64, [8, 128], BF16)
            sm = AL.get(64, 64, F32)
            beta, zz, ez, spl, gg, gcum, gLb, eg, ngcum, dkk, kdec, nbeta, nbe = [sm[:, i * 4:(i + 1) * 4] for i in range(13)]
            gc = AL.get(64, [4, 16], F32)
            egLT = AL.get(128, [4, 16], F32)
            gd = AL.get(64, [4, 64], F32)
            gam = AL.get(64, [4, 64], F32)
            gamT = AL.get(64, [4, 64], F32)
            Nb = [AL.get(64, [4, 64], BF16) for _ in range(2)]
            Mb = [AL.get(64, [4, 64], BF16) for _ in range(2)]
            Qb = AL.get(64, [4, 64], BF16)
            Qf = AL.get(64, [4, 64], F32)
            rb = AL.get(64, [4, 128], BF16)
            ub = AL.get(64, [4, 128], BF16)
            ATb = AL.get(64, [4, 64], BF16)
            khat = AL.get(64, [4, 128], BF16)
            khm = AL.get(64, [4, 128], BF16)
            Sd = AL.get(128, [4, 128], F32)
            Sdb = AL.get(128, [4, 128], BF16)
            ss4 = AL.get(64, 4, F32)
            rs4 = AL.get(64, 4, F32)
            ogt = AL.get(64, 512, BF16)
            oT = AL.get(128, [4, 64], BF16)
            m1t = AL.get(64, 1024, F32)
            mb = AL.get(64, 1024, BF16)
            mT = AL.get(128, [8, 64], BF16)
            S.op('pool', lambda e: e.memset(Sd, 0.0), writes=['Sd'])
            S.op('pool', lambda e: e.memset(Sdb, 0.0), writes=['Sdb'])
            S.op('pool', lambda e: e.memset(rawext, 0.0), writes=['rawext'])

            def tileA2(ti, T):
                nch, C = T['nch'], T['C']
                r0 = ti * 64
                W = 3 + C
                rx = rawext[:, 0:12 * nch * W].rearrange("p (g s w) -> p g s w", g=12, s=nch)
                S.dma('sp', lambda e: e.dma_start(out=xt, in_=x_d[r0:r0 + 64, :]), writes=['xt'])
                S.dma('sp', lambda e: e.dma_start(out=m1t, in_=m1_d[r0:r0 + 64, :]), reads=[('m1', ti)], writes=['m1t'])
                rmsnorm_to_bf16(xt, 'xt', gmixb, 'gmixb', hb, 'hb', ss, rs)
                transpose_tm(hb, 'hb', 8, hT, 'hT')
                praw = PB[:, 3 * 512:3 * 512 + 768].rearrange("p (g t) -> p g t", g=12)
                for g in range(12):
                    for k in range(8):
                        S.op('pe', lambda e, g=g, k=k: e.matmul(praw[:, g, :], lhsT=winB[:, k, g * 128:(g + 1) * 128],
                                                                rhs=hT[:, k, :], start=(k == 0), stop=(k == 7)),
                             reads=['winB', 'hT'], writes=bk(3, 4))
                if nch == 1:
                    if ti > 0:
                        S.op('pool', lambda e: e.tensor_copy(out=rx[:, :, 0, 0:3], in_=rx[:, :, 0, 64:67]),
                             reads=['rawext'], writes=['rawext'])
                    S.op('act', lambda e: e.copy(out=rx[:, :, 0, 3:67], in_=praw), reads=bk(3, 4), writes=['rawext'])
                else:
                    cvin = F[0:3]
                    for i3 in range(3):
                        S.dma('sp', lambda e, i3=i3: e.dma_start(out=cvin[i3][0:48, :],
                                                                 in_=scv_d[:, i3 * 512:(i3 + 1) * 512]),
                              writes=['F%d' % i3])
                    for g in range(12):
                        S.op('pe', lambda e, g=g: e.transpose(
                            out=bank(0, 128, 48, g * 64), in_=cvin[g // 4][0:48, (g % 4) * 128:(g % 4 + 1) * 128],
                            identity=identf[0:48, 0:48]), reads=['F%d' % (g // 4), 'cst'], writes=bk(0, 1))
                    S.op('dve', lambda e: e.tensor_copy(
                        out=rx[:, :, :, 0:3],
                        in_=PB[:, 0:768].rearrange("p (g x) -> p g x", g=12)[:, :, 0:48].rearrange(
                            "p g (s j) -> p g s j", s=16)),
                        reads=bk(0, 1), writes=['rawext'])
                    S.op('act', lambda e: e.copy(out=rx[:, :, :, 3:7],
                                                 in_=praw.rearrange("p g (s t) -> p g s t", s=16)),
                         reads=bk(3, 4), writes=['rawext'])
                accv = acc.rearrange("p g (s t) -> p g s t", s=nch)
                for g in range(12):
                    S.op('dve', lambda e, g=g: e.tensor_scalar(out=accv[:, g], in0=rx[:, g, :, 0:C],
                                                               scalar1=cw[:, g, 0:1], scalar2=None, op0=ALU.mult),
                         reads=['rawext', 'cw'], writes=[('acc', g)])
                    for j in range(1, 4):
                        S.op('dve', lambda e, g=g, j=j: e.scalar_tensor_tensor(
                            out=accv[:, g], in0=rx[:, g, :, j:j + C], scalar=cw[:, g, j:j + 1], in1=accv[:, g],
                            op0=ALU.mult, op1=ALU.add), reads=['rawext', 'cw', ('acc', g)], writes=[('acc', g)])
                acck = [('acc', g) for g in range(12)]
                S.op('act', lambda e: e.activation(out=acc, in_=acc, func=AF.Silu), reads=acck, writes=acck)
                S.op('act', lambda e: e.activation(out=sqn, in_=acc[:, 0:8, :].rearrange("p g t -> p (g t)"),
                                                   func=AF.Square), reads=acck, writes=['sqn'])
                S.op('pe', lambda e: e.matmul(bank(0), lhsT=onesf, rhs=sqn, start=True, stop=True),
                     reads=['cst', 'sqn'], writes=bk(0))
                S.op('act', lambda e: e.activation(out=sqn, in_=bank(0), func=AF.Sqrt, bias=epsc), reads=bk(0) + ['cst'],
                     writes=['sqn'])
                S.op('dve', lambda e: e.reciprocal(out=sqn, in_=sqn), reads=['sqn'], writes=['sqn'])
                sq3 = sqn.rearrange("p (g t) -> p g t", g=8)
                S.op('dve', lambda e: e.scalar_tensor_tensor(out=qkn[:, 0:4, :], in0=acc[:, 0:4, :], scalar=128.0 ** -0.5,
                                                             in1=sq3[:, 0:4, :], op0=ALU.mult, op1=ALU.mult),
                     reads=acck + ['sqn'], writes=['qkn'])
                S.op('dve', lambda e: e.tensor_tensor(out=qkn[:, 4:8, :], in0=acc[:, 4:8, :], in1=sq3[:, 4:8, :],
                                                      op=ALU.mult), reads=acck + ['sqn'], writes=['qkn'])
                S.op('act', lambda e: e.copy(out=vcb, in_=acc[:, 8:12, :]), reads=acck, writes=['vcb'])
                for h in range(4):
                    S.op('pe', lambda e, h=h: e.transpose(out=PTt[0:64, h * 128:(h + 1) * 128], in_=vcb[:, h, :],
                                                          identity=identb), reads=['vcb', 'identb'], writes=['pt'])
                for h in range(4):
                    S.op('pe', lambda e, h=h: e.transpose(out=PTt[0:64, (4 + h) * 128:(5 + h) * 128], in_=qkn[:, 4 + h, :],
                                                          identity=identb), reads=['qkn', 'identb'], writes=['pt'])
                S.op('act', lambda e: e.copy(out=vk, in_=PTt[0:64, :].rearrange("p (k v) -> p k v", k=8)),
                     reads=['pt'], writes=['vk'])
                proj_tm(bank(1, 64, 8), bk(1), hT, 'hT', winB, 'winB', 2048, 8)
                S.op('act', lambda e: e.activation(out=beta, in_=bank(1, 64, 4), func=AF.Sigmoid), reads=bk(1), writes=['sm'])
                S.op('dve', lambda e: e.tensor_tensor(out=zz, in0=bank(1, 64, 4, 4), in1=dtbb, op=ALU.add),
                     reads=bk(1) + ['dtbb'], writes=['sm'])
                S.op('act', lambda e: e.activation(out=ez, in_=zz, func=AF.Exp), reads=['sm'], writes=['sm'])
                S.op('act', lambda e: e.activation(out=spl, in_=ez, func=AF.Ln, bias=C_('one', 64)), reads=['sm', 'cst'],
                     writes=['sm'])
                S.op('dve', lambda e: e.tensor_tensor(out=gg, in0=spl, in1=negA, op=ALU.mult), reads=['sm', 'negA'],
                     writes=['sm'])
                S.op('pe', lambda e: e.matmul(bank(2, 64, 4), lhsT=T['tri'], rhs=gg, start=True, stop=True),
                     reads=['cst', 'sm'], writes=bk(2))
                S.op('pe', lambda e: e.matmul(bank(2, 64, 4, 4), lhsT=T['blk'], rhs=gg, start=True, stop=True),
                     reads=['cst', 'sm'], writes=bk(2))
                S.op('dve', lambda e: e.tensor_copy(out=sm[:, 20:28], in_=bank(2, 64, 8)), reads=bk(2), writes=['sm'])
                S.op('act', lambda e: e.activation(out=eg, in_=gcum, func=AF.Exp), reads=['sm'], writes=['sm'])
                S.op('dve', lambda e: e.tensor_scalar(out=ngcum, in0=gcum, scalar1=-1.0, scalar2=None, op0=ALU.mult),
                     reads=['sm'], writes=['sm'])
                S.op('dve', lambda e: e.tensor_tensor(out=dkk, in0=gLb, in1=gcum, op=ALU.subtract), reads=['sm'],
                     writes=['sm'])
                S.op('act', lambda e: e.activation(out=kdec, in_=dkk, func=AF.Exp), reads=['sm'], writes=['sm'])
                S.op('dve', lambda e: e.tensor_scalar(out=nbeta, in0=beta, scalar1=-1.0, scalar2=None, op0=ALU.mult),
                     reads=['sm'], writes=['sm'])
                S.op('dve', lambda e: e.tensor_tensor(out=nbe, in0=nbeta, in1=eg, op=ALU.mult), reads=['sm'],
                     writes=['sm'])
                S.op('dve', lambda e: e.tensor_tensor(out=gc[:, :, 0:nch], in0=gg.unsqueeze(2).to_broadcast([64, 4, nch]),
                                                      in1=T['cind'][:, 0:nch].unsqueeze(1).to_broadcast([64, 4, nch]),
                                                      op=ALU.mult), reads=['sm', 'cst'], writes=['gc'])
                for h in range(4):
                    S.op('pe', lambda e, h=h: e.matmul(bank(1, 128, nch, 64 + h * 16), lhsT=onesf[0:64, :],
                                                       rhs=gc[:, h, 0:nch], start=True, stop=True),
                         reads=['cst', 'gc'], writes=bk(1))
                S.op('act', lambda e: e.activation(
                    out=egLT[:, :, 0:nch], in_=bank(1, 128, 64, 64).rearrange("p (h c) -> p h c", h=4)[:, :, 0:nch],
                    func=AF.Exp), reads=bk(1), writes=['egLT'])
                S.op('dve', lambda e: e.tensor_tensor(out=gd, in0=gcum.unsqueeze(2).to_broadcast([64, 4, 64]),
                                                      in1=identf[0:64, 0:64].unsqueeze(1).to_broadcast([64, 4, 64]),
                                                      op=ALU.mult), reads=['sm', 'cst'], writes=['gd'])
                gd2 = gd.rearrange("p h t -> p (h t)")
                S.op('pe', lambda e: e.matmul(bank(2, 64, 256), lhsT=onesf[0:64, 0:64], rhs=gd2, start=True, stop=False),
                     reads=['cst', 'gd'], writes=bk(2))
                S.op('pe', lambda e: e.matmul(bank(2, 64, 256), lhsT=identf[0:64, 0:64], rhs=T['pmS'], start=False,
                                              stop=True), reads=['cst'], writes=bk(2))
                S.op('pe', lambda e: e.matmul(bank(2, 64, 256, 256), lhsT=onesf[0:64, 0:64], rhs=gd2, start=True,
                                              stop=False), reads=['cst', 'gd'], writes=bk(2))
                S.op('pe', lambda e: e.matmul(bank(2, 64, 256, 256), lhsT=identf[0:64, 0:64], rhs=T['nmT'], start=False,
                                              stop=True), reads=['cst'], writes=bk(2))
                for h in range(4):
                    S.op('act', lambda e, h=h: e.activation(out=gam[:, h, :], in_=bank(2, 64, 64, h * 64), func=AF.Exp,
                                                            scale=-1.0, bias=gcum[:, h:h + 1]),
                         reads=bk(2) + ['sm'], writes=['gam'])
                for h in range(4):
                    S.op('act', lambda e, h=h: e.activation(out=gamT[:, h, :], in_=bank(2, 64, 64, 256 + h * 64),
                                                            func=AF.Exp, bias=ngcum[:, h:h + 1]),
                         reads=bk(2) + ['sm'], writes=['gamT'])
                for h in range(4):
                    S.op('pe', lambda e, h=h: e.matmul(bank(0, 64, 64, h * 64), lhsT=qkn[:, 4 + h, :], rhs=qkn[:, 4 + h, :],
                                                       start=True, stop=True), reads=['qkn'], writes=bk(0))
                for h in range(4):
                    S.op('dve', lambda e, h=h: e.scalar_tensor_tensor(
                        out=Nb[0][:, h, :], in0=bank(0, 64, 64, h * 64), scalar=nbeta[:, h:h + 1], in1=gam[:, h, :],
                        op0=ALU.mult, op1=ALU.mult), reads=bk(0) + ['sm', 'gam'], writes=['Nb0'])
                for h in range(4):
                    S.op('pe', lambda e, h=h: e.transpose(out=PTt[0:64, h * 64:(h + 1) * 64], in_=Nb[0][:, h, :],
                                                          identity=identb[0:64, 0:64]),
                         reads=['Nb0', 'identb'], writes=['pt'])
                ptv = PTt[0:64, 0:256].rearrange("p (h t) -> p h t", h=4)
                S.op('dve', lambda e: e.tensor_copy(out=Mb[0], in_=ptv), reads=['pt'], writes=['Mb0'])
                S.op('dve', lambda e: e.tensor_tensor(out=Qf, in0=ptv,
                                                      in1=identf[0:64, 0:64].unsqueeze(1).to_broadcast([64, 4, 64]),
                                                      op=ALU.add), reads=['pt', 'cst'], writes=['Qf'])
                S.op('act', lambda e: e.copy(out=Qb, in_=Qf), reads=['Qf'], writes=['Qb'])
                nsteps = {64: 5, 4: 1}[C]
                cur = 0
                for i in range(nsteps):
                    last = (i == nsteps - 1)
                    nxt = 1 - cur
                    for h in range(4):
                        S.op('pe', lambda e, h=h, cur=cur: e.matmul(bank(0, 64, 64, h * 64), lhsT=Mb[cur][:, h, :],
                                                                    rhs=Nb[cur][:, h, :], start=True, stop=True),
                             reads=['Mb%d' % cur, 'Nb%d' % cur], writes=bk(0))
                    if not last:
                        for h in range(4):
                            S.op('pe', lambda e, h=h, cur=cur: e.matmul(bank(1, 64, 64, h * 64), lhsT=Nb[cur][:, h, :],
                                                                        rhs=Mb[cur][:, h, :], start=True, stop=True),
                                 reads=['Mb%d' % cur, 'Nb%d' % cur], writes=bk(1))
                    S.op('act', lambda e, nxt=nxt: e.copy(out=Nb[nxt],
                                                          in_=bank(0, 64, 256).rearrange("p (h t) -> p h t", h=4)),
                         reads=bk(0), writes=['Nb%d' % nxt])
                    if not last:
                        S.op('dve', lambda e, nxt=nxt: e.tensor_copy(
                            out=Mb[nxt], in_=bank(1, 64, 256).rearrange("p (h t) -> p h t", h=4)),
                            reads=bk(1), writes=['Mb%d' % nxt])
                    for h in range(4):
                        S.op('pe', lambda e, h=h, nxt=nxt: e.matmul(bank(2, 64, 64, h * 64), lhsT=Nb[nxt][:, h, :],
                                                                    rhs=Qb[:, h, :], start=True, stop=True),
                             reads=['Nb%d' % nxt, 'Qb'], writes=bk(2))
                    S.op('dve', lambda e: e.tensor_tensor(out=Qf, in0=Qf,
                                                          in1=bank(2, 64, 256).rearrange("p (h t) -> p h t", h=4),
                                                          op=ALU.add), reads=['Qf'] + bk(2), writes=['Qf'])
                    S.op('act', lambda e: e.copy(out=Qb, in_=Qf), reads=['Qf'], writes=['Qb'])
                    cur = nxt
                for h in range(4):
                    S.op('pe', lambda e, h=h: e.matmul(bank(0, 64, 64, h * 64), lhsT=qkn[:, 4 + h, :], rhs=qkn[:, h, :],
                                                       start=True, stop=True), reads=['qkn'], writes=bk(0))
                S.op('dve', lambda e: e.tensor_tensor(out=ATb, in0=bank(0, 64, 256).rearrange("p (h t) -> p h t", h=4),
                                                      in1=gamT, op=ALU.mult), reads=bk(0) + ['gamT'], writes=['ATb'])
                for c in range(nch):
                    if nch > 1:
                        S.dma('sp', lambda e, c=c: e.dma_start(out=Sd, in_=sd_d[c].rearrange("h d v -> d h v")),
                              writes=['Sd'])
                        S.op('act', lambda e: e.copy(out=Sdb, in_=Sd), reads=['Sd'], writes=['Sdb'])
                        S.op('dve', lambda e, c=c: e.tensor_tensor(
                            out=qknm, in0=qkn, in1=cmb[:, c, :].unsqueeze(1).to_broadcast([128, 8, 64]), op=ALU.mult),
                            reads=['qkn', 'cmb'], writes=['qknm'])
                        src, skey = qknm, 'qknm'
                    else:
                        src, skey = qkn, 'qkn'
                    for h in range(4):
                        S.op('pe', lambda e, h=h, src=src, c=c: e.matmul(
                            bank(5, 64, 128, h * 128), lhsT=src[:, 4 + h, :], rhs=Sdb[:, h, :], start=(c == 0 and h == 0),
                            stop=(c == nch - 1), skip_group_check=True), reads=[skey, 'Sdb'], writes=bk(5))
                        S.op('pe', lambda e, h=h, src=src, c=c: e.matmul(
                            bank(6, 64, 128, h * 128), lhsT=src[:, h, :], rhs=Sdb[:, h, :], start=(c == 0 and h == 0),
                            stop=(c == nch - 1), skip_group_check=True), reads=[skey, 'Sdb'], writes=bk(6))
                t1, bv, t2, ob = F[3], F[4], F[3], F[4]
                S.op('dve', lambda e: e.tensor_tensor(out=t1.rearrange("p (h v) -> p h v", h=4),
                                                      in0=bank(5, 64).rearrange("p (h v) -> p h v", h=4),
                                                      in1=nbe.unsqueeze(2).to_broadcast([64, 4, 128]), op=ALU.mult),
                     reads=bk(5) + ['sm'], writes=['F3'])
                S.op('dve', lambda e: e.tensor_tensor(out=bv.rearrange("p (h v) -> p h v", h=4), in0=vk[:, 0:4, :],
                                                      in1=beta.unsqueeze(2).to_broadcast([64, 4, 128]), op=ALU.mult),
                     reads=['vk', 'sm'], writes=['F4'])
                S.op('dve', lambda e: e.tensor_tensor(out=rb.rearrange("p h v -> p (h v)"), in0=t1, in1=bv, op=ALU.add),
                     reads=['F3', 'F4'], writes=['rb'])
                for h in range(4):
                    S.op('pe', lambda e, h=h: e.matmul(bank(1, 64, 128, h * 128), lhsT=Qb[:, h, :], rhs=rb[:, h, :],
                                                       start=True, stop=True), reads=['Qb', 'rb'], writes=bk(1))
                S.op('act', lambda e: e.copy(out=ub, in_=bank(1, 64).rearrange("p (h v) -> p h v", h=4)),
                     reads=bk(1), writes=['ub'])
                for h in range(4):
                    S.op('pe', lambda e, h=h: e.matmul(bank(2, 64, 128, h * 128), lhsT=ATb[:, h, :], rhs=ub[:, h, :],
                                                       start=True, stop=True), reads=['ATb', 'ub'], writes=bk(2))
                S.op('dve', lambda e: e.tensor_tensor(out=t2.rearrange("p (h v) -> p h v", h=4),
                                                      in0=bank(6, 64).rearrange("p (h v) -> p h v", h=4),
                                                      in1=eg.unsqueeze(2).to_broadcast([64, 4, 128]), op=ALU.mult),
                     reads=bk(6) + ['sm'], writes=['F3'])
                S.op('dve', lambda e: e.tensor_tensor(out=ob, in0=t2, in1=bank(2, 64), op=ALU.add),
                     reads=['F3'] + bk(2), writes=['F4'])
                S.op('dve', lambda e: e.tensor_tensor(out=khat, in0=vk[:, 4:8, :],
                                                      in1=kdec.unsqueeze(2).to_broadcast([64, 4, 128]), op=ALU.mult),
                     reads=['vk', 'sm'], writes=['khat'])
                for c in range(nch):
                    if nch > 1:
                        S.dma('sp', lambda e, c=c: e.dma_start(out=Sd, in_=sd_d[c].rearrange("h d v -> d h v")),
                              writes=['Sd'])
                        S.op('dve', lambda e, c=c: e.tensor_scalar(out=khm, in0=khat, scalar1=T['cind'][:, c:c + 1],
                                                                   scalar2=None, op0=ALU.mult),
                             reads=['khat', 'cst'], writes=['khm'])
                        ks_, kkey = khm, 'khm'
                    else:
                        ks_, kkey = khat, 'khat'
                    for h in range(4):
                        S.op('pe', lambda e, h=h, ks_=ks_: e.matmul(bank(0, 128, 128, h * 128), lhsT=ks_[:, h, :],
                                                                    rhs=ub[:, h, :], start=True, stop=True),
                             reads=[kkey, 'ub'], writes=bk(0))
                    S.op('dve', lambda e, c=c: e.tensor_tensor(
                        out=Sd, in0=Sd, in1=egLT[:, :, c:c + 1].to_broadcast([128, 4, 128]), op=ALU.mult),
                        reads=['Sd', 'egLT'], writes=['Sd'])
                    S.op('dve', lambda e: e.tensor_tensor(out=Sd, in0=Sd, in1=bank(0).rearrange("p (h v) -> p h v", h=4),
                                                          op=ALU.add), reads=['Sd'] + bk(0), writes=['Sd'])
                    if nch > 1:
                        S.dma('sp', lambda e, c=c: e.dma_start(out=ds_d[c].rearrange("h d v -> d h v"), in_=Sd),
                              reads=['Sd'], final=True)
                    else:
                        S.op('act', lambda e: e.copy(out=Sdb, in_=Sd), reads=['Sd'], writes=['Sdb'])
                if nch == 1 and ti == NPT - 1:
                    S.dma('sp', lambda e: e.dma_start(out=dp_d.rearrange("h d v -> d h v"), in_=Sd),
                          reads=['Sd'], final=True)
                if nch > 1 or ti == NPT - 1:
                    for i3 in range(3):
                        proj_tm(bank(0, 64), bk(0), hT, 'hT', winB, 'winB', i3 * 512, 512)
                        S.op('act', lambda e: e.copy(out=F[0], in_=bank(0, 64)), reads=bk(0), writes=['F0'])
                        if nch == 1:
                            S.dma('sp', lambda e, i3=i3: e.dma_start(out=cp_d[:, i3 * 512:(i3 + 1) * 512],
                                                                     in_=F[0][61:64, :]), reads=['F0'], final=True)
                        else:
                            for s in range(16):
                                S.dma('sp', lambda e, i3=i3, s=s: e.dma_start(
                                    out=cs_d[3 * s:3 * s + 3, i3 * 512:(i3 + 1) * 512], in_=F[0][4 * s + 1:4 * s + 4, :]),
                                    reads=['F0'], final=True)
                proj_tm(bank(0, 64), bk(0), hT, 'hT', winB, 'winB', 1536, 512)
                S.op('act', lambda e: e.activation(out=F[5], in_=bank(0, 64), func=AF.Silu), reads=bk(0), writes=['F5'])
                head_norm_gate_keys(ob, 'F4', gnbb, 'gnbb', F[5], 'F5', F[0], 'F0', F[1], 'F1', ogt, 'ogt', ss4, rs4)
                transpose_tm(ogt, 'ogt', 4, oT, 'oT')
                for half in range(2):
                    for k in range(4):
                        S.op('pe', lambda e, k=k, half=half: e.matmul(
                            bank(3 + half, 64), lhsT=oT[:, k, :], rhs=wbrB[:, k, half * 512:(half + 1) * 512],
                            start=(k == 0), stop=(k == 3)), reads=['oT', 'wbrB'], writes=bk(3 + half))
                    proj_tm(bank(half, 64), bk(half), hT, 'hT', wgB, 'wgB', half * 512, 512)
                    sg = F[2 + half]
                    S.op('act', lambda e, half=half, sg=sg: e.activation(out=sg, in_=bank(half, 64), func=AF.Sigmoid),
                         reads=bk(half), writes=['F%d' % (2 + half)])
                    S.op('dve', lambda e, half=half, sg=sg: e.tensor_tensor(out=sg, in0=bank(3 + half, 64), in1=sg,
                                                                            op=ALU.mult),
                         reads=bk(3 + half) + ['F%d' % (2 + half)], writes=['F%d' % (2 + half)])
                    S.op('dve', lambda e, half=half, sg=sg: e.tensor_tensor(
                        out=mb[:, half * 512:(half + 1) * 512], in0=sg, in1=m1t[:, half * 512:(half + 1) * 512],
                        op=ALU.add), reads=['F%d' % (2 + half), 'm1t'], writes=['mb'])
                transpose_tm(mb, 'mb', 8, mT, 'mT')
                for half in range(2):
                    proj_tm(bank(3 + half, 64), bk(3 + half), mT, 'mT', woutb, 'woutb', half * 512, 512)
                    S.op('dve', lambda e, half=half: e.tensor_tensor(
                        out=m1t[:, half * 512:(half + 1) * 512], in0=bank(3 + half, 64),
                        in1=xt[:, half * 512:(half + 1) * 512], op=ALU.add), reads=bk(3 + half) + ['xt'], writes=['m1t'])
                S.dma('sp', lambda e: e.dma_start(out=x1_d[r0:r0 + 64, :], in_=m1t), reads=['m1t'],
                      writes=[('x1', ti)])

            if 'A2' in PHASES:
                for ti in (range(NPT) if TILES is None else TILES):
                    tileA2(ti, TP)
                tileA2(NPT, TS)
        phase_A2()
        S.barrier()

        def phase_B():
            AL.off = pers_mark
            wqb = AL.get(128, [8, 2048], BF16)
            keysTb = AL.get(128, [16, 128], BF16)
            wpgb = AL.get(128, [8, 1024], BF16)
            wpleb = AL.get(128, [2, 1024], BF16)
            HB = [AL.get(128, 1024, BF16) for _ in range(NHB)]
            dltb = AL.get(128, [64, 64], BF16)
            WMd = AL.get(128, [64, 64], BF16)
            UV = [AL.get(128, 2048, BF16) for _ in range(NUV)]
            S.op('pool', lambda e: e.memset(WMd, 0.0), writes=['WMd'])
            junkb = AL.get(128, 1024, BF16)
            load_w_bf16(wqb, wq_d, 0, 8, 0, 2048, 'wqb')
            load_w_bf16(wpgb, wpg_d, 0, 8, 0, 1024, 'wpgb')
            load_w_bf16(wpleb, wple_d, 0, 2, 0, 1024, 'wpleb')
            S.dma('pool', lambda e: e.dma_start(out=keysTb, in_=keysT_d.rearrange("c d k -> d c k")), writes=['keysTb'])
            for cc in range(0, 4096, 2048):
                S.dma('pool', lambda e, cc=cc: e.dma_start(
                    out=dltb.rearrange("p n m -> p (n m)")[:, cc:cc + 2048], in_=dlt_d[:, cc:cc + 2048]), writes=['dltb'])
            gffnb = AL.get(64, 1024, F32)
            gpleb = AL.get(64, 1024, F32)
            gfinb = AL.get(64, 1024, F32)
            bcast_load(gffnb, gffn_d[0:1, :], 64, 1024, 'gffnb')
            bcast_load(gpleb, gple_d[0:1, :], 64, 1024, 'gpleb')
            bcast_load(gfinb, gfin_d[0:1, :], 64, 1024, 'gfinb')
            xtF = AL.get(64, 1024, F32)
            xtV2 = [AL.get(64, 1024, F32) for _ in range(2)]
            hbF = [AL.get(64, 1024, BF16) for _ in range(2)]
            hT = AL.get(128, [8, 64], BF16)
            hb2 = AL.get(64, 1024, BF16)
            hT2 = AL.get(128, [8, 64], BF16)
            ss = AL.get(64, 1, F32)
            rs = AL.get(64, 1, F32)
            ss2 = AL.get(64, 1, F32)
            rs2 = AL.get(64, 1, F32)
            qTb = AL.get(128, [16, 64], BF16)
            sc2 = [AL.get(64, [16, 128], F32) for _ in range(2)]
            v1 = AL.get(64, [16, 16], F32)
            i1 = AL.get(64, [16, 16], U32)
            i1f = AL.get(64, [16, 16], F32)
            wk = AL.get(64, 256, F32)
            cand = AL.get(64, [8, 256], F32)
            eq = cand.rearrange("p h (k i) -> p h k i", k=16)
            v2 = AL.get(64, [8, 16], F32)
            ci = AL.get(64, [8, 16], U32)
            cih = AL.get(64, [8, 16], U32)
            cil = AL.get(64, [8, 16], U32)
            cihf = AL.get(64, [8, 16], F32)
            cilf = AL.get(64, [8, 16], F32)
            iaf = AL.get(64, 128, F32)
            ibf = AL.get(64, 128, F32)
            idxf = AL.get(64, 128, F32)
            gte = AL.get(64, [8, 16], F32)
            gsum = AL.get(64, 8, F32)
            IDXT = [AL.get(128, 64, I32) for _ in range(3)]
            gateT = [AL.get(128, 64, F32) for _ in range(2)]
            ACTT = AL.get(128, 64, F32)
            X2c = AL.get(128, 64, F32)
            INc = AL.get(128, 64, F32)
            Sc = AL.get(128, 64, F32)
            gsig = AL.get(64, 1024, F32)
            ptl2 = [AL.get(64, 256, F32) for _ in range(2)]
            ptb = AL.get(64, 256, BF16)
            pTt = AL.get(128, [2, 64], BF16)
            yt = AL.get(64, 1024, F32)
            iota16 = C_('iota16')

            def topk16(src, srckey, width, vals, vkey, idxs, ikey):
                S.op('dve', lambda e: e.max(out=vals[:, 0:8], in_=src), reads=[srckey], writes=[vkey])
                S.op('dve', lambda e: e.max_index(out=idxs[:, 0:8], in_max=vals[:, 0:8], in_values=src),
                     reads=[srckey, vkey], writes=[ikey])
                S.op('dve', lambda e: e.match_replace(out=wk[:, 0:width], in_to_replace=vals[:, 0:8], in_values=src,
                                                      imm_value=-1e30), reads=[srckey, vkey], writes=['wk'])
                S.op('dve', lambda e: e.max(out=vals[:, 8:16], in_=wk[:, 0:width]), reads=['wk'], writes=[vkey])
                S.op('dve', lambda e: e.max_index(out=idxs[:, 8:16], in_max=vals[:, 8:16], in_values=wk[:, 0:width]),
                     reads=['wk', vkey], writes=[ikey])

            def rms_bf16(xin, xkey, gb, gkey, hout, hkey, ss_, sskey, rs_, rskey):
                S.op('act', lambda e: e.activation(out=hout, in_=xin, func=AF.Square, accum_out=ss_),
                     reads=[xkey], writes=[hkey, sskey])
                S.op('act', lambda e: e.activation(out=rs_, in_=ss_, func=AF.Sqrt, scale=1.0 / 1024,
                                                   bias=epsc[0:64, :]), reads=[sskey, 'cst'], writes=[rskey])
                S.op('dve', lambda e: e.reciprocal(out=rs_, in_=rs_), reads=[rskey], writes=[rskey])
                S.op('dve', lambda e: e.scalar_tensor_tensor(out=hout, in0=xin, scalar=rs_[:, 0:1], in1=gb,
                                                             op0=ALU.mult, op1=ALU.mult),
                     reads=[xkey, rskey, gkey], writes=[hkey])

            def front_pe(ti, pos):
                par = 0
                sc = sc2[pos % 2]
                sck = 'sc%d' % (pos % 2)
                r0 = ti * 64
                hb = hbF[par]
                hkey = 'hbF%d' % par
                S.dma('sp', lambda e: e.dma_start(out=xtF, in_=x1_d[r0:r0 + 64, :]), reads=[('x1', ti)], writes=['xtF'])
                rms_bf16(xtF, 'xtF', gffnb, 'gffnb', hb, hkey, ss, 'ss', rs, 'rs')
                S.dma('sp', lambda e: e.dma_start(out=h2_d[r0:r0 + 64, :], in_=hb), reads=[hkey], writes=[('h2', ti)])
                transpose_tm(hb, hkey, 8, hT, 'hT')
                pq = PB[:, 1024:2048].rearrange("p (c t) -> p c t", c=16)
                for hc in range(16):
                    for k in range(8):
                        S.op('pe', lambda e, hc=hc, k=k: e.matmul(pq[:, hc, :], lhsT=wqb[:, k, hc * 128:(hc + 1) * 128],
                                                                  rhs=hT[:, k, :], start=(k == 0), stop=(k == 7)),
                             reads=['wqb', 'hT'], writes=bk(2, 3))
                S.op('act', lambda e: e.copy(out=qTb, in_=pq), reads=bk(2, 3), writes=['qTb'])
                for half in range(2):
                    for j in range(8):
                        hc = half * 8 + j
                        S.op('pe', lambda e, hc=hc, j=j: e.matmul(PB[0:64, 1024 + j * 128:1024 + (j + 1) * 128],
                                                                  lhsT=qTb[:, hc, :], rhs=keysTb[:, hc, :],
                                                                  start=True, stop=True),
                             reads=['qTb', 'keysTb'], writes=bk(2, 3))
                    S.op('act', lambda e, half=half: e.copy(
                        out=sc[:, half * 8:(half + 1) * 8, :],
                        in_=PB[0:64, 1024:2048].rearrange("p (c k) -> p c k", c=8)), reads=bk(2, 3), writes=[sck])

            def front_dve(ti, pos):
                par = pos % 2
                ip = pos % 3
                sc = sc2[pos % 2]
                sck = 'sc%d' % (pos % 2)
                for hc in range(16):
                    topk16(sc[:, hc, :], sck, 128, v1[:, hc, :], 'v1', i1[:, hc, :], 'i1')
                S.op('dve', lambda e: e.tensor_copy(out=i1f, in_=i1), reads=['i1'], writes=['i1f'])
                for h in range(8):
                    S.op('dve', lambda e, h=h: e.tensor_tensor(
                        out=cand[:, h, :].rearrange("p (i j) -> p i j", i=16),
                        in0=v1[:, 2 * h, :].unsqueeze(2).to_broadcast([64, 16, 16]),
                        in1=v1[:, 2 * h + 1, :].unsqueeze(1).to_broadcast([64, 16, 16]), op=ALU.add),
                        reads=['v1'], writes=['cand'])
                for h in range(8):
                    topk16(cand[:, h, :], 'cand', 256, v2[:, h, :], 'v2', ci[:, h, :], 'ci')
                S.op('dve', lambda e: e.tensor_scalar(out=cih, in0=ci, scalar1=4, scalar2=None,
                                                      op0=ALU.logical_shift_right), reads=['ci'], writes=['cih'])
                S.op('dve', lambda e: e.tensor_scalar(out=cil, in0=ci, scalar1=15, scalar2=None, op0=ALU.bitwise_and),
                     reads=['ci'], writes=['cil'])
                S.op('dve', lambda e: e.tensor_copy(out=cihf, in_=cih), reads=['cih'], writes=['cihf'])
                S.op('dve', lambda e: e.tensor_copy(out=cilf, in_=cil), reads=['cil'], writes=['cilf'])
                i1v = i1f.rearrange("p (h c) i -> p h c i", c=2)
                for (cf, ckey, cpos, dst, dkey) in ((cihf, 'cihf', 0, iaf, 'iaf'), (cilf, 'cilf', 1, ibf, 'ibf')):
                    S.op('dve', lambda e, cf=cf: e.tensor_tensor(
                        out=eq, in0=cf.unsqueeze(3).to_broadcast([64, 8, 16, 16]),
                        in1=iota16.unsqueeze(1).unsqueeze(1).to_broadcast([64, 8, 16, 16]), op=ALU.is_equal),
                        reads=[ckey, 'cst'], writes=['cand'])
                    S.op('dve', lambda e, cpos=cpos: e.tensor_tensor(
                        out=eq, in0=eq, in1=i1v[:, :, cpos, :].unsqueeze(2).to_broadcast([64, 8, 16, 16]), op=ALU.mult),
                        reads=['cand', 'i1f'], writes=['cand'])
                    S.op('dve', lambda e, dst=dst: e.reduce_sum(out=dst, in_=eq.rearrange("p h k i -> p (h k) i"),
                                                                axis=AX.X), reads=['cand'], writes=[dkey])
                S.op('dve', lambda e: e.scalar_tensor_tensor(out=idxf, in0=iaf, scalar=128.0, in1=ibf, op0=ALU.mult,
                                                             op1=ALU.add), reads=['iaf', 'ibf'], writes=['idxf'])
                S.op('dve', lambda e: e.tensor_tensor(out=gte, in0=v2, in1=v2[:, :, 0:1].to_broadcast([64, 8, 16]),
                                                      op=ALU.subtract), reads=['v2'], writes=['gte'])
                S.op('act', lambda e: e.activation(out=gte, in_=gte, func=AF.Exp), reads=['gte'], writes=['gte'])
                S.op('dve', lambda e: e.reduce_sum(out=gsum, in_=gte, axis=AX.X), reads=['gte'], writes=['gsum'])
                S.op('dve', lambda e: e.reciprocal(out=gsum, in_=gsum), reads=['gsum'], writes=['gsum'])
                S.op('dve', lambda e: e.tensor_tensor(out=gte, in0=gte, in1=gsum.unsqueeze(2).to_broadcast([64, 8, 16]),
                                                      op=ALU.mult), reads=['gte', 'gsum'], writes=['gte'])
                S.op('pe', lambda e: e.transpose(out=bank(6, 128, 64), in_=idxf, identity=identf[0:64, 0:64]),
                     reads=['idxf', 'cst'], writes=bk(6))
                S.op('pe', lambda e: e.transpose(out=bank(6, 128, 64, 64), in_=gte.rearrange("p h k -> p (h k)"),
                                                 identity=identf[0:64, 0:64]), reads=['gte', 'cst'], writes=bk(6))
                S.op('dve', lambda e: e.tensor_copy(out=IDXT[ip], in_=bank(6, 128, 64)), reads=bk(6),
                     writes=['IDXT%d' % ip])
                S.op('dve', lambda e: e.tensor_copy(out=gateT[par], in_=bank(6, 128, 64, 64)), reads=bk(6),
                     writes=['gateT%d' % par])

            def uvstage(ti, pos):
                par = pos % 2
                ip = pos % 3
                r0 = ti * 64
                xtV = xtV2[par]
                ptl = ptl2[par]
                xk = 'xtV%d' % par
                pk = 'ptl%d' % par
                gT = gateT[par]
                gk = 'gateT%d' % par
                S.begin()
                S.dma('sp', lambda e: e.dma_start(out=xtV, in_=x1_d[r0:r0 + 64, :]), reads=[('x1', ti)], writes=[xk])
                S.dma('sp', lambda e: e.dma_start(out=ptl, in_=p_d[r0:r0 + 64, :]), writes=[pk])
                head = S.end()
                segs = []
                for n in range(64):
                    S.begin()
                    g = (pos * 64 + n) % NUV
                    uv_ = UV[g]
                    uvkey = 'UV%d' % g
                    S.dma('pool', lambda e, n=n, uv_=uv_: e.indirect_dma_start(
                        out=uv_, out_offset=None, in_=uvb_d,
                        in_offset=bass.IndirectOffsetOnAxis(ap=IDXT[ip][:, n:n + 1], axis=0)),
                        reads=['IDXT%d' % ip, 'uvb'], writes=[uvkey])
                    gh = (pos * 64 + n) % NHB
                    hbb = HB[gh]
                    hbkey = 'HB%d' % gh
                    S.dma('sp', lambda e, n=n, hbb=hbb: e.dma_start(
                        out=hbb, in_=h2_d[ti * 64 + n:ti * 64 + n + 1, :].to_broadcast([128, 1024])),
                        reads=[('h2', ti)], writes=[hbkey])
                    S.op('dve', lambda e, n=n, uv_=uv_, hbb=hbb: e.scalar_tensor_tensor(
                        out=junkb, in0=uv_[:, 0:1024], scalar=1.0, in1=hbb, op0=ALU.mult, op1=ALU.mult,
                        accum_out=ACTT[:, n:n + 1]), reads=[uvkey, hbkey], writes=[('ACT', n)])
                    xa = ACTT[:, n:n + 1]
                    S.op('act', lambda e, n=n, xa=xa: e.activation(out=X2c[:, n:n + 1], in_=xa, func=AF.Identity,
                                                                  scale=xa), reads=[('ACT', n)], writes=[('X2', n)])
                    S.op('act', lambda e, n=n: e.activation(out=X2c[:, n:n + 1], in_=X2c[:, n:n + 1], func=AF.Identity,
                                                            scale=0.044715, bias=C_('one')), reads=[('X2', n), 'cst'],
                         writes=[('X2', n)])
                    S.op('act', lambda e, n=n, xa=xa: e.activation(out=INc[:, n:n + 1], in_=X2c[:, n:n + 1],
                                                                  func=AF.Identity, scale=xa),
                         reads=[('X2', n), ('ACT', n)], writes=[('IN', n)])
                    S.op('act', lambda e, n=n: e.activation(out=Sc[:, n:n + 1], in_=INc[:, n:n + 1], func=AF.Sigmoid,
                                                            scale=1.5957691216057308), reads=[('IN', n)],
                         writes=[('S', n)])
                    S.op('act', lambda e, n=n, xa=xa: e.activation(out=Sc[:, n:n + 1], in_=Sc[:, n:n + 1],
                                                                  func=AF.Identity, scale=xa),
                         reads=[('S', n), ('ACT', n)], writes=[('S', n)])
                    S.op('act', lambda e, n=n: e.activation(out=WMd[:, n, n:n + 1], in_=Sc[:, n:n + 1],
                                                            func=AF.Identity, scale=gT[:, n:n + 1]),
                         reads=[('S', n), gk, 'WMd'], writes=[('WM', n)])
                    for half in range(2):
                        S.op('pe', lambda e, n=n, half=half, uv_=uv_: e.matmul(
                            bank(4 + half, 64), lhsT=WMd[:, n, :],
                            rhs=uv_[:, 1024 + half * 512:1024 + (half + 1) * 512],
                            start=(n == 0), stop=(n == 63)), reads=[('WM', n), uvkey], writes=bk(4 + half))
                    segs.append(S.end())
                S.begin()
                S.op('dve', lambda e: e.tensor_tensor(out=xtV, in0=xtV, in1=PB[0:64, 2048:3072], op=ALU.add),
                     reads=[xk] + bk(4, 5), writes=[xk])
                tail_a = S.end()
                S.begin()
                pt6 = PB[:, 6 * 512 + 256:7 * 512].bitcast(BF16)
                rms_bf16(xtV, xk, gpleb, 'gpleb', hb2, 'hb2', ss2, 'ss2', rs2, 'rs2')
                transpose_tm(hb2, 'hb2', 8, hT2, 'hT2', ptgt=pt6, pkey='b6u')
                S.op('dve', lambda e: e.tensor_copy(out=ptb, in_=ptl), reads=[pk], writes=['ptb'])
                transpose_tm(ptb, 'ptb', 2, pTt, 'pTt', ptgt=pt6, pkey='b6u')
                for half in range(2):
                    hs = slice(half * 512, (half + 1) * 512)
                    proj_tm(bank(0, 64), bk(0), hT2, 'hT2', wpgb, 'wpgb', half * 512, 512)
                    proj_tm(bank(1, 64), bk(1), pTt, 'pTt', wpleb, 'wpleb', half * 512, 512, nk=2)
                    S.op('act', lambda e, hs=hs: e.activation(out=gsig[:, hs], in_=bank(0, 64), func=AF.Sigmoid),
                         reads=bk(0), writes=['gsig'])
                    S.op('dve', lambda e, hs=hs: e.tensor_tensor(out=gsig[:, hs], in0=gsig[:, hs], in1=bank(1, 64),
                                                                  op=ALU.mult), reads=['gsig'] + bk(1), writes=['gsig'])
                    S.op('dve', lambda e, hs=hs: e.tensor_tensor(out=xtV[:, hs], in0=xtV[:, hs], in1=gsig[:, hs],
                                                                  op=ALU.add), reads=[xk, 'gsig'], writes=[xk])
                S.op('act', lambda e: e.activation(out=yt, in_=xtV, func=AF.Square, accum_out=ss2), reads=[xk],
                     writes=['yt', 'ss2'])
                S.op('act', lambda e: e.activation(out=rs2, in_=ss2, func=AF.Sqrt, scale=1.0 / 1024,
                                                   bias=epsc[0:64, :]), reads=['ss2', 'cst'], writes=['rs2'])
                S.op('dve', lambda e: e.reciprocal(out=rs2, in_=rs2), reads=['rs2'], writes=['rs2'])
                S.op('dve', lambda e: e.scalar_tensor_tensor(out=yt, in0=xtV, scalar=rs2[:, 0:1], in1=gfinb,
                                                             op0=ALU.mult, op1=ALU.mult),
                     reads=[xk, 'rs2', 'gfinb'], writes=['yt'])
                S.dma('sp', lambda e: e.dma_start(out=y_d[r0:r0 + 64, :], in_=yt), reads=['yt'], final=True)
                return head, segs, tail_a, S.end()

            if 'B' in PHASES:
                TL = list(range(NPT + 1)) if TILES is None else list(TILES) + [NPT]
                nT = len(TL)
                front_pe(TL[0], 0)
                front_dve(TL[0], 0)
                if nT > 1:
                    front_pe(TL[1], 1)
                LP = []
                for j in range(nT):
                    head, segs, tail_a, tail_b = uvstage(TL[j], j)
                    LFd, LFp = [], []
                    if j + 1 < nT:
                        S.begin()
                        front_dve(TL[j + 1], j + 1)
                        LFd = S.end()
                    if j + 2 < nT:
                        S.begin()
                        front_pe(TL[j + 2], j + 2)
                        LFp = S.end()
                    streams = [LFd, LFp, LP]
                    pers = [(len(L) + 63) // 64 for L in streams]
                    S.run(head)
                    for n in range(64):
                        S.run(segs[n])
                        for L, per in zip(streams, pers):
                            S.run(L[n * per:(n + 1) * per])
                    for L, per in zip(streams, pers):
                        S.run(L[64 * per:])
                    S.run(tail_a)
                    LP = tail_b
                S.run(LP)
        phase_B()
        print('ops', {e: len(v) for e, v in S.prog.items()}, 'nsem', S.nsem, 'arena', AL.off)
        S.emit()
    return nc


_CACHE = {}


def kernel(x_prompt, x_sample, state_hgrn, state_delta, state_conv, p_prompt, p_sample,
           lb_param, g_mix, w_in, conv_w, a_log, dt_bias, g_norm_a, g_norm_b, w_br_a, w_br_b,
           w_out, g_ffn, peer_wq, peer_keys, expert_u, expert_v, g_ple, w_ple, w_ple_gate,
           g_final):
    f = lambda a: np.ascontiguousarray(np.asarray(a, dtype=np.float32))
    if 'nc' not in _CACHE:
        _CACHE['nc'] = build_program()
        _CACHE['consts'] = _build_consts()
    nc = _CACHE['nc']
    cst, cmf, zsel, dlt = _CACHE['consts']
    x_prompt, x_sample = f(x_prompt), f(x_sample)
    p_prompt, p_sample = f(p_prompt), f(p_sample)
    state_hgrn, state_delta, state_conv = f(state_hgrn), f(state_delta), f(state_conv)
    keysT = np.ascontiguousarray(np.transpose(f(peer_keys)[0], (0, 1, 3, 2)).reshape(16, 128, 128))
    shared = dict(
        lbp=f(lb_param), gmix=f(g_mix), w_in=f(w_in)[0], convw=f(conv_w)[0], alog=f(a_log), dtb=f(dt_bias),
        gna=f(g_norm_a), gnb=f(g_norm_b), wbra=f(w_br_a)[0], wbrb=f(w_br_b)[0], wout=f(w_out)[0], gffn=f(g_ffn),
        wq=f(peer_wq)[0], keysT=keysT, eu=f(expert_u)[0], ev=f(expert_v)[0], gple=f(g_ple), wple=f(w_ple)[0],
        wpg=f(w_ple_gate)[0], gfin=f(g_final).reshape(1, 1024), cst=cst, cmf=cmf, zsel=zsel, dlt=dlt)
    in_maps = []
    for b in range(8):
        m = dict(shared)
        m['x'] = np.ascontiguousarray(np.concatenate([x_prompt[b], x_sample[16 * b:16 * b + 16].reshape(64, 1024)], 0))
        m['p'] = np.ascontiguousarray(np.concatenate([p_prompt[0, b], p_sample[0, 16 * b:16 * b + 16].reshape(64, 256)], 0))
        m['sh'] = np.ascontiguousarray(state_hgrn[0, 16 * b:16 * b + 16])
        m['sd'] = np.ascontiguousarray(state_delta[0, 16 * b:16 * b + 16])
        m['scv'] = np.ascontiguousarray(state_conv[0, 16 * b:16 * b + 16].reshape(48, 1536))
        in_maps.append(m)
    res = run_bass_kernel_spmd(nc, in_maps, core_ids=list(range(8)))
    R = res.results
    y_prompt = np.stack([R[b]['y'][0:2048] for b in range(8)], 0)
    y_sample = np.concatenate([R[b]['y'][2048:2112].reshape(16, 4, 1024) for b in range(8)], 0)
    hp = np.stack([R[b]['hp'] for b in range(8)], 0)[None]
    dp = np.stack([R[b]['dp'] for b in range(8)], 0)[None]
    cp = np.stack([R[b]['cp'] for b in range(8)], 0)[None]
    hs = np.concatenate([R[b]['hs'] for b in range(8)], 0)[None]
    ds = np.concatenate([R[b]['ds'] for b in range(8)], 0)[None]
    cs = np.concatenate([R[b]['cs'].reshape(16, 3, 1536) for b in range(8)], 0)[None]
    _CACHE['dbg'] = R
    return tuple(np.ascontiguousarray(a.astype(np.float32)) for a in (y_prompt, y_sample, hp, dp, cp, hs, ds, cs))
```

```python
import numpy as np
from contextlib import ExitStack
import concourse.bass as bass
import concourse.mybir as mybir
from concourse.bass_utils import run_bass_kernel_spmd

F32 = mybir.dt.float32
BF16 = mybir.dt.bfloat16
I32 = mybir.dt.int32
U32 = mybir.dt.uint32
AF = mybir.ActivationFunctionType
ALU = mybir.AluOpType
AX = mybir.AxisListType

EPS = 1e-6
NPT = 32
NTOK = 2112
EPOCH = 12000
DMA_POOL = 8
DMA_EPOCH = 700
ARENA_COLS = 105984
NBUF = 6
NHB = 6
NUV = 8
NEG = -30000.0
DBG = set()
DBGT = 0
PHASES = ('A1', 'A2', 'B')
TILES = None


class Sched:
    def __init__(self, nc, es):
        self.nc = nc
        self.es = es
        self.eng = {'pe': nc.tensor, 'act': nc.scalar, 'dve': nc.vector,
                    'pool': nc.gpsimd, 'sp': nc.sync}
        self.prog = {e: [] for e in self.eng}
        self.cnt = {e: 0 for e in self.eng}
        self.sem = {}
        self.nsem = 0
        for e in self.eng:
            self.sem[e] = self._newsem(e)
        self.waited = {e: {} for e in self.eng}
        self.dpool = {}
        self.res_w = {}
        self.res_r = {}
        self.final_tokens = []
        self.pending = {e: [] for e in self.eng}
        self.cap = None

    def begin(self):
        self.cap = []

    def end(self):
        L = self.cap
        self.cap = None
        return L

    def run(self, L):
        for it in L:
            if it[0] == 'op':
                self.op(it[1], it[2], it[3], it[4])
            else:
                self.dma(it[1], it[2], it[3], it[4], it[5])

    def _newsem(self, name):
        self.nsem += 1
        return self.es.enter_context(self.nc.semaphore(f"s{self.nsem}_{name}"))

    def _need(self, e, tok, waits):
        if tok is None:
            return
        sem, val = tok[0], tok[1]
        if e == 'pe' and tok[2] == 'pe':
            return
        w = self.waited[e]
        if w.get(id(sem), 0) >= val:
            return
        w[id(sem)] = val
        waits.append((sem, val))

    def _deps(self, e, reads, writes, waits):
        for t in self.pending[e]:
            self._need(e, t, waits)
        self.pending[e] = []
        for k in reads:
            self._need(e, self.res_w.get(k), waits)
        for k in writes:
            self._need(e, self.res_w.get(k), waits)
            for t in self.res_r.get(k, ()):
                self._need(e, t, waits)

    def _commit(self, tok, reads, writes):
        for k in reads:
            self.res_r.setdefault(k, []).append(tok)
        for k in writes:
            self.res_w[k] = tok
            self.res_r[k] = []

    def op(self, e, fn, reads=(), writes=()):
        if self.cap is not None:
            self.cap.append(('op', e, fn, tuple(reads), tuple(writes)))
            return None
        waits = []
        self._deps(e, reads, writes, waits)
        if self.cnt[e] >= EPOCH:
            self.sem[e] = self._newsem(e)
            self.cnt[e] = 0
        self.cnt[e] += 1
        tok = (self.sem[e], self.cnt[e], e)
        self.prog[e].append((waits, fn, self.sem[e], 1))
        self._commit(tok, reads, writes)
        return tok

    def dma(self, e, fn, reads=(), writes=(), final=False):
        if self.cap is not None:
            self.cap.append(('dma', e, fn, tuple(reads), tuple(writes), final))
            return None
        waits = []
        self._deps(e, reads, writes, waits)
        pool = self.dpool.setdefault(e, {'sems': [], 'uses': [], 'i': 0})
        i = pool['i'] % DMA_POOL
        pool['i'] += 1
        if len(pool['sems']) <= i:
            pool['sems'].append(self._newsem(e + 'd'))
            pool['uses'].append(0)
        if pool['uses'][i] >= DMA_EPOCH:
            pool['sems'][i] = self._newsem(e + 'd')
            pool['uses'][i] = 0
        sem = pool['sems'][i]
        if pool['uses'][i] > 0:
            self._need(e, (sem, 16 * pool['uses'][i], 'dma'), waits)
        pool['uses'][i] += 1
        tok = (sem, 16 * pool['uses'][i], 'dma')
        self.prog[e].append((waits, fn, sem, 16))
        self._commit(tok, reads, writes)
        if final:
            self.final_tokens.append(tok)
        return tok

    def barrier(self):
        toks = []
        for e in self.eng:
            if self.cnt[e] > 0:
                toks.append((self.sem[e], self.cnt[e], e + '_bar'))
        for e, pool in self.dpool.items():
            for sem, u in zip(pool['sems'], pool['uses']):
                if u > 0:
                    toks.append((sem, 16 * u, 'dma'))
        for e in self.eng:
            self.pending[e] = list(toks)

    def emit(self):
        nc = self.nc
        fw = []
        for t in self.final_tokens:
            self._need('sp', t, fw)
        with nc.Block() as block:
            def run(e, engine):
                for waits, fn, sem, inc in self.prog[e]:
                    for (s, v) in waits:
                        engine.wait_ge(s, v)
                    fn(engine).then_inc(sem, inc)

            @block.tensor
            def _(eng):
                run('pe', eng)

            @block.scalar
            def _(eng):
                run('act', eng)

            @block.vector
            def _(eng):
                run('dve', eng)

            @block.gpsimd
            def _(eng):
                run('pool', eng)

            @block.sync
            def _(eng):
                run('sp', eng)
                for (s, v) in fw:
                    eng.wait_ge(s, v)


class Alloc:
    def __init__(self, arena, ncols):
        self.a = arena
        self.n = ncols
        self.off = 0

    def get(self, parts, free, dt):
        if isinstance(free, int):
            free = [free]
        nel = int(np.prod(free))
        cols = nel * (1 if dt == BF16 else 2)
        cols = (cols + 1) // 2 * 2
        o = self.off
        self.off += cols
        assert self.off <= self.n, f"arena overflow {self.off} > {self.n}"
        ap = self.a[0:parts, o:o + cols]
        if dt != BF16:
            ap = ap.bitcast(dt)
        if len(free) > 1:
            ds = [f"d{i}" for i in range(len(free))]
            kw = {ds[i]: free[i] for i in range(1, len(free))}
            ap = ap.rearrange(f"p ({' '.join(ds)}) -> p {' '.join(ds)}", **kw)
        return ap


def _tile_consts(nch, C):
    t = np.arange(64)
    ch = t // C
    same = ch[:, None] == ch[None, :]
    tri = (same & (t[:, None] <= t[None, :])).astype(np.float32)
    blk = same.astype(np.float32)
    nmT = np.where(tri > 0, 0.0, NEG).astype(np.float32)
    strict = same & (t[None, :] < t[:, None])
    pmS = np.where(strict, 0.0, -NEG).astype(np.float32)
    cind = np.zeros((64, 16), np.float32)
    cind[t, ch] = 1.0
    return tri, blk, np.tile(nmT, (1, 4)), np.tile(pmS, (1, 4)), cind


CST_COLS = {}


def _build_consts():
    cols = []
    off = [0]

    def add(name, arr):
        a = np.zeros((128, arr.shape[1]), np.float32)
        a[:arr.shape[0]] = arr
        CST_COLS[name] = (off[0], off[0] + arr.shape[1], arr.shape[0])
        off[0] += arr.shape[1]
        cols.append(a)

    add('ident', np.eye(128, dtype=np.float32))
    add('ones', np.ones((128, 128), np.float32))
    for nm, (nch, C) in (('p', (1, 64)), ('s', (16, 4))):
        tri, blk, nmT, pmS, cind = _tile_consts(nch, C)
        add('tri_' + nm, tri)
        add('blk_' + nm, blk)
        add('nmT_' + nm, nmT)
        add('pmS_' + nm, pmS)
        add('cind_' + nm, cind)
    add('iota16', np.tile(np.arange(16, dtype=np.float32)[None, :], (64, 1)))
    add('eps', np.full((128, 1), EPS, np.float32))
    add('one', np.ones((128, 1), np.float32))
    cst = np.concatenate(cols, axis=1)
    t = np.arange(64)
    cm = (t[None, :] // 4 == np.arange(16)[:, None]).astype(np.float32)
    cmf = np.tile(cm.reshape(1, 16 * 64), (128, 1))
    z = np.zeros((64, 64, 128), np.float32)
    z[t, t, :] = 1.0
    z = z.reshape(64, 64 * 128)
    dl = np.tile(np.eye(64, dtype=np.float32).reshape(1, 64 * 64), (128, 1))
    return cst, cmf, z, dl


def build_program():
    cst_np, _, _, _ = _build_consts()
    NCST = cst_np.shape[1]
    nc = bass.Bass("TRN2", target_bir_lowering=False)

    def din(name, shape, dt=F32):
        return nc.dram_tensor(name, shape, dt, kind="ExternalInput").ap()

    def dout(name, shape, dt=F32):
        return nc.dram_tensor(name, shape, dt, kind="ExternalOutput").ap()

    x_d = din("x", [NTOK, 1024])
    p_d = din("p", [NTOK, 256])
    sh_d = din("sh", [16, 4, 128, 128])
    sd_d = din("sd", [16, 4, 128, 128])
    scv_d = din("scv", [48, 1536])
    lbp_d = din("lbp", [2, 512])
    gmix_d = din("gmix", [1, 1024])
    win_d = din("w_in", [1024, 6152])
    convw_d = din("convw", [4, 1536])
    alog_d = din("alog", [1, 4])
    dtb_d = din("dtb", [1, 4])
    gna_d = din("gna", [1, 128])
    gnb_d = din("gnb", [1, 128])
    wbra_d = din("wbra", [512, 1024])
    wbrb_d = din("wbrb", [512, 1024])
    wout_d = din("wout", [1024, 1024])
    gffn_d = din("gffn", [1, 1024])
    wq_d = din("wq", [1024, 2048])
    keysT_d = din("keysT", [16, 128, 128])
    eu_d = din("eu", [16384, 1024])
    ev_d = din("ev", [16384, 1024])
    gple_d = din("gple", [1, 1024])
    wple_d = din("wple", [256, 1024])
    wpg_d = din("wpg", [1024, 1024])
    gfin_d = din("gfin", [1, 1024])
    cst_d = din("cst", [128, NCST])
    cmf_d = din("cmf", [128, 1024])
    zsel_d = din("zsel", [64, 8192])
    dlt_d = din("dlt", [128, 4096])

    y_d = dout("y", [NTOK, 1024])
    hp_d = dout("hp", [4, 128, 128])
    dp_d = dout("dp", [4, 128, 128])
    cp_d = dout("cp", [3, 1536])
    hs_d = dout("hs", [16, 4, 128, 128])
    ds_d = dout("ds", [16, 4, 128, 128])
    cs_d = dout("cs", [48, 1536])
    m1_d = dout("m1s", [NTOK, 1024])
    x1_d = dout("x1s", [NTOK, 1024])
    uvb_d = nc.dram_tensor("uvb", [16384, 2048], BF16, kind="Internal").ap()
    h2_d = nc.dram_tensor("h2s", [NTOK, 1024], BF16, kind="Internal").ap()

    es = ExitStack()
    with es:
        S = Sched(nc, es)

        def dbg(name, ap, key, ti=0, want=0):
            if name not in DBG or ti != want:
                return
            shp = list(ap.shape)
            dd = nc.dram_tensor("dbg_" + name, shp, ap.dtype, kind="ExternalOutput").ap()
            S.dma('sp', lambda e: e.dma_start(out=dd, in_=ap), reads=[key] if not isinstance(key, list) else key,
                  final=True)
        ARENA = es.enter_context(nc.sbuf_tensor("arena", [128, ARENA_COLS], BF16))
        PB = es.enter_context(nc.psum_tensor("pb", [128, 7 * 512], F32))
        PTt = es.enter_context(nc.psum_tensor("pt", [128, 1024], BF16))
        AL = Alloc(ARENA, ARENA_COLS)

        def bank(j, parts=128, n=512, off=0):
            return PB[0:parts, j * 512 + off:j * 512 + off + n]

        def bk(*js):
            return ['b%d' % j for j in js]

        CST = AL.get(128, NCST, F32)
        S.dma('sp', lambda e: e.dma_start(out=CST, in_=cst_d), writes=['cst'])

        def C_(name, parts=None):
            a, b, r = CST_COLS[name]
            return CST[0:(parts or r), a:b]

        identf = C_('ident')
        onesf = C_('ones')
        epsc = C_('eps')
        identb = AL.get(128, 128, BF16)
        S.op('dve', lambda e: e.tensor_copy(out=identb, in_=identf), reads=['cst'], writes=['identb'])
        TP = dict(nch=1, C=64, tri=C_('tri_p'), blk=C_('blk_p'), nmT=C_('nmT_p'), pmS=C_('pmS_p'),
                  cind=C_('cind_p'))
        TS = dict(nch=16, C=4, tri=C_('tri_s'), blk=C_('blk_s'), nmT=C_('nmT_s'), pmS=C_('pmS_s'),
                  cind=C_('cind_s'))
        cmb = AL.get(128, [16, 64], BF16)
        S.dma('pool', lambda e: e.dma_start(out=cmb, in_=cmf_d.rearrange("p (c t) -> p c t", c=16)),
              writes=['cmb'])
        wdummy = AL.get(128, 2, F32)
        pers_mark = AL.off

        def load_w_bf16(dst3, src, r0, nk, c0, ncols, key):
            for k in range(nk):
                for cc in range(0, ncols, 2048):
                    w = min(2048, ncols - cc)
                    S.dma('pool', lambda e, k=k, cc=cc, w=w: e.dma_start(
                        out=dst3[:, k, cc:cc + w],
                        in_=src[r0 + k * 128:r0 + (k + 1) * 128, c0 + cc:c0 + cc + w]), writes=[key])

        cast_rr = [0]

        def load_w_fast(dst3, src, r0, nk, c0, ncols, key, stage):
            for k in range(nk):
                for cc in range(0, ncols, 2048):
                    w = min(2048, ncols - cc)
                    i = cast_rr[0] % len(stage)
                    eng = ('act', 'dve', 'pool')[cast_rr[0] % 3]
                    cast_rr[0] += 1
                    st = stage[i]
                    skey = 'wstage%d' % i
                    S.dma('sp', lambda e, k=k, cc=cc, w=w, st=st: e.dma_start(
                        out=st[:, 0:w], in_=src[r0 + k * 128:r0 + (k + 1) * 128, c0 + cc:c0 + cc + w]),
                        writes=[skey])
                    if eng == 'act':
                        S.op('act', lambda e, k=k, cc=cc, w=w, st=st: e.copy(out=dst3[:, k, cc:cc + w], in_=st[:, 0:w]),
                             reads=[skey], writes=[(key, k, cc)])
                    else:
                        S.op(eng, lambda e, k=k, cc=cc, w=w, st=st: e.tensor_copy(out=dst3[:, k, cc:cc + w],
                                                                                  in_=st[:, 0:w]),
                             reads=[skey], writes=[(key, k, cc)])
            S.op('pool', lambda e: e.memset(wdummy, 0.0),
                 reads=[(key, k, cc) for k in range(nk) for cc in range(0, ncols, 2048)], writes=[key])

        def bcast_load(dst, src_row, parts, n, key):
            S.dma('sp', lambda e: e.dma_start(out=dst, in_=src_row.to_broadcast([parts, n])), writes=[key])

        def rmsnorm_to_bf16(xt, xkey, gb, gkey, hb, hkey, ss, rs):
            S.op('act', lambda e: e.activation(out=hb, in_=xt, func=AF.Square, accum_out=ss),
                 reads=[xkey], writes=[hkey, 'ss'])
            S.op('act', lambda e: e.activation(out=rs, in_=ss, func=AF.Sqrt, scale=1.0 / 1024, bias=epsc[0:64, :]),
                 reads=['ss', 'cst'], writes=['rs'])
            S.op('dve', lambda e: e.reciprocal(out=rs, in_=rs), reads=['rs'], writes=['rs'])
            S.op('dve', lambda e: e.scalar_tensor_tensor(out=hb, in0=xt, scalar=rs[:, 0:1], in1=gb,
                                                         op0=ALU.mult, op1=ALU.mult),
                 reads=[xkey, 'rs', gkey], writes=[hkey])

        def transpose_tm(src, skey, nblk, dst, dkey, eng='act', ptgt=None, pkey='pt'):
            if ptgt is None:
                ptgt = PTt
            for k in range(nblk):
                S.op('pe', lambda e, k=k: e.transpose(out=ptgt[:, k * 64:(k + 1) * 64],
                                                      in_=src[:, k * 128:(k + 1) * 128],
                                                      identity=identb[0:64, 0:64]),
                     reads=[skey, 'identb'], writes=[pkey])
            pv = ptgt[:, 0:nblk * 64].rearrange("p (k t) -> p k t", k=nblk)
            if eng == 'act':
                S.op('act', lambda e: e.copy(out=dst, in_=pv), reads=[pkey], writes=[dkey])
            else:
                S.op('dve', lambda e: e.tensor_copy(out=dst, in_=pv), reads=[pkey], writes=[dkey])

        def proj_tm(pout, pkeys, hT, hTkey, w3, wkey, c0, ncols, nk=8):
            for k in range(nk):
                S.op('pe', lambda e, k=k: e.matmul(pout, lhsT=hT[:, k, :], rhs=w3[:, k, c0:c0 + ncols],
                                                   start=(k == 0), stop=(k == nk - 1)),
                     reads=[hTkey, wkey], writes=pkeys)

        def head_norm_gate_keys(o_sb, okey, gnb_, gkey, gate_sb, gatekey, sq, sqkey, on, onkey, ogt_, ogtkey, ss4, rs4):
            o3 = o_sb.rearrange("p (h v) -> p h v", h=4)
            S.op('dve', lambda e: e.tensor_tensor(out=sq, in0=o_sb, in1=o_sb, op=ALU.mult),
                 reads=[okey], writes=[sqkey])
            S.op('dve', lambda e: e.reduce_sum(out=ss4, in_=sq.rearrange("p (h v) -> p h v", h=4), axis=AX.X),
                 reads=[sqkey], writes=['ss4'])
            S.op('act', lambda e: e.activation(out=rs4, in_=ss4, func=AF.Sqrt, scale=1.0 / 128, bias=epsc[0:64, :]),
                 reads=['ss4', 'cst'], writes=['rs4'])
            S.op('dve', lambda e: e.reciprocal(out=rs4, in_=rs4), reads=['rs4'], writes=['rs4'])
            on3 = on.rearrange("p (h v) -> p h v", h=4)
            S.op('dve', lambda e: e.tensor_tensor(out=on3, in0=o3, in1=rs4.unsqueeze(2).to_broadcast([64, 4, 128]),
                                                  op=ALU.mult), reads=[okey, 'rs4'], writes=[onkey])
            S.op('dve', lambda e: e.tensor_tensor(out=on3, in0=on3, in1=gnb_.unsqueeze(1).to_broadcast([64, 4, 128]),
                                                  op=ALU.mult), reads=[onkey, gkey], writes=[onkey])
            S.op('dve', lambda e: e.tensor_tensor(out=ogt_, in0=on, in1=gate_sb, op=ALU.mult),
                 reads=[onkey, gatekey], writes=[ogtkey])

        def phase_A1():
            AL.off = pers_mark
            winA = AL.get(128, [8, 2048], BF16)
            wgA = AL.get(128, [8, 1024], BF16)
            wbrA = AL.get(128, [4, 1024], BF16)
            stageA = [AL.get(128, 2048, F32) for _ in range(4)]
            load_w_fast(winA, win_d, 0, 8, 0, 2048, 'winA', stageA)
            load_w_fast(wgA, win_d, 0, 8, 4104, 1024, 'wgA', stageA)
            load_w_fast(wbrA, wbra_d, 0, 4, 0, 1024, 'wbrA', stageA)
            S.begin()
            for (src_t, c0) in ((eu_d, 0), (ev_d, 1024)):
                for r in range(0, 16384, 512):
                    S.dma('pool', lambda e, r=r, src_t=src_t, c0=c0: e.dma_start(
                        out=uvb_d[r:r + 512, c0:c0 + 1024], in_=src_t[r:r + 512, :]), writes=['uvb'])
            conv_dmas = S.end()
            gmixb = AL.get(64, 1024, F32)
            bcast_load(gmixb, gmix_d[0:1, :], 64, 1024, 'gmixb')
            lbb = AL.get(64, 512, F32)
            omlb = AL.get(64, 512, F32)
            lb1 = AL.get(64, 512, F32)
            bcast_load(lbb, lbp_d[0:1, :], 64, 512, 'lbb')
            bcast_load(lb1, lbp_d[1:2, :], 64, 512, 'lb1')
            S.op('dve', lambda e: e.tensor_tensor(out=lbb, in0=lbb, in1=lb1, op=ALU.subtract),
                 reads=['lbb', 'lb1'], writes=['lbb'])
            S.op('act', lambda e: e.activation(out=lbb, in_=lbb, func=AF.Sigmoid), reads=['lbb'], writes=['lbb'])
            S.op('dve', lambda e: e.tensor_scalar(out=omlb, in0=lbb, scalar1=-1.0, scalar2=1.0, op0=ALU.mult,
                                                  op1=ALU.add), reads=['lbb'], writes=['omlb'])
            gnab = AL.get(64, 128, F32)
            bcast_load(gnab, gna_d[0:1, :], 64, 128, 'gnab')
            xt = AL.get(64, 1024, F32)
            hb = AL.get(64, 1024, BF16)
            hT = AL.get(128, [8, 64], BF16)
            ss = AL.get(64, 1, F32)
            rs = AL.get(64, 1, F32)
            F = [AL.get(64, 512, F32) for _ in range(7)]
            qt = AL.get(64, 512, BF16)
            kt = AL.get(64, 512, BF16)
            va = AL.get(64, 512, BF16)
            km = AL.get(64, 512, BF16)
            qkT = AL.get(128, [8, 64], BF16)
            qm = AL.get(128, [4, 64], BF16)
            attm = AL.get(64, [4, 64], BF16)
            ebL = AL.get(128, [4, 16], F32)
            Sa = AL.get(128, [4, 128], F32)
            Sab = AL.get(128, [4, 128], BF16)
            ss4 = AL.get(64, 4, F32)
            rs4 = AL.get(64, 4, F32)
            ogt = AL.get(64, 512, BF16)
            oT = AL.get(128, [4, 64], BF16)
            m1t = AL.get(64, 1024, F32)
            S.op('pool', lambda e: e.memset(Sa, 0.0), writes=['Sa'])
            S.op('pool', lambda e: e.memset(Sab, 0.0), writes=['Sab'])

            def tileA1(ti, T):
                nch = T['nch']
                r0 = ti * 64
                S.dma('sp', lambda e: e.dma_start(out=xt, in_=x_d[r0:r0 + 64, :]), writes=['xt'])
                rmsnorm_to_bf16(xt, 'xt', gmixb, 'gmixb', hb, 'hb', ss, rs)
                dbg('xt', xt, 'xt', ti)
                dbg('rs', rs, 'rs', ti)
                dbg('hb', hb, 'hb', ti)
                transpose_tm(hb, 'hb', 8, hT, 'hT')
                dbg('hT', hT, 'hT', ti)
                dbg('winA', winA[:, :, 0:512], 'winA', ti)
                sig, q, kk, logf, eb, enb, og = F
                proj_tm(bank(0, 64), bk(0), hT, 'hT', winA, 'winA', 512, 512)
                S.op('act', lambda e: e.activation(out=sig, in_=bank(0, 64), func=AF.Sigmoid), reads=bk(0), writes=['F0'])
                proj_tm(bank(1, 64), bk(1), hT, 'hT', winA, 'winA', 0, 512)
                S.op('act', lambda e: e.activation(out=q, in_=bank(1, 64), func=AF.Silu), reads=bk(1), writes=['F1'])
                S.op('dve', lambda e: e.tensor_tensor(out=sig, in0=sig, in1=omlb, op=ALU.mult),
                     reads=['F0', 'omlb'], writes=['F0'])
                S.op('dve', lambda e: e.tensor_tensor(out=sig, in0=sig, in1=lbb, op=ALU.add),
                     reads=['F0', 'lbb'], writes=['F0'])
                S.op('dve', lambda e: e.tensor_scalar(out=kk, in0=sig, scalar1=-1.0, scalar2=1.0, op0=ALU.mult,
                                                      op1=ALU.add), reads=['F0'], writes=['F2'])
                S.op('act', lambda e: e.activation(out=logf, in_=sig, func=AF.Ln), reads=['F0'], writes=['F3'])
                dbg('q', q, 'F1', ti)
                dbg('f', sig, 'F0', ti)
                dbg('logf', logf, 'F3', ti)
                S.op('pe', lambda e: e.matmul(bank(2, 64), lhsT=T['tri'], rhs=logf, start=True, stop=True),
                     reads=['cst', 'F3'], writes=bk(2))
                S.op('act', lambda e: e.activation(out=eb, in_=bank(2, 64), func=AF.Exp), reads=bk(2), writes=['F4'])
                S.op('act', lambda e: e.activation(out=enb, in_=bank(2, 64), func=AF.Exp, scale=-1.0),
                     reads=bk(2), writes=['F5'])
                S.op('dve', lambda e: e.tensor_tensor(out=qt, in0=q, in1=eb, op=ALU.mult),
                     reads=['F1', 'F4'], writes=['qt'])
                S.op('dve', lambda e: e.tensor_tensor(out=kt, in0=kk, in1=enb, op=ALU.mult),
                     reads=['F2', 'F5'], writes=['kt'])
                for h in range(4):
                    S.op('pe', lambda e, h=h: e.matmul(bank(0, 128, nch, h * 16), lhsT=logf[:, h * 128:(h + 1) * 128],
                                                       rhs=T['cind'][:, 0:nch], start=True, stop=True),
                         reads=['F3', 'cst'], writes=bk(0))
                S.op('act', lambda e: e.activation(out=ebL[:, :, 0:nch],
                                                   in_=bank(0, 128, 64).rearrange("p (h c) -> p h c", h=4)[:, :, 0:nch],
                                                   func=AF.Exp), reads=bk(0), writes=['ebL'])
                proj_tm(bank(1, 64), bk(1), hT, 'hT', winA, 'winA', 1024, 512)
                S.op('act', lambda e: e.copy(out=va, in_=bank(1, 64)), reads=bk(1), writes=['va'])
                proj_tm(bank(2, 64), bk(2), hT, 'hT', winA, 'winA', 1536, 512)
                S.op('act', lambda e: e.activation(out=og, in_=bank(2, 64), func=AF.Silu), reads=bk(2), writes=['F6'])
                for h in range(4):
                    S.op('pe', lambda e, h=h: e.transpose(out=PTt[:, h * 64:(h + 1) * 64], in_=qt[:, h * 128:(h + 1) * 128],
                                                          identity=identb[0:64, 0:64]),
                         reads=['qt', 'identb'], writes=['pt'])
                for h in range(4):
                    S.op('pe', lambda e, h=h: e.transpose(out=PTt[:, (4 + h) * 64:(5 + h) * 64],
                                                          in_=kt[:, h * 128:(h + 1) * 128], identity=identb[0:64, 0:64]),
                         reads=['kt', 'identb'], writes=['pt'])
                S.op('act', lambda e: e.copy(out=qkT, in_=PTt[:, 0:512].rearrange("p (k t) -> p k t", k=8)),
                     reads=['pt'], writes=['qkT'])
                for h in range(4):
                    S.op('pe', lambda e, h=h: e.matmul(bank(0, 64, 64, h * 64), lhsT=qkT[:, 4 + h, :], rhs=qkT[:, h, :],
                                                       start=True, stop=True), reads=['qkT'], writes=bk(0))
                S.op('dve', lambda e: e.tensor_tensor(out=attm, in0=bank(0, 64, 256).rearrange("p (h t) -> p h t", h=4),
                                                      in1=T['tri'].unsqueeze(1).to_broadcast([64, 4, 64]), op=ALU.mult),
                     reads=bk(0) + ['cst'], writes=['attm'])
                for h in range(4):
                    S.op('pe', lambda e, h=h: e.matmul(bank(5, 64, 128, h * 128), lhsT=attm[:, h, :],
                                                       rhs=va[:, h * 128:(h + 1) * 128], start=(h == 0), stop=False,
                                                       skip_group_check=True),
                         reads=['attm', 'va'], writes=bk(5))
                for c in range(nch):
                    if nch > 1:
                        S.dma('sp', lambda e, c=c: e.dma_start(out=Sa, in_=sh_d[c].rearrange("h d v -> d h v")),
                              writes=['Sa'])
                        S.op('act', lambda e: e.copy(out=Sab, in_=Sa), reads=['Sa'], writes=['Sab'])
                        S.op('dve', lambda e, c=c: e.tensor_tensor(
                            out=qm, in0=qkT[:, 0:4, :], in1=cmb[:, c, :].unsqueeze(1).to_broadcast([128, 4, 64]),
                            op=ALU.mult), reads=['qkT', 'cmb'], writes=['qm'])
                        S.op('dve', lambda e, c=c: e.tensor_scalar(out=km, in0=kt, scalar1=T['cind'][:, c:c + 1],
                                                                   scalar2=None, op0=ALU.mult),
                             reads=['kt', 'cst'], writes=['km'])
                        qsrc, qkey, ksrc, kkey = qm, 'qm', km, 'km'
                    else:
                        qsrc, qkey, ksrc, kkey = qkT, 'qkT', kt, 'kt'
                    for h in range(4):
                        S.op('pe', lambda e, h=h, qsrc=qsrc, c=c: e.matmul(
                            bank(5, 64, 128, h * 128), lhsT=qsrc[:, h, :], rhs=Sab[:, h, :],
                            start=False, stop=(c == nch - 1), skip_group_check=True), reads=[qkey, 'Sab'], writes=bk(5))
                    for h in range(4):
                        S.op('pe', lambda e, h=h, ksrc=ksrc: e.matmul(
                            bank(6, 128, 128, h * 128), lhsT=ksrc[:, h * 128:(h + 1) * 128],
                            rhs=va[:, h * 128:(h + 1) * 128], start=True, stop=True),
                            reads=[kkey, 'va'], writes=bk(6))
                    S.op('dve', lambda e: e.tensor_tensor(out=Sa, in0=bank(6).rearrange("p (h v) -> p h v", h=4),
                                                          in1=Sa, op=ALU.add), reads=bk(6) + ['Sa'], writes=['Sa'])
                    S.op('dve', lambda e, c=c: e.tensor_tensor(
                        out=Sa, in0=Sa, in1=ebL[:, :, c:c + 1].to_broadcast([128, 4, 128]), op=ALU.mult),
                        reads=['Sa', 'ebL'], writes=['Sa'])
                    if nch > 1:
                        S.dma('sp', lambda e, c=c: e.dma_start(out=hs_d[c].rearrange("h d v -> d h v"), in_=Sa),
                              reads=['Sa'], final=True)
                    else:
                        S.op('act', lambda e: e.copy(out=Sab, in_=Sa), reads=['Sa'], writes=['Sab'])
                if nch == 1 and ti == NPT - 1:
                    S.dma('sp', lambda e: e.dma_start(out=hp_d.rearrange("h d v -> d h v"), in_=Sa),
                          reads=['Sa'], final=True)
                osb, sq, on = F[0], F[1], F[2]
                S.op('act', lambda e: e.copy(out=osb, in_=bank(5, 64)), reads=bk(5), writes=['F0'])
                dbg('osb', osb, 'F0', ti, DBGT)
                dbg('og', og, 'F6', ti, DBGT)
                head_norm_gate_keys(osb, 'F0', gnab, 'gnab', og, 'F6', sq, 'F1', on, 'F2', ogt, 'ogt', ss4, rs4)
                dbg('on', on, 'F2', ti, DBGT)
                transpose_tm(ogt, 'ogt', 4, oT, 'oT')
                for half in range(2):
                    for k in range(4):
                        S.op('pe', lambda e, k=k, half=half: e.matmul(
                            bank(3 + half, 64), lhsT=oT[:, k, :], rhs=wbrA[:, k, half * 512:(half + 1) * 512],
                            start=(k == 0), stop=(k == 3)), reads=['oT', 'wbrA'], writes=bk(3 + half))
                    proj_tm(bank(half, 64), bk(half), hT, 'hT', wgA, 'wgA', half * 512, 512)
                    sg = F[3 + half]
                    S.op('act', lambda e, half=half, sg=sg: e.activation(out=sg, in_=bank(half, 64), func=AF.Sigmoid),
                         reads=bk(half), writes=['F%d' % (3 + half)])
                    S.op('dve', lambda e, half=half, sg=sg: e.tensor_tensor(
                        out=m1t[:, half * 512:(half + 1) * 512], in0=bank(3 + half, 64), in1=sg, op=ALU.mult),
                        reads=bk(3 + half) + ['F%d' % (3 + half)], writes=['m1t'])
                S.dma('sp', lambda e: e.dma_start(out=m1_d[r0:r0 + 64, :], in_=m1t), reads=['m1t'],
                      writes=[('m1', ti)])

            if 'A1' in PHASES:
                for ti in (range(NPT) if TILES is None else TILES):
                    tileA1(ti, TP)
                    S.run(conv_dmas[0:2])
                    del conv_dmas[0:2]
                tileA1(NPT, TS)
            S.run(conv_dmas)
        phase_A1()
        S.barrier()

        def phase_A2():
            AL.off = pers_mark
            winB = AL.get(128, [8, 2056], BF16)
            wgB = AL.get(128, [8, 1024], BF16)
            wbrB = AL.get(128, [4, 1024], BF16)
            woutb = AL.get(128, [8, 1024], BF16)
            stageB = [AL.get(128, 2048, F32) for _ in range(4)]
            load_w_fast(winB, win_d, 0, 8, 2048, 2056, 'winB', stageB)
            load_w_fast(wgB, win_d, 0, 8, 5128, 1024, 'wgB', stageB)
            load_w_fast(wbrB, wbrb_d, 0, 4, 0, 1024, 'wbrB', stageB)
            load_w_fast(woutb, wout_d, 0, 8, 0, 1024, 'woutb', stageB)
            gmixb = AL.get(64, 1024, F32)
            bcast_load(gmixb, gmix_d[0:1, :], 64, 1024, 'gmixb')
            gnbb = AL.get(64, 128, F32)
            bcast_load(gnbb, gnb_d[0:1, :], 64, 128, 'gnbb')
            negA = AL.get(64, 4, F32)
            dtbb = AL.get(64, 4, F32)
            bcast_load(negA, alog_d[0:1, :], 64, 4, 'negA')
            bcast_load(dtbb, dtb_d[0:1, :], 64, 4, 'dtbb')
            S.op('act', lambda e: e.activation(out=negA, in_=negA, func=AF.Exp), reads=['negA'], writes=['negA'])
            S.op('dve', lambda e: e.tensor_scalar(out=negA, in0=negA, scalar1=-1.0, scalar2=None, op0=ALU.mult),
                 reads=['negA'], writes=['negA'])
            cwin = AL.get(4, 1536, F32)
            cw = AL.get(128, [12, 4], F32)
            S.dma('sp', lambda e: e.dma_start(out=cwin, in_=convw_d), writes=['cwin'])
            for g in range(12):
                S.op('pe', lambda e, g=g: e.transpose(out=bank(0, 128, 4, g * 4), in_=cwin[0:4, g * 128:(g + 1) * 128],
                                                      identity=identf[0:4, 0:4]), reads=['cwin', 'cst'], writes=bk(0))
            S.op('dve', lambda e: e.tensor_copy(out=cw, in_=bank(0, 128, 48).rearrange("p (g j) -> p g j", g=12)),
                 reads=bk(0), writes=['cw'])
            xt = AL.get(64, 1024, F32)
            hb = AL.get(64, 1024, BF16)
            hT = AL.get(128, [8, 64], BF16)
            ss = AL.get(64, 1, F32)
            rs = AL.get(64, 1, F32)
            F = [AL.get(64, 512, F32) for _ in range(6)]
            rawext = AL.get(128, 12 * 112, F32)
            rawtm = AL.get(64, 1536, F32)
            ctmp = AL.get(128, [12, 64], F32)
            acc = AL.get(128, [12, 64], F32)
            sqn = AL.get(128, 512, F32)
            qkn = AL.get(128, [8, 64], BF16)
            qknm = AL.get(128, [8, 64], BF16)
            vcb = AL.get(128, [4, 64], BF16)
            vk = AL.get(64, [8, 128], BF16)
            sm = AL.get(64, 64, F32)
            beta, zz, ez, spl, gg, gcum, gLb, eg, ngcum, dkk, kdec, nbeta, nbe = [sm[:, i * 4:(i + 1) * 4] for i in range(13)]
            gc = AL.get(64, [4, 16], F32)
            egLT = AL.get(128, [4, 16], F32)
            gd = AL.get(64, [4, 64], F32)
            gam = AL.get(64, [4, 64], F32)
            gamT = AL.get(64, [4, 64], F32)
            Nb = [AL.get(64, [4, 64], BF16) for _ in range(2)]
            Mb = [AL.get(64, [4, 64], BF16) for _ in range(2)]
            Qb = AL.get(64, [4, 64], BF16)
            Qf = AL.get(64, [4, 64], F32)
            rb = AL.get(64, [4, 128], BF16)
            ub = AL.get(64, [4, 128], BF16)
            ATb = AL.get(64, [4, 64], BF16)
            khat = AL.get(64, [4, 128], BF16)
            khm = AL.get(64, [4, 128], BF16)
            Sd = AL.get(128, [4, 128], F32)
            Sdb = AL.get(128, [4, 128], BF16)
            ss4 = AL.get(64, 4, F32)
            rs4 = AL.get(64, 4, F32)
            ogt = AL.get(64, 512, BF16)
            oT = AL.get(128, [4, 64], BF16)
            m1t = AL.get(64, 1024, F32)
            mb = AL.get(64, 1024, BF16)
            mT = AL.get(128, [8, 64], BF16)
            S.op('pool', lambda e: e.memset(Sd, 0.0), writes=['Sd'])
            S.op('pool', lambda e: e.memset(Sdb, 0.0), writes=['Sdb'])
            S.op('pool', lambda e: e.memset(rawext, 0.0), writes=['rawext'])

            def tileA2(ti, T):
                nch, C = T['nch'], T['C']
                r0 = ti * 64
                W = 3 + C
                rx = rawext[:, 0:12 * nch * W].rearrange("p (g s w) -> p g s w", g=12, s=nch)
                S.dma('sp', lambda e: e.dma_start(out=xt, in_=x_d[r0:r0 + 64, :]), writes=['xt'])
                S.dma('sp', lambda e: e.dma_start(out=m1t, in_=m1_d[r0:r0 + 64, :]), reads=[('m1', ti)], writes=['m1t'])
                rmsnorm_to_bf16(xt, 'xt', gmixb, 'gmixb', hb, 'hb', ss, rs)
                transpose_tm(hb, 'hb', 8, hT, 'hT')
                praw = PB[:, 3 * 512:3 * 512 + 768].rearrange("p (g t) -> p g t", g=12)
                for i3 in range(3):
                    proj_tm(bank(i3, 64), bk(i3), hT, 'hT', winB, 'winB', i3 * 512, 512)
                    S.op('act', lambda e, i3=i3: e.copy(out=rawtm[:, i3 * 512:(i3 + 1) * 512], in_=bank(i3, 64)),
                         reads=bk(i3), writes=[('rawtm', i3)])
                for g in range(12):
                    S.op('pe', lambda e, g=g: e.transpose(out=praw[:, g, :], in_=rawtm[:, g * 128:(g + 1) * 128],
                                                          identity=identf[0:64, 0:64]),
                         reads=[('rawtm', g // 4), 'cst'], writes=bk(3, 4))
                if nch == 1:
                    if ti > 0:
                        S.op('pool', lambda e: e.tensor_copy(out=rx[:, :, 0, 0:3], in_=rx[:, :, 0, 64:67]),
                             reads=['rawext'], writes=['rawext'])
                    S.op('act', lambda e: e.copy(out=rx[:, :, 0, 3:67], in_=praw), reads=bk(3, 4), writes=['rawext'])
                else:
                    cvin = F[0:3]
                    for i3 in range(3):
                        S.dma('sp', lambda e, i3=i3: e.dma_start(out=cvin[i3][0:48, :],
                                                                 in_=scv_d[:, i3 * 512:(i3 + 1) * 512]),
                              writes=['F%d' % i3])
                    for g in range(12):
                        S.op('pe', lambda e, g=g: e.transpose(
                            out=bank(0, 128, 48, g * 64), in_=cvin[g // 4][0:48, (g % 4) * 128:(g % 4 + 1) * 128],
                            identity=identf[0:48, 0:48]), reads=['F%d' % (g // 4), 'cst'], writes=bk(0, 1))
                    S.op('dve', lambda e: e.tensor_copy(
                        out=rx[:, :, :, 0:3],
                        in_=PB[:, 0:768].rearrange("p (g x) -> p g x", g=12)[:, :, 0:48].rearrange(
                            "p g (s j) -> p g s j", s=16)),
                        reads=bk(0, 1), writes=['rawext'])
                    S.op('act', lambda e: e.copy(out=rx[:, :, :, 3:7],
                                                 in_=praw.rearrange("p g (s t) -> p g s t", s=16)),
                         reads=bk(3, 4), writes=['rawext'])
                accv = acc.rearrange("p g (s t) -> p g s t", s=nch)
                tmpv = ctmp.rearrange("p g (s t) -> p g s t", s=nch)
                for j in range(4):
                    dst = accv if j == 0 else tmpv
                    dkey = [('acc', g) for g in range(12)] if j == 0 else ['ctmp']
                    S.op('dve', lambda e, j=j, dst=dst: e.tensor_tensor(
                        out=dst, in0=rx[:, :, :, j:j + C],
                        in1=cw[:, :, j:j + 1].unsqueeze(3).to_broadcast([128, 12, nch, C]), op=ALU.mult),
                        reads=['rawext', 'cw'], writes=dkey)
                    if j > 0:
                        S.op('dve', lambda e: e.tensor_tensor(out=acc, in0=acc, in1=ctmp, op=ALU.add),
                             reads=[('acc', g) for g in range(12)] + ['ctmp'], writes=[('acc', g) for g in range(12)])
                acck = [('acc', g) for g in range(12)]
                S.op('act', lambda e: e.activation(out=acc, in_=acc, func=AF.Silu), reads=acck, writes=acck)
                S.op('act', lambda e: e.activation(out=sqn, in_=acc[:, 0:8, :].rearrange("p g t -> p (g t)"),
                                                   func=AF.Square), reads=acck, writes=['sqn'])
                S.op('pe', lambda e: e.matmul(bank(0), lhsT=onesf, rhs=sqn, start=True, stop=True),
                     reads=['cst', 'sqn'], writes=bk(0))
                S.op('act', lambda e: e.activation(out=sqn, in_=bank(0), func=AF.Sqrt, bias=epsc), reads=bk(0) + ['cst'],
                     writes=['sqn'])
                S.op('dve', lambda e: e.reciprocal(out=sqn, in_=sqn), reads=['sqn'], writes=['sqn'])
                sq3 = sqn.rearrange("p (g t) -> p g t", g=8)
                S.op('dve', lambda e: e.scalar_tensor_tensor(out=qkn[:, 0:4, :], in0=acc[:, 0:4, :], scalar=128.0 ** -0.5,
                                                             in1=sq3[:, 0:4, :], op0=ALU.mult, op1=ALU.mult),
                     reads=acck + ['sqn'], writes=['qkn'])
                S.op('dve', lambda e: e.tensor_tensor(out=qkn[:, 4:8, :], in0=acc[:, 4:8, :], in1=sq3[:, 4:8, :],
                                                      op=ALU.mult), reads=acck + ['sqn'], writes=['qkn'])
                S.op('act', lambda e: e.copy(out=vcb, in_=acc[:, 8:12, :]), reads=acck, writes=['vcb'])
                for h in range(4):
                    S.op('pe', lambda e, h=h: e.transpose(out=PTt[0:64, h * 128:(h + 1) * 128], in_=vcb[:, h, :],
                                                          identity=identb), reads=['vcb', 'identb'], writes=['pt'])
                for h in range(4):
                    S.op('pe', lambda e, h=h: e.transpose(out=PTt[0:64, (4 + h) * 128:(5 + h) * 128], in_=qkn[:, 4 + h, :],
                                                          identity=identb), reads=['qkn', 'identb'], writes=['pt'])
                S.op('act', lambda e: e.copy(out=vk, in_=PTt[0:64, :].rearrange("p (k v) -> p k v", k=8)),
                     reads=['pt'], writes=['vk'])
                proj_tm(bank(1, 64, 8), bk(1), hT, 'hT', winB, 'winB', 2048, 8)
                S.op('act', lambda e: e.activation(out=beta, in_=bank(1, 64, 4), func=AF.Sigmoid), reads=bk(1), writes=['sm'])
                S.op('dve', lambda e: e.tensor_tensor(out=zz, in0=bank(1, 64, 4, 4), in1=dtbb, op=ALU.add),
                     reads=bk(1) + ['dtbb'], writes=['sm'])
                S.op('act', lambda e: e.activation(out=ez, in_=zz, func=AF.Exp), reads=['sm'], writes=['sm'])
                S.op('act', lambda e: e.activation(out=spl, in_=ez, func=AF.Ln, bias=C_('one', 64)), reads=['sm', 'cst'],
                     writes=['sm'])
                S.op('dve', lambda e: e.tensor_tensor(out=gg, in0=spl, in1=negA, op=ALU.mult), reads=['sm', 'negA'],
                     writes=['sm'])
                S.op('pe', lambda e: e.matmul(bank(2, 64, 4), lhsT=T['tri'], rhs=gg, start=True, stop=True),
                     reads=['cst', 'sm'], writes=bk(2))
                S.op('pe', lambda e: e.matmul(bank(2, 64, 4, 4), lhsT=T['blk'], rhs=gg, start=True, stop=True),
                     reads=['cst', 'sm'], writes=bk(2))
                S.op('dve', lambda e: e.tensor_copy(out=sm[:, 20:28], in_=bank(2, 64, 8)), reads=bk(2), writes=['sm'])
                S.op('act', lambda e: e.activation(out=eg, in_=gcum, func=AF.Exp), reads=['sm'], writes=['sm'])
                S.op('dve', lambda e: e.tensor_scalar(out=ngcum, in0=gcum, scalar1=-1.0, scalar2=None, op0=ALU.mult),
                     reads=['sm'], writes=['sm'])
                S.op('dve', lambda e: e.tensor_tensor(out=dkk, in0=gLb, in1=gcum, op=ALU.subtract), reads=['sm'],
                     writes=['sm'])
                S.op('act', lambda e: e.activation(out=kdec, in_=dkk, func=AF.Exp), reads=['sm'], writes=['sm'])
                S.op('dve', lambda e: e.tensor_scalar(out=nbeta, in0=beta, scalar1=-1.0, scalar2=None, op0=ALU.mult),
                     reads=['sm'], writes=['sm'])
                S.op('dve', lambda e: e.tensor_tensor(out=nbe, in0=nbeta, in1=eg, op=ALU.mult), reads=['sm'],
                     writes=['sm'])
                S.op('dve', lambda e: e.tensor_tensor(out=gc[:, :, 0:nch], in0=gg.unsqueeze(2).to_broadcast([64, 4, nch]),
                                                      in1=T['cind'][:, 0:nch].unsqueeze(1).to_broadcast([64, 4, nch]),
                                                      op=ALU.mult), reads=['sm', 'cst'], writes=['gc'])
                for h in range(4):
                    S.op('pe', lambda e, h=h: e.matmul(bank(1, 128, nch, 64 + h * 16), lhsT=onesf[0:64, :],
                                                       rhs=gc[:, h, 0:nch], start=True, stop=True),
                         reads=['cst', 'gc'], writes=bk(1))
                S.op('act', lambda e: e.activation(
                    out=egLT[:, :, 0:nch], in_=bank(1, 128, 64, 64).rearrange("p (h c) -> p h c", h=4)[:, :, 0:nch],
                    func=AF.Exp), reads=bk(1), writes=['egLT'])
                S.op('dve', lambda e: e.tensor_tensor(out=gd, in0=gcum.unsqueeze(2).to_broadcast([64, 4, 64]),
                                                      in1=identf[0:64, 0:64].unsqueeze(1).to_broadcast([64, 4, 64]),
                                                      op=ALU.mult), reads=['sm', 'cst'], writes=['gd'])
                gd2 = gd.rearrange("p h t -> p (h t)")
                S.op('pe', lambda e: e.matmul(bank(2, 64, 256), lhsT=onesf[0:64, 0:64], rhs=gd2, start=True, stop=False),
                     reads=['cst', 'gd'], writes=bk(2))
                S.op('pe', lambda e: e.matmul(bank(2, 64, 256), lhsT=identf[0:64, 0:64], rhs=T['pmS'], start=False,
                                              stop=True), reads=['cst'], writes=bk(2))
                S.op('pe', lambda e: e.matmul(bank(2, 64, 256, 256), lhsT=onesf[0:64, 0:64], rhs=gd2, start=True,
                                              stop=False), reads=['cst', 'gd'], writes=bk(2))
                S.op('pe', lambda e: e.matmul(bank(2, 64, 256, 256), lhsT=identf[0:64, 0:64], rhs=T['nmT'], start=False,
                                              stop=True), reads=['cst'], writes=bk(2))
                for h in range(4):
                    S.op('act', lambda e, h=h: e.activation(out=gam[:, h, :], in_=bank(2, 64, 64, h * 64), func=AF.Exp,
                                                            scale=-1.0, bias=gcum[:, h:h + 1]),
                         reads=bk(2) + ['sm'], writes=['gam'])
                for h in range(4):
                    S.op('act', lambda e, h=h: e.activation(out=gamT[:, h, :], in_=bank(2, 64, 64, 256 + h * 64),
                                                            func=AF.Exp, bias=ngcum[:, h:h + 1]),
                         reads=bk(2) + ['sm'], writes=['gamT'])
                for h in range(4):
                    S.op('pe', lambda e, h=h: e.matmul(bank(0, 64, 64, h * 64), lhsT=qkn[:, 4 + h, :], rhs=qkn[:, 4 + h, :],
                                                       start=True, stop=True), reads=['qkn'], writes=bk(0))
                for h in range(4):
                    S.op('dve', lambda e, h=h: e.scalar_tensor_tensor(
                        out=Nb[0][:, h, :], in0=bank(0, 64, 64, h * 64), scalar=nbeta[:, h:h + 1], in1=gam[:, h, :],
                        op0=ALU.mult, op1=ALU.mult), reads=bk(0) + ['sm', 'gam'], writes=['Nb0'])
                for h in range(4):
                    S.op('pe', lambda e, h=h: e.transpose(out=PTt[0:64, h * 64:(h + 1) * 64], in_=Nb[0][:, h, :],
                                                          identity=identb[0:64, 0:64]),
                         reads=['Nb0', 'identb'], writes=['pt'])
                ptv = PTt[0:64, 0:256].rearrange("p (h t) -> p h t", h=4)
                S.op('dve', lambda e: e.tensor_copy(out=Mb[0], in_=ptv), reads=['pt'], writes=['Mb0'])
                S.op('dve', lambda e: e.tensor_tensor(out=Qf, in0=ptv,
                                                      in1=identf[0:64, 0:64].unsqueeze(1).to_broadcast([64, 4, 64]),
                                                      op=ALU.add), reads=['pt', 'cst'], writes=['Qf'])
                S.op('act', lambda e: e.copy(out=Qb, in_=Qf), reads=['Qf'], writes=['Qb'])
                nsteps = {64: 5, 4: 1}[C]
                cur = 0
                for i in range(nsteps):
                    last = (i == nsteps - 1)
                    nxt = 1 - cur
                    for h in range(4):
                        S.op('pe', lambda e, h=h, cur=cur: e.matmul(bank(0, 64, 64, h * 64), lhsT=Mb[cur][:, h, :],
                                                                    rhs=Nb[cur][:, h, :], start=True, stop=True),
                             reads=['Mb%d' % cur, 'Nb%d' % cur], writes=bk(0))
                    if not last:
                        for h in range(4):
                            S.op('pe', lambda e, h=h, cur=cur: e.matmul(bank(1, 64, 64, h * 64), lhsT=Nb[cur][:, h, :],
                                                                        rhs=Mb[cur][:, h, :], start=True, stop=True),
                                 reads=['Mb%d' % cur, 'Nb%d' % cur], writes=bk(1))
                    S.op('act', lambda e, nxt=nxt: e.copy(out=Nb[nxt],
                                                          in_=bank(0, 64, 256).rearrange("p (h t) -> p h t", h=4)),
                         reads=bk(0), writes=['Nb%d' % nxt])
                    if not last:
                        S.op('dve', lambda e, nxt=nxt: e.tensor_copy(
                            out=Mb[nxt], in_=bank(1, 64, 256).rearrange("p (h t) -> p h t", h=4)),
                            reads=bk(1), writes=['Mb%d' % nxt])
                    for h in range(4):
                        S.op('pe', lambda e, h=h, nxt=nxt: e.matmul(bank(2, 64, 64, h * 64), lhsT=Nb[nxt][:, h, :],
                                                                    rhs=Qb[:, h, :], start=True, stop=True),
                             reads=['Nb%d' % nxt, 'Qb'], writes=bk(2))
                    S.op('dve', lambda e: e.tensor_tensor(out=Qf, in0=Qf,
                                                          in1=bank(2, 64, 256).rearrange("p (h t) -> p h t", h=4),
                                                          op=ALU.add), reads=['Qf'] + bk(2), writes=['Qf'])
                    S.op('act', lambda e: e.copy(out=Qb, in_=Qf), reads=['Qf'], writes=['Qb'])
                    cur = nxt
                for h in range(4):
                    S.op('pe', lambda e, h=h: e.matmul(bank(0, 64, 64, h * 64), lhsT=qkn[:, 4 + h, :], rhs=qkn[:, h, :],
                                                       start=True, stop=True), reads=['qkn'], writes=bk(0))
                S.op('dve', lambda e: e.tensor_tensor(out=ATb, in0=bank(0, 64, 256).rearrange("p (h t) -> p h t", h=4),
                                                      in1=gamT, op=ALU.mult), reads=bk(0) + ['gamT'], writes=['ATb'])
                for c in range(nch):
                    if nch > 1:
                        S.dma('sp', lambda e, c=c: e.dma_start(out=Sd, in_=sd_d[c].rearrange("h d v -> d h v")),
                              writes=['Sd'])
                        S.op('act', lambda e: e.copy(out=Sdb, in_=Sd), reads=['Sd'], writes=['Sdb'])
                        S.op('dve', lambda e, c=c: e.tensor_tensor(
                            out=qknm, in0=qkn, in1=cmb[:, c, :].unsqueeze(1).to_broadcast([128, 8, 64]), op=ALU.mult),
                            reads=['qkn', 'cmb'], writes=['qknm'])
                        src, skey = qknm, 'qknm'
                    else:
                        src, skey = qkn, 'qkn'
                    for h in range(4):
                        S.op('pe', lambda e, h=h, src=src, c=c: e.matmul(
                            bank(5, 64, 128, h * 128), lhsT=src[:, 4 + h, :], rhs=Sdb[:, h, :], start=(c == 0 and h == 0),
                            stop=(c == nch - 1), skip_group_check=True), reads=[skey, 'Sdb'], writes=bk(5))
                        S.op('pe', lambda e, h=h, src=src, c=c: e.matmul(
                            bank(6, 64, 128, h * 128), lhsT=src[:, h, :], rhs=Sdb[:, h, :], start=(c == 0 and h == 0),
                            stop=(c == nch - 1), skip_group_check=True), reads=[skey, 'Sdb'], writes=bk(6))
                t1, bv, t2, ob = F[3], F[4], F[3], F[4]
                S.op('dve', lambda e: e.tensor_tensor(out=t1.rearrange("p (h v) -> p h v", h=4),
                                                      in0=bank(5, 64).rearrange("p (h v) -> p h v", h=4),
                                                      in1=nbe.unsqueeze(2).to_broadcast([64, 4, 128]), op=ALU.mult),
                     reads=bk(5) + ['sm'], writes=['F3'])
                S.op('dve', lambda e: e.tensor_tensor(out=bv.rearrange("p (h v) -> p h v", h=4), in0=vk[:, 0:4, :],
                                                      in1=beta.unsqueeze(2).to_broadcast([64, 4, 128]), op=ALU.mult),
                     reads=['vk', 'sm'], writes=['F4'])
                S.op('dve', lambda e: e.tensor_tensor(out=rb.rearrange("p h v -> p (h v)"), in0=t1, in1=bv, op=ALU.add),
                     reads=['F3', 'F4'], writes=['rb'])
                for h in range(4):
                    S.op('pe', lambda e, h=h: e.matmul(bank(1, 64, 128, h * 128), lhsT=Qb[:, h, :], rhs=rb[:, h, :],
                                                       start=True, stop=True), reads=['Qb', 'rb'], writes=bk(1))
                S.op('act', lambda e: e.copy(out=ub, in_=bank(1, 64).rearrange("p (h v) -> p h v", h=4)),
                     reads=bk(1), writes=['ub'])
                for h in range(4):
                    S.op('pe', lambda e, h=h: e.matmul(bank(2, 64, 128, h * 128), lhsT=ATb[:, h, :], rhs=ub[:, h, :],
                                                       start=True, stop=True), reads=['ATb', 'ub'], writes=bk(2))
                S.op('dve', lambda e: e.tensor_tensor(out=t2.rearrange("p (h v) -> p h v", h=4),
                                                      in0=bank(6, 64).rearrange("p (h v) -> p h v", h=4),
                                                      in1=eg.unsqueeze(2).to_broadcast([64, 4, 128]), op=ALU.mult),
                     reads=bk(6) + ['sm'], writes=['F3'])
                S.op('dve', lambda e: e.tensor_tensor(out=ob, in0=t2, in1=bank(2, 64), op=ALU.add),
                     reads=['F3'] + bk(2), writes=['F4'])
                S.op('dve', lambda e: e.tensor_tensor(out=khat, in0=vk[:, 4:8, :],
                                                      in1=kdec.unsqueeze(2).to_broadcast([64, 4, 128]), op=ALU.mult),
                     reads=['vk', 'sm'], writes=['khat'])
                for c in range(nch):
                    if nch > 1:
                        S.dma('sp', lambda e, c=c: e.dma_start(out=Sd, in_=sd_d[c].rearrange("h d v -> d h v")),
                              writes=['Sd'])
                        S.op('dve', lambda e, c=c: e.tensor_scalar(out=khm, in0=khat, scalar1=T['cind'][:, c:c + 1],
                                                                   scalar2=None, op0=ALU.mult),
                             reads=['khat', 'cst'], writes=['khm'])
                        ks_, kkey = khm, 'khm'
                    else:
                        ks_, kkey = khat, 'khat'
                    for h in range(4):
                        S.op('pe', lambda e, h=h, ks_=ks_: e.matmul(bank(0, 128, 128, h * 128), lhsT=ks_[:, h, :],
                                                                    rhs=ub[:, h, :], start=True, stop=True),
                             reads=[kkey, 'ub'], writes=bk(0))
                    S.op('dve', lambda e, c=c: e.tensor_tensor(
                        out=Sd, in0=Sd, in1=egLT[:, :, c:c + 1].to_broadcast([128, 4, 128]), op=ALU.mult),
                        reads=['Sd', 'egLT'], writes=['Sd'])
                    S.op('dve', lambda e: e.tensor_tensor(out=Sd, in0=Sd, in1=bank(0).rearrange("p (h v) -> p h v", h=4),
                                                          op=ALU.add), reads=['Sd'] + bk(0), writes=['Sd'])
                    if nch > 1:
                        S.dma('sp', lambda e, c=c: e.dma_start(out=ds_d[c].rearrange("h d v -> d h v"), in_=Sd),
                              reads=['Sd'], final=True)
                    else:
                        S.op('act', lambda e: e.copy(out=Sdb, in_=Sd), reads=['Sd'], writes=['Sdb'])
                if nch == 1 and ti == NPT - 1:
                    S.dma('sp', lambda e: e.dma_start(out=dp_d.rearrange("h d v -> d h v"), in_=Sd),
                          reads=['Sd'], final=True)
                if nch > 1 or ti == NPT - 1:
                    rk = [('rawtm', i3) for i3 in range(3)]
                    if nch == 1:
                        S.dma('sp', lambda e: e.dma_start(out=cp_d, in_=rawtm[61:64, :]), reads=rk, final=True)
                    else:
                        for sq_ in range(16):
                            S.dma('sp', lambda e, sq_=sq_: e.dma_start(
                                out=cs_d[3 * sq_:3 * sq_ + 3, :], in_=rawtm[4 * sq_ + 1:4 * sq_ + 4, :]),
                                reads=rk, final=True)
                proj_tm(bank(0, 64), bk(0), hT, 'hT', winB, 'winB', 1536, 512)
                S.op('act', lambda e: e.activation(out=F[5], in_=bank(0, 64), func=AF.Silu), reads=bk(0), writes=['F5'])
                head_norm_gate_keys(ob, 'F4', gnbb, 'gnbb', F[5], 'F5', F[0], 'F0', F[1], 'F1', ogt, 'ogt', ss4, rs4)
                transpose_tm(ogt, 'ogt', 4, oT, 'oT')
                for half in range(2):
                    for k in range(4):
                        S.op('pe', lambda e, k=k, half=half: e.matmul(
                            bank(3 + half, 64), lhsT=oT[:, k, :], rhs=wbrB[:, k, half * 512:(half + 1) * 512],
                            start=(k == 0), stop=(k == 3)), reads=['oT', 'wbrB'], writes=bk(3 + half))
                    proj_tm(bank(half, 64), bk(half), hT, 'hT', wgB, 'wgB', half * 512, 512)
                    sg = F[2 + half]
                    S.op('act', lambda e, half=half, sg=sg: e.activation(out=sg, in_=bank(half, 64), func=AF.Sigmoid),
                         reads=bk(half), writes=['F%d' % (2 + half)])
                    S.op('dve', lambda e, half=half, sg=sg: e.tensor_tensor(out=sg, in0=bank(3 + half, 64), in1=sg,
                                                                            op=ALU.mult),
                         reads=bk(3 + half) + ['F%d' % (2 + half)], writes=['F%d' % (2 + half)])
                    S.op('dve', lambda e, half=half, sg=sg: e.tensor_tensor(
                        out=mb[:, half * 512:(half + 1) * 512], in0=sg, in1=m1t[:, half * 512:(half + 1) * 512],
                        op=ALU.add), reads=['F%d' % (2 + half), 'm1t'], writes=['mb'])
                transpose_tm(mb, 'mb', 8, mT, 'mT')
                for half in range(2):
                    proj_tm(bank(3 + half, 64), bk(3 + half), mT, 'mT', woutb, 'woutb', half * 512, 512)
                    S.op('dve', lambda e, half=half: e.tensor_tensor(
                        out=m1t[:, half * 512:(half + 1) * 512], in0=bank(3 + half, 64),
                        in1=xt[:, half * 512:(half + 1) * 512], op=ALU.add), reads=bk(3 + half) + ['xt'], writes=['m1t'])
                S.dma('sp', lambda e: e.dma_start(out=x1_d[r0:r0 + 64, :], in_=m1t), reads=['m1t'],
                      writes=[('x1', ti)])

            if 'A2' in PHASES:
                for ti in (range(NPT) if TILES is None else TILES):
                    tileA2(ti, TP)
                tileA2(NPT, TS)
        phase_A2()
        S.barrier()

        def phase_B():
            AL.off = pers_mark
            wqb = AL.get(128, [8, 2048], BF16)
            keysTb = AL.get(128, [16, 128], BF16)
            wpgb = AL.get(128, [8, 1024], BF16)
            wpleb = AL.get(128, [2, 1024], BF16)
            HB = [AL.get(128, 1024, BF16) for _ in range(NHB)]
            dltb = AL.get(128, [64, 64], BF16)
            WMd = AL.get(128, [64, 64], BF16)
            UV = [AL.get(128, 2048, BF16) for _ in range(NUV)]
            S.op('pool', lambda e: e.memset(WMd, 0.0), writes=['WMd'])
            junkb = AL.get(128, 1024, BF16)
            load_w_bf16(wqb, wq_d, 0, 8, 0, 2048, 'wqb')
            load_w_bf16(wpgb, wpg_d, 0, 8, 0, 1024, 'wpgb')
            load_w_bf16(wpleb, wple_d, 0, 2, 0, 1024, 'wpleb')
            S.dma('pool', lambda e: e.dma_start(out=keysTb, in_=keysT_d.rearrange("c d k -> d c k")), writes=['keysTb'])
            for cc in range(0, 4096, 2048):
                S.dma('pool', lambda e, cc=cc: e.dma_start(
                    out=dltb.rearrange("p n m -> p (n m)")[:, cc:cc + 2048], in_=dlt_d[:, cc:cc + 2048]), writes=['dltb'])
            gffnb = AL.get(64, 1024, F32)
            gpleb = AL.get(64, 1024, F32)
            gfinb = AL.get(64, 1024, F32)
            bcast_load(gffnb, gffn_d[0:1, :], 64, 1024, 'gffnb')
            bcast_load(gpleb, gple_d[0:1, :], 64, 1024, 'gpleb')
            bcast_load(gfinb, gfin_d[0:1, :], 64, 1024, 'gfinb')
            xtF = AL.get(64, 1024, F32)
            xtV2 = [AL.get(64, 1024, F32) for _ in range(2)]
            hbF = [AL.get(64, 1024, BF16) for _ in range(2)]
            hT = AL.get(128, [8, 64], BF16)
            hb2 = AL.get(64, 1024, BF16)
            hT2 = AL.get(128, [8, 64], BF16)
            ss = AL.get(64, 1, F32)
            rs = AL.get(64, 1, F32)
            ss2 = AL.get(64, 1, F32)
            rs2 = AL.get(64, 1, F32)
            qTb = AL.get(128, [16, 64], BF16)
            sc2 = [AL.get(64, [16, 128], F32) for _ in range(2)]
            v1 = AL.get(64, [16, 16], F32)
            i1 = AL.get(64, [16, 16], U32)
            i1f = AL.get(64, [16, 16], F32)
            wk = AL.get(64, 256, F32)
            cand = AL.get(64, [8, 256], F32)
            eq = cand.rearrange("p h (k i) -> p h k i", k=16)
            v2 = AL.get(64, [8, 16], F32)
            ci = AL.get(64, [8, 16], U32)
            cih = AL.get(64, [8, 16], U32)
            cil = AL.get(64, [8, 16], U32)
            cihf = AL.get(64, [8, 16], F32)
            cilf = AL.get(64, [8, 16], F32)
            iaf = AL.get(64, 128, F32)
            ibf = AL.get(64, 128, F32)
            idxf = AL.get(64, 128, F32)
            gte = AL.get(64, [8, 16], F32)
            gsum = AL.get(64, 8, F32)
            IDXT = [AL.get(128, 64, I32) for _ in range(3)]
            gateT = [AL.get(128, 64, F32) for _ in range(2)]
            ACTT = AL.get(128, 64, F32)
            X2c = AL.get(128, 64, F32)
            INc = AL.get(128, 64, F32)
            Sc = AL.get(128, 64, F32)
            gsig = AL.get(64, 1024, F32)
            ptl2 = [AL.get(64, 256, F32) for _ in range(2)]
            ptb = AL.get(64, 256, BF16)
            pTt = AL.get(128, [2, 64], BF16)
            yt = AL.get(64, 1024, F32)
            iota16 = C_('iota16')

            def topk16(src, srckey, width, vals, vkey, idxs, ikey):
                S.op('dve', lambda e: e.max(out=vals[:, 0:8], in_=src), reads=[srckey], writes=[vkey])
                S.op('dve', lambda e: e.max_index(out=idxs[:, 0:8], in_max=vals[:, 0:8], in_values=src),
                     reads=[srckey, vkey], writes=[ikey])
                S.op('dve', lambda e: e.match_replace(out=wk[:, 0:width], in_to_replace=vals[:, 0:8], in_values=src,
                                                      imm_value=-1e30), reads=[srckey, vkey], writes=['wk'])
                S.op('dve', lambda e: e.max(out=vals[:, 8:16], in_=wk[:, 0:width]), reads=['wk'], writes=[vkey])
                S.op('dve', lambda e: e.max_index(out=idxs[:, 8:16], in_max=vals[:, 8:16], in_values=wk[:, 0:width]),
                     reads=['wk', vkey], writes=[ikey])

            def rms_bf16(xin, xkey, gb, gkey, hout, hkey, ss_, sskey, rs_, rskey):
                S.op('act', lambda e: e.activation(out=hout, in_=xin, func=AF.Square, accum_out=ss_),
                     reads=[xkey], writes=[hkey, sskey])
                S.op('act', lambda e: e.activation(out=rs_, in_=ss_, func=AF.Sqrt, scale=1.0 / 1024,
                                                   bias=epsc[0:64, :]), reads=[sskey, 'cst'], writes=[rskey])
                S.op('dve', lambda e: e.reciprocal(out=rs_, in_=rs_), reads=[rskey], writes=[rskey])
                S.op('dve', lambda e: e.scalar_tensor_tensor(out=hout, in0=xin, scalar=rs_[:, 0:1], in1=gb,
                                                             op0=ALU.mult, op1=ALU.mult),
                     reads=[xkey, rskey, gkey], writes=[hkey])

            def front_pe(ti, pos):
                par = 0
                sc = sc2[pos % 2]
                sck = 'sc%d' % (pos % 2)
                r0 = ti * 64
                hb = hbF[par]
                hkey = 'hbF%d' % par
                S.dma('sp', lambda e: e.dma_start(out=xtF, in_=x1_d[r0:r0 + 64, :]), reads=[('x1', ti)], writes=['xtF'])
                rms_bf16(xtF, 'xtF', gffnb, 'gffnb', hb, hkey, ss, 'ss', rs, 'rs')
                S.dma('sp', lambda e: e.dma_start(out=h2_d[r0:r0 + 64, :], in_=hb), reads=[hkey], writes=[('h2', ti)])
                transpose_tm(hb, hkey, 8, hT, 'hT')
                pq = PB[:, 1024:2048].rearrange("p (c t) -> p c t", c=16)
                for hc in range(16):
                    for k in range(8):
                        S.op('pe', lambda e, hc=hc, k=k: e.matmul(pq[:, hc, :], lhsT=wqb[:, k, hc * 128:(hc + 1) * 128],
                                                                  rhs=hT[:, k, :], start=(k == 0), stop=(k == 7)),
                             reads=['wqb', 'hT'], writes=bk(2, 3))
                S.op('act', lambda e: e.copy(out=qTb, in_=pq), reads=bk(2, 3), writes=['qTb'])
                for half in range(2):
                    for j in range(8):
                        hc = half * 8 + j
                        S.op('pe', lambda e, hc=hc, j=j: e.matmul(PB[0:64, 1024 + j * 128:1024 + (j + 1) * 128],
                                                                  lhsT=qTb[:, hc, :], rhs=keysTb[:, hc, :],
                                                                  start=True, stop=True),
                             reads=['qTb', 'keysTb'], writes=bk(2, 3))
                    S.op('act', lambda e, half=half: e.copy(
                        out=sc[:, half * 8:(half + 1) * 8, :],
                        in_=PB[0:64, 1024:2048].rearrange("p (c k) -> p c k", c=8)), reads=bk(2, 3), writes=[sck])

            def front_dve(ti, pos):
                par = pos % 2
                ip = pos % 3
                sc = sc2[pos % 2]
                sck = 'sc%d' % (pos % 2)
                for hc in range(16):
                    topk16(sc[:, hc, :], sck, 128, v1[:, hc, :], 'v1', i1[:, hc, :], 'i1')
                S.op('dve', lambda e: e.tensor_copy(out=i1f, in_=i1), reads=['i1'], writes=['i1f'])
                for h in range(8):
                    S.op('dve', lambda e, h=h: e.tensor_tensor(
                        out=cand[:, h, :].rearrange("p (i j) -> p i j", i=16),
                        in0=v1[:, 2 * h, :].unsqueeze(2).to_broadcast([64, 16, 16]),
                        in1=v1[:, 2 * h + 1, :].unsqueeze(1).to_broadcast([64, 16, 16]), op=ALU.add),
                        reads=['v1'], writes=['cand'])
                for h in range(8):
                    topk16(cand[:, h, :], 'cand', 256, v2[:, h, :], 'v2', ci[:, h, :], 'ci')
                S.op('dve', lambda e: e.tensor_scalar(out=cih, in0=ci, scalar1=4, scalar2=None,
                                                      op0=ALU.logical_shift_right), reads=['ci'], writes=['cih'])
                S.op('dve', lambda e: e.tensor_scalar(out=cil, in0=ci, scalar1=15, scalar2=None, op0=ALU.bitwise_and),
                     reads=['ci'], writes=['cil'])
                S.op('dve', lambda e: e.tensor_copy(out=cihf, in_=cih), reads=['cih'], writes=['cihf'])
                S.op('dve', lambda e: e.tensor_copy(out=cilf, in_=cil), reads=['cil'], writes=['cilf'])
                i1v = i1f.rearrange("p (h c) i -> p h c i", c=2)
                for (cf, ckey, cpos, dst, dkey) in ((cihf, 'cihf', 0, iaf, 'iaf'), (cilf, 'cilf', 1, ibf, 'ibf')):
                    S.op('dve', lambda e, cf=cf: e.tensor_tensor(
                        out=eq, in0=cf.unsqueeze(3).to_broadcast([64, 8, 16, 16]),
                        in1=iota16.unsqueeze(1).unsqueeze(1).to_broadcast([64, 8, 16, 16]), op=ALU.is_equal),
                        reads=[ckey, 'cst'], writes=['cand'])
                    S.op('dve', lambda e, cpos=cpos: e.tensor_tensor(
                        out=eq, in0=eq, in1=i1v[:, :, cpos, :].unsqueeze(2).to_broadcast([64, 8, 16, 16]), op=ALU.mult),
                        reads=['cand', 'i1f'], writes=['cand'])
                    S.op('dve', lambda e, dst=dst: e.reduce_sum(out=dst, in_=eq.rearrange("p h k i -> p (h k) i"),
                                                                axis=AX.X), reads=['cand'], writes=[dkey])
                S.op('dve', lambda e: e.scalar_tensor_tensor(out=idxf, in0=iaf, scalar=128.0, in1=ibf, op0=ALU.mult,
                                                             op1=ALU.add), reads=['iaf', 'ibf'], writes=['idxf'])
                S.op('dve', lambda e: e.tensor_tensor(out=gte, in0=v2, in1=v2[:, :, 0:1].to_broadcast([64, 8, 16]),
                                                      op=ALU.subtract), reads=['v2'], writes=['gte'])
                S.op('act', lambda e: e.activation(out=gte, in_=gte, func=AF.Exp), reads=['gte'], writes=['gte'])
                S.op('dve', lambda e: e.reduce_sum(out=gsum, in_=gte, axis=AX.X), reads=['gte'], writes=['gsum'])
                S.op('dve', lambda e: e.reciprocal(out=gsum, in_=gsum), reads=['gsum'], writes=['gsum'])
                S.op('dve', lambda e: e.tensor_tensor(out=gte, in0=gte, in1=gsum.unsqueeze(2).to_broadcast([64, 8, 16]),
                                                      op=ALU.mult), reads=['gte', 'gsum'], writes=['gte'])
                S.op('pe', lambda e: e.transpose(out=bank(6, 128, 64), in_=idxf, identity=identf[0:64, 0:64]),
                     reads=['idxf', 'cst'], writes=bk(6))
                S.op('pe', lambda e: e.transpose(out=bank(6, 128, 64, 64), in_=gte.rearrange("p h k -> p (h k)"),
                                                 identity=identf[0:64, 0:64]), reads=['gte', 'cst'], writes=bk(6))
                S.op('dve', lambda e: e.tensor_copy(out=IDXT[ip], in_=bank(6, 128, 64)), reads=bk(6),
                     writes=['IDXT%d' % ip])
                S.op('dve', lambda e: e.tensor_copy(out=gateT[par], in_=bank(6, 128, 64, 64)), reads=bk(6),
                     writes=['gateT%d' % par])

            def uvstage(ti, pos):
                par = pos % 2
                ip = pos % 3
                r0 = ti * 64
                xtV = xtV2[par]
                ptl = ptl2[par]
                xk = 'xtV%d' % par
                pk = 'ptl%d' % par
                gT = gateT[par]
                gk = 'gateT%d' % par
                S.begin()
                S.dma('sp', lambda e: e.dma_start(out=xtV, in_=x1_d[r0:r0 + 64, :]), reads=[('x1', ti)], writes=[xk])
                S.dma('sp', lambda e: e.dma_start(out=ptl, in_=p_d[r0:r0 + 64, :]), writes=[pk])
                head = S.end()
                segs = []
                for n in range(64):
                    S.begin()
                    g = (pos * 64 + n) % NUV
                    uv_ = UV[g]
                    uvkey = 'UV%d' % g
                    S.dma('pool', lambda e, n=n, uv_=uv_: e.indirect_dma_start(
                        out=uv_, out_offset=None, in_=uvb_d,
                        in_offset=bass.IndirectOffsetOnAxis(ap=IDXT[ip][:, n:n + 1], axis=0)),
                        reads=['IDXT%d' % ip, 'uvb'], writes=[uvkey])
                    gh = (pos * 64 + n) % NHB
                    hbb = HB[gh]
                    hbkey = 'HB%d' % gh
                    S.dma('sp', lambda e, n=n, hbb=hbb: e.dma_start(
                        out=hbb, in_=h2_d[ti * 64 + n:ti * 64 + n + 1, :].to_broadcast([128, 1024])),
                        reads=[('h2', ti)], writes=[hbkey])
                    S.op('dve', lambda e, n=n, uv_=uv_, hbb=hbb: e.scalar_tensor_tensor(
                        out=junkb, in0=uv_[:, 0:1024], scalar=1.0, in1=hbb, op0=ALU.mult, op1=ALU.mult,
                        accum_out=ACTT[:, n:n + 1]), reads=[uvkey, hbkey], writes=[('ACT', n)])
                    xa = ACTT[:, n:n + 1]
                    S.op('act', lambda e, n=n, xa=xa: e.activation(out=X2c[:, n:n + 1], in_=xa, func=AF.Identity,
                                                                  scale=xa), reads=[('ACT', n)], writes=[('X2', n)])
                    S.op('act', lambda e, n=n: e.activation(out=X2c[:, n:n + 1], in_=X2c[:, n:n + 1], func=AF.Identity,
                                                            scale=0.044715, bias=C_('one')), reads=[('X2', n), 'cst'],
                         writes=[('X2', n)])
                    S.op('act', lambda e, n=n, xa=xa: e.activation(out=INc[:, n:n + 1], in_=X2c[:, n:n + 1],
                                                                  func=AF.Identity, scale=xa),
                         reads=[('X2', n), ('ACT', n)], writes=[('IN', n)])
                    S.op('act', lambda e, n=n: e.activation(out=Sc[:, n:n + 1], in_=INc[:, n:n + 1], func=AF.Sigmoid,
                                                            scale=1.5957691216057308), reads=[('IN', n)],
                         writes=[('S', n)])
                    S.op('act', lambda e, n=n, xa=xa: e.activation(out=Sc[:, n:n + 1], in_=Sc[:, n:n + 1],
                                                                  func=AF.Identity, scale=xa),
                         reads=[('S', n), ('ACT', n)], writes=[('S', n)])
                    S.op('act', lambda e, n=n: e.activation(out=WMd[:, n, n:n + 1], in_=Sc[:, n:n + 1],
                                                            func=AF.Identity, scale=gT[:, n:n + 1]),
                         reads=[('S', n), gk, 'WMd'], writes=[('WM', n)])
                    for half in range(2):
                        S.op('pe', lambda e, n=n, half=half, uv_=uv_: e.matmul(
                            bank(4 + half, 64), lhsT=WMd[:, n, :],
                            rhs=uv_[:, 1024 + half * 512:1024 + (half + 1) * 512],
                            start=(n == 0), stop=(n == 63)), reads=[('WM', n), uvkey], writes=bk(4 + half))
                    segs.append(S.end())
                S.begin()
                S.op('dve', lambda e: e.tensor_tensor(out=xtV, in0=xtV, in1=PB[0:64, 2048:3072], op=ALU.add),
                     reads=[xk] + bk(4, 5), writes=[xk])
                tail_a = S.end()
                S.begin()
                pt6 = PB[:, 6 * 512 + 256:7 * 512].bitcast(BF16)
                rms_bf16(xtV, xk, gpleb, 'gpleb', hb2, 'hb2', ss2, 'ss2', rs2, 'rs2')
                transpose_tm(hb2, 'hb2', 8, hT2, 'hT2', ptgt=pt6, pkey='b6u')
                S.op('dve', lambda e: e.tensor_copy(out=ptb, in_=ptl), reads=[pk], writes=['ptb'])
                transpose_tm(ptb, 'ptb', 2, pTt, 'pTt', ptgt=pt6, pkey='b6u')
                for half in range(2):
                    hs = slice(half * 512, (half + 1) * 512)
                    proj_tm(bank(0, 64), bk(0), hT2, 'hT2', wpgb, 'wpgb', half * 512, 512)
                    proj_tm(bank(1, 64), bk(1), pTt, 'pTt', wpleb, 'wpleb', half * 512, 512, nk=2)
                    S.op('act', lambda e, hs=hs: e.activation(out=gsig[:, hs], in_=bank(0, 64), func=AF.Sigmoid),
                         reads=bk(0), writes=['gsig'])
                    S.op('dve', lambda e, hs=hs: e.tensor_tensor(out=gsig[:, hs], in0=gsig[:, hs], in1=bank(1, 64),
                                                                  op=ALU.mult), reads=['gsig'] + bk(1), writes=['gsig'])
                    S.op('dve', lambda e, hs=hs: e.tensor_tensor(out=xtV[:, hs], in0=xtV[:, hs], in1=gsig[:, hs],
                                                                  op=ALU.add), reads=[xk, 'gsig'], writes=[xk])
                S.op('act', lambda e: e.activation(out=yt, in_=xtV, func=AF.Square, accum_out=ss2), reads=[xk],
                     writes=['yt', 'ss2'])
                S.op('act', lambda e: e.activation(out=rs2, in_=ss2, func=AF.Sqrt, scale=1.0 / 1024,
                                                   bias=epsc[0:64, :]), reads=['ss2', 'cst'], writes=['rs2'])
                S.op('dve', lambda e: e.reciprocal(out=rs2, in_=rs2), reads=['rs2'], writes=['rs2'])
                S.op('dve', lambda e: e.scalar_tensor_tensor(out=yt, in0=xtV, scalar=rs2[:, 0:1], in1=gfinb,
                                                             op0=ALU.mult, op1=ALU.mult),
                     reads=[xk, 'rs2', 'gfinb'], writes=['yt'])
                S.dma('sp', lambda e: e.dma_start(out=y_d[r0:r0 + 64, :], in_=yt), reads=['yt'], final=True)
                return head, segs, tail_a, S.end()

            if 'B' in PHASES:
                TL = list(range(NPT + 1)) if TILES is None else list(TILES) + [NPT]
                nT = len(TL)
                front_pe(TL[0], 0)
                front_dve(TL[0], 0)
                if nT > 1:
                    front_pe(TL[1], 1)
                LP = []
                for j in range(nT):
                    head, segs, tail_a, tail_b = uvstage(TL[j], j)
                    LFd, LFp = [], []
                    if j + 1 < nT:
                        S.begin()
                        front_dve(TL[j + 1], j + 1)
                        LFd = S.end()
                    if j + 2 < nT:
                        S.begin()
                        front_pe(TL[j + 2], j + 2)
                        LFp = S.end()
                    streams = [LFd, LFp, LP]
                    pers = [(len(L) + 63) // 64 for L in streams]
                    S.run(head)
                    for n in range(64):
                        S.run(segs[n])
                        for L, per in zip(streams, pers):
                            S.run(L[n * per:(n + 1) * per])
                    for L, per in zip(streams, pers):
                        S.run(L[64 * per:])
                    S.run(tail_a)
                    LP = tail_b
                S.run(LP)
        phase_B()
        print('ops', {e: len(v) for e, v in S.prog.items()}, 'nsem', S.nsem, 'arena', AL.off)
        S.emit()
    return nc


_CACHE = {}


def kernel(x_prompt, x_sample, state_hgrn, state_delta, state_conv, p_prompt, p_sample,
           lb_param, g_mix, w_in, conv_w, a_log, dt_bias, g_norm_a, g_norm_b, w_br_a, w_br_b,
           w_out, g_ffn, peer_wq, peer_keys, expert_u, expert_v, g_ple, w_ple, w_ple_gate,
           g_final):
    f = lambda a: np.ascontiguousarray(np.asarray(a, dtype=np.float32))
    if 'nc' not in _CACHE:
        _CACHE['nc'] = build_program()
        _CACHE['consts'] = _build_consts()
    nc = _CACHE['nc']
    cst, cmf, zsel, dlt = _CACHE['consts']
    x_prompt, x_sample = f(x_prompt), f(x_sample)
    p_prompt, p_sample = f(p_prompt), f(p_sample)
    state_hgrn, state_delta, state_conv = f(state_hgrn), f(state_delta), f(state_conv)
    keysT = np.ascontiguousarray(np.transpose(f(peer_keys)[0], (0, 1, 3, 2)).reshape(16, 128, 128))
    shared = dict(
        lbp=f(lb_param), gmix=f(g_mix), w_in=f(w_in)[0], convw=f(conv_w)[0], alog=f(a_log), dtb=f(dt_bias),
        gna=f(g_norm_a), gnb=f(g_norm_b), wbra=f(w_br_a)[0], wbrb=f(w_br_b)[0], wout=f(w_out)[0], gffn=f(g_ffn),
        wq=f(peer_wq)[0], keysT=keysT, eu=f(expert_u)[0], ev=f(expert_v)[0], gple=f(g_ple), wple=f(w_ple)[0],
        wpg=f(w_ple_gate)[0], gfin=f(g_final).reshape(1, 1024), cst=cst, cmf=cmf, zsel=zsel, dlt=dlt)
    in_maps = []
    for b in range(8):
        m = dict(shared)
        m['x'] = np.ascontiguousarray(np.concatenate([x_prompt[b], x_sample[16 * b:16 * b + 16].reshape(64, 1024)], 0))
        m['p'] = np.ascontiguousarray(np.concatenate([p_prompt[0, b], p_sample[0, 16 * b:16 * b + 16].reshape(64, 256)], 0))
        m['sh'] = np.ascontiguousarray(state_hgrn[0, 16 * b:16 * b + 16])
        m['sd'] = np.ascontiguousarray(state_delta[0, 16 * b:16 * b + 16])
        m['scv'] = np.ascontiguousarray(state_conv[0, 16 * b:16 * b + 16].reshape(48, 1536))
        in_maps.append(m)
    res = run_bass_kernel_spmd(nc, in_maps, core_ids=list(range(8)))
    R = res.results
    y_prompt = np.stack([R[b]['y'][0:2048] for b in range(8)], 0)
    y_sample = np.concatenate([R[b]['y'][2048:2112].reshape(16, 4, 1024) for b in range(8)], 0)
    hp = np.stack([R[b]['hp'] for b in range(8)], 0)[None]
    dp = np.stack([R[b]['dp'] for b in range(8)], 0)[None]
    cp = np.stack([R[b]['cp'] for b in range(8)], 0)[None]
    hs = np.concatenate([R[b]['hs'] for b in range(8)], 0)[None]
    ds = np.concatenate([R[b]['ds'] for b in range(8)], 0)[None]
    cs = np.concatenate([R[b]['cs'].reshape(16, 3, 1536) for b in range(8)], 0)[None]
    _CACHE['dbg'] = R
    return tuple(np.ascontiguousarray(a.astype(np.float32)) for a in (y_prompt, y_sample, hp, dp, cp, hs, ds, cs))
```

```python
import numpy as np
from contextlib import ExitStack
import concourse.bass as bass
import concourse.mybir as mybir
from concourse.bass_utils import run_bass_kernel_spmd

F32 = mybir.dt.float32
BF16 = mybir.dt.bfloat16
I32 = mybir.dt.int32
U32 = mybir.dt.uint32
AF = mybir.ActivationFunctionType
ALU = mybir.AluOpType
AX = mybir.AxisListType

EPS = 1e-6
NPT = 32
NTOK = 2112
EPOCH = 12000
DMA_POOL = 8
DMA_EPOCH = 700
ARENA_COLS = 105984
NBUF = 6
NHB = 6
NUV = 8
NEG = -30000.0
DBG = set()
DBGT = 0
PHASES = ('A1', 'A2', 'B')
TILES = None


class Sched:
    def __init__(self, nc, es):
        self.nc = nc
        self.es = es
        self.eng = {'pe': nc.tensor, 'act': nc.scalar, 'dve': nc.vector,
                    'pool': nc.gpsimd, 'sp': nc.sync}
        self.prog = {e: [] for e in self.eng}
        self.cnt = {e: 0 for e in self.eng}
        self.sem = {}
        self.nsem = 0
        for e in self.eng:
            self.sem[e] = self._newsem(e)
        self.waited = {e: {} for e in self.eng}
        self.dpool = {}
        self.res_w = {}
        self.res_r = {}
        self.final_tokens = []
        self.pending = {e: [] for e in self.eng}
        self.cap = None

    def begin(self):
        self.cap = []

    def end(self):
        L = self.cap
        self.cap = None
        return L

    def run(self, L):
        for it in L:
            if it[0] == 'op':
                self.op(it[1], it[2], it[3], it[4])
            else:
                self.dma(it[1], it[2], it[3], it[4], it[5])

    def _newsem(self, name):
        self.nsem += 1
        return self.es.enter_context(self.nc.semaphore(f"s{self.nsem}_{name}"))

    def _need(self, e, tok, waits):
        if tok is None:
            return
        sem, val = tok[0], tok[1]
        if e == 'pe' and tok[2] == 'pe':
            return
        w = self.waited[e]
        if w.get(id(sem), 0) >= val:
            return
        w[id(sem)] = val
        waits.append((sem, val))

    def _deps(self, e, reads, writes, waits):
        for t in self.pending[e]:
            self._need(e, t, waits)
        self.pending[e] = []
        for k in reads:
            self._need(e, self.res_w.get(k), waits)
        for k in writes:
            self._need(e, self.res_w.get(k), waits)
            for t in self.res_r.get(k, ()):
                self._need(e, t, waits)

    def _commit(self, tok, reads, writes):
        for k in reads:
            self.res_r.setdefault(k, []).append(tok)
        for k in writes:
            self.res_w[k] = tok
            self.res_r[k] = []

    def op(self, e, fn, reads=(), writes=()):
        if self.cap is not None:
            self.cap.append(('op', e, fn, tuple(reads), tuple(writes)))
            return None
        waits = []
        self._deps(e, reads, writes, waits)
        if self.cnt[e] >= EPOCH:
            self.sem[e] = self._newsem(e)
            self.cnt[e] = 0
        self.cnt[e] += 1
        tok = (self.sem[e], self.cnt[e], e)
        self.prog[e].append((waits, fn, self.sem[e], 1))
        self._commit(tok, reads, writes)
        return tok

    def dma(self, e, fn, reads=(), writes=(), final=False):
        if self.cap is not None:
            self.cap.append(('dma', e, fn, tuple(reads), tuple(writes), final))
            return None
        waits = []
        self._deps(e, reads, writes, waits)
        pool = self.dpool.setdefault(e, {'sems': [], 'uses': [], 'i': 0})
        i = pool['i'] % DMA_POOL
        pool['i'] += 1
        if len(pool['sems']) <= i:
            pool['sems'].append(self._newsem(e + 'd'))
            pool['uses'].append(0)
        if pool['uses'][i] >= DMA_EPOCH:
            pool['sems'][i] = self._newsem(e + 'd')
            pool['uses'][i] = 0
        sem = pool['sems'][i]
        if pool['uses'][i] > 0:
            self._need(e, (sem, 16 * pool['uses'][i], 'dma'), waits)
        pool['uses'][i] += 1
        tok = (sem, 16 * pool['uses'][i], 'dma')
        self.prog[e].append((waits, fn, sem, 16))
        self._commit(tok, reads, writes)
        if final:
            self.final_tokens.append(tok)
        return tok

    def barrier(self):
        toks = []
        for e in self.eng:
            if self.cnt[e] > 0:
                toks.append((self.sem[e], self.cnt[e], e + '_bar'))
        for e, pool in self.dpool.items():
            for sem, u in zip(pool['sems'], pool['uses']):
                if u > 0:
                    toks.append((sem, 16 * u, 'dma'))
        for e in self.eng:
            self.pending[e] = list(toks)

    def emit(self):
        nc = self.nc
        fw = []
        for t in self.final_tokens:
            self._need('sp', t, fw)
        with nc.Block() as block:
            def run(e, engine):
                for waits, fn, sem, inc in self.prog[e]:
                    for (s, v) in waits:
                        engine.wait_ge(s, v)
                    fn(engine).then_inc(sem, inc)

            @block.tensor
            def _(eng):
                run('pe', eng)

            @block.scalar
            def _(eng):
                run('act', eng)

            @block.vector
            def _(eng):
                run('dve', eng)

            @block.gpsimd
            def _(eng):
                run('pool', eng)

            @block.sync
            def _(eng):
                run('sp', eng)
                for (s, v) in fw:
                    eng.wait_ge(s, v)


class Alloc:
    def __init__(self, arena, ncols):
        self.a = arena
        self.n = ncols
        self.off = 0

    def get(self, parts, free, dt):
        if isinstance(free, int):
            free = [free]
        nel = int(np.prod(free))
        cols = nel * (1 if dt == BF16 else 2)
        cols = (cols + 1) // 2 * 2
        o = self.off
        self.off += cols
        assert self.off <= self.n, f"arena overflow {self.off} > {self.n}"
        ap = self.a[0:parts, o:o + cols]
        if dt != BF16:
            ap = ap.bitcast(dt)
        if len(free) > 1:
            ds = [f"d{i}" for i in range(len(free))]
            kw = {ds[i]: free[i] for i in range(1, len(free))}
            ap = ap.rearrange(f"p ({' '.join(ds)}) -> p {' '.join(ds)}", **kw)
        return ap


def _tile_consts(nch, C):
    t = np.arange(64)
    ch = t // C
    same = ch[:, None] == ch[None, :]
    tri = (same & (t[:, None] <= t[None, :])).astype(np.float32)
    blk = same.astype(np.float32)
    nmT = np.where(tri > 0, 0.0, NEG).astype(np.float32)
    strict = same & (t[None, :] < t[:, None])
    pmS = np.where(strict, 0.0, -NEG).astype(np.float32)
    cind = np.zeros((64, 16), np.float32)
    cind[t, ch] = 1.0
    return tri, blk, np.tile(nmT, (1, 4)), np.tile(pmS, (1, 4)), cind


CST_COLS = {}


def _build_consts():
    cols = []
    off = [0]

    def add(name, arr):
        a = np.zeros((128, arr.shape[1]), np.float32)
        a[:arr.shape[0]] = arr
        CST_COLS[name] = (off[0], off[0] + arr.shape[1], arr.shape[0])
        off[0] += arr.shape[1]
        cols.append(a)

    add('ident', np.eye(128, dtype=np.float32))
    add('ones', np.ones((128, 128), np.float32))
    for nm, (nch, C) in (('p', (1, 64)), ('s', (16, 4))):
        tri, blk, nmT, pmS, cind = _tile_consts(nch, C)
        add('tri_' + nm, tri)
        add('blk_' + nm, blk)
        add('nmT_' + nm, nmT)
        add('pmS_' + nm, pmS)
        add('cind_' + nm, cind)
    add('iota16', np.tile(np.arange(16, dtype=np.float32)[None, :], (64, 1)))
    add('eps', np.full((128, 1), EPS, np.float32))
    add('one', np.ones((128, 1), np.float32))
    cst = np.concatenate(cols, axis=1)
    t = np.arange(64)
    cm = (t[None, :] // 4 == np.arange(16)[:, None]).astype(np.float32)
    cmf = np.tile(cm.reshape(1, 16 * 64), (128, 1))
    z = np.zeros((64, 64, 128), np.float32)
    z[t, t, :] = 1.0
    z = z.reshape(64, 64 * 128)
    dl = np.tile(np.eye(64, dtype=np.float32).reshape(1, 64 * 64), (128, 1))
    return cst, cmf, z, dl


def build_program():
    cst_np, _, _, _ = _build_consts()
    NCST = cst_np.shape[1]
    nc = bass.Bass("TRN2", target_bir_lowering=False)

    def din(name, shape, dt=F32):
        return nc.dram_tensor(name, shape, dt, kind="ExternalInput").ap()

    def dout(name, shape, dt=F32):
        return nc.dram_tensor(name, shape, dt, kind="ExternalOutput").ap()

    x_d = din("x", [NTOK, 1024])
    p_d = din("p", [NTOK, 256])
    sh_d = din("sh", [16, 4, 128, 128])
    sd_d = din("sd", [16, 4, 128, 128])
    scv_d = din("scv", [48, 1536])
    lbp_d = din("lbp", [2, 512])
    gmix_d = din("gmix", [1, 1024])
    win_d = din("w_in", [1024, 6152])
    convw_d = din("convw", [4, 1536])
    alog_d = din("alog", [1, 4])
    dtb_d = din("dtb", [1, 4])
    gna_d = din("gna", [1, 128])
    gnb_d = din("gnb", [1, 128])
    wbra_d = din("wbra", [512, 1024])
    wbrb_d = din("wbrb", [512, 1024])
    wout_d = din("wout", [1024, 1024])
    gffn_d = din("gffn", [1, 1024])
    wq_d = din("wq", [1024, 2048])
    keysT_d = din("keysT", [16, 128, 128])
    eu_d = din("eu", [16384, 1024])
    ev_d = din("ev", [16384, 1024])
    gple_d = din("gple", [1, 1024])
    wple_d = din("wple", [256, 1024])
    wpg_d = din("wpg", [1024, 1024])
    gfin_d = din("gfin", [1, 1024])
    cst_d = din("cst", [128, NCST])
    cmf_d = din("cmf", [128, 1024])
    zsel_d = din("zsel", [64, 8192])
    dlt_d = din("dlt", [128, 4096])

    y_d = dout("y", [NTOK, 1024])
    hp_d = dout("hp", [4, 128, 128])
    dp_d = dout("dp", [4, 128, 128])
    cp_d = dout("cp", [3, 1536])
    hs_d = dout("hs", [16, 4, 128, 128])
    ds_d = dout("ds", [16, 4, 128, 128])
    cs_d = dout("cs", [48, 1536])
    m1_d = dout("m1s", [NTOK, 1024])
    x1_d = dout("x1s", [NTOK, 1024])
    uvb_d = nc.dram_tensor("uvb", [16384, 2048], BF16, kind="Internal").ap()
    h2_d = nc.dram_tensor("h2s", [NTOK, 1024], BF16, kind="Internal").ap()

    es = ExitStack()
    with es:
        S = Sched(nc, es)

        def dbg(name, ap, key, ti=0, want=0):
            if name not in DBG or ti != want:
                return
            shp = list(ap.shape)
            dd = nc.dram_tensor("dbg_" + name, shp, ap.dtype, kind="ExternalOutput").ap()
            S.dma('sp', lambda e: e.dma_start(out=dd, in_=ap), reads=[key] if not isinstance(key, list) else key,
                  final=True)
        ARENA = es.enter_context(nc.sbuf_tensor("arena", [128, ARENA_COLS], BF16))
        PB = es.enter_context(nc.psum_tensor("pb", [128, 7 * 512], F32))
        PTt = es.enter_context(nc.psum_tensor("pt", [128, 1024], BF16))
        AL = Alloc(ARENA, ARENA_COLS)

        def bank(j, parts=128, n=512, off=0):
            return PB[0:parts, j * 512 + off:j * 512 + off + n]

        def bk(*js):
            return ['b%d' % j for j in js]

        CST = AL.get(128, NCST, F32)
        S.dma('sp', lambda e: e.dma_start(out=CST, in_=cst_d), writes=['cst'])

        def C_(name, parts=None):
            a, b, r = CST_COLS[name]
            return CST[0:(parts or r), a:b]

        identf = C_('ident')
        onesf = C_('ones')
        epsc = C_('eps')
        identb = AL.get(128, 128, BF16)
        S.op('dve', lambda e: e.tensor_copy(out=identb, in_=identf), reads=['cst'], writes=['identb'])
        TP = dict(nch=1, C=64, tri=C_('tri_p'), blk=C_('blk_p'), nmT=C_('nmT_p'), pmS=C_('pmS_p'),
                  cind=C_('cind_p'))
        TS = dict(nch=16, C=4, tri=C_('tri_s'), blk=C_('blk_s'), nmT=C_('nmT_s'), pmS=C_('pmS_s'),
                  cind=C_('cind_s'))
        cmb = AL.get(128, [16, 64], BF16)
        S.dma('pool', lambda e: e.dma_start(out=cmb, in_=cmf_d.rearrange("p (c t) -> p c t", c=16)),
              writes=['cmb'])
        wdummy = AL.get(128, 2, F32)
        pers_mark = AL.off

        def load_w_bf16(dst3, src, r0, nk, c0, ncols, key):
            for k in range(nk):
                for cc in range(0, ncols, 2048):
                    w = min(2048, ncols - cc)
                    S.dma('pool', lambda e, k=k, cc=cc, w=w: e.dma_start(
                        out=dst3[:, k, cc:cc + w],
                        in_=src[r0 + k * 128:r0 + (k + 1) * 128, c0 + cc:c0 + cc + w]), writes=[key])

        cast_rr = [0]

        def load_w_fast(dst3, src, r0, nk, c0, ncols, key, stage):
            for k in range(nk):
                for cc in range(0, ncols, 2048):
                    w = min(2048, ncols - cc)
                    i = cast_rr[0] % len(stage)
                    eng = ('act', 'dve', 'pool')[cast_rr[0] % 3]
                    cast_rr[0] += 1
                    st = stage[i]
                    skey = 'wstage%d' % i
                    S.dma('sp', lambda e, k=k, cc=cc, w=w, st=st: e.dma_start(
                        out=st[:, 0:w], in_=src[r0 + k * 128:r0 + (k + 1) * 128, c0 + cc:c0 + cc + w]),
                        writes=[skey])
                    if eng == 'act':
                        S.op('act', lambda e, k=k, cc=cc, w=w, st=st: e.copy(out=dst3[:, k, cc:cc + w], in_=st[:, 0:w]),
                             reads=[skey], writes=[(key, k, cc)])
                    else:
                        S.op(eng, lambda e, k=k, cc=cc, w=w, st=st: e.tensor_copy(out=dst3[:, k, cc:cc + w],
                                                                                  in_=st[:, 0:w]),
                             reads=[skey], writes=[(key, k, cc)])
            S.op('pool', lambda e: e.memset(wdummy, 0.0),
                 reads=[(key, k, cc) for k in range(nk) for cc in range(0, ncols, 2048)], writes=[key])

        def bcast_load(dst, src_row, parts, n, key):
            S.dma('sp', lambda e: e.dma_start(out=dst, in_=src_row.to_broadcast([parts, n])), writes=[key])

        def rmsnorm_to_bf16(xt, xkey, gb, gkey, hb, hkey, ss, rs):
            S.op('act', lambda e: e.activation(out=hb, in_=xt, func=AF.Square, accum_out=ss),
                 reads=[xkey], writes=[hkey, 'ss'])
            S.op('act', lambda e: e.activation(out=rs, in_=ss, func=AF.Sqrt, scale=1.0 / 1024, bias=epsc[0:64, :]),
                 reads=['ss', 'cst'], writes=['rs'])
            S.op('dve', lambda e: e.reciprocal(out=rs, in_=rs), reads=['rs'], writes=['rs'])
            S.op('dve', lambda e: e.scalar_tensor_tensor(out=hb, in0=xt, scalar=rs[:, 0:1], in1=gb,
                                                         op0=ALU.mult, op1=ALU.mult),
                 reads=[xkey, 'rs', gkey], writes=[hkey])

        def transpose_tm(src, skey, nblk, dst, dkey, eng='act', ptgt=None, pkey='pt'):
            if ptgt is None:
                ptgt = PTt
            for k in range(nblk):
                S.op('pe', lambda e, k=k: e.transpose(out=ptgt[:, k * 64:(k + 1) * 64],
                                                      in_=src[:, k * 128:(k + 1) * 128],
                                                      identity=identb[0:64, 0:64]),
                     reads=[skey, 'identb'], writes=[pkey])
            pv = ptgt[:, 0:nblk * 64].rearrange("p (k t) -> p k t", k=nblk)
            if eng == 'act':
                S.op('act', lambda e: e.copy(out=dst, in_=pv), reads=[pkey], writes=[dkey])
            else:
                S.op('dve', lambda e: e.tensor_copy(out=dst, in_=pv), reads=[pkey], writes=[dkey])

        def proj_tm(pout, pkeys, hT, hTkey, w3, wkey, c0, ncols, nk=8):
            for k in range(nk):
                S.op('pe', lambda e, k=k: e.matmul(pout, lhsT=hT[:, k, :], rhs=w3[:, k, c0:c0 + ncols],
                                                   start=(k == 0), stop=(k == nk - 1)),
                     reads=[hTkey, wkey], writes=pkeys)

        def head_norm_gate_keys(o_sb, okey, gnb_, gkey, gate_sb, gatekey, sq, sqkey, on, onkey, ogt_, ogtkey, ss4, rs4):
            o3 = o_sb.rearrange("p (h v) -> p h v", h=4)
            S.op('dve', lambda e: e.tensor_tensor(out=sq, in0=o_sb, in1=o_sb, op=ALU.mult),
                 reads=[okey], writes=[sqkey])
            S.op('dve', lambda e: e.reduce_sum(out=ss4, in_=sq.rearrange("p (h v) -> p h v", h=4), axis=AX.X),
                 reads=[sqkey], writes=['ss4'])
            S.op('act', lambda e: e.activation(out=rs4, in_=ss4, func=AF.Sqrt, scale=1.0 / 128, bias=epsc[0:64, :]),
                 reads=['ss4', 'cst'], writes=['rs4'])
            S.op('dve', lambda e: e.reciprocal(out=rs4, in_=rs4), reads=['rs4'], writes=['rs4'])
            on3 = on.rearrange("p (h v) -> p h v", h=4)
            S.op('dve', lambda e: e.tensor_tensor(out=on3, in0=o3, in1=rs4.unsqueeze(2).to_broadcast([64, 4, 128]),
                                                  op=ALU.mult), reads=[okey, 'rs4'], writes=[onkey])
            S.op('dve', lambda e: e.tensor_tensor(out=on3, in0=on3, in1=gnb_.unsqueeze(1).to_broadcast([64, 4, 128]),
                                                  op=ALU.mult), reads=[onkey, gkey], writes=[onkey])
            S.op('dve', lambda e: e.tensor_tensor(out=ogt_, in0=on, in1=gate_sb, op=ALU.mult),
                 reads=[onkey, gatekey], writes=[ogtkey])

        def phase_A1():
            AL.off = pers_mark
            winA = AL.get(128, [8, 2048], BF16)
            wgA = AL.get(128, [8, 1024], BF16)
            wbrA = AL.get(128, [4, 1024], BF16)
            stageA = [AL.get(128, 2048, F32) for _ in range(4)]
            load_w_fast(winA, win_d, 0, 8, 0, 2048, 'winA', stageA)
            load_w_fast(wgA, win_d, 0, 8, 4104, 1024, 'wgA', stageA)
            load_w_fast(wbrA, wbra_d, 0, 4, 0, 1024, 'wbrA', stageA)
            S.begin()
            for (src_t, c0) in ((eu_d, 0), (ev_d, 1024)):
                for r in range(0, 16384, 512):
                    S.dma('pool', lambda e, r=r, src_t=src_t, c0=c0: e.dma_start(
                        out=uvb_d[r:r + 512, c0:c0 + 1024], in_=src_t[r:r + 512, :]), writes=['uvb'])
            conv_dmas = S.end()
            gmixb = AL.get(64, 1024, F32)
            bcast_load(gmixb, gmix_d[0:1, :], 64, 1024, 'gmixb')
            lbb = AL.get(64, 512, F32)
            omlb = AL.get(64, 512, F32)
            lb1 = AL.get(64, 512, F32)
            bcast_load(lbb, lbp_d[0:1, :], 64, 512, 'lbb')
            bcast_load(lb1, lbp_d[1:2, :], 64, 512, 'lb1')
            S.op('dve', lambda e: e.tensor_tensor(out=lbb, in0=lbb, in1=lb1, op=ALU.subtract),
                 reads=['lbb', 'lb1'], writes=['lbb'])
            S.op('act', lambda e: e.activation(out=lbb, in_=lbb, func=AF.Sigmoid), reads=['lbb'], writes=['lbb'])
            S.op('dve', lambda e: e.tensor_scalar(out=omlb, in0=lbb, scalar1=-1.0, scalar2=1.0, op0=ALU.mult,
                                                  op1=ALU.add), reads=['lbb'], writes=['omlb'])
            gnab = AL.get(64, 128, F32)
            bcast_load(gnab, gna_d[0:1, :], 64, 128, 'gnab')
            xt = AL.get(64, 1024, F32)
            hb = AL.get(64, 1024, BF16)
            hT = AL.get(128, [8, 64], BF16)
            ss = AL.get(64, 1, F32)
            rs = AL.get(64, 1, F32)
            F = [AL.get(64, 512, F32) for _ in range(7)]
            qt = AL.get(64, 512, BF16)
            kt = AL.get(64, 512, BF16)
            va = AL.get(64, 512, BF16)
            km = AL.get(64, 512, BF16)
            qkT = AL.get(128, [8, 64], BF16)
            qm = AL.get(128, [4, 64], BF16)
            attm = AL.get(64, [4, 64], BF16)
            ebL = AL.get(128, [4, 16], F32)
            Sa = AL.get(128, [4, 128], F32)
            Sab = AL.get(128, [4, 128], BF16)
            SaL = [Sa, AL.get(128, [4, 128], F32)]
            SabL = [Sab, AL.get(128, [4, 128], BF16)]
            qmL = [qm, AL.get(128, [4, 64], BF16)]
            kmL = [km, AL.get(64, 512, BF16)]
            ss4 = AL.get(64, 4, F32)
            rs4 = AL.get(64, 4, F32)
            ogt = AL.get(64, 512, BF16)
            oT = AL.get(128, [4, 64], BF16)
            m1t = AL.get(64, 1024, F32)
            S.op('pool', lambda e: e.memset(Sa, 0.0), writes=['Sa0'])
            S.op('pool', lambda e: e.memset(Sab, 0.0), writes=['Sab0'])

            def tileA1(ti, T):
                nch = T['nch']
                r0 = ti * 64
                S.dma('sp', lambda e: e.dma_start(out=xt, in_=x_d[r0:r0 + 64, :]), writes=['xt'])
                rmsnorm_to_bf16(xt, 'xt', gmixb, 'gmixb', hb, 'hb', ss, rs)
                dbg('xt', xt, 'xt', ti)
                dbg('rs', rs, 'rs', ti)
                dbg('hb', hb, 'hb', ti)
                transpose_tm(hb, 'hb', 8, hT, 'hT')
                dbg('hT', hT, 'hT', ti)
                dbg('winA', winA[:, :, 0:512], 'winA', ti)
                sig, q, kk, logf, eb, enb, og = F
                proj_tm(bank(0, 64), bk(0), hT, 'hT', winA, 'winA', 512, 512)
                S.op('act', lambda e: e.activation(out=sig, in_=bank(0, 64), func=AF.Sigmoid), reads=bk(0), writes=['F0'])
                proj_tm(bank(1, 64), bk(1), hT, 'hT', winA, 'winA', 0, 512)
                S.op('act', lambda e: e.activation(out=q, in_=bank(1, 64), func=AF.Silu), reads=bk(1), writes=['F1'])
                S.op('dve', lambda e: e.tensor_tensor(out=sig, in0=sig, in1=omlb, op=ALU.mult),
                     reads=['F0', 'omlb'], writes=['F0'])
                S.op('dve', lambda e: e.tensor_tensor(out=sig, in0=sig, in1=lbb, op=ALU.add),
                     reads=['F0', 'lbb'], writes=['F0'])
                S.op('dve', lambda e: e.tensor_scalar(out=kk, in0=sig, scalar1=-1.0, scalar2=1.0, op0=ALU.mult,
                                                      op1=ALU.add), reads=['F0'], writes=['F2'])
                S.op('act', lambda e: e.activation(out=logf, in_=sig, func=AF.Ln), reads=['F0'], writes=['F3'])
                dbg('q', q, 'F1', ti)
                dbg('f', sig, 'F0', ti)
                dbg('logf', logf, 'F3', ti)
                S.op('pe', lambda e: e.matmul(bank(2, 64), lhsT=T['tri'], rhs=logf, start=True, stop=True),
                     reads=['cst', 'F3'], writes=bk(2))
                S.op('act', lambda e: e.activation(out=eb, in_=bank(2, 64), func=AF.Exp), reads=bk(2), writes=['F4'])
                S.op('act', lambda e: e.activation(out=enb, in_=bank(2, 64), func=AF.Exp, scale=-1.0),
                     reads=bk(2), writes=['F5'])
                S.op('dve', lambda e: e.tensor_tensor(out=qt, in0=q, in1=eb, op=ALU.mult),
                     reads=['F1', 'F4'], writes=['qt'])
                S.op('dve', lambda e: e.tensor_tensor(out=kt, in0=kk, in1=enb, op=ALU.mult),
                     reads=['F2', 'F5'], writes=['kt'])
                for h in range(4):
                    S.op('pe', lambda e, h=h: e.matmul(bank(0, 128, nch, h * 16), lhsT=logf[:, h * 128:(h + 1) * 128],
                                                       rhs=T['cind'][:, 0:nch], start=True, stop=True),
                         reads=['F3', 'cst'], writes=bk(0))
                S.op('act', lambda e: e.activation(out=ebL[:, :, 0:nch],
                                                   in_=bank(0, 128, 64).rearrange("p (h c) -> p h c", h=4)[:, :, 0:nch],
                                                   func=AF.Exp), reads=bk(0), writes=['ebL'])
                proj_tm(bank(1, 64), bk(1), hT, 'hT', winA, 'winA', 1024, 512)
                S.op('act', lambda e: e.copy(out=va, in_=bank(1, 64)), reads=bk(1), writes=['va'])
                proj_tm(bank(2, 64), bk(2), hT, 'hT', winA, 'winA', 1536, 512)
                S.op('act', lambda e: e.activation(out=og, in_=bank(2, 64), func=AF.Silu), reads=bk(2), writes=['F6'])
                for h in range(4):
                    S.op('pe', lambda e, h=h: e.transpose(out=PTt[:, h * 64:(h + 1) * 64], in_=qt[:, h * 128:(h + 1) * 128],
                                                          identity=identb[0:64, 0:64]),
                         reads=['qt', 'identb'], writes=['pt'])
                for h in range(4):
                    S.op('pe', lambda e, h=h: e.transpose(out=PTt[:, (4 + h) * 64:(5 + h) * 64],
                                                          in_=kt[:, h * 128:(h + 1) * 128], identity=identb[0:64, 0:64]),
                         reads=['kt', 'identb'], writes=['pt'])
                S.op('act', lambda e: e.copy(out=qkT, in_=PTt[:, 0:512].rearrange("p (k t) -> p k t", k=8)),
                     reads=['pt'], writes=['qkT'])
                for h in range(4):
                    S.op('pe', lambda e, h=h: e.matmul(bank(0, 64, 64, h * 64), lhsT=qkT[:, 4 + h, :], rhs=qkT[:, h, :],
                                                       start=True, stop=True), reads=['qkT'], writes=bk(0))
                S.op('dve', lambda e: e.tensor_tensor(out=attm, in0=bank(0, 64, 256).rearrange("p (h t) -> p h t", h=4),
                                                      in1=T['tri'].unsqueeze(1).to_broadcast([64, 4, 64]), op=ALU.mult),
                     reads=bk(0) + ['cst'], writes=['attm'])
                for h in range(4):
                    S.op('pe', lambda e, h=h: e.matmul(bank(5, 64, 128, h * 128), lhsT=attm[:, h, :],
                                                       rhs=va[:, h * 128:(h + 1) * 128], start=(h == 0), stop=False,
                                                       skip_group_check=True),
                         reads=['attm', 'va'], writes=bk(5))
                for c in range(nch):
                    pb = c % 2 if nch > 1 else 0
                    Sa_, Sab_, qm_, km_ = SaL[pb], SabL[pb], qmL[pb], kmL[pb]
                    sak, sabk, qmk, kmk = 'Sa%d' % pb, 'Sab%d' % pb, 'qm%d' % pb, 'km%d' % pb
                    if nch > 1:
                        S.dma('sp', lambda e, c=c, Sa_=Sa_: e.dma_start(out=Sa_, in_=sh_d[c].rearrange("h d v -> d h v")),
                              writes=[sak])
                        S.op('act', lambda e, Sa_=Sa_, Sab_=Sab_: e.copy(out=Sab_, in_=Sa_), reads=[sak], writes=[sabk])
                        S.op('dve', lambda e, c=c, qm_=qm_: e.tensor_tensor(
                            out=qm_, in0=qkT[:, 0:4, :], in1=cmb[:, c, :].unsqueeze(1).to_broadcast([128, 4, 64]),
                            op=ALU.mult), reads=['qkT', 'cmb'], writes=[qmk])
                        S.op('dve', lambda e, c=c, km_=km_: e.tensor_scalar(out=km_, in0=kt, scalar1=T['cind'][:, c:c + 1],
                                                                            scalar2=None, op0=ALU.mult),
                             reads=['kt', 'cst'], writes=[kmk])
                        qsrc, qkey, ksrc, kkey = qm_, qmk, km_, kmk
                    else:
                        qsrc, qkey, ksrc, kkey = qkT, 'qkT', kt, 'kt'
                    for h in range(4):
                        S.op('pe', lambda e, h=h, qsrc=qsrc, c=c, Sab_=Sab_: e.matmul(
                            bank(5, 64, 128, h * 128), lhsT=qsrc[:, h, :], rhs=Sab_[:, h, :],
                            start=False, stop=(c == nch - 1), skip_group_check=True), reads=[qkey, sabk], writes=bk(5))
                    for h in range(4):
                        S.op('pe', lambda e, h=h, ksrc=ksrc: e.matmul(
                            bank(6, 128, 128, h * 128), lhsT=ksrc[:, h * 128:(h + 1) * 128],
                            rhs=va[:, h * 128:(h + 1) * 128], start=True, stop=True),
                            reads=[kkey, 'va'], writes=bk(6))
                    S.op('dve', lambda e, Sa_=Sa_: e.tensor_tensor(out=Sa_, in0=bank(6).rearrange("p (h v) -> p h v", h=4),
                                                                   in1=Sa_, op=ALU.add), reads=bk(6) + [sak], writes=[sak])
                    S.op('dve', lambda e, c=c, Sa_=Sa_: e.tensor_tensor(
                        out=Sa_, in0=Sa_, in1=ebL[:, :, c:c + 1].to_broadcast([128, 4, 128]), op=ALU.mult),
                        reads=[sak, 'ebL'], writes=[sak])
                    if nch > 1:
                        S.dma('sp', lambda e, c=c, Sa_=Sa_: e.dma_start(out=hs_d[c].rearrange("h d v -> d h v"), in_=Sa_),
                              reads=[sak], final=True)
                    else:
                        S.op('act', lambda e, Sa_=Sa_, Sab_=Sab_: e.copy(out=Sab_, in_=Sa_), reads=[sak], writes=[sabk])
                if nch == 1 and ti == NPT - 1:
                    S.dma('sp', lambda e: e.dma_start(out=hp_d.rearrange("h d v -> d h v"), in_=Sa),
                          reads=['Sa0'], final=True)
                osb, sq, on = F[0], F[1], F[2]
                S.op('act', lambda e: e.copy(out=osb, in_=bank(5, 64)), reads=bk(5), writes=['F0'])
                dbg('osb', osb, 'F0', ti, DBGT)
                dbg('og', og, 'F6', ti, DBGT)
                head_norm_gate_keys(osb, 'F0', gnab, 'gnab', og, 'F6', sq, 'F1', on, 'F2', ogt, 'ogt', ss4, rs4)
                dbg('on', on, 'F2', ti, DBGT)
                transpose_tm(ogt, 'ogt', 4, oT, 'oT')
                for half in range(2):
                    for k in range(4):
                        S.op('pe', lambda e, k=k, half=half: e.matmul(
                            bank(3 + half, 64), lhsT=oT[:, k, :], rhs=wbrA[:, k, half * 512:(half + 1) * 512],
                            start=(k == 0), stop=(k == 3)), reads=['oT', 'wbrA'], writes=bk(3 + half))
                    proj_tm(bank(half, 64), bk(half), hT, 'hT', wgA, 'wgA', half * 512, 512)
                    sg = F[3 + half]
                    S.op('act', lambda e, half=half, sg=sg: e.activation(out=sg, in_=bank(half, 64), func=AF.Sigmoid),
                         reads=bk(half), writes=['F%d' % (3 + half)])
                    S.op('dve', lambda e, half=half, sg=sg: e.tensor_tensor(
                        out=m1t[:, half * 512:(half + 1) * 512], in0=bank(3 + half, 64), in1=sg, op=ALU.mult),
                        reads=bk(3 + half) + ['F%d' % (3 + half)], writes=['m1t'])
                S.dma('sp', lambda e: e.dma_start(out=m1_d[r0:r0 + 64, :], in_=m1t), reads=['m1t'],
                      writes=[('m1', ti)])

            if 'A1' in PHASES:
                for ti in (range(NPT) if TILES is None else TILES):
                    tileA1(ti, TP)
                    S.run(conv_dmas[0:2])
                    del conv_dmas[0:2]
                tileA1(NPT, TS)
            S.run(conv_dmas)
        phase_A1()
        S.barrier()

        def phase_A2():
            AL.off = pers_mark
            winB = AL.get(128, [8, 2056], BF16)
            wgB = AL.get(128, [8, 1024], BF16)
            wbrB = AL.get(128, [4, 1024], BF16)
            woutb = AL.get(128, [8, 1024], BF16)
            stageB = [AL.get(128, 2048, F32) for _ in range(4)]
            load_w_fast(winB, win_d, 0, 8, 2048, 2056, 'winB', stageB)
            load_w_fast(wgB, win_d, 0, 8, 5128, 1024, 'wgB', stageB)
            load_w_fast(wbrB, wbrb_d, 0, 4, 0, 1024, 'wbrB', stageB)
            load_w_fast(woutb, wout_d, 0, 8, 0, 1024, 'woutb', stageB)
            gmixb = AL.get(64, 1024, F32)
            bcast_load(gmixb, gmix_d[0:1, :], 64, 1024, 'gmixb')
            gnbb = AL.get(64, 128, F32)
            bcast_load(gnbb, gnb_d[0:1, :], 64, 128, 'gnbb')
            negA = AL.get(64, 4, F32)
            dtbb = AL.get(64, 4, F32)
            bcast_load(negA, alog_d[0:1, :], 64, 4, 'negA')
            bcast_load(dtbb, dtb_d[0:1, :], 64, 4, 'dtbb')
            S.op('act', lambda e: e.activation(out=negA, in_=negA, func=AF.Exp), reads=['negA'], writes=['negA'])
            S.op('dve', lambda e: e.tensor_scalar(out=negA, in0=negA, scalar1=-1.0, scalar2=None, op0=ALU.mult),
                 reads=['negA'], writes=['negA'])
            cwin = AL.get(4, 1536, F32)
            cw = AL.get(128, [12, 4], F32)
            S.dma('sp', lambda e: e.dma_start(out=cwin, in_=convw_d), writes=['cwin'])
            for g in range(12):
                S.op('pe', lambda e, g=g: e.transpose(out=bank(0, 128, 4, g * 4), in_=cwin[0:4, g * 128:(g + 1) * 128],
                                                      identity=identf[0:4, 0:4]), reads=['cwin', 'cst'], writes=bk(0))
            S.op('dve', lambda e: e.tensor_copy(out=cw, in_=bank(0, 128, 48).rearrange("p (g j) -> p g j", g=12)),
                 reads=bk(0), writes=['cw'])
            xt = AL.get(64, 1024, F32)
            hb = AL.get(64, 1024, BF16)
            hT = AL.get(128, [8, 64], BF16)
            ss = AL.get(64, 1, F32)
            rs = AL.get(64, 1, F32)
            F = [AL.get(64, 512, F32) for _ in range(6)]
            rawext = AL.get(128, 12 * 112, F32)
            rawtm = AL.get(64, 1536, F32)
            ctmp = AL.get(128, [12, 64], F32)
            acc = AL.get(128, [12, 64], F32)
            sqn = AL.get(128, 512, F32)
            qkn = AL.get(128, [8, 64], BF16)
            qknm = AL.get(128, [8, 64], BF16)
            vcb = AL.get(128, [4, 64], BF16)
            vk = AL.get(64, [8, 128], BF16)
            sm = AL.get(64, 64, F32)
            beta, zz, ez, spl, gg, gcum, gLb, eg, ngcum, dkk, kdec, nbeta, nbe = [sm[:, i * 4:(i + 1) * 4] for i in range(13)]
            gc = AL.get(64, [4, 16], F32)
            egLT = AL.get(128, [4, 16], F32)
            gd = AL.get(64, [4, 64], F32)
            gam = AL.get(64, [4, 64], F32)
            gamT = AL.get(64, [4, 64], F32)
            Nb = [AL.get(64, [4, 64], BF16) for _ in range(2)]
            Mb = [AL.get(64, [4, 64], BF16) for _ in range(2)]
            Qb = AL.get(64, [4, 64], BF16)
            Qf = AL.get(64, [4, 64], F32)
            rb = AL.get(64, [4, 128], BF16)
            ub = AL.get(64, [4, 128], BF16)
            ATb = AL.get(64, [4, 64], BF16)
            khat = AL.get(64, [4, 128], BF16)
            khm = AL.get(64, [4, 128], BF16)
            Sd = AL.get(128, [4, 128], F32)
            Sdb = AL.get(128, [4, 128], BF16)
            SdL = [Sd, AL.get(128, [4, 128], F32)]
            SdbL = [Sdb, AL.get(128, [4, 128], BF16)]
            qknmL = [qknm, AL.get(128, [8, 64], BF16)]
            khmL = [khm, AL.get(64, [4, 128], BF16)]
            ss4 = AL.get(64, 4, F32)
            rs4 = AL.get(64, 4, F32)
            ogt = AL.get(64, 512, BF16)
            oT = AL.get(128, [4, 64], BF16)
            m1t = AL.get(64, 1024, F32)
            mb = AL.get(64, 1024, BF16)
            mT = AL.get(128, [8, 64], BF16)
            S.op('pool', lambda e: e.memset(Sd, 0.0), writes=['Sd0'])
            S.op('pool', lambda e: e.memset(Sdb, 0.0), writes=['Sdb0'])
            S.op('pool', lambda e: e.memset(rawext, 0.0), writes=['rawext'])

            def tileA2(ti, T):
                nch, C = T['nch'], T['C']
                r0 = ti * 64
                W = 3 + C
                rx = rawext[:, 0:12 * nch * W].rearrange("p (g s w) -> p g s w", g=12, s=nch)
                S.dma('sp', lambda e: e.dma_start(out=xt, in_=x_d[r0:r0 + 64, :]), writes=['xt'])
                S.dma('sp', lambda e: e.dma_start(out=m1t, in_=m1_d[r0:r0 + 64, :]), reads=[('m1', ti)], writes=['m1t'])
                rmsnorm_to_bf16(xt, 'xt', gmixb, 'gmixb', hb, 'hb', ss, rs)
                transpose_tm(hb, 'hb', 8, hT, 'hT')
                praw = PB[:, 3 * 512:3 * 512 + 768].rearrange("p (g t) -> p g t", g=12)
                for i3 in range(3):
                    proj_tm(bank(i3, 64), bk(i3), hT, 'hT', winB, 'winB', i3 * 512, 512)
                    S.op('act', lambda e, i3=i3: e.copy(out=rawtm[:, i3 * 512:(i3 + 1) * 512], in_=bank(i3, 64)),
                         reads=bk(i3), writes=[('rawtm', i3)])
                for g in range(12):
                    S.op('pe', lambda e, g=g: e.transpose(out=praw[:, g, :], in_=rawtm[:, g * 128:(g + 1) * 128],
                                                          identity=identf[0:64, 0:64]),
                         reads=[('rawtm', g // 4), 'cst'], writes=bk(3, 4))
                if nch == 1:
                    if ti > 0:
                        S.op('pool', lambda e: e.tensor_copy(out=rx[:, :, 0, 0:3], in_=rx[:, :, 0, 64:67]),
                             reads=['rawext'], writes=['rawext'])
                    S.op('act', lambda e: e.copy(out=rx[:, :, 0, 3:67], in_=praw), reads=bk(3, 4), writes=['rawext'])
                else:
                    cvin = F[0:3]
                    for i3 in range(3):
                        S.dma('sp', lambda e, i3=i3: e.dma_start(out=cvin[i3][0:48, :],
                                                                 in_=scv_d[:, i3 * 512:(i3 + 1) * 512]),
                              writes=['F%d' % i3])
                    for g in range(12):
                        S.op('pe', lambda e, g=g: e.transpose(
                            out=bank(0, 128, 48, g * 64), in_=cvin[g // 4][0:48, (g % 4) * 128:(g % 4 + 1) * 128],
                            identity=identf[0:48, 0:48]), reads=['F%d' % (g // 4), 'cst'], writes=bk(0, 1))
                    S.op('dve', lambda e: e.tensor_copy(
                        out=rx[:, :, :, 0:3],
                        in_=PB[:, 0:768].rearrange("p (g x) -> p g x", g=12)[:, :, 0:48].rearrange(
                            "p g (s j) -> p g s j", s=16)),
                        reads=bk(0, 1), writes=['rawext'])
                    S.op('act', lambda e: e.copy(out=rx[:, :, :, 3:7],
                                                 in_=praw.rearrange("p g (s t) -> p g s t", s=16)),
                         reads=bk(3, 4), writes=['rawext'])
                accv = acc.rearrange("p g (s t) -> p g s t", s=nch)
                tmpv = ctmp.rearrange("p g (s t) -> p g s t", s=nch)
                for j in range(4):
                    dst = accv if j == 0 else tmpv
                    dkey = [('acc', g) for g in range(12)] if j == 0 else ['ctmp']
                    S.op('dve', lambda e, j=j, dst=dst: e.tensor_tensor(
                        out=dst, in0=rx[:, :, :, j:j + C],
                        in1=cw[:, :, j:j + 1].unsqueeze(3).to_broadcast([128, 12, nch, C]), op=ALU.mult),
                        reads=['rawext', 'cw'], writes=dkey)
                    if j > 0:
                        S.op('dve', lambda e: e.tensor_tensor(out=acc, in0=acc, in1=ctmp, op=ALU.add),
                             reads=[('acc', g) for g in range(12)] + ['ctmp'], writes=[('acc', g) for g in range(12)])
                acck = [('acc', g) for g in range(12)]
                S.op('act', lambda e: e.activation(out=acc, in_=acc, func=AF.Silu), reads=acck, writes=acck)
                S.op('act', lambda e: e.activation(out=sqn, in_=acc[:, 0:8, :].rearrange("p g t -> p (g t)"),
                                                   func=AF.Square), reads=acck, writes=['sqn'])
                S.op('pe', lambda e: e.matmul(bank(0), lhsT=onesf, rhs=sqn, start=True, stop=True),
                     reads=['cst', 'sqn'], writes=bk(0))
                S.op('act', lambda e: e.activation(out=sqn, in_=bank(0), func=AF.Sqrt, bias=epsc), reads=bk(0) + ['cst'],
                     writes=['sqn'])
                S.op('dve', lambda e: e.reciprocal(out=sqn, in_=sqn), reads=['sqn'], writes=['sqn'])
                sq3 = sqn.rearrange("p (g t) -> p g t", g=8)
                S.op('dve', lambda e: e.scalar_tensor_tensor(out=qkn[:, 0:4, :], in0=acc[:, 0:4, :], scalar=128.0 ** -0.5,
                                                             in1=sq3[:, 0:4, :], op0=ALU.mult, op1=ALU.mult),
                     reads=acck + ['sqn'], writes=['qkn'])
                S.op('dve', lambda e: e.tensor_tensor(out=qkn[:, 4:8, :], in0=acc[:, 4:8, :], in1=sq3[:, 4:8, :],
                                                      op=ALU.mult), reads=acck + ['sqn'], writes=['qkn'])
                S.op('act', lambda e: e.copy(out=vcb, in_=acc[:, 8:12, :]), reads=acck, writes=['vcb'])
                for h in range(4):
                    S.op('pe', lambda e, h=h: e.transpose(out=PTt[0:64, h * 128:(h + 1) * 128], in_=vcb[:, h, :],
                                                          identity=identb), reads=['vcb', 'identb'], writes=['pt'])
                for h in range(4):
                    S.op('pe', lambda e, h=h: e.transpose(out=PTt[0:64, (4 + h) * 128:(5 + h) * 128], in_=qkn[:, 4 + h, :],
                                                          identity=identb), reads=['qkn', 'identb'], writes=['pt'])
                S.op('act', lambda e: e.copy(out=vk, in_=PTt[0:64, :].rearrange("p (k v) -> p k v", k=8)),
                     reads=['pt'], writes=['vk'])
                proj_tm(bank(1, 64, 8), bk(1), hT, 'hT', winB, 'winB', 2048, 8)
                S.op('act', lambda e: e.activation(out=beta, in_=bank(1, 64, 4), func=AF.Sigmoid), reads=bk(1), writes=['sm'])
                S.op('dve', lambda e: e.tensor_tensor(out=zz, in0=bank(1, 64, 4, 4), in1=dtbb, op=ALU.add),
                     reads=bk(1) + ['dtbb'], writes=['sm'])
                S.op('act', lambda e: e.activation(out=ez, in_=zz, func=AF.Exp), reads=['sm'], writes=['sm'])
                S.op('act', lambda e: e.activation(out=spl, in_=ez, func=AF.Ln, bias=C_('one', 64)), reads=['sm', 'cst'],
                     writes=['sm'])
                S.op('dve', lambda e: e.tensor_tensor(out=gg, in0=spl, in1=negA, op=ALU.mult), reads=['sm', 'negA'],
                     writes=['sm'])
                S.op('pe', lambda e: e.matmul(bank(2, 64, 4), lhsT=T['tri'], rhs=gg, start=True, stop=True),
                     reads=['cst', 'sm'], writes=bk(2))
                S.op('pe', lambda e: e.matmul(bank(2, 64, 4, 4), lhsT=T['blk'], rhs=gg, start=True, stop=True),
                     reads=['cst', 'sm'], writes=bk(2))
                S.op('dve', lambda e: e.tensor_copy(out=sm[:, 20:28], in_=bank(2, 64, 8)), reads=bk(2), writes=['sm'])
                S.op('act', lambda e: e.activation(out=eg, in_=gcum, func=AF.Exp), reads=['sm'], writes=['sm'])
                S.op('dve', lambda e: e.tensor_scalar(out=ngcum, in0=gcum, scalar1=-1.0, scalar2=None, op0=ALU.mult),
                     reads=['sm'], writes=['sm'])
                S.op('dve', lambda e: e.tensor_tensor(out=dkk, in0=gLb, in1=gcum, op=ALU.subtract), reads=['sm'],
                     writes=['sm'])
                S.op('act', lambda e: e.activation(out=kdec, in_=dkk, func=AF.Exp), reads=['sm'], writes=['sm'])
                S.op('dve', lambda e: e.tensor_scalar(out=nbeta, in0=beta, scalar1=-1.0, scalar2=None, op0=ALU.mult),
                     reads=['sm'], writes=['sm'])
                S.op('dve', lambda e: e.tensor_tensor(out=nbe, in0=nbeta, in1=eg, op=ALU.mult), reads=['sm'],
                     writes=['sm'])
                S.op('dve', lambda e: e.tensor_tensor(out=gc[:, :, 0:nch], in0=gg.unsqueeze(2).to_broadcast([64, 4, nch]),
                                                      in1=T['cind'][:, 0:nch].unsqueeze(1).to_broadcast([64, 4, nch]),
                                                      op=ALU.mult), reads=['sm', 'cst'], writes=['gc'])
                for h in range(4):
                    S.op('pe', lambda e, h=h: e.matmul(bank(1, 128, nch, 64 + h * 16), lhsT=onesf[0:64, :],
                                                       rhs=gc[:, h, 0:nch], start=True, stop=True),
                         reads=['cst', 'gc'], writes=bk(1))
                S.op('act', lambda e: e.activation(
                    out=egLT[:, :, 0:nch], in_=bank(1, 128, 64, 64).rearrange("p (h c) -> p h c", h=4)[:, :, 0:nch],
                    func=AF.Exp), reads=bk(1), writes=['egLT'])
                S.op('dve', lambda e: e.tensor_tensor(out=gd, in0=gcum.unsqueeze(2).to_broadcast([64, 4, 64]),
                                                      in1=identf[0:64, 0:64].unsqueeze(1).to_broadcast([64, 4, 64]),
                                                      op=ALU.mult), reads=['sm', 'cst'], writes=['gd'])
                gd2 = gd.rearrange("p h t -> p (h t)")
                S.op('pe', lambda e: e.matmul(bank(2, 64, 256), lhsT=onesf[0:64, 0:64], rhs=gd2, start=True, stop=False),
                     reads=['cst', 'gd'], writes=bk(2))
                S.op('pe', lambda e: e.matmul(bank(2, 64, 256), lhsT=identf[0:64, 0:64], rhs=T['pmS'], start=False,
                                              stop=True), reads=['cst'], writes=bk(2))
                S.op('pe', lambda e: e.matmul(bank(2, 64, 256, 256), lhsT=onesf[0:64, 0:64], rhs=gd2, start=True,
                                              stop=False), reads=['cst', 'gd'], writes=bk(2))
                S.op('pe', lambda e: e.matmul(bank(2, 64, 256, 256), lhsT=identf[0:64, 0:64], rhs=T['nmT'], start=False,
                                              stop=True), reads=['cst'], writes=bk(2))
                for h in range(4):
                    S.op('act', lambda e, h=h: e.activation(out=gam[:, h, :], in_=bank(2, 64, 64, h * 64), func=AF.Exp,
                                                            scale=-1.0, bias=gcum[:, h:h + 1]),
                         reads=bk(2) + ['sm'], writes=['gam'])
                for h in range(4):
                    S.op('act', lambda e, h=h: e.activation(out=gamT[:, h, :], in_=bank(2, 64, 64, 256 + h * 64),
                                                            func=AF.Exp, bias=ngcum[:, h:h + 1]),
                         reads=bk(2) + ['sm'], writes=['gamT'])
                for h in range(4):
                    S.op('pe', lambda e, h=h: e.matmul(bank(0, 64, 64, h * 64), lhsT=qkn[:, 4 + h, :], rhs=qkn[:, 4 + h, :],
                                                       start=True, stop=True), reads=['qkn'], writes=bk(0))
                for h in range(4):
                    S.op('dve', lambda e, h=h: e.scalar_tensor_tensor(
                        out=Nb[0][:, h, :], in0=bank(0, 64, 64, h * 64), scalar=nbeta[:, h:h + 1], in1=gam[:, h, :],
                        op0=ALU.mult, op1=ALU.mult), reads=bk(0) + ['sm', 'gam'], writes=['Nb0'])
                for h in range(4):
                    S.op('pe', lambda e, h=h: e.transpose(out=PTt[0:64, h * 64:(h + 1) * 64], in_=Nb[0][:, h, :],
                                                          identity=identb[0:64, 0:64]),
                         reads=['Nb0', 'identb'], writes=['pt'])
                ptv = PTt[0:64, 0:256].rearrange("p (h t) -> p h t", h=4)
                S.op('dve', lambda e: e.tensor_copy(out=Mb[0], in_=ptv), reads=['pt'], writes=['Mb0'])
                S.op('dve', lambda e: e.tensor_tensor(out=Qf, in0=ptv,
                                                      in1=identf[0:64, 0:64].unsqueeze(1).to_broadcast([64, 4, 64]),
                                                      op=ALU.add), reads=['pt', 'cst'], writes=['Qf'])
                S.op('act', lambda e: e.copy(out=Qb, in_=Qf), reads=['Qf'], writes=['Qb'])
                nsteps = {64: 5, 4: 1}[C]
                cur = 0
                for i in range(nsteps):
                    last = (i == nsteps - 1)
                    nxt = 1 - cur
                    for h in range(4):
                        S.op('pe', lambda e, h=h, cur=cur: e.matmul(bank(0, 64, 64, h * 64), lhsT=Mb[cur][:, h, :],
                                                                    rhs=Nb[cur][:, h, :], start=True, stop=True),
                             reads=['Mb%d' % cur, 'Nb%d' % cur], writes=bk(0))
                    if not last:
                        for h in range(4):
                            S.op('pe', lambda e, h=h, cur=cur: e.matmul(bank(1, 64, 64, h * 64), lhsT=Nb[cur][:, h, :],
                                                                        rhs=Mb[cur][:, h, :], start=True, stop=True),
                                 reads=['Mb%d' % cur, 'Nb%d' % cur], writes=bk(1))
                    S.op('act', lambda e, nxt=nxt: e.copy(out=Nb[nxt],
                                                          in_=bank(0, 64, 256).rearrange("p (h t) -> p h t", h=4)),
                         reads=bk(0), writes=['Nb%d' % nxt])
                    if not last:
                        S.op('dve', lambda e, nxt=nxt: e.tensor_copy(
                            out=Mb[nxt], in_=bank(1, 64, 256).rearrange("p (h t) -> p h t", h=4)),
                            reads=bk(1), writes=['Mb%d' % nxt])
                    for h in range(4):
                        S.op('pe', lambda e, h=h, nxt=nxt: e.matmul(bank(2, 64, 64, h * 64), lhsT=Nb[nxt][:, h, :],
                                                                    rhs=Qb[:, h, :], start=True, stop=True),
                             reads=['Nb%d' % nxt, 'Qb'], writes=bk(2))
                    S.op('dve', lambda e: e.tensor_tensor(out=Qf, in0=Qf,
                                                          in1=bank(2, 64, 256).rearrange("p (h t) -> p h t", h=4),
                                                          op=ALU.add), reads=['Qf'] + bk(2), writes=['Qf'])
                    S.op('act', lambda e: e.copy(out=Qb, in_=Qf), reads=['Qf'], writes=['Qb'])
                    cur = nxt
                for h in range(4):
                    S.op('pe', lambda e, h=h: e.matmul(bank(0, 64, 64, h * 64), lhsT=qkn[:, 4 + h, :], rhs=qkn[:, h, :],
                                                       start=True, stop=True), reads=['qkn'], writes=bk(0))
                S.op('dve', lambda e: e.tensor_tensor(out=ATb, in0=bank(0, 64, 256).rearrange("p (h t) -> p h t", h=4),
                                                      in1=gamT, op=ALU.mult), reads=bk(0) + ['gamT'], writes=['ATb'])
                for c in range(nch):
                    pb = c % 2 if nch > 1 else 0
                    Sd_, Sdb_, qknm_ = SdL[pb], SdbL[pb], qknmL[pb]
                    sdk, sdbk, qnk = 'Sd%d' % pb, 'Sdb%d' % pb, 'qknm%d' % pb
                    if nch > 1:
                        S.dma('sp', lambda e, c=c, Sd_=Sd_: e.dma_start(out=Sd_, in_=sd_d[c].rearrange("h d v -> d h v")),
                              writes=[sdk])
                        S.op('act', lambda e, Sd_=Sd_, Sdb_=Sdb_: e.copy(out=Sdb_, in_=Sd_), reads=[sdk], writes=[sdbk])
                        S.op('dve', lambda e, c=c, qknm_=qknm_: e.tensor_tensor(
                            out=qknm_, in0=qkn, in1=cmb[:, c, :].unsqueeze(1).to_broadcast([128, 8, 64]), op=ALU.mult),
                            reads=['qkn', 'cmb'], writes=[qnk])
                        src, skey = qknm_, qnk
                    else:
                        src, skey = qkn, 'qkn'
                    for h in range(4):
                        S.op('pe', lambda e, h=h, src=src, c=c, Sdb_=Sdb_: e.matmul(
                            bank(5, 64, 128, h * 128), lhsT=src[:, 4 + h, :], rhs=Sdb_[:, h, :], start=(c == 0 and h == 0),
                            stop=(c == nch - 1), skip_group_check=True), reads=[skey, sdbk], writes=bk(5))
                        S.op('pe', lambda e, h=h, src=src, c=c, Sdb_=Sdb_: e.matmul(
                            bank(6, 64, 128, h * 128), lhsT=src[:, h, :], rhs=Sdb_[:, h, :], start=(c == 0 and h == 0),
                            stop=(c == nch - 1), skip_group_check=True), reads=[skey, sdbk], writes=bk(6))
                t1, bv, t2, ob = F[3], F[4], F[3], F[4]
                S.op('dve', lambda e: e.tensor_tensor(out=t1.rearrange("p (h v) -> p h v", h=4),
                                                      in0=bank(5, 64).rearrange("p (h v) -> p h v", h=4),
                                                      in1=nbe.unsqueeze(2).to_broadcast([64, 4, 128]), op=ALU.mult),
                     reads=bk(5) + ['sm'], writes=['F3'])
                S.op('dve', lambda e: e.tensor_tensor(out=bv.rearrange("p (h v) -> p h v", h=4), in0=vk[:, 0:4, :],
                                                      in1=beta.unsqueeze(2).to_broadcast([64, 4, 128]), op=ALU.mult),
                     reads=['vk', 'sm'], writes=['F4'])
                S.op('dve', lambda e: e.tensor_tensor(out=rb.rearrange("p h v -> p (h v)"), in0=t1, in1=bv, op=ALU.add),
                     reads=['F3', 'F4'], writes=['rb'])
                for h in range(4):
                    S.op('pe', lambda e, h=h: e.matmul(bank(1, 64, 128, h * 128), lhsT=Qb[:, h, :], rhs=rb[:, h, :],
                                                       start=True, stop=True), reads=['Qb', 'rb'], writes=bk(1))
                S.op('act', lambda e: e.copy(out=ub, in_=bank(1, 64).rearrange("p (h v) -> p h v", h=4)),
                     reads=bk(1), writes=['ub'])
                for h in range(4):
                    S.op('pe', lambda e, h=h: e.matmul(bank(2, 64, 128, h * 128), lhsT=ATb[:, h, :], rhs=ub[:, h, :],
                                                       start=True, stop=True), reads=['ATb', 'ub'], writes=bk(2))
                S.op('dve', lambda e: e.tensor_tensor(out=t2.rearrange("p (h v) -> p h v", h=4),
                                                      in0=bank(6, 64).rearrange("p (h v) -> p h v", h=4),
                                                      in1=eg.unsqueeze(2).to_broadcast([64, 4, 128]), op=ALU.mult),
                     reads=bk(6) + ['sm'], writes=['F3'])
                S.op('dve', lambda e: e.tensor_tensor(out=ob, in0=t2, in1=bank(2, 64), op=ALU.add),
                     reads=['F3'] + bk(2), writes=['F4'])
                S.op('dve', lambda e: e.tensor_tensor(out=khat, in0=vk[:, 4:8, :],
                                                      in1=kdec.unsqueeze(2).to_broadcast([64, 4, 128]), op=ALU.mult),
                     reads=['vk', 'sm'], writes=['khat'])
                for c in range(nch):
                    pb = c % 2 if nch > 1 else 0
                    Sd_, Sdb_, khm_ = SdL[pb], SdbL[pb], khmL[pb]
                    sdk, sdbk, khk = 'Sd%d' % pb, 'Sdb%d' % pb, 'khm%d' % pb
                    if nch > 1:
                        S.dma('sp', lambda e, c=c, Sd_=Sd_: e.dma_start(out=Sd_, in_=sd_d[c].rearrange("h d v -> d h v")),
                              writes=[sdk])
                        S.op('dve', lambda e, c=c, khm_=khm_: e.tensor_scalar(out=khm_, in0=khat,
                                                                              scalar1=T['cind'][:, c:c + 1],
                                                                              scalar2=None, op0=ALU.mult),
                             reads=['khat', 'cst'], writes=[khk])
                        ks_, kkey = khm_, khk
                    else:
                        ks_, kkey = khat, 'khat'
                    for h in range(4):
                        S.op('pe', lambda e, h=h, ks_=ks_: e.matmul(bank(0, 128, 128, h * 128), lhsT=ks_[:, h, :],
                                                                    rhs=ub[:, h, :], start=True, stop=True),
                             reads=[kkey, 'ub'], writes=bk(0))
                    S.op('dve', lambda e, c=c, Sd_=Sd_: e.tensor_tensor(
                        out=Sd_, in0=Sd_, in1=egLT[:, :, c:c + 1].to_broadcast([128, 4, 128]), op=ALU.mult),
                        reads=[sdk, 'egLT'], writes=[sdk])
                    S.op('dve', lambda e, Sd_=Sd_: e.tensor_tensor(out=Sd_, in0=Sd_,
                                                                   in1=bank(0).rearrange("p (h v) -> p h v", h=4),
                                                                   op=ALU.add), reads=[sdk] + bk(0), writes=[sdk])
                    if nch > 1:
                        S.dma('sp', lambda e, c=c, Sd_=Sd_: e.dma_start(out=ds_d[c].rearrange("h d v -> d h v"), in_=Sd_),
                              reads=[sdk], final=True)
                    else:
                        S.op('act', lambda e, Sd_=Sd_, Sdb_=Sdb_: e.copy(out=Sdb_, in_=Sd_), reads=[sdk], writes=[sdbk])
                if nch == 1 and ti == NPT - 1:
                    S.dma('sp', lambda e: e.dma_start(out=dp_d.rearrange("h d v -> d h v"), in_=Sd),
                          reads=['Sd0'], final=True)
                if nch > 1 or ti == NPT - 1:
                    rk = [('rawtm', i3) for i3 in range(3)]
                    if nch == 1:
                        S.dma('sp', lambda e: e.dma_start(out=cp_d, in_=rawtm[61:64, :]), reads=rk, final=True)
                    else:
                        for sq_ in range(16):
                            S.dma('sp', lambda e, sq_=sq_: e.dma_start(
                                out=cs_d[3 * sq_:3 * sq_ + 3, :], in_=rawtm[4 * sq_ + 1:4 * sq_ + 4, :]),
                                reads=rk, final=True)
                proj_tm(bank(0, 64), bk(0), hT, 'hT', winB, 'winB', 1536, 512)
                S.op('act', lambda e: e.activation(out=F[5], in_=bank(0, 64), func=AF.Silu), reads=bk(0), writes=['F5'])
                head_norm_gate_keys(ob, 'F4', gnbb, 'gnbb', F[5], 'F5', F[0], 'F0', F[1], 'F1', ogt, 'ogt', ss4, rs4)
                transpose_tm(ogt, 'ogt', 4, oT, 'oT')
                for half in range(2):
                    for k in range(4):
                        S.op('pe', lambda e, k=k, half=half: e.matmul(
                            bank(3 + half, 64), lhsT=oT[:, k, :], rhs=wbrB[:, k, half * 512:(half + 1) * 512],
                            start=(k == 0), stop=(k == 3)), reads=['oT', 'wbrB'], writes=bk(3 + half))
                    proj_tm(bank(half, 64), bk(half), hT, 'hT', wgB, 'wgB', half * 512, 512)
                    sg = F[2 + half]
                    S.op('act', lambda e, half=half, sg=sg: e.activation(out=sg, in_=bank(half, 64), func=AF.Sigmoid),
                         reads=bk(half), writes=['F%d' % (2 + half)])
                    S.op('dve', lambda e, half=half, sg=sg: e.tensor_tensor(out=sg, in0=bank(3 + half, 64), in1=sg,
                                                                            op=ALU.mult),
                         reads=bk(3 + half) + ['F%d' % (2 + half)], writes=['F%d' % (2 + half)])
                    S.op('dve', lambda e, half=half, sg=sg: e.tensor_tensor(
                        out=mb[:, half * 512:(half + 1) * 512], in0=sg, in1=m1t[:, half * 512:(half + 1) * 512],
                        op=ALU.add), reads=['F%d' % (2 + half), 'm1t'], writes=['mb'])
                transpose_tm(mb, 'mb', 8, mT, 'mT')
                for half in range(2):
                    proj_tm(bank(3 + half, 64), bk(3 + half), mT, 'mT', woutb, 'woutb', half * 512, 512)
                    S.op('dve', lambda e, half=half: e.tensor_tensor(
                        out=m1t[:, half * 512:(half + 1) * 512], in0=bank(3 + half, 64),
                        in1=xt[:, half * 512:(half + 1) * 512], op=ALU.add), reads=bk(3 + half) + ['xt'], writes=['m1t'])
                S.dma('sp', lambda e: e.dma_start(out=x1_d[r0:r0 + 64, :], in_=m1t), reads=['m1t'],
                      writes=[('x1', ti)])

            if 'A2' in PHASES:
                for ti in (range(NPT) if TILES is None else TILES):
                    tileA2(ti, TP)
                tileA2(NPT, TS)
        phase_A2()
        S.barrier()

        def phase_B():
            AL.off = pers_mark
            wqb = AL.get(128, [8, 2048], BF16)
            keysTb = AL.get(128, [16, 128], BF16)
            wpgb = AL.get(128, [8, 1024], BF16)
            wpleb = AL.get(128, [2, 1024], BF16)
            HB = [AL.get(128, 1024, BF16) for _ in range(NHB)]
            dltb = AL.get(128, [64, 64], BF16)
            WMd = AL.get(128, [64, 64], BF16)
            UV = [AL.get(128, 2048, BF16) for _ in range(NUV)]
            S.op('pool', lambda e: e.memset(WMd, 0.0), writes=['WMd'])
            junkb = AL.get(128, 1024, BF16)
            load_w_bf16(wqb, wq_d, 0, 8, 0, 2048, 'wqb')
            load_w_bf16(wpgb, wpg_d, 0, 8, 0, 1024, 'wpgb')
            load_w_bf16(wpleb, wple_d, 0, 2, 0, 1024, 'wpleb')
            S.dma('pool', lambda e: e.dma_start(out=keysTb, in_=keysT_d.rearrange("c d k -> d c k")), writes=['keysTb'])
            for cc in range(0, 4096, 2048):
                S.dma('pool', lambda e, cc=cc: e.dma_start(
                    out=dltb.rearrange("p n m -> p (n m)")[:, cc:cc + 2048], in_=dlt_d[:, cc:cc + 2048]), writes=['dltb'])
            gffnb = AL.get(64, 1024, F32)
            gpleb = AL.get(64, 1024, F32)
            gfinb = AL.get(64, 1024, F32)
            bcast_load(gffnb, gffn_d[0:1, :], 64, 1024, 'gffnb')
            bcast_load(gpleb, gple_d[0:1, :], 64, 1024, 'gpleb')
            bcast_load(gfinb, gfin_d[0:1, :], 64, 1024, 'gfinb')
            xtF = AL.get(64, 1024, F32)
            xtV2 = [AL.get(64, 1024, F32) for _ in range(2)]
            hbF = [AL.get(64, 1024, BF16) for _ in range(2)]
            hT = AL.get(128, [8, 64], BF16)
            hb2 = AL.get(64, 1024, BF16)
            hT2 = AL.get(128, [8, 64], BF16)
            ss = AL.get(64, 1, F32)
            rs = AL.get(64, 1, F32)
            ss2 = AL.get(64, 1, F32)
            rs2 = AL.get(64, 1, F32)
            qTb = AL.get(128, [16, 64], BF16)
            sc2 = [AL.get(64, [16, 128], F32) for _ in range(2)]
            v1 = AL.get(64, [16, 16], F32)
            i1 = AL.get(64, [16, 16], U32)
            i1f = AL.get(64, [16, 16], F32)
            wk = AL.get(64, 256, F32)
            cand = AL.get(64, [8, 256], F32)
            eq = cand.rearrange("p h (k i) -> p h k i", k=16)
            v2 = AL.get(64, [8, 16], F32)
            ci = AL.get(64, [8, 16], U32)
            cih = AL.get(64, [8, 16], U32)
            cil = AL.get(64, [8, 16], U32)
            cihf = AL.get(64, [8, 16], F32)
            cilf = AL.get(64, [8, 16], F32)
            iaf = AL.get(64, 128, F32)
            ibf = AL.get(64, 128, F32)
            idxf = AL.get(64, 128, F32)
            gte = AL.get(64, [8, 16], F32)
            gsum = AL.get(64, 8, F32)
            IDXT = [AL.get(128, 64, I32) for _ in range(3)]
            gateT = [AL.get(128, 64, F32) for _ in range(2)]
            ACTT = AL.get(128, 64, F32)
            X2c = AL.get(128, 64, F32)
            INc = AL.get(128, 64, F32)
            Sc = AL.get(128, 64, F32)
            gsig = AL.get(64, 1024, F32)
            ptl2 = [AL.get(64, 256, F32) for _ in range(2)]
            ptb = AL.get(64, 256, BF16)
            pTt = AL.get(128, [2, 64], BF16)
            yt = AL.get(64, 1024, F32)
            iota16 = C_('iota16')

            def topk16(src, srckey, width, vals, vkey, idxs, ikey):
                S.op('dve', lambda e: e.max(out=vals[:, 0:8], in_=src), reads=[srckey], writes=[vkey])
                S.op('dve', lambda e: e.max_index(out=idxs[:, 0:8], in_max=vals[:, 0:8], in_values=src),
                     reads=[srckey, vkey], writes=[ikey])
                S.op('dve', lambda e: e.match_replace(out=wk[:, 0:width], in_to_replace=vals[:, 0:8], in_values=src,
                                                      imm_value=-1e30), reads=[srckey, vkey], writes=['wk'])
                S.op('dve', lambda e: e.max(out=vals[:, 8:16], in_=wk[:, 0:width]), reads=['wk'], writes=[vkey])
                S.op('dve', lambda e: e.max_index(out=idxs[:, 8:16], in_max=vals[:, 8:16], in_values=wk[:, 0:width]),
                     reads=['wk', vkey], writes=[ikey])

            def rms_bf16(xin, xkey, gb, gkey, hout, hkey, ss_, sskey, rs_, rskey):
                S.op('act', lambda e: e.activation(out=hout, in_=xin, func=AF.Square, accum_out=ss_),
                     reads=[xkey], writes=[hkey, sskey])
                S.op('act', lambda e: e.activation(out=rs_, in_=ss_, func=AF.Sqrt, scale=1.0 / 1024,
                                                   bias=epsc[0:64, :]), reads=[sskey, 'cst'], writes=[rskey])
                S.op('dve', lambda e: e.reciprocal(out=rs_, in_=rs_), reads=[rskey], writes=[rskey])
                S.op('dve', lambda e: e.scalar_tensor_tensor(out=hout, in0=xin, scalar=rs_[:, 0:1], in1=gb,
                                                             op0=ALU.mult, op1=ALU.mult),
                     reads=[xkey, rskey, gkey], writes=[hkey])

            def front_pe(ti, pos):
                par = 0
                sc = sc2[pos % 2]
                sck = 'sc%d' % (pos % 2)
                r0 = ti * 64
                hb = hbF[par]
                hkey = 'hbF%d' % par
                S.dma('sp', lambda e: e.dma_start(out=xtF, in_=x1_d[r0:r0 + 64, :]), reads=[('x1', ti)], writes=['xtF'])
                rms_bf16(xtF, 'xtF', gffnb, 'gffnb', hb, hkey, ss, 'ss', rs, 'rs')
                S.dma('sp', lambda e: e.dma_start(out=h2_d[r0:r0 + 64, :], in_=hb), reads=[hkey], writes=[('h2', ti)])
                transpose_tm(hb, hkey, 8, hT, 'hT')
                pq = PB[:, 1024:2048].rearrange("p (c t) -> p c t", c=16)
                for hc in range(16):
                    for k in range(8):
                        S.op('pe', lambda e, hc=hc, k=k: e.matmul(pq[:, hc, :], lhsT=wqb[:, k, hc * 128:(hc + 1) * 128],
                                                                  rhs=hT[:, k, :], start=(k == 0), stop=(k == 7)),
                             reads=['wqb', 'hT'], writes=bk(2, 3))
                S.op('act', lambda e: e.copy(out=qTb, in_=pq), reads=bk(2, 3), writes=['qTb'])
                for half in range(2):
                    for j in range(8):
                        hc = half * 8 + j
                        S.op('pe', lambda e, hc=hc, j=j: e.matmul(PB[0:64, 1024 + j * 128:1024 + (j + 1) * 128],
                                                                  lhsT=qTb[:, hc, :], rhs=keysTb[:, hc, :],
                                                                  start=True, stop=True),
                             reads=['qTb', 'keysTb'], writes=bk(2, 3))
                    S.op('act', lambda e, half=half: e.copy(
                        out=sc[:, half * 8:(half + 1) * 8, :],
                        in_=PB[0:64, 1024:2048].rearrange("p (c k) -> p c k", c=8)), reads=bk(2, 3), writes=[sck])

            def front_dve(ti, pos):
                par = pos % 2
                ip = pos % 3
                sc = sc2[pos % 2]
                sck = 'sc%d' % (pos % 2)
                for hc in range(16):
                    topk16(sc[:, hc, :], sck, 128, v1[:, hc, :], 'v1', i1[:, hc, :], 'i1')
                S.op('dve', lambda e: e.tensor_copy(out=i1f, in_=i1), reads=['i1'], writes=['i1f'])
                for h in range(8):
                    S.op('dve', lambda e, h=h: e.tensor_tensor(
                        out=cand[:, h, :].rearrange("p (i j) -> p i j", i=16),
                        in0=v1[:, 2 * h, :].unsqueeze(2).to_broadcast([64, 16, 16]),
                        in1=v1[:, 2 * h + 1, :].unsqueeze(1).to_broadcast([64, 16, 16]), op=ALU.add),
                        reads=['v1'], writes=['cand'])
                for h in range(8):
                    topk16(cand[:, h, :], 'cand', 256, v2[:, h, :], 'v2', ci[:, h, :], 'ci')
                S.op('dve', lambda e: e.tensor_scalar(out=cih, in0=ci, scalar1=4, scalar2=None,
                                                      op0=ALU.logical_shift_right), reads=['ci'], writes=['cih'])
                S.op('dve', lambda e: e.tensor_scalar(out=cil, in0=ci, scalar1=15, scalar2=None, op0=ALU.bitwise_and),
                     reads=['ci'], writes=['cil'])
                S.op('dve', lambda e: e.tensor_copy(out=cihf, in_=cih), reads=['cih'], writes=['cihf'])
                S.op('dve', lambda e: e.tensor_copy(out=cilf, in_=cil), reads=['cil'], writes=['cilf'])
                i1v = i1f.rearrange("p (h c) i -> p h c i", c=2)
                for (cf, ckey, cpos, dst, dkey) in ((cihf, 'cihf', 0, iaf, 'iaf'), (cilf, 'cilf', 1, ibf, 'ibf')):
                    S.op('dve', lambda e, cf=cf: e.tensor_tensor(
                        out=eq, in0=cf.unsqueeze(3).to_broadcast([64, 8, 16, 16]),
                        in1=iota16.unsqueeze(1).unsqueeze(1).to_broadcast([64, 8, 16, 16]), op=ALU.is_equal),
                        reads=[ckey, 'cst'], writes=['cand'])
                    S.op('dve', lambda e, cpos=cpos: e.tensor_tensor(
                        out=eq, in0=eq, in1=i1v[:, :, cpos, :].unsqueeze(2).to_broadcast([64, 8, 16, 16]), op=ALU.mult),
                        reads=['cand', 'i1f'], writes=['cand'])
                    S.op('dve', lambda e, dst=dst: e.reduce_sum(out=dst, in_=eq.rearrange("p h k i -> p (h k) i"),
                                                                axis=AX.X), reads=['cand'], writes=[dkey])
                S.op('dve', lambda e: e.scalar_tensor_tensor(out=idxf, in0=iaf, scalar=128.0, in1=ibf, op0=ALU.mult,
                                                             op1=ALU.add), reads=['iaf', 'ibf'], writes=['idxf'])
                S.op('dve', lambda e: e.tensor_tensor(out=gte, in0=v2, in1=v2[:, :, 0:1].to_broadcast([64, 8, 16]),
                                                      op=ALU.subtract), reads=['v2'], writes=['gte'])
                S.op('act', lambda e: e.activation(out=gte, in_=gte, func=AF.Exp), reads=['gte'], writes=['gte'])
                S.op('dve', lambda e: e.reduce_sum(out=gsum, in_=gte, axis=AX.X), reads=['gte'], writes=['gsum'])
                S.op('dve', lambda e: e.reciprocal(out=gsum, in_=gsum), reads=['gsum'], writes=['gsum'])
                S.op('dve', lambda e: e.tensor_tensor(out=gte, in0=gte, in1=gsum.unsqueeze(2).to_broadcast([64, 8, 16]),
                                                      op=ALU.mult), reads=['gte', 'gsum'], writes=['gte'])
                S.op('pe', lambda e: e.transpose(out=bank(6, 128, 64), in_=idxf, identity=identf[0:64, 0:64]),
                     reads=['idxf', 'cst'], writes=bk(6))
                S.op('pe', lambda e: e.transpose(out=bank(6, 128, 64, 64), in_=gte.rearrange("p h k -> p (h k)"),
                                                 identity=identf[0:64, 0:64]), reads=['gte', 'cst'], writes=bk(6))
                S.op('dve', lambda e: e.tensor_copy(out=IDXT[ip], in_=bank(6, 128, 64)), reads=bk(6),
                     writes=['IDXT%d' % ip])
                S.op('dve', lambda e: e.tensor_copy(out=gateT[par], in_=bank(6, 128, 64, 64)), reads=bk(6),
                     writes=['gateT%d' % par])

            def uvstage(ti, pos):
                par = pos % 2
                ip = pos % 3
                r0 = ti * 64
                xtV = xtV2[par]
                ptl = ptl2[par]
                xk = 'xtV%d' % par
                pk = 'ptl%d' % par
                gT = gateT[par]
                gk = 'gateT%d' % par
                S.begin()
                S.dma('sp', lambda e: e.dma_start(out=xtV, in_=x1_d[r0:r0 + 64, :]), reads=[('x1', ti)], writes=[xk])
                S.dma('sp', lambda e: e.dma_start(out=ptl, in_=p_d[r0:r0 + 64, :]), writes=[pk])
                head = S.end()
                segs = []
                for n in range(64):
                    S.begin()
                    g = (pos * 64 + n) % NUV
                    uv_ = UV[g]
                    uvkey = 'UV%d' % g
                    S.dma('pool', lambda e, n=n, uv_=uv_: e.indirect_dma_start(
                        out=uv_, out_offset=None, in_=uvb_d,
                        in_offset=bass.IndirectOffsetOnAxis(ap=IDXT[ip][:, n:n + 1], axis=0)),
                        reads=['IDXT%d' % ip, 'uvb'], writes=[uvkey])
                    gh = (pos * 64 + n) % NHB
                    hbb = HB[gh]
                    hbkey = 'HB%d' % gh
                    S.dma('sp', lambda e, n=n, hbb=hbb: e.dma_start(
                        out=hbb, in_=h2_d[ti * 64 + n:ti * 64 + n + 1, :].to_broadcast([128, 1024])),
                        reads=[('h2', ti)], writes=[hbkey])
                    S.op('dve', lambda e, n=n, uv_=uv_, hbb=hbb: e.scalar_tensor_tensor(
                        out=junkb, in0=uv_[:, 0:1024], scalar=1.0, in1=hbb, op0=ALU.mult, op1=ALU.mult,
                        accum_out=ACTT[:, n:n + 1]), reads=[uvkey, hbkey], writes=[('ACT', n)])
                    xa = ACTT[:, n:n + 1]
                    S.op('act', lambda e, n=n, xa=xa: e.activation(out=X2c[:, n:n + 1], in_=xa, func=AF.Identity,
                                                                  scale=xa), reads=[('ACT', n)], writes=[('X2', n)])
                    S.op('act', lambda e, n=n: e.activation(out=X2c[:, n:n + 1], in_=X2c[:, n:n + 1], func=AF.Identity,
                                                            scale=0.044715, bias=C_('one')), reads=[('X2', n), 'cst'],
                         writes=[('X2', n)])
                    S.op('act', lambda e, n=n, xa=xa: e.activation(out=INc[:, n:n + 1], in_=X2c[:, n:n + 1],
                                                                  func=AF.Identity, scale=xa),
                         reads=[('X2', n), ('ACT', n)], writes=[('IN', n)])
                    S.op('act', lambda e, n=n: e.activation(out=Sc[:, n:n + 1], in_=INc[:, n:n + 1], func=AF.Sigmoid,
                                                            scale=1.5957691216057308), reads=[('IN', n)],
                         writes=[('S', n)])
                    S.op('act', lambda e, n=n, xa=xa: e.activation(out=Sc[:, n:n + 1], in_=Sc[:, n:n + 1],
                                                                  func=AF.Identity, scale=xa),
                         reads=[('S', n), ('ACT', n)], writes=[('S', n)])
                    S.op('act', lambda e, n=n: e.activation(out=WMd[:, n, n:n + 1], in_=Sc[:, n:n + 1],
                                                            func=AF.Identity, scale=gT[:, n:n + 1]),
                         reads=[('S', n), gk, 'WMd'], writes=[('WM', n)])
                    for half in range(2):
                        S.op('pe', lambda e, n=n, half=half, uv_=uv_: e.matmul(
                            bank(4 + half, 64), lhsT=WMd[:, n, :],
                            rhs=uv_[:, 1024 + half * 512:1024 + (half + 1) * 512],
                            start=(n == 0), stop=(n == 63)), reads=[('WM', n), uvkey], writes=bk(4 + half))
                    segs.append(S.end())
                S.begin()
                S.op('dve', lambda e: e.tensor_tensor(out=xtV, in0=xtV, in1=PB[0:64, 2048:3072], op=ALU.add),
                     reads=[xk] + bk(4, 5), writes=[xk])
                tail_a = S.end()
                S.begin()
                pt6 = PB[:, 6 * 512 + 256:7 * 512].bitcast(BF16)
                rms_bf16(xtV, xk, gpleb, 'gpleb', hb2, 'hb2', ss2, 'ss2', rs2, 'rs2')
                transpose_tm(hb2, 'hb2', 8, hT2, 'hT2', ptgt=pt6, pkey='b6u')
                S.op('dve', lambda e: e.tensor_copy(out=ptb, in_=ptl), reads=[pk], writes=['ptb'])
                transpose_tm(ptb, 'ptb', 2, pTt, 'pTt', ptgt=pt6, pkey='b6u')
                for half in range(2):
                    hs = slice(half * 512, (half + 1) * 512)
                    proj_tm(bank(0, 64), bk(0), hT2, 'hT2', wpgb, 'wpgb', half * 512, 512)
                    proj_tm(bank(1, 64), bk(1), pTt, 'pTt', wpleb, 'wpleb', half * 512, 512, nk=2)
                    S.op('act', lambda e, hs=hs: e.activation(out=gsig[:, hs], in_=bank(0, 64), func=AF.Sigmoid),
                         reads=bk(0), writes=['gsig'])
                    S.op('dve', lambda e, hs=hs: e.tensor_tensor(out=gsig[:, hs], in0=gsig[:, hs], in1=bank(1, 64),
                                                                  op=ALU.mult), reads=['gsig'] + bk(1), writes=['gsig'])
                    S.op('dve', lambda e, hs=hs: e.tensor_tensor(out=xtV[:, hs], in0=xtV[:, hs], in1=gsig[:, hs],
                                                                  op=ALU.add), reads=[xk, 'gsig'], writes=[xk])
                S.op('act', lambda e: e.activation(out=yt, in_=xtV, func=AF.Square, accum_out=ss2), reads=[xk],
                     writes=['yt', 'ss2'])
                S.op('act', lambda e: e.activation(out=rs2, in_=ss2, func=AF.Sqrt, scale=1.0 / 1024,
                                                   bias=epsc[0:64, :]), reads=['ss2', 'cst'], writes=['rs2'])
                S.op('dve', lambda e: e.reciprocal(out=rs2, in_=rs2), reads=['rs2'], writes=['rs2'])
                S.op('dve', lambda e: e.scalar_tensor_tensor(out=yt, in0=xtV, scalar=rs2[:, 0:1], in1=gfinb,
                                                             op0=ALU.mult, op1=ALU.mult),
                     reads=[xk, 'rs2', 'gfinb'], writes=['yt'])
                S.dma('sp', lambda e: e.dma_start(out=y_d[r0:r0 + 64, :], in_=yt), reads=['yt'], final=True)
                return head, segs, tail_a, S.end()

            if 'B' in PHASES:
                TL = list(range(NPT + 1)) if TILES is None else list(TILES) + [NPT]
                nT = len(TL)
                front_pe(TL[0], 0)
                front_dve(TL[0], 0)
                if nT > 1:
                    front_pe(TL[1], 1)
                LP = []
                for j in range(nT):
                    head, segs, tail_a, tail_b = uvstage(TL[j], j)
                    LFd, LFp = [], []
                    if j + 1 < nT:
                        S.begin()
                        front_dve(TL[j + 1], j + 1)
                        LFd = S.end()
                    if j + 2 < nT:
                        S.begin()
                        front_pe(TL[j + 2], j + 2)
                        LFp = S.end()
                    streams = [LFd, LFp, LP]
                    pers = [(len(L) + 63) // 64 for L in streams]
                    S.run(head)
                    for n in range(64):
                        S.run(segs[n])
                        for L, per in zip(streams, pers):
                            S.run(L[n * per:(n + 1) * per])
                    for L, per in zip(streams, pers):
                        S.run(L[64 * per:])
                    S.run(tail_a)
                    LP = tail_b
                S.run(LP)
        phase_B()
        print('ops', {e: len(v) for e, v in S.prog.items()}, 'nsem', S.nsem, 'arena', AL.off)
        S.emit()
    return nc


_CACHE = {}


def kernel(x_prompt, x_sample, state_hgrn, state_delta, state_conv, p_prompt, p_sample,
           lb_param, g_mix, w_in, conv_w, a_log, dt_bias, g_norm_a, g_norm_b, w_br_a, w_br_b,
           w_out, g_ffn, peer_wq, peer_keys, expert_u, expert_v, g_ple, w_ple, w_ple_gate,
           g_final):
    f = lambda a: np.ascontiguousarray(np.asarray(a, dtype=np.float32))
    if 'nc' not in _CACHE:
        _CACHE['nc'] = build_program()
        _CACHE['consts'] = _build_consts()
    nc = _CACHE['nc']
    cst, cmf, zsel, dlt = _CACHE['consts']
    x_prompt, x_sample = f(x_prompt), f(x_sample)
    p_prompt, p_sample = f(p_prompt), f(p_sample)
    state_hgrn, state_delta, state_conv = f(state_hgrn), f(state_delta), f(state_conv)
    keysT = np.ascontiguousarray(np.transpose(f(peer_keys)[0], (0, 1, 3, 2)).reshape(16, 128, 128))
    shared = dict(
        lbp=f(lb_param), gmix=f(g_mix), w_in=f(w_in)[0], convw=f(conv_w)[0], alog=f(a_log), dtb=f(dt_bias),
        gna=f(g_norm_a), gnb=f(g_norm_b), wbra=f(w_br_a)[0], wbrb=f(w_br_b)[0], wout=f(w_out)[0], gffn=f(g_ffn),
        wq=f(peer_wq)[0], keysT=keysT, eu=f(expert_u)[0], ev=f(expert_v)[0], gple=f(g_ple), wple=f(w_ple)[0],
        wpg=f(w_ple_gate)[0], gfin=f(g_final).reshape(1, 1024), cst=cst, cmf=cmf, zsel=zsel, dlt=dlt)
    in_maps = []
    for b in range(8):
        m = dict(shared)
        m['x'] = np.ascontiguousarray(np.concatenate([x_prompt[b], x_sample[16 * b:16 * b + 16].reshape(64, 1024)], 0))
        m['p'] = np.ascontiguousarray(np.concatenate([p_prompt[0, b], p_sample[0, 16 * b:16 * b + 16].reshape(64, 256)], 0))
        m['sh'] = np.ascontiguousarray(state_hgrn[0, 16 * b:16 * b + 16])
        m['sd'] = np.ascontiguousarray(state_delta[0, 16 * b:16 * b + 16])
        m['scv'] = np.ascontiguousarray(state_conv[0, 16 * b:16 * b + 16].reshape(48, 1536))
        in_maps.append(m)
    res = run_bass_kernel_spmd(nc, in_maps, core_ids=list(range(8)))
    R = res.results
    y_prompt = np.stack([R[b]['y'][0:2048] for b in range(8)], 0)
    y_sample = np.concatenate([R[b]['y'][2048:2112].reshape(16, 4, 1024) for b in range(8)], 0)
    hp = np.stack([R[b]['hp'] for b in range(8)], 0)[None]
    dp = np.stack([R[b]['dp'] for b in range(8)], 0)[None]
    cp = np.stack([R[b]['cp'] for b in range(8)], 0)[None]
    hs = np.concatenate([R[b]['hs'] for b in range(8)], 0)[None]
    ds = np.concatenate([R[b]['ds'] for b in range(8)], 0)[None]
    cs = np.concatenate([R[b]['cs'].reshape(16, 3, 1536) for b in range(8)], 0)[None]
    _CACHE['dbg'] = R
    return tuple(np.ascontiguousarray(a.astype(np.float32)) for a in (y_prompt, y_sample, hp, dp, cp, hs, ds, cs))
```

```python
import numpy as np
from contextlib import ExitStack
import concourse.bass as bass
import concourse.mybir as mybir
from concourse.bass_utils import run_bass_kernel_spmd

F32 = mybir.dt.float32
BF16 = mybir.dt.bfloat16
I32 = mybir.dt.int32
U32 = mybir.dt.uint32
AF = mybir.ActivationFunctionType
ALU = mybir.AluOpType
AX = mybir.AxisListType

EPS = 1e-6
NPT = 32
NTOK = 2112
EPOCH = 12000
DMA_POOL = 8
DMA_EPOCH = 700
ARENA_COLS = 105984
NBUF = 6
NHB = 6
NUV = 8
NEG = -30000.0
DBG = set()
DBGT = 0
PHASES = ('A1', 'A2', 'B')
TILES = None


class Sched:
    def __init__(self, nc, es):
        self.nc = nc
        self.es = es
        self.eng = {'pe': nc.tensor, 'act': nc.scalar, 'dve': nc.vector,
                    'pool': nc.gpsimd, 'sp': nc.sync}
        self.prog = {e: [] for e in self.eng}
        self.cnt = {e: 0 for e in self.eng}
        self.sem = {}
        self.nsem = 0
        for e in self.eng:
            self.sem[e] = self._newsem(e)
        self.waited = {e: {} for e in self.eng}
        self.dpool = {}
        self.res_w = {}
        self.res_r = {}
        self.final_tokens = []
        self.pending = {e: [] for e in self.eng}
        self.cap = None

    def begin(self):
        self.cap = []

    def end(self):
        L = self.cap
        self.cap = None
        return L

    def run(self, L):
        for it in L:
            if it[0] == 'op':
                self.op(it[1], it[2], it[3], it[4])
            else:
                self.dma(it[1], it[2], it[3], it[4], it[5])

    def _newsem(self, name):
        self.nsem += 1
        return self.es.enter_context(self.nc.semaphore(f"s{self.nsem}_{name}"))

    def _need(self, e, tok, waits):
        if tok is None:
            return
        sem, val = tok[0], tok[1]
        if e == 'pe' and tok[2] == 'pe':
            return
        w = self.waited[e]
        if w.get(id(sem), 0) >= val:
            return
        w[id(sem)] = val
        waits.append((sem, val))

    def _deps(self, e, reads, writes, waits):
        for t in self.pending[e]:
            self._need(e, t, waits)
        self.pending[e] = []
        for k in reads:
            self._need(e, self.res_w.get(k), waits)
        for k in writes:
            self._need(e, self.res_w.get(k), waits)
            for t in self.res_r.get(k, ()):
                self._need(e, t, waits)

    def _commit(self, tok, reads, writes):
        for k in reads:
            self.res_r.setdefault(k, []).append(tok)
        for k in writes:
            self.res_w[k] = tok
            self.res_r[k] = []

    def op(self, e, fn, reads=(), writes=()):
        if self.cap is not None:
            self.cap.append(('op', e, fn, tuple(reads), tuple(writes)))
            return None
        waits = []
        self._deps(e, reads, writes, waits)
        if self.cnt[e] >= EPOCH:
            self.sem[e] = self._newsem(e)
            self.cnt[e] = 0
        self.cnt[e] += 1
        tok = (self.sem[e], self.cnt[e], e)
        self.prog[e].append((waits, fn, self.sem[e], 1))
        self._commit(tok, reads, writes)
        return tok

    def dma(self, e, fn, reads=(), writes=(), final=False):
        if self.cap is not None:
            self.cap.append(('dma', e, fn, tuple(reads), tuple(writes), final))
            return None
        waits = []
        self._deps(e, reads, writes, waits)
        pool = self.dpool.setdefault(e, {'sems': [], 'uses': [], 'i': 0})
        i = pool['i'] % DMA_POOL
        pool['i'] += 1
        if len(pool['sems']) <= i:
            pool['sems'].append(self._newsem(e + 'd'))
            pool['uses'].append(0)
        if pool['uses'][i] >= DMA_EPOCH:
            pool['sems'][i] = self._newsem(e + 'd')
            pool['uses'][i] = 0
        sem = pool['sems'][i]
        if pool['uses'][i] > 0:
            self._need(e, (sem, 16 * pool['uses'][i], 'dma'), waits)
        pool['uses'][i] += 1
        tok = (sem, 16 * pool['uses'][i], 'dma')
        self.prog[e].append((waits, fn, sem, 16))
        self._commit(tok, reads, writes)
        if final:
            self.final_tokens.append(tok)
        return tok

    def barrier(self):
        toks = []
        for e in self.eng:
            if self.cnt[e] > 0:
                toks.append((self.sem[e], self.cnt[e], e + '_bar'))
        for e, pool in self.dpool.items():
            for sem, u in zip(pool['sems'], pool['uses']):
                if u > 0:
                    toks.append((sem, 16 * u, 'dma'))
        for e in self.eng:
            self.pending[e] = list(toks)

    def emit(self):
        nc = self.nc
        fw = []
        for t in self.final_tokens:
            self._need('sp', t, fw)
        with nc.Block() as block:
            def run(e, engine):
                for waits, fn, sem, inc in self.prog[e]:
                    for (s, v) in waits:
                        engine.wait_ge(s, v)
                    fn(engine).then_inc(sem, inc)

            @block.tensor
            def _(eng):
                run('pe', eng)

            @block.scalar
            def _(eng):
                run('act', eng)

            @block.vector
            def _(eng):
                run('dve', eng)

            @block.gpsimd
            def _(eng):
                run('pool', eng)

            @block.sync
            def _(eng):
                run('sp', eng)
                for (s, v) in fw:
                    eng.wait_ge(s, v)


class Alloc:
    def __init__(self, arena, ncols):
        self.a = arena
        self.n = ncols
        self.off = 0

    def get(self, parts, free, dt):
        if isinstance(free, int):
            free = [free]
        nel = int(np.prod(free))
        cols = nel * (1 if dt == BF16 else 2)
        cols = (cols + 1) // 2 * 2
        o = self.off
        self.off += cols
        assert self.off <= self.n, f"arena overflow {self.off} > {self.n}"
        ap = self.a[0:parts, o:o + cols]
        if dt != BF16:
            ap = ap.bitcast(dt)
        if len(free) > 1:
            ds = [f"d{i}" for i in range(len(free))]
            kw = {ds[i]: free[i] for i in range(1, len(free))}
            ap = ap.rearrange(f"p ({' '.join(ds)}) -> p {' '.join(ds)}", **kw)
        return ap


def _tile_consts(nch, C):
    t = np.arange(64)
    ch = t // C
    same = ch[:, None] == ch[None, :]
    tri = (same & (t[:, None] <= t[None, :])).astype(np.float32)
    blk = same.astype(np.float32)
    nmT = np.where(tri > 0, 0.0, NEG).astype(np.float32)
    strict = same & (t[None, :] < t[:, None])
    pmS = np.where(strict, 0.0, -NEG).astype(np.float32)
    cind = np.zeros((64, 16), np.float32)
    cind[t, ch] = 1.0
    return tri, blk, np.tile(nmT, (1, 4)), np.tile(pmS, (1, 4)), cind


CST_COLS = {}


def _build_consts():
    cols = []
    off = [0]

    def add(name, arr):
        a = np.zeros((128, arr.shape[1]), np.float32)
        a[:arr.shape[0]] = arr
        CST_COLS[name] = (off[0], off[0] + arr.shape[1], arr.shape[0])
        off[0] += arr.shape[1]
        cols.append(a)

    add('ident', np.eye(128, dtype=np.float32))
    add('ones', np.ones((128, 128), np.float32))
    for nm, (nch, C) in (('p', (1, 64)), ('s', (16, 4))):
        tri, blk, nmT, pmS, cind = _tile_consts(nch, C)
        add('tri_' + nm, tri)
        add('blk_' + nm, blk)
        add('nmT_' + nm, nmT)
        add('pmS_' + nm, pmS)
        add('cind_' + nm, cind)
    add('iota16', np.tile(np.arange(16, dtype=np.float32)[None, :], (64, 1)))
    add('eps', np.full((128, 1), EPS, np.float32))
    add('one', np.ones((128, 1), np.float32))
    cst = np.concatenate(cols, axis=1)
    t = np.arange(64)
    cm = (t[None, :] // 4 == np.arange(16)[:, None]).astype(np.float32)
    cmf = np.tile(cm.reshape(1, 16 * 64), (128, 1))
    z = np.zeros((64, 64, 128), np.float32)
    z[t, t, :] = 1.0
    z = z.reshape(64, 64 * 128)
    dl = np.tile(np.eye(64, dtype=np.float32).reshape(1, 64 * 64), (128, 1))
    return cst, cmf, z, dl


def build_program():
    cst_np, _, _, _ = _build_consts()
    NCST = cst_np.shape[1]
    nc = bass.Bass("TRN2", target_bir_lowering=False)

    def din(name, shape, dt=F32):
        return nc.dram_tensor(name, shape, dt, kind="ExternalInput").ap()

    def dout(name, shape, dt=F32):
        return nc.dram_tensor(name, shape, dt, kind="ExternalOutput").ap()

    x_d = din("x", [NTOK, 1024])
    p_d = din("p", [NTOK, 256])
    sh_d = din("sh", [16, 4, 128, 128])
    sd_d = din("sd", [16, 4, 128, 128])
    scv_d = din("scv", [48, 1536])
    lbp_d = din("lbp", [2, 512])
    gmix_d = din("gmix", [1, 1024])
    win_d = din("w_in", [1024, 6152])
    convw_d = din("convw", [4, 1536])
    alog_d = din("alog", [1, 4])
    dtb_d = din("dtb", [1, 4])
    gna_d = din("gna", [1, 128])
    gnb_d = din("gnb", [1, 128])
    wbra_d = din("wbra", [512, 1024])
    wbrb_d = din("wbrb", [512, 1024])
    wout_d = din("wout", [1024, 1024])
    gffn_d = din("gffn", [1, 1024])
    wq_d = din("wq", [1024, 2048])
    keysT_d = din("keysT", [16, 128, 128])
    eu_d = din("eu", [16384, 1024])
    ev_d = din("ev", [16384, 1024])
    gple_d = din("gple", [1, 1024])
    wple_d = din("wple", [256, 1024])
    wpg_d = din("wpg", [1024, 1024])
    gfin_d = din("gfin", [1, 1024])
    cst_d = din("cst", [128, NCST])
    cmf_d = din("cmf", [128, 1024])
    zsel_d = din("zsel", [64, 8192])
    dlt_d = din("dlt", [128, 4096])

    y_d = dout("y", [NTOK, 1024])
    hp_d = dout("hp", [4, 128, 128])
    dp_d = dout("dp", [4, 128, 128])
    cp_d = dout("cp", [3, 1536])
    hs_d = dout("hs", [16, 4, 128, 128])
    ds_d = dout("ds", [16, 4, 128, 128])
    cs_d = dout("cs", [48, 1536])
    m1_d = dout("m1s", [NTOK, 1024])
    x1_d = dout("x1s", [NTOK, 1024])
    uvb_d = nc.dram_tensor("uvb", [16384, 2048], BF16, kind="Internal").ap()
    h2_d = nc.dram_tensor("h2s", [NTOK, 1024], BF16, kind="Internal").ap()

    es = ExitStack()
    with es:
        S = Sched(nc, es)

        def dbg(name, ap, key, ti=0, want=0):
            if name not in DBG or ti != want:
                return
            shp = list(ap.shape)
            dd = nc.dram_tensor("dbg_" + name, shp, ap.dtype, kind="ExternalOutput").ap()
            S.dma('sp', lambda e: e.dma_start(out=dd, in_=ap), reads=[key] if not isinstance(key, list) else key,
                  final=True)
        ARENA = es.enter_context(nc.sbuf_tensor("arena", [128, ARENA_COLS], BF16))
        PB = es.enter_context(nc.psum_tensor("pb", [128, 7 * 512], F32))
        PTt = es.enter_context(nc.psum_tensor("pt", [128, 1024], BF16))
        AL = Alloc(ARENA, ARENA_COLS)

        def bank(j, parts=128, n=512, off=0):
            return PB[0:parts, j * 512 + off:j * 512 + off + n]

        def bk(*js):
            return ['b%d' % j for j in js]

        CST = AL.get(128, NCST, F32)
        S.dma('sp', lambda e: e.dma_start(out=CST, in_=cst_d), writes=['cst'])

        def C_(name, parts=None):
            a, b, r = CST_COLS[name]
            return CST[0:(parts or r), a:b]

        identf = C_('ident')
        onesf = C_('ones')
        epsc = C_('eps')
        identb = AL.get(128, 128, BF16)
        S.op('dve', lambda e: e.tensor_copy(out=identb, in_=identf), reads=['cst'], writes=['identb'])
        TP = dict(nch=1, C=64, tri=C_('tri_p'), blk=C_('blk_p'), nmT=C_('nmT_p'), pmS=C_('pmS_p'),
                  cind=C_('cind_p'))
        TS = dict(nch=16, C=4, tri=C_('tri_s'), blk=C_('blk_s'), nmT=C_('nmT_s'), pmS=C_('pmS_s'),
                  cind=C_('cind_s'))
        cmb = AL.get(128, [16, 64], BF16)
        S.dma('pool', lambda e: e.dma_start(out=cmb, in_=cmf_d.rearrange("p (c t) -> p c t", c=16)),
              writes=['cmb'])
        wdummy = AL.get(128, 2, F32)
        HT_COLS = (NPT + 1) * 512
        _save = AL.off
        AL.off = ARENA_COLS - HT_COLS
        hT_all = AL.get(128, [NPT + 1, 8, 64], BF16)
        AL.off = _save
        pers_mark = AL.off

        def load_w_bf16(dst3, src, r0, nk, c0, ncols, key):
            for k in range(nk):
                for cc in range(0, ncols, 2048):
                    w = min(2048, ncols - cc)
                    S.dma('pool', lambda e, k=k, cc=cc, w=w: e.dma_start(
                        out=dst3[:, k, cc:cc + w],
                        in_=src[r0 + k * 128:r0 + (k + 1) * 128, c0 + cc:c0 + cc + w]), writes=[key])

        cast_rr = [0]

        def load_w_fast(dst3, src, r0, nk, c0, ncols, key, stage):
            for k in range(nk):
                for cc in range(0, ncols, 2048):
                    w = min(2048, ncols - cc)
                    i = cast_rr[0] % len(stage)
                    eng = ('act', 'dve', 'pool')[cast_rr[0] % 3]
                    cast_rr[0] += 1
                    st = stage[i]
                    skey = 'wstage%d' % i
                    S.dma('sp', lambda e, k=k, cc=cc, w=w, st=st: e.dma_start(
                        out=st[:, 0:w], in_=src[r0 + k * 128:r0 + (k + 1) * 128, c0 + cc:c0 + cc + w]),
                        writes=[skey])
                    if eng == 'act':
                        S.op('act', lambda e, k=k, cc=cc, w=w, st=st: e.copy(out=dst3[:, k, cc:cc + w], in_=st[:, 0:w]),
                             reads=[skey], writes=[(key, k, cc)])
                    else:
                        S.op(eng, lambda e, k=k, cc=cc, w=w, st=st: e.tensor_copy(out=dst3[:, k, cc:cc + w],
                                                                                  in_=st[:, 0:w]),
                             reads=[skey], writes=[(key, k, cc)])
            S.op('pool', lambda e: e.memset(wdummy, 0.0),
                 reads=[(key, k, cc) for k in range(nk) for cc in range(0, ncols, 2048)], writes=[key])

        def bcast_load(dst, src_row, parts, n, key):
            S.dma('sp', lambda e: e.dma_start(out=dst, in_=src_row.to_broadcast([parts, n])), writes=[key])

        def rmsnorm_to_bf16(xt, xkey, gb, gkey, hb, hkey, ss, rs):
            S.op('act', lambda e: e.activation(out=hb, in_=xt, func=AF.Square, accum_out=ss),
                 reads=[xkey], writes=[hkey, 'ss'])
            S.op('act', lambda e: e.activation(out=rs, in_=ss, func=AF.Sqrt, scale=1.0 / 1024, bias=epsc[0:64, :]),
                 reads=['ss', 'cst'], writes=['rs'])
            S.op('dve', lambda e: e.reciprocal(out=rs, in_=rs), reads=['rs'], writes=['rs'])
            S.op('dve', lambda e: e.scalar_tensor_tensor(out=hb, in0=xt, scalar=rs[:, 0:1], in1=gb,
                                                         op0=ALU.mult, op1=ALU.mult),
                 reads=[xkey, 'rs', gkey], writes=[hkey])

        def transpose_tm(src, skey, nblk, dst, dkey, eng='act', ptgt=None, pkey='pt'):
            if ptgt is None:
                ptgt = PTt
            for k in range(nblk):
                S.op('pe', lambda e, k=k: e.transpose(out=ptgt[:, k * 64:(k + 1) * 64],
                                                      in_=src[:, k * 128:(k + 1) * 128],
                                                      identity=identb[0:64, 0:64]),
                     reads=[skey, 'identb'], writes=[pkey])
            pv = ptgt[:, 0:nblk * 64].rearrange("p (k t) -> p k t", k=nblk)
            if eng == 'act':
                S.op('act', lambda e: e.copy(out=dst, in_=pv), reads=[pkey], writes=[dkey])
            else:
                S.op('dve', lambda e: e.tensor_copy(out=dst, in_=pv), reads=[pkey], writes=[dkey])

        def proj_tm(pout, pkeys, hT, hTkey, w3, wkey, c0, ncols, nk=8):
            for k in range(nk):
                S.op('pe', lambda e, k=k: e.matmul(pout, lhsT=hT[:, k, :], rhs=w3[:, k, c0:c0 + ncols],
                                                   start=(k == 0), stop=(k == nk - 1)),
                     reads=[hTkey, wkey], writes=pkeys)

        def head_norm_gate_keys(o_sb, okey, gnb_, gkey, gate_sb, gatekey, sq, sqkey, on, onkey, ogt_, ogtkey, ss4, rs4):
            o3 = o_sb.rearrange("p (h v) -> p h v", h=4)
            S.op('dve', lambda e: e.tensor_tensor(out=sq, in0=o_sb, in1=o_sb, op=ALU.mult),
                 reads=[okey], writes=[sqkey])
            S.op('dve', lambda e: e.reduce_sum(out=ss4, in_=sq.rearrange("p (h v) -> p h v", h=4), axis=AX.X),
                 reads=[sqkey], writes=['ss4'])
            S.op('act', lambda e: e.activation(out=rs4, in_=ss4, func=AF.Sqrt, scale=1.0 / 128, bias=epsc[0:64, :]),
                 reads=['ss4', 'cst'], writes=['rs4'])
            S.op('dve', lambda e: e.reciprocal(out=rs4, in_=rs4), reads=['rs4'], writes=['rs4'])
            on3 = on.rearrange("p (h v) -> p h v", h=4)
            S.op('dve', lambda e: e.tensor_tensor(out=on3, in0=o3, in1=rs4.unsqueeze(2).to_broadcast([64, 4, 128]),
                                                  op=ALU.mult), reads=[okey, 'rs4'], writes=[onkey])
            S.op('dve', lambda e: e.tensor_tensor(out=on3, in0=on3, in1=gnb_.unsqueeze(1).to_broadcast([64, 4, 128]),
                                                  op=ALU.mult), reads=[onkey, gkey], writes=[onkey])
            S.op('dve', lambda e: e.tensor_tensor(out=ogt_, in0=on, in1=gate_sb, op=ALU.mult),
                 reads=[onkey, gatekey], writes=[ogtkey])

        def phase_A1():
            AL.off = pers_mark
            AL.n = ARENA_COLS - HT_COLS
            winA = AL.get(128, [8, 2048], BF16)
            wgA = AL.get(128, [8, 1024], BF16)
            wbrA = AL.get(128, [4, 1024], BF16)
            stageA = [AL.get(128, 2048, F32) for _ in range(3)]
            load_w_fast(winA, win_d, 0, 8, 0, 2048, 'winA', stageA)
            load_w_fast(wgA, win_d, 0, 8, 4104, 1024, 'wgA', stageA)
            load_w_fast(wbrA, wbra_d, 0, 4, 0, 1024, 'wbrA', stageA)
            S.begin()
            for (src_t, c0) in ((eu_d, 0), (ev_d, 1024)):
                for r in range(0, 16384, 512):
                    S.dma('pool', lambda e, r=r, src_t=src_t, c0=c0: e.dma_start(
                        out=uvb_d[r:r + 512, c0:c0 + 1024], in_=src_t[r:r + 512, :]), writes=['uvb'])
            conv_dmas = S.end()
            gmixb = AL.get(64, 1024, F32)
            bcast_load(gmixb, gmix_d[0:1, :], 64, 1024, 'gmixb')
            lbb = AL.get(64, 512, F32)
            omlb = AL.get(64, 512, F32)
            lb1 = AL.get(64, 512, F32)
            bcast_load(lbb, lbp_d[0:1, :], 64, 512, 'lbb')
            bcast_load(lb1, lbp_d[1:2, :], 64, 512, 'lb1')
            S.op('dve', lambda e: e.tensor_tensor(out=lbb, in0=lbb, in1=lb1, op=ALU.subtract),
                 reads=['lbb', 'lb1'], writes=['lbb'])
            S.op('act', lambda e: e.activation(out=lbb, in_=lbb, func=AF.Sigmoid), reads=['lbb'], writes=['lbb'])
            S.op('dve', lambda e: e.tensor_scalar(out=omlb, in0=lbb, scalar1=-1.0, scalar2=1.0, op0=ALU.mult,
                                                  op1=ALU.add), reads=['lbb'], writes=['omlb'])
            gnab = AL.get(64, 128, F32)
            bcast_load(gnab, gna_d[0:1, :], 64, 128, 'gnab')
            xt = AL.get(64, 1024, F32)
            hb = AL.get(64, 1024, BF16)
            hT = AL.get(128, [8, 64], BF16)
            ss = AL.get(64, 1, F32)
            rs = AL.get(64, 1, F32)
            F = [AL.get(64, 512, F32) for _ in range(7)]
            qt = AL.get(64, 512, BF16)
            kt = AL.get(64, 512, BF16)
            va = AL.get(64, 512, BF16)
            km = AL.get(64, 512, BF16)
            qkT = AL.get(128, [8, 64], BF16)
            qm = AL.get(128, [4, 64], BF16)
            attm = AL.get(64, [4, 64], BF16)
            ebL = AL.get(128, [4, 16], F32)
            Sa = AL.get(128, [4, 128], F32)
            Sab = AL.get(128, [4, 128], BF16)
            SaL = [Sa, AL.get(128, [4, 128], F32)]
            SabL = [Sab, AL.get(128, [4, 128], BF16)]
            qmL = [qm, AL.get(128, [4, 64], BF16)]
            kmL = [km, AL.get(64, 512, BF16)]
            ss4 = AL.get(64, 4, F32)
            rs4 = AL.get(64, 4, F32)
            ogt = AL.get(64, 512, BF16)
            oT = AL.get(128, [4, 64], BF16)
            m1t = AL.get(64, 1024, F32)
            S.op('pool', lambda e: e.memset(Sa, 0.0), writes=['Sa0'])
            S.op('pool', lambda e: e.memset(Sab, 0.0), writes=['Sab0'])

            def tileA1(ti, T):
                nch = T['nch']
                r0 = ti * 64
                hT = hT_all[:, ti]
                S.dma('sp', lambda e: e.dma_start(out=xt, in_=x_d[r0:r0 + 64, :]), writes=['xt'])
                rmsnorm_to_bf16(xt, 'xt', gmixb, 'gmixb', hb, 'hb', ss, rs)
                dbg('xt', xt, 'xt', ti)
                dbg('rs', rs, 'rs', ti)
                dbg('hb', hb, 'hb', ti)
                transpose_tm(hb, 'hb', 8, hT, 'hT')
                dbg('hT', hT, 'hT', ti)
                dbg('winA', winA[:, :, 0:512], 'winA', ti)
                sig, q, kk, logf, eb, enb, og = F
                proj_tm(bank(0, 64), bk(0), hT, 'hT', winA, 'winA', 512, 512)
                S.op('act', lambda e: e.activation(out=sig, in_=bank(0, 64), func=AF.Sigmoid), reads=bk(0), writes=['F0'])
                proj_tm(bank(1, 64), bk(1), hT, 'hT', winA, 'winA', 0, 512)
                S.op('act', lambda e: e.activation(out=q, in_=bank(1, 64), func=AF.Silu), reads=bk(1), writes=['F1'])
                S.op('dve', lambda e: e.tensor_tensor(out=sig, in0=sig, in1=omlb, op=ALU.mult),
                     reads=['F0', 'omlb'], writes=['F0'])
                S.op('dve', lambda e: e.tensor_tensor(out=sig, in0=sig, in1=lbb, op=ALU.add),
                     reads=['F0', 'lbb'], writes=['F0'])
                S.op('dve', lambda e: e.tensor_scalar(out=kk, in0=sig, scalar1=-1.0, scalar2=1.0, op0=ALU.mult,
                                                      op1=ALU.add), reads=['F0'], writes=['F2'])
                S.op('act', lambda e: e.activation(out=logf, in_=sig, func=AF.Ln), reads=['F0'], writes=['F3'])
                dbg('q', q, 'F1', ti)
                dbg('f', sig, 'F0', ti)
                dbg('logf', logf, 'F3', ti)
                S.op('pe', lambda e: e.matmul(bank(2, 64), lhsT=T['tri'], rhs=logf, start=True, stop=True),
                     reads=['cst', 'F3'], writes=bk(2))
                S.op('act', lambda e: e.activation(out=eb, in_=bank(2, 64), func=AF.Exp), reads=bk(2), writes=['F4'])
                S.op('act', lambda e: e.activation(out=enb, in_=bank(2, 64), func=AF.Exp, scale=-1.0),
                     reads=bk(2), writes=['F5'])
                S.op('dve', lambda e: e.tensor_tensor(out=qt, in0=q, in1=eb, op=ALU.mult),
                     reads=['F1', 'F4'], writes=['qt'])
                S.op('dve', lambda e: e.tensor_tensor(out=kt, in0=kk, in1=enb, op=ALU.mult),
                     reads=['F2', 'F5'], writes=['kt'])
                for h in range(4):
                    S.op('pe', lambda e, h=h: e.matmul(bank(0, 128, nch, h * 16), lhsT=logf[:, h * 128:(h + 1) * 128],
                                                       rhs=T['cind'][:, 0:nch], start=True, stop=True),
                         reads=['F3', 'cst'], writes=bk(0))
                S.op('act', lambda e: e.activation(out=ebL[:, :, 0:nch],
                                                   in_=bank(0, 128, 64).rearrange("p (h c) -> p h c", h=4)[:, :, 0:nch],
                                                   func=AF.Exp), reads=bk(0), writes=['ebL'])
                proj_tm(bank(1, 64), bk(1), hT, 'hT', winA, 'winA', 1024, 512)
                S.op('act', lambda e: e.copy(out=va, in_=bank(1, 64)), reads=bk(1), writes=['va'])
                proj_tm(bank(2, 64), bk(2), hT, 'hT', winA, 'winA', 1536, 512)
                S.op('act', lambda e: e.activation(out=og, in_=bank(2, 64), func=AF.Silu), reads=bk(2), writes=['F6'])
                for h in range(4):
                    S.op('pe', lambda e, h=h: e.transpose(out=PTt[:, h * 64:(h + 1) * 64], in_=qt[:, h * 128:(h + 1) * 128],
                                                          identity=identb[0:64, 0:64]),
                         reads=['qt', 'identb'], writes=['pt'])
                for h in range(4):
                    S.op('pe', lambda e, h=h: e.transpose(out=PTt[:, (4 + h) * 64:(5 + h) * 64],
                                                          in_=kt[:, h * 128:(h + 1) * 128], identity=identb[0:64, 0:64]),
                         reads=['kt', 'identb'], writes=['pt'])
                S.op('act', lambda e: e.copy(out=qkT, in_=PTt[:, 0:512].rearrange("p (k t) -> p k t", k=8)),
                     reads=['pt'], writes=['qkT'])
                for h in range(4):
                    S.op('pe', lambda e, h=h: e.matmul(bank(0, 64, 64, h * 64), lhsT=qkT[:, 4 + h, :], rhs=qkT[:, h, :],
                                                       start=True, stop=True), reads=['qkT'], writes=bk(0))
                S.op('dve', lambda e: e.tensor_tensor(out=attm, in0=bank(0, 64, 256).rearrange("p (h t) -> p h t", h=4),
                                                      in1=T['tri'].unsqueeze(1).to_broadcast([64, 4, 64]), op=ALU.mult),
                     reads=bk(0) + ['cst'], writes=['attm'])
                for h in range(4):
                    S.op('pe', lambda e, h=h: e.matmul(bank(5, 64, 128, h * 128), lhsT=attm[:, h, :],
                                                       rhs=va[:, h * 128:(h + 1) * 128], start=(h == 0), stop=False,
                                                       skip_group_check=True),
                         reads=['attm', 'va'], writes=bk(5))
                for c in range(nch):
                    pb = c % 2 if nch > 1 else 0
                    Sa_, Sab_, qm_, km_ = SaL[pb], SabL[pb], qmL[pb], kmL[pb]
                    sak, sabk, qmk, kmk = 'Sa%d' % pb, 'Sab%d' % pb, 'qm%d' % pb, 'km%d' % pb
                    if nch > 1:
                        S.dma('sp', lambda e, c=c, Sa_=Sa_: e.dma_start(out=Sa_, in_=sh_d[c].rearrange("h d v -> d h v")),
                              writes=[sak])
                        S.op('act', lambda e, Sa_=Sa_, Sab_=Sab_: e.copy(out=Sab_, in_=Sa_), reads=[sak], writes=[sabk])
                        S.op('dve', lambda e, c=c, qm_=qm_: e.tensor_tensor(
                            out=qm_, in0=qkT[:, 0:4, :], in1=cmb[:, c, :].unsqueeze(1).to_broadcast([128, 4, 64]),
                            op=ALU.mult), reads=['qkT', 'cmb'], writes=[qmk])
                        S.op('dve', lambda e, c=c, km_=km_: e.tensor_scalar(out=km_, in0=kt, scalar1=T['cind'][:, c:c + 1],
                                                                            scalar2=None, op0=ALU.mult),
                             reads=['kt', 'cst'], writes=[kmk])
                        qsrc, qkey, ksrc, kkey = qm_, qmk, km_, kmk
                    else:
                        qsrc, qkey, ksrc, kkey = qkT, 'qkT', kt, 'kt'
                    for h in range(4):
                        S.op('pe', lambda e, h=h, qsrc=qsrc, c=c, Sab_=Sab_: e.matmul(
                            bank(5, 64, 128, h * 128), lhsT=qsrc[:, h, :], rhs=Sab_[:, h, :],
                            start=False, stop=(c == nch - 1), skip_group_check=True), reads=[qkey, sabk], writes=bk(5))
                    for h in range(4):
                        S.op('pe', lambda e, h=h, ksrc=ksrc: e.matmul(
                            bank(6, 128, 128, h * 128), lhsT=ksrc[:, h * 128:(h + 1) * 128],
                            rhs=va[:, h * 128:(h + 1) * 128], start=True, stop=True),
                            reads=[kkey, 'va'], writes=bk(6))
                    S.op('dve', lambda e, Sa_=Sa_: e.tensor_tensor(out=Sa_, in0=bank(6).rearrange("p (h v) -> p h v", h=4),
                                                                   in1=Sa_, op=ALU.add), reads=bk(6) + [sak], writes=[sak])
                    S.op('dve', lambda e, c=c, Sa_=Sa_: e.tensor_tensor(
                        out=Sa_, in0=Sa_, in1=ebL[:, :, c:c + 1].to_broadcast([128, 4, 128]), op=ALU.mult),
                        reads=[sak, 'ebL'], writes=[sak])
                    if nch > 1:
                        S.dma('sp', lambda e, c=c, Sa_=Sa_: e.dma_start(out=hs_d[c].rearrange("h d v -> d h v"), in_=Sa_),
                              reads=[sak], final=True)
                    else:
                        S.op('act', lambda e, Sa_=Sa_, Sab_=Sab_: e.copy(out=Sab_, in_=Sa_), reads=[sak], writes=[sabk])
                if nch == 1 and ti == NPT - 1:
                    S.dma('sp', lambda e: e.dma_start(out=hp_d.rearrange("h d v -> d h v"), in_=Sa),
                          reads=['Sa0'], final=True)
                osb, sq, on = F[0], F[1], F[2]
                S.op('act', lambda e: e.copy(out=osb, in_=bank(5, 64)), reads=bk(5), writes=['F0'])
                dbg('osb', osb, 'F0', ti, DBGT)
                dbg('og', og, 'F6', ti, DBGT)
                head_norm_gate_keys(osb, 'F0', gnab, 'gnab', og, 'F6', sq, 'F1', on, 'F2', ogt, 'ogt', ss4, rs4)
                dbg('on', on, 'F2', ti, DBGT)
                transpose_tm(ogt, 'ogt', 4, oT, 'oT')
                for half in range(2):
                    for k in range(4):
                        S.op('pe', lambda e, k=k, half=half: e.matmul(
                            bank(3 + half, 64), lhsT=oT[:, k, :], rhs=wbrA[:, k, half * 512:(half + 1) * 512],
                            start=(k == 0), stop=(k == 3)), reads=['oT', 'wbrA'], writes=bk(3 + half))
                    proj_tm(bank(half, 64), bk(half), hT, 'hT', wgA, 'wgA', half * 512, 512)
                    sg = F[3 + half]
                    S.op('act', lambda e, half=half, sg=sg: e.activation(out=sg, in_=bank(half, 64), func=AF.Sigmoid),
                         reads=bk(half), writes=['F%d' % (3 + half)])
                    S.op('dve', lambda e, half=half, sg=sg: e.tensor_tensor(
                        out=m1t[:, half * 512:(half + 1) * 512], in0=bank(3 + half, 64), in1=sg, op=ALU.mult),
                        reads=bk(3 + half) + ['F%d' % (3 + half)], writes=['m1t'])
                S.dma('sp', lambda e: e.dma_start(out=m1_d[r0:r0 + 64, :], in_=m1t), reads=['m1t'],
                      writes=[('m1', ti)])

            if 'A1' in PHASES:
                for ti in (range(NPT) if TILES is None else TILES):
                    tileA1(ti, TP)
                    S.run(conv_dmas[0:2])
                    del conv_dmas[0:2]
                tileA1(NPT, TS)
            S.run(conv_dmas)
        phase_A1()
        S.barrier()

        def phase_A2():
            AL.off = pers_mark
            AL.n = ARENA_COLS - HT_COLS
            winB = AL.get(128, [8, 2056], BF16)
            wgB = AL.get(128, [8, 1024], BF16)
            wbrB = AL.get(128, [4, 1024], BF16)
            woutb = AL.get(128, [8, 1024], BF16)
            stageB = [AL.get(128, 2048, F32) for _ in range(2)]
            load_w_fast(winB, win_d, 0, 8, 2048, 2056, 'winB', stageB)
            load_w_fast(wgB, win_d, 0, 8, 5128, 1024, 'wgB', stageB)
            load_w_fast(wbrB, wbrb_d, 0, 4, 0, 1024, 'wbrB', stageB)
            load_w_fast(woutb, wout_d, 0, 8, 0, 1024, 'woutb', stageB)
            gnbb = AL.get(64, 128, F32)
            bcast_load(gnbb, gnb_d[0:1, :], 64, 128, 'gnbb')
            negA = AL.get(64, 4, F32)
            dtbb = AL.get(64, 4, F32)
            bcast_load(negA, alog_d[0:1, :], 64, 4, 'negA')
            bcast_load(dtbb, dtb_d[0:1, :], 64, 4, 'dtbb')
            S.op('act', lambda e: e.activation(out=negA, in_=negA, func=AF.Exp), reads=['negA'], writes=['negA'])
            S.op('dve', lambda e: e.tensor_scalar(out=negA, in0=negA, scalar1=-1.0, scalar2=None, op0=ALU.mult),
                 reads=['negA'], writes=['negA'])
            cwin = AL.get(4, 1536, F32)
            cw = AL.get(128, [12, 4], F32)
            S.dma('sp', lambda e: e.dma_start(out=cwin, in_=convw_d), writes=['cwin'])
            for g in range(12):
                S.op('pe', lambda e, g=g: e.transpose(out=bank(0, 128, 4, g * 4), in_=cwin[0:4, g * 128:(g + 1) * 128],
                                                      identity=identf[0:4, 0:4]), reads=['cwin', 'cst'], writes=bk(0))
            S.op('dve', lambda e: e.tensor_copy(out=cw, in_=bank(0, 128, 48).rearrange("p (g j) -> p g j", g=12)),
                 reads=bk(0), writes=['cw'])
            xt = AL.get(64, 1024, F32)
            F = [AL.get(64, 512, F32) for _ in range(6)]
            rawext = AL.get(128, 12 * 112, F32)
            rawtm = AL.get(64, 1536, F32)
            ctmp = AL.get(128, [12, 64], F32)
            acc = AL.get(128, [12, 64], F32)
            sqn = AL.get(128, 512, F32)
            qkn = AL.get(128, [8, 64], BF16)
            qknm = AL.get(128, [8, 64], BF16)
            vcb = AL.get(128, [4, 64], BF16)
            vk = AL.get(64, [8, 128], BF16)
            sm = AL.get(64, 64, F32)
            beta, zz, ez, spl, gg, gcum, gLb, eg, ngcum, dkk, kdec, nbeta, nbe = [sm[:, i * 4:(i + 1) * 4] for i in range(13)]
            gc = AL.get(64, [4, 16], F32)
            egLT = AL.get(128, [4, 16], F32)
            gd = AL.get(64, [4, 64], F32)
            gam = AL.get(64, [4, 64], F32)
            gamT = AL.get(64, [4, 64], F32)
            Nb = [AL.get(64, [4, 64], BF16) for _ in range(2)]
            Mb = [AL.get(64, [4, 64], BF16) for _ in range(2)]
            Qb = AL.get(64, [4, 64], BF16)
            Qf = AL.get(64, [4, 64], F32)
            rb = AL.get(64, [4, 128], BF16)
            ub = AL.get(64, [4, 128], BF16)
            ATb = AL.get(64, [4, 64], BF16)
            khat = AL.get(64, [4, 128], BF16)
            khm = AL.get(64, [4, 128], BF16)
            Sd = AL.get(128, [4, 128], F32)
            Sdb = AL.get(128, [4, 128], BF16)
            SdL = [Sd, AL.get(128, [4, 128], F32)]
            SdbL = [Sdb, AL.get(128, [4, 128], BF16)]
            qknmL = [qknm, AL.get(128, [8, 64], BF16)]
            khmL = [khm, AL.get(64, [4, 128], BF16)]
            ss4 = AL.get(64, 4, F32)
            rs4 = AL.get(64, 4, F32)
            ogt = AL.get(64, 512, BF16)
            oT = AL.get(128, [4, 64], BF16)
            m1t = AL.get(64, 1024, F32)
            mb = AL.get(64, 1024, BF16)
            mT = AL.get(128, [8, 64], BF16)
            S.op('pool', lambda e: e.memset(Sd, 0.0), writes=['Sd0'])
            S.op('pool', lambda e: e.memset(Sdb, 0.0), writes=['Sdb0'])
            S.op('pool', lambda e: e.memset(rawext, 0.0), writes=['rawext'])

            def tileA2(ti, T):
                nch, C = T['nch'], T['C']
                r0 = ti * 64
                W = 3 + C
                rx = rawext[:, 0:12 * nch * W].rearrange("p (g s w) -> p g s w", g=12, s=nch)
                S.dma('sp', lambda e: e.dma_start(out=xt, in_=x_d[r0:r0 + 64, :]), writes=['xt'])
                S.dma('sp', lambda e: e.dma_start(out=m1t, in_=m1_d[r0:r0 + 64, :]), reads=[('m1', ti)], writes=['m1t'])
                hT = hT_all[:, ti]
                praw = PB[:, 3 * 512:3 * 512 + 768].rearrange("p (g t) -> p g t", g=12)
                for i3 in range(3):
                    proj_tm(bank(i3, 64), bk(i3), hT, 'hT', winB, 'winB', i3 * 512, 512)
                    S.op('act', lambda e, i3=i3: e.copy(out=rawtm[:, i3 * 512:(i3 + 1) * 512], in_=bank(i3, 64)),
                         reads=bk(i3), writes=[('rawtm', i3)])
                for g in range(12):
                    S.op('pe', lambda e, g=g: e.transpose(out=praw[:, g, :], in_=rawtm[:, g * 128:(g + 1) * 128],
                                                          identity=identf[0:64, 0:64]),
                         reads=[('rawtm', g // 4), 'cst'], writes=bk(3, 4))
                if nch == 1:
                    if ti > 0:
                        S.op('pool', lambda e: e.tensor_copy(out=rx[:, :, 0, 0:3], in_=rx[:, :, 0, 64:67]),
                             reads=['rawext'], writes=['rawext'])
                    S.op('act', lambda e: e.copy(out=rx[:, :, 0, 3:67], in_=praw), reads=bk(3, 4), writes=['rawext'])
                else:
                    cvin = F[0:3]
                    for i3 in range(3):
                        S.dma('sp', lambda e, i3=i3: e.dma_start(out=cvin[i3][0:48, :],
                                                                 in_=scv_d[:, i3 * 512:(i3 + 1) * 512]),
                              writes=['F%d' % i3])
                    for g in range(12):
                        S.op('pe', lambda e, g=g: e.transpose(
                            out=bank(0, 128, 48, g * 64), in_=cvin[g // 4][0:48, (g % 4) * 128:(g % 4 + 1) * 128],
                            identity=identf[0:48, 0:48]), reads=['F%d' % (g // 4), 'cst'], writes=bk(0, 1))
                    S.op('dve', lambda e: e.tensor_copy(
                        out=rx[:, :, :, 0:3],
                        in_=PB[:, 0:768].rearrange("p (g x) -> p g x", g=12)[:, :, 0:48].rearrange(
                            "p g (s j) -> p g s j", s=16)),
                        reads=bk(0, 1), writes=['rawext'])
                    S.op('act', lambda e: e.copy(out=rx[:, :, :, 3:7],
                                                 in_=praw.rearrange("p g (s t) -> p g s t", s=16)),
                         reads=bk(3, 4), writes=['rawext'])
                accv = acc.rearrange("p g (s t) -> p g s t", s=nch)
                tmpv = ctmp.rearrange("p g (s t) -> p g s t", s=nch)
                for j in range(4):
                    dst = accv if j == 0 else tmpv
                    dkey = [('acc', g) for g in range(12)] if j == 0 else ['ctmp']
                    S.op('dve', lambda e, j=j, dst=dst: e.tensor_tensor(
                        out=dst, in0=rx[:, :, :, j:j + C],
                        in1=cw[:, :, j:j + 1].unsqueeze(3).to_broadcast([128, 12, nch, C]), op=ALU.mult),
                        reads=['rawext', 'cw'], writes=dkey)
                    if j > 0:
                        S.op('dve', lambda e: e.tensor_tensor(out=acc, in0=acc, in1=ctmp, op=ALU.add),
                             reads=[('acc', g) for g in range(12)] + ['ctmp'], writes=[('acc', g) for g in range(12)])
                acck = [('acc', g) for g in range(12)]
                S.op('act', lambda e: e.activation(out=acc, in_=acc, func=AF.Silu), reads=acck, writes=acck)
                S.op('act', lambda e: e.activation(out=sqn, in_=acc[:, 0:8, :].rearrange("p g t -> p (g t)"),
                                                   func=AF.Square), reads=acck, writes=['sqn'])
                S.op('pe', lambda e: e.matmul(bank(0), lhsT=onesf, rhs=sqn, start=True, stop=True),
                     reads=['cst', 'sqn'], writes=bk(0))
                S.op('act', lambda e: e.activation(out=sqn, in_=bank(0), func=AF.Sqrt, bias=epsc), reads=bk(0) + ['cst'],
                     writes=['sqn'])
                S.op('dve', lambda e: e.reciprocal(out=sqn, in_=sqn), reads=['sqn'], writes=['sqn'])
                sq3 = sqn.rearrange("p (g t) -> p g t", g=8)
                S.op('dve', lambda e: e.scalar_tensor_tensor(out=qkn[:, 0:4, :], in0=acc[:, 0:4, :], scalar=128.0 ** -0.5,
                                                             in1=sq3[:, 0:4, :], op0=ALU.mult, op1=ALU.mult),
                     reads=acck + ['sqn'], writes=['qkn'])
                S.op('dve', lambda e: e.tensor_tensor(out=qkn[:, 4:8, :], in0=acc[:, 4:8, :], in1=sq3[:, 4:8, :],
                                                      op=ALU.mult), reads=acck + ['sqn'], writes=['qkn'])
                S.op('act', lambda e: e.copy(out=vcb, in_=acc[:, 8:12, :]), reads=acck, writes=['vcb'])
                for h in range(4):
                    S.op('pe', lambda e, h=h: e.transpose(out=PTt[0:64, h * 128:(h + 1) * 128], in_=vcb[:, h, :],
                                                          identity=identb), reads=['vcb', 'identb'], writes=['pt'])
                for h in range(4):
                    S.op('pe', lambda e, h=h: e.transpose(out=PTt[0:64, (4 + h) * 128:(5 + h) * 128], in_=qkn[:, 4 + h, :],
                                                          identity=identb), reads=['qkn', 'identb'], writes=['pt'])
                S.op('act', lambda e: e.copy(out=vk, in_=PTt[0:64, :].rearrange("p (k v) -> p k v", k=8)),
                     reads=['pt'], writes=['vk'])
                proj_tm(bank(1, 64, 8), bk(1), hT, 'hT', winB, 'winB', 2048, 8)
                S.op('act', lambda e: e.activation(out=beta, in_=bank(1, 64, 4), func=AF.Sigmoid), reads=bk(1), writes=['sm'])
                S.op('dve', lambda e: e.tensor_tensor(out=zz, in0=bank(1, 64, 4, 4), in1=dtbb, op=ALU.add),
                     reads=bk(1) + ['dtbb'], writes=['sm'])
                S.op('act', lambda e: e.activation(out=ez, in_=zz, func=AF.Exp), reads=['sm'], writes=['sm'])
                S.op('act', lambda e: e.activation(out=spl, in_=ez, func=AF.Ln, bias=C_('one', 64)), reads=['sm', 'cst'],
                     writes=['sm'])
                S.op('dve', lambda e: e.tensor_tensor(out=gg, in0=spl, in1=negA, op=ALU.mult), reads=['sm', 'negA'],
                     writes=['sm'])
                S.op('pe', lambda e: e.matmul(bank(2, 64, 4), lhsT=T['tri'], rhs=gg, start=True, stop=True),
                     reads=['cst', 'sm'], writes=bk(2))
                S.op('pe', lambda e: e.matmul(bank(2, 64, 4, 4), lhsT=T['blk'], rhs=gg, start=True, stop=True),
                     reads=['cst', 'sm'], writes=bk(2))
                S.op('dve', lambda e: e.tensor_copy(out=sm[:, 20:28], in_=bank(2, 64, 8)), reads=bk(2), writes=['sm'])
                S.op('act', lambda e: e.activation(out=eg, in_=gcum, func=AF.Exp), reads=['sm'], writes=['sm'])
                S.op('dve', lambda e: e.tensor_scalar(out=ngcum, in0=gcum, scalar1=-1.0, scalar2=None, op0=ALU.mult),
                     reads=['sm'], writes=['sm'])
                S.op('dve', lambda e: e.tensor_tensor(out=dkk, in0=gLb, in1=gcum, op=ALU.subtract), reads=['sm'],
                     writes=['sm'])
                S.op('act', lambda e: e.activation(out=kdec, in_=dkk, func=AF.Exp), reads=['sm'], writes=['sm'])
                S.op('dve', lambda e: e.tensor_scalar(out=nbeta, in0=beta, scalar1=-1.0, scalar2=None, op0=ALU.mult),
                     reads=['sm'], writes=['sm'])
                S.op('dve', lambda e: e.tensor_tensor(out=nbe, in0=nbeta, in1=eg, op=ALU.mult), reads=['sm'],
                     writes=['sm'])
                S.op('dve', lambda e: e.tensor_tensor(out=gc[:, :, 0:nch], in0=gg.unsqueeze(2).to_broadcast([64, 4, nch]),
                                                      in1=T['cind'][:, 0:nch].unsqueeze(1).to_broadcast([64, 4, nch]),
                                                      op=ALU.mult), reads=['sm', 'cst'], writes=['gc'])
                for h in range(4):
                    S.op('pe', lambda e, h=h: e.matmul(bank(1, 128, nch, 64 + h * 16), lhsT=onesf[0:64, :],
                                                       rhs=gc[:, h, 0:nch], start=True, stop=True),
                         reads=['cst', 'gc'], writes=bk(1))
                S.op('act', lambda e: e.activation(
                    out=egLT[:, :, 0:nch], in_=bank(1, 128, 64, 64).rearrange("p (h c) -> p h c", h=4)[:, :, 0:nch],
                    func=AF.Exp), reads=bk(1), writes=['egLT'])
                S.op('dve', lambda e: e.tensor_tensor(out=gd, in0=gcum.unsqueeze(2).to_broadcast([64, 4, 64]),
                                                      in1=identf[0:64, 0:64].unsqueeze(1).to_broadcast([64, 4, 64]),
                                                      op=ALU.mult), reads=['sm', 'cst'], writes=['gd'])
                gd2 = gd.rearrange("p h t -> p (h t)")
                S.op('pe', lambda e: e.matmul(bank(2, 64, 256), lhsT=onesf[0:64, 0:64], rhs=gd2, start=True, stop=False),
                     reads=['cst', 'gd'], writes=bk(2))
                S.op('pe', lambda e: e.matmul(bank(2, 64, 256), lhsT=identf[0:64, 0:64], rhs=T['pmS'], start=False,
                                              stop=True), reads=['cst'], writes=bk(2))
                S.op('pe', lambda e: e.matmul(bank(2, 64, 256, 256), lhsT=onesf[0:64, 0:64], rhs=gd2, start=True,
                                              stop=False), reads=['cst', 'gd'], writes=bk(2))
                S.op('pe', lambda e: e.matmul(bank(2, 64, 256, 256), lhsT=identf[0:64, 0:64], rhs=T['nmT'], start=False,
                                              stop=True), reads=['cst'], writes=bk(2))
                for h in range(4):
                    S.op('act', lambda e, h=h: e.activation(out=gam[:, h, :], in_=bank(2, 64, 64, h * 64), func=AF.Exp,
                                                            scale=-1.0, bias=gcum[:, h:h + 1]),
                         reads=bk(2) + ['sm'], writes=['gam'])
                for h in range(4):
                    S.op('act', lambda e, h=h: e.activation(out=gamT[:, h, :], in_=bank(2, 64, 64, 256 + h * 64),
                                                            func=AF.Exp, bias=ngcum[:, h:h + 1]),
                         reads=bk(2) + ['sm'], writes=['gamT'])
                for h in range(4):
                    S.op('pe', lambda e, h=h: e.matmul(bank(0, 64, 64, h * 64), lhsT=qkn[:, 4 + h, :], rhs=qkn[:, 4 + h, :],
                                                       start=True, stop=True), reads=['qkn'], writes=bk(0))
                for h in range(4):
                    S.op('dve', lambda e, h=h: e.scalar_tensor_tensor(
                        out=Nb[0][:, h, :], in0=bank(0, 64, 64, h * 64), scalar=nbeta[:, h:h + 1], in1=gam[:, h, :],
                        op0=ALU.mult, op1=ALU.mult), reads=bk(0) + ['sm', 'gam'], writes=['Nb0'])
                for h in range(4):
                    S.op('pe', lambda e, h=h: e.transpose(out=PTt[0:64, h * 64:(h + 1) * 64], in_=Nb[0][:, h, :],
                                                          identity=identb[0:64, 0:64]),
                         reads=['Nb0', 'identb'], writes=['pt'])
                ptv = PTt[0:64, 0:256].rearrange("p (h t) -> p h t", h=4)
                S.op('dve', lambda e: e.tensor_copy(out=Mb[0], in_=ptv), reads=['pt'], writes=['Mb0'])
                S.op('dve', lambda e: e.tensor_tensor(out=Qf, in0=ptv,
                                                      in1=identf[0:64, 0:64].unsqueeze(1).to_broadcast([64, 4, 64]),
                                                      op=ALU.add), reads=['pt', 'cst'], writes=['Qf'])
                S.op('act', lambda e: e.copy(out=Qb, in_=Qf), reads=['Qf'], writes=['Qb'])
                nsteps = {64: 5, 4: 1}[C]
                cur = 0
                for i in range(nsteps):
                    last = (i == nsteps - 1)
                    nxt = 1 - cur
                    for h in range(4):
                        S.op('pe', lambda e, h=h, cur=cur: e.matmul(bank(0, 64, 64, h * 64), lhsT=Mb[cur][:, h, :],
                                                                    rhs=Nb[cur][:, h, :], start=True, stop=True),
                             reads=['Mb%d' % cur, 'Nb%d' % cur], writes=bk(0))
                    if not last:
                        for h in range(4):
                            S.op('pe', lambda e, h=h, cur=cur: e.matmul(bank(1, 64, 64, h * 64), lhsT=Nb[cur][:, h, :],
                                                                        rhs=Mb[cur][:, h, :], start=True, stop=True),
                                 reads=['Mb%d' % cur, 'Nb%d' % cur], writes=bk(1))
                    S.op('act', lambda e, nxt=nxt: e.copy(out=Nb[nxt],
                                                          in_=bank(0, 64, 256).rearrange("p (h t) -> p h t", h=4)),
                         reads=bk(0), writes=['Nb%d' % nxt])
                    if not last:
                        S.op('dve', lambda e, nxt=nxt: e.tensor_copy(
                            out=Mb[nxt], in_=bank(1, 64, 256).rearrange("p (h t) -> p h t", h=4)),
                            reads=bk(1), writes=['Mb%d' % nxt])
                    for h in range(4):
                        S.op('pe', lambda e, h=h, nxt=nxt: e.matmul(bank(2, 64, 64, h * 64), lhsT=Nb[nxt][:, h, :],
                                                                    rhs=Qb[:, h, :], start=True, stop=True),
                             reads=['Nb%d' % nxt, 'Qb'], writes=bk(2))
                    S.op('dve', lambda e: e.tensor_tensor(out=Qf, in0=Qf,
                                                          in1=bank(2, 64, 256).rearrange("p (h t) -> p h t", h=4),
                                                          op=ALU.add), reads=['Qf'] + bk(2), writes=['Qf'])
                    S.op('act', lambda e: e.copy(out=Qb, in_=Qf), reads=['Qf'], writes=['Qb'])
                    cur = nxt
                for h in range(4):
                    S.op('pe', lambda e, h=h: e.matmul(bank(0, 64, 64, h * 64), lhsT=qkn[:, 4 + h, :], rhs=qkn[:, h, :],
                                                       start=True, stop=True), reads=['qkn'], writes=bk(0))
                S.op('dve', lambda e: e.tensor_tensor(out=ATb, in0=bank(0, 64, 256).rearrange("p (h t) -> p h t", h=4),
                                                      in1=gamT, op=ALU.mult), reads=bk(0) + ['gamT'], writes=['ATb'])
                for c in range(nch):
                    pb = c % 2 if nch > 1 else 0
                    Sd_, Sdb_, qknm_ = SdL[pb], SdbL[pb], qknmL[pb]
                    sdk, sdbk, qnk = 'Sd%d' % pb, 'Sdb%d' % pb, 'qknm%d' % pb
                    if nch > 1:
                        S.dma('sp', lambda e, c=c, Sd_=Sd_: e.dma_start(out=Sd_, in_=sd_d[c].rearrange("h d v -> d h v")),
                              writes=[sdk])
                        S.op('act', lambda e, Sd_=Sd_, Sdb_=Sdb_: e.copy(out=Sdb_, in_=Sd_), reads=[sdk], writes=[sdbk])
                        S.op('dve', lambda e, c=c, qknm_=qknm_: e.tensor_tensor(
                            out=qknm_, in0=qkn, in1=cmb[:, c, :].unsqueeze(1).to_broadcast([128, 8, 64]), op=ALU.mult),
                            reads=['qkn', 'cmb'], writes=[qnk])
                        src, skey = qknm_, qnk
                    else:
                        src, skey = qkn, 'qkn'
                    for h in range(4):
                        S.op('pe', lambda e, h=h, src=src, c=c, Sdb_=Sdb_: e.matmul(
                            bank(5, 64, 128, h * 128), lhsT=src[:, 4 + h, :], rhs=Sdb_[:, h, :], start=(c == 0 and h == 0),
                            stop=(c == nch - 1), skip_group_check=True), reads=[skey, sdbk], writes=bk(5))
                        S.op('pe', lambda e, h=h, src=src, c=c, Sdb_=Sdb_: e.matmul(
                            bank(6, 64, 128, h * 128), lhsT=src[:, h, :], rhs=Sdb_[:, h, :], start=(c == 0 and h == 0),
                            stop=(c == nch - 1), skip_group_check=True), reads=[skey, sdbk], writes=bk(6))
                t1, bv, t2, ob = F[3], F[4], F[3], F[4]
                S.op('dve', lambda e: e.tensor_tensor(out=t1.rearrange("p (h v) -> p h v", h=4),
                                                      in0=bank(5, 64).rearrange("p (h v) -> p h v", h=4),
                                                      in1=nbe.unsqueeze(2).to_broadcast([64, 4, 128]), op=ALU.mult),
                     reads=bk(5) + ['sm'], writes=['F3'])
                S.op('dve', lambda e: e.tensor_tensor(out=bv.rearrange("p (h v) -> p h v", h=4), in0=vk[:, 0:4, :],
                                                      in1=beta.unsqueeze(2).to_broadcast([64, 4, 128]), op=ALU.mult),
                     reads=['vk', 'sm'], writes=['F4'])
                S.op('dve', lambda e: e.tensor_tensor(out=rb.rearrange("p h v -> p (h v)"), in0=t1, in1=bv, op=ALU.add),
                     reads=['F3', 'F4'], writes=['rb'])
                for h in range(4):
                    S.op('pe', lambda e, h=h: e.matmul(bank(1, 64, 128, h * 128), lhsT=Qb[:, h, :], rhs=rb[:, h, :],
                                                       start=True, stop=True), reads=['Qb', 'rb'], writes=bk(1))
                S.op('act', lambda e: e.copy(out=ub, in_=bank(1, 64).rearrange("p (h v) -> p h v", h=4)),
                     reads=bk(1), writes=['ub'])
                for h in range(4):
                    S.op('pe', lambda e, h=h: e.matmul(bank(2, 64, 128, h * 128), lhsT=ATb[:, h, :], rhs=ub[:, h, :],
                                                       start=True, stop=True), reads=['ATb', 'ub'], writes=bk(2))
                S.op('dve', lambda e: e.tensor_tensor(out=t2.rearrange("p (h v) -> p h v", h=4),
                                                      in0=bank(6, 64).rearrange("p (h v) -> p h v", h=4),
                                                      in1=eg.unsqueeze(2).to_broadcast([64, 4, 128]), op=ALU.mult),
                     reads=bk(6) + ['sm'], writes=['F3'])
                S.op('dve', lambda e: e.tensor_tensor(out=ob, in0=t2, in1=bank(2, 64), op=ALU.add),
                     reads=['F3'] + bk(2), writes=['F4'])
                S.op('dve', lambda e: e.tensor_tensor(out=khat, in0=vk[:, 4:8, :],
                                                      in1=kdec.unsqueeze(2).to_broadcast([64, 4, 128]), op=ALU.mult),
                     reads=['vk', 'sm'], writes=['khat'])
                for c in range(nch):
                    pb = c % 2 if nch > 1 else 0
                    Sd_, Sdb_, khm_ = SdL[pb], SdbL[pb], khmL[pb]
                    sdk, sdbk, khk = 'Sd%d' % pb, 'Sdb%d' % pb, 'khm%d' % pb
                    if nch > 1:
                        S.dma('sp', lambda e, c=c, Sd_=Sd_: e.dma_start(out=Sd_, in_=sd_d[c].rearrange("h d v -> d h v")),
                              writes=[sdk])
                        S.op('dve', lambda e, c=c, khm_=khm_: e.tensor_scalar(out=khm_, in0=khat,
                                                                              scalar1=T['cind'][:, c:c + 1],
                                                                              scalar2=None, op0=ALU.mult),
                             reads=['khat', 'cst'], writes=[khk])
                        ks_, kkey = khm_, khk
                    else:
                        ks_, kkey = khat, 'khat'
                    for h in range(4):
                        S.op('pe', lambda e, h=h, ks_=ks_: e.matmul(bank(0, 128, 128, h * 128), lhsT=ks_[:, h, :],
                                                                    rhs=ub[:, h, :], start=True, stop=True),
                             reads=[kkey, 'ub'], writes=bk(0))
                    S.op('dve', lambda e, c=c, Sd_=Sd_: e.tensor_tensor(
                        out=Sd_, in0=Sd_, in1=egLT[:, :, c:c + 1].to_broadcast([128, 4, 128]), op=ALU.mult),
                        reads=[sdk, 'egLT'], writes=[sdk])
                    S.op('dve', lambda e, Sd_=Sd_: e.tensor_tensor(out=Sd_, in0=Sd_,
                                                                   in1=bank(0).rearrange("p (h v) -> p h v", h=4),
                                                                   op=ALU.add), reads=[sdk] + bk(0), writes=[sdk])
                    if nch > 1:
                        S.dma('sp', lambda e, c=c, Sd_=Sd_: e.dma_start(out=ds_d[c].rearrange("h d v -> d h v"), in_=Sd_),
                              reads=[sdk], final=True)
                    else:
                        S.op('act', lambda e, Sd_=Sd_, Sdb_=Sdb_: e.copy(out=Sdb_, in_=Sd_), reads=[sdk], writes=[sdbk])
                if nch == 1 and ti == NPT - 1:
                    S.dma('sp', lambda e: e.dma_start(out=dp_d.rearrange("h d v -> d h v"), in_=Sd),
                          reads=['Sd0'], final=True)
                if nch > 1 or ti == NPT - 1:
                    rk = [('rawtm', i3) for i3 in range(3)]
                    if nch == 1:
                        S.dma('sp', lambda e: e.dma_start(out=cp_d, in_=rawtm[61:64, :]), reads=rk, final=True)
                    else:
                        for sq_ in range(16):
                            S.dma('sp', lambda e, sq_=sq_: e.dma_start(
                                out=cs_d[3 * sq_:3 * sq_ + 3, :], in_=rawtm[4 * sq_ + 1:4 * sq_ + 4, :]),
                                reads=rk, final=True)
                proj_tm(bank(0, 64), bk(0), hT, 'hT', winB, 'winB', 1536, 512)
                S.op('act', lambda e: e.activation(out=F[5], in_=bank(0, 64), func=AF.Silu), reads=bk(0), writes=['F5'])
                head_norm_gate_keys(ob, 'F4', gnbb, 'gnbb', F[5], 'F5', F[0], 'F0', F[1], 'F1', ogt, 'ogt', ss4, rs4)
                transpose_tm(ogt, 'ogt', 4, oT, 'oT')
                for half in range(2):
                    for k in range(4):
                        S.op('pe', lambda e, k=k, half=half: e.matmul(
                            bank(3 + half, 64), lhsT=oT[:, k, :], rhs=wbrB[:, k, half * 512:(half + 1) * 512],
                            start=(k == 0), stop=(k == 3)), reads=['oT', 'wbrB'], writes=bk(3 + half))
                    proj_tm(bank(half, 64), bk(half), hT, 'hT', wgB, 'wgB', half * 512, 512)
                    sg = F[2 + half]
                    S.op('act', lambda e, half=half, sg=sg: e.activation(out=sg, in_=bank(half, 64), func=AF.Sigmoid),
                         reads=bk(half), writes=['F%d' % (2 + half)])
                    S.op('dve', lambda e, half=half, sg=sg: e.tensor_tensor(out=sg, in0=bank(3 + half, 64), in1=sg,
                                                                            op=ALU.mult),
                         reads=bk(3 + half) + ['F%d' % (2 + half)], writes=['F%d' % (2 + half)])
                    S.op('dve', lambda e, half=half, sg=sg: e.tensor_tensor(
                        out=mb[:, half * 512:(half + 1) * 512], in0=sg, in1=m1t[:, half * 512:(half + 1) * 512],
                        op=ALU.add), reads=['F%d' % (2 + half), 'm1t'], writes=['mb'])
                transpose_tm(mb, 'mb', 8, mT, 'mT')
                for half in range(2):
                    proj_tm(bank(3 + half, 64), bk(3 + half), mT, 'mT', woutb, 'woutb', half * 512, 512)
                    S.op('dve', lambda e, half=half: e.tensor_tensor(
                        out=m1t[:, half * 512:(half + 1) * 512], in0=bank(3 + half, 64),
                        in1=xt[:, half * 512:(half + 1) * 512], op=ALU.add), reads=bk(3 + half) + ['xt'], writes=['m1t'])
                S.dma('sp', lambda e: e.dma_start(out=x1_d[r0:r0 + 64, :], in_=m1t), reads=['m1t'],
                      writes=[('x1', ti)])

            if 'A2' in PHASES:
                for ti in (range(NPT) if TILES is None else TILES):
                    tileA2(ti, TP)
                tileA2(NPT, TS)
        phase_A2()
        S.barrier()

        def phase_B():
            AL.off = pers_mark
            AL.n = ARENA_COLS
            wqb = AL.get(128, [8, 2048], BF16)
            keysTb = AL.get(128, [16, 128], BF16)
            wpgb = AL.get(128, [8, 1024], BF16)
            wpleb = AL.get(128, [2, 1024], BF16)
            HB = [AL.get(128, 1024, BF16) for _ in range(NHB)]
            dltb = AL.get(128, [64, 64], BF16)
            WMd = AL.get(128, [64, 64], BF16)
            UV = [AL.get(128, 2048, BF16) for _ in range(NUV)]
            S.op('pool', lambda e: e.memset(WMd, 0.0), writes=['WMd'])
            junkb = AL.get(128, 1024, BF16)
            load_w_bf16(wqb, wq_d, 0, 8, 0, 2048, 'wqb')
            load_w_bf16(wpgb, wpg_d, 0, 8, 0, 1024, 'wpgb')
            load_w_bf16(wpleb, wple_d, 0, 2, 0, 1024, 'wpleb')
            S.dma('pool', lambda e: e.dma_start(out=keysTb, in_=keysT_d.rearrange("c d k -> d c k")), writes=['keysTb'])
            for cc in range(0, 4096, 2048):
                S.dma('pool', lambda e, cc=cc: e.dma_start(
                    out=dltb.rearrange("p n m -> p (n m)")[:, cc:cc + 2048], in_=dlt_d[:, cc:cc + 2048]), writes=['dltb'])
            gffnb = AL.get(64, 1024, F32)
            gpleb = AL.get(64, 1024, F32)
            gfinb = AL.get(64, 1024, F32)
            bcast_load(gffnb, gffn_d[0:1, :], 64, 1024, 'gffnb')
            bcast_load(gpleb, gple_d[0:1, :], 64, 1024, 'gpleb')
            bcast_load(gfinb, gfin_d[0:1, :], 64, 1024, 'gfinb')
            xtF = AL.get(64, 1024, F32)
            xtV2 = [AL.get(64, 1024, F32) for _ in range(2)]
            hbF = [AL.get(64, 1024, BF16) for _ in range(2)]
            hT = AL.get(128, [8, 64], BF16)
            hb2 = AL.get(64, 1024, BF16)
            hT2 = AL.get(128, [8, 64], BF16)
            ss = AL.get(64, 1, F32)
            rs = AL.get(64, 1, F32)
            ss2 = AL.get(64, 1, F32)
            rs2 = AL.get(64, 1, F32)
            qTb = AL.get(128, [16, 64], BF16)
            sc2 = [AL.get(64, [16, 128], F32) for _ in range(2)]
            v1 = AL.get(64, [16, 16], F32)
            i1 = AL.get(64, [16, 16], U32)
            i1f = AL.get(64, [16, 16], F32)
            wk = AL.get(64, 256, F32)
            cand = AL.get(64, [8, 256], F32)
            eq = cand.rearrange("p h (k i) -> p h k i", k=16)
            v2 = AL.get(64, [8, 16], F32)
            ci = AL.get(64, [8, 16], U32)
            cih = AL.get(64, [8, 16], U32)
            cil = AL.get(64, [8, 16], U32)
            cihf = AL.get(64, [8, 16], F32)
            cilf = AL.get(64, [8, 16], F32)
            iaf = AL.get(64, 128, F32)
            ibf = AL.get(64, 128, F32)
            idxf = AL.get(64, 128, F32)
            gte = AL.get(64, [8, 16], F32)
            gsum = AL.get(64, 8, F32)
            IDXT = [AL.get(128, 64, I32) for _ in range(3)]
            gateT = [AL.get(128, 64, F32) for _ in range(2)]
            ACTT = AL.get(128, 64, F32)
            X2c = AL.get(128, 64, F32)
            INc = AL.get(128, 64, F32)
            Sc = AL.get(128, 64, F32)
            gsig = AL.get(64, 1024, F32)
            ptl2 = [AL.get(64, 256, F32) for _ in range(2)]
            ptb = AL.get(64, 256, BF16)
            pTt = AL.get(128, [2, 64], BF16)
            yt = AL.get(64, 1024, F32)
            iota16 = C_('iota16')

            def topk16(src, srckey, width, vals, vkey, idxs, ikey):
                S.op('dve', lambda e: e.max(out=vals[:, 0:8], in_=src), reads=[srckey], writes=[vkey])
                S.op('dve', lambda e: e.max_index(out=idxs[:, 0:8], in_max=vals[:, 0:8], in_values=src),
                     reads=[srckey, vkey], writes=[ikey])
                S.op('dve', lambda e: e.match_replace(out=wk[:, 0:width], in_to_replace=vals[:, 0:8], in_values=src,
                                                      imm_value=-1e30), reads=[srckey, vkey], writes=['wk'])
                S.op('dve', lambda e: e.max(out=vals[:, 8:16], in_=wk[:, 0:width]), reads=['wk'], writes=[vkey])
                S.op('dve', lambda e: e.max_index(out=idxs[:, 8:16], in_max=vals[:, 8:16], in_values=wk[:, 0:width]),
                     reads=['wk', vkey], writes=[ikey])

            def rms_bf16(xin, xkey, gb, gkey, hout, hkey, ss_, sskey, rs_, rskey):
                S.op('act', lambda e: e.activation(out=hout, in_=xin, func=AF.Square, accum_out=ss_),
                     reads=[xkey], writes=[hkey, sskey])
                S.op('act', lambda e: e.activation(out=rs_, in_=ss_, func=AF.Sqrt, scale=1.0 / 1024,
                                                   bias=epsc[0:64, :]), reads=[sskey, 'cst'], writes=[rskey])
                S.op('dve', lambda e: e.reciprocal(out=rs_, in_=rs_), reads=[rskey], writes=[rskey])
                S.op('dve', lambda e: e.scalar_tensor_tensor(out=hout, in0=xin, scalar=rs_[:, 0:1], in1=gb,
                                                             op0=ALU.mult, op1=ALU.mult),
                     reads=[xkey, rskey, gkey], writes=[hkey])

            def front_pe(ti, pos):
                par = 0
                sc = sc2[pos % 2]
                sck = 'sc%d' % (pos % 2)
                r0 = ti * 64
                hb = hbF[par]
                hkey = 'hbF%d' % par
                S.dma('sp', lambda e: e.dma_start(out=xtF, in_=x1_d[r0:r0 + 64, :]), reads=[('x1', ti)], writes=['xtF'])
                rms_bf16(xtF, 'xtF', gffnb, 'gffnb', hb, hkey, ss, 'ss', rs, 'rs')
                S.dma('sp', lambda e: e.dma_start(out=h2_d[r0:r0 + 64, :], in_=hb), reads=[hkey], writes=[('h2', ti)])
                transpose_tm(hb, hkey, 8, hT, 'hT')
                pq = PB[:, 1024:2048].rearrange("p (c t) -> p c t", c=16)
                for hc in range(16):
                    for k in range(8):
                        S.op('pe', lambda e, hc=hc, k=k: e.matmul(pq[:, hc, :], lhsT=wqb[:, k, hc * 128:(hc + 1) * 128],
                                                                  rhs=hT[:, k, :], start=(k == 0), stop=(k == 7)),
                             reads=['wqb', 'hT'], writes=bk(2, 3))
                S.op('act', lambda e: e.copy(out=qTb, in_=pq), reads=bk(2, 3), writes=['qTb'])
                for half in range(2):
                    for j in range(8):
                        hc = half * 8 + j
                        S.op('pe', lambda e, hc=hc, j=j: e.matmul(PB[0:64, 1024 + j * 128:1024 + (j + 1) * 128],
                                                                  lhsT=qTb[:, hc, :], rhs=keysTb[:, hc, :],
                                                                  start=True, stop=True),
                             reads=['qTb', 'keysTb'], writes=bk(2, 3))
                    S.op('act', lambda e, half=half: e.copy(
                        out=sc[:, half * 8:(half + 1) * 8, :],
                        in_=PB[0:64, 1024:2048].rearrange("p (c k) -> p c k", c=8)), reads=bk(2, 3), writes=[sck])

            def front_dve(ti, pos):
                par = pos % 2
                ip = pos % 3
                sc = sc2[pos % 2]
                sck = 'sc%d' % (pos % 2)
                for hc in range(16):
                    topk16(sc[:, hc, :], sck, 128, v1[:, hc, :], 'v1', i1[:, hc, :], 'i1')
                S.op('dve', lambda e: e.tensor_copy(out=i1f, in_=i1), reads=['i1'], writes=['i1f'])
                for h in range(8):
                    S.op('dve', lambda e, h=h: e.tensor_tensor(
                        out=cand[:, h, :].rearrange("p (i j) -> p i j", i=16),
                        in0=v1[:, 2 * h, :].unsqueeze(2).to_broadcast([64, 16, 16]),
                        in1=v1[:, 2 * h + 1, :].unsqueeze(1).to_broadcast([64, 16, 16]), op=ALU.add),
                        reads=['v1'], writes=['cand'])
                for h in range(8):
                    topk16(cand[:, h, :], 'cand', 256, v2[:, h, :], 'v2', ci[:, h, :], 'ci')
                S.op('dve', lambda e: e.tensor_scalar(out=cih, in0=ci, scalar1=4, scalar2=None,
                                                      op0=ALU.logical_shift_right), reads=['ci'], writes=['cih'])
                S.op('dve', lambda e: e.tensor_scalar(out=cil, in0=ci, scalar1=15, scalar2=None, op0=ALU.bitwise_and),
                     reads=['ci'], writes=['cil'])
                S.op('dve', lambda e: e.tensor_copy(out=cihf, in_=cih), reads=['cih'], writes=['cihf'])
                S.op('dve', lambda e: e.tensor_copy(out=cilf, in_=cil), reads=['cil'], writes=['cilf'])
                i1v = i1f.rearrange("p (h c) i -> p h c i", c=2)
                for (cf, ckey, cpos, dst, dkey) in ((cihf, 'cihf', 0, iaf, 'iaf'), (cilf, 'cilf', 1, ibf, 'ibf')):
                    S.op('dve', lambda e, cf=cf: e.tensor_tensor(
                        out=eq, in0=cf.unsqueeze(3).to_broadcast([64, 8, 16, 16]),
                        in1=iota16.unsqueeze(1).unsqueeze(1).to_broadcast([64, 8, 16, 16]), op=ALU.is_equal),
                        reads=[ckey, 'cst'], writes=['cand'])
                    S.op('dve', lambda e, cpos=cpos: e.tensor_tensor(
                        out=eq, in0=eq, in1=i1v[:, :, cpos, :].unsqueeze(2).to_broadcast([64, 8, 16, 16]), op=ALU.mult),
                        reads=['cand', 'i1f'], writes=['cand'])
                    S.op('dve', lambda e, dst=dst: e.reduce_sum(out=dst, in_=eq.rearrange("p h k i -> p (h k) i"),
                                                                axis=AX.X), reads=['cand'], writes=[dkey])
                S.op('dve', lambda e: e.scalar_tensor_tensor(out=idxf, in0=iaf, scalar=128.0, in1=ibf, op0=ALU.mult,
                                                             op1=ALU.add), reads=['iaf', 'ibf'], writes=['idxf'])
                S.op('dve', lambda e: e.tensor_tensor(out=gte, in0=v2, in1=v2[:, :, 0:1].to_broadcast([64, 8, 16]),
                                                      op=ALU.subtract), reads=['v2'], writes=['gte'])
                S.op('act', lambda e: e.activation(out=gte, in_=gte, func=AF.Exp), reads=['gte'], writes=['gte'])
                S.op('dve', lambda e: e.reduce_sum(out=gsum, in_=gte, axis=AX.X), reads=['gte'], writes=['gsum'])
                S.op('dve', lambda e: e.reciprocal(out=gsum, in_=gsum), reads=['gsum'], writes=['gsum'])
                S.op('dve', lambda e: e.tensor_tensor(out=gte, in0=gte, in1=gsum.unsqueeze(2).to_broadcast([64, 8, 16]),
                                                      op=ALU.mult), reads=['gte', 'gsum'], writes=['gte'])
                S.op('pe', lambda e: e.transpose(out=bank(6, 128, 64), in_=idxf, identity=identf[0:64, 0:64]),
                     reads=['idxf', 'cst'], writes=bk(6))
                S.op('pe', lambda e: e.transpose(out=bank(6, 128, 64, 64), in_=gte.rearrange("p h k -> p (h k)"),
                                                 identity=identf[0:64, 0:64]), reads=['gte', 'cst'], writes=bk(6))
                S.op('dve', lambda e: e.tensor_copy(out=IDXT[ip], in_=bank(6, 128, 64)), reads=bk(6),
                     writes=['IDXT%d' % ip])
                S.op('dve', lambda e: e.tensor_copy(out=gateT[par], in_=bank(6, 128, 64, 64)), reads=bk(6),
                     writes=['gateT%d' % par])

            def uvstage(ti, pos):
                par = pos % 2
                ip = pos % 3
                r0 = ti * 64
                xtV = xtV2[par]
                ptl = ptl2[par]
                xk = 'xtV%d' % par
                pk = 'ptl%d' % par
                gT = gateT[par]
                gk = 'gateT%d' % par
                S.begin()
                S.dma('sp', lambda e: e.dma_start(out=xtV, in_=x1_d[r0:r0 + 64, :]), reads=[('x1', ti)], writes=[xk])
                S.dma('sp', lambda e: e.dma_start(out=ptl, in_=p_d[r0:r0 + 64, :]), writes=[pk])
                head = S.end()
                segs = []
                for n in range(64):
                    S.begin()
                    g = (pos * 64 + n) % NUV
                    uv_ = UV[g]
                    uvkey = 'UV%d' % g
                    S.dma('pool', lambda e, n=n, uv_=uv_: e.indirect_dma_start(
                        out=uv_, out_offset=None, in_=uvb_d,
                        in_offset=bass.IndirectOffsetOnAxis(ap=IDXT[ip][:, n:n + 1], axis=0)),
                        reads=['IDXT%d' % ip, 'uvb'], writes=[uvkey])
                    gh = (pos * 64 + n) % NHB
                    hbb = HB[gh]
                    hbkey = 'HB%d' % gh
                    S.dma('sp', lambda e, n=n, hbb=hbb: e.dma_start(
                        out=hbb, in_=h2_d[ti * 64 + n:ti * 64 + n + 1, :].to_broadcast([128, 1024])),
                        reads=[('h2', ti)], writes=[hbkey])
                    S.op('dve', lambda e, n=n, uv_=uv_, hbb=hbb: e.scalar_tensor_tensor(
                        out=junkb, in0=uv_[:, 0:1024], scalar=1.0, in1=hbb, op0=ALU.mult, op1=ALU.mult,
                        accum_out=ACTT[:, n:n + 1]), reads=[uvkey, hbkey], writes=[('ACT', n)])
                    xa = ACTT[:, n:n + 1]
                    S.op('act', lambda e, n=n, xa=xa: e.activation(out=X2c[:, n:n + 1], in_=xa, func=AF.Identity,
                                                                  scale=xa), reads=[('ACT', n)], writes=[('X2', n)])
                    S.op('act', lambda e, n=n: e.activation(out=X2c[:, n:n + 1], in_=X2c[:, n:n + 1], func=AF.Identity,
                                                            scale=0.044715, bias=C_('one')), reads=[('X2', n), 'cst'],
                         writes=[('X2', n)])
                    S.op('act', lambda e, n=n, xa=xa: e.activation(out=INc[:, n:n + 1], in_=X2c[:, n:n + 1],
                                                                  func=AF.Identity, scale=xa),
                         reads=[('X2', n), ('ACT', n)], writes=[('IN', n)])
                    S.op('act', lambda e, n=n: e.activation(out=Sc[:, n:n + 1], in_=INc[:, n:n + 1], func=AF.Sigmoid,
                                                            scale=1.5957691216057308), reads=[('IN', n)],
                         writes=[('S', n)])
                    S.op('act', lambda e, n=n, xa=xa: e.activation(out=Sc[:, n:n + 1], in_=Sc[:, n:n + 1],
                                                                  func=AF.Identity, scale=xa),
                         reads=[('S', n), ('ACT', n)], writes=[('S', n)])
                    S.op('act', lambda e, n=n: e.activation(out=WMd[:, n, n:n + 1], in_=Sc[:, n:n + 1],
                                                            func=AF.Identity, scale=gT[:, n:n + 1]),
                         reads=[('S', n), gk, 'WMd'], writes=[('WM', n)])
                    for half in range(2):
                        S.op('pe', lambda e, n=n, half=half, uv_=uv_: e.matmul(
                            bank(4 + half, 64), lhsT=WMd[:, n, :],
                            rhs=uv_[:, 1024 + half * 512:1024 + (half + 1) * 512],
                            start=(n == 0), stop=(n == 63)), reads=[('WM', n), uvkey], writes=bk(4 + half))
                    segs.append(S.end())
                S.begin()
                S.op('dve', lambda e: e.tensor_tensor(out=xtV, in0=xtV, in1=PB[0:64, 2048:3072], op=ALU.add),
                     reads=[xk] + bk(4, 5), writes=[xk])
                tail_a = S.end()
                S.begin()
                pt6 = PB[:, 6 * 512 + 256:7 * 512].bitcast(BF16)
                rms_bf16(xtV, xk, gpleb, 'gpleb', hb2, 'hb2', ss2, 'ss2', rs2, 'rs2')
                transpose_tm(hb2, 'hb2', 8, hT2, 'hT2', ptgt=pt6, pkey='b6u')
                S.op('dve', lambda e: e.tensor_copy(out=ptb, in_=ptl), reads=[pk], writes=['ptb'])
                transpose_tm(ptb, 'ptb', 2, pTt, 'pTt', ptgt=pt6, pkey='b6u')
                for half in range(2):
                    hs = slice(half * 512, (half + 1) * 512)
                    proj_tm(bank(0, 64), bk(0), hT2, 'hT2', wpgb, 'wpgb', half * 512, 512)
                    proj_tm(bank(1, 64), bk(1), pTt, 'pTt', wpleb, 'wpleb', half * 512, 512, nk=2)
                    S.op('act', lambda e, hs=hs: e.activation(out=gsig[:, hs], in_=bank(0, 64), func=AF.Sigmoid),
                         reads=bk(0), writes=['gsig'])
                    S.op('dve', lambda e, hs=hs: e.tensor_tensor(out=gsig[:, hs], in0=gsig[:, hs], in1=bank(1, 64),
                                                                  op=ALU.mult), reads=['gsig'] + bk(1), writes=['gsig'])
                    S.op('dve', lambda e, hs=hs: e.tensor_tensor(out=xtV[:, hs], in0=xtV[:, hs], in1=gsig[:, hs],
                                                                  op=ALU.add), reads=[xk, 'gsig'], writes=[xk])
                S.op('act', lambda e: e.activation(out=yt, in_=xtV, func=AF.Square, accum_out=ss2), reads=[xk],
                     writes=['yt', 'ss2'])
                S.op('act', lambda e: e.activation(out=rs2, in_=ss2, func=AF.Sqrt, scale=1.0 / 1024,
                                                   bias=epsc[0:64, :]), reads=['ss2', 'cst'], writes=['rs2'])
                S.op('dve', lambda e: e.reciprocal(out=rs2, in_=rs2), reads=['rs2'], writes=['rs2'])
                S.op('dve', lambda e: e.scalar_tensor_tensor(out=yt, in0=xtV, scalar=rs2[:, 0:1], in1=gfinb,
                                                             op0=ALU.mult, op1=ALU.mult),
                     reads=[xk, 'rs2', 'gfinb'], writes=['yt'])
                S.dma('sp', lambda e: e.dma_start(out=y_d[r0:r0 + 64, :], in_=yt), reads=['yt'], final=True)
                return head, segs, tail_a, S.end()

            if 'B' in PHASES:
                TL = list(range(NPT + 1)) if TILES is None else list(TILES) + [NPT]
                nT = len(TL)
                front_pe(TL[0], 0)
                front_dve(TL[0], 0)
                if nT > 1:
                    front_pe(TL[1], 1)
                LP = []
                for j in range(nT):
                    head, segs, tail_a, tail_b = uvstage(TL[j], j)
                    LFd, LFp = [], []
                    if j + 1 < nT:
                        S.begin()
                        front_dve(TL[j + 1], j + 1)
                        LFd = S.end()
                    if j + 2 < nT:
                        S.begin()
                        front_pe(TL[j + 2], j + 2)
                        LFp = S.end()
                    streams = [LFd, LFp, LP]
                    pers = [(len(L) + 63) // 64 for L in streams]
                    S.run(head)
                    for n in range(64):
                        S.run(segs[n])
                        for L, per in zip(streams, pers):
                            S.run(L[n * per:(n + 1) * per])
                    for L, per in zip(streams, pers):
                        S.run(L[64 * per:])
                    S.run(tail_a)
                    LP = tail_b
                S.run(LP)
        phase_B()
        print('ops', {e: len(v) for e, v in S.prog.items()}, 'nsem', S.nsem, 'arena', AL.off)
        S.emit()
    return nc


_CACHE = {}


def kernel(x_prompt, x_sample, state_hgrn, state_delta, state_conv, p_prompt, p_sample,
           lb_param, g_mix, w_in, conv_w, a_log, dt_bias, g_norm_a, g_norm_b, w_br_a, w_br_b,
           w_out, g_ffn, peer_wq, peer_keys, expert_u, expert_v, g_ple, w_ple, w_ple_gate,
           g_final):
    f = lambda a: np.ascontiguousarray(np.asarray(a, dtype=np.float32))
    if 'nc' not in _CACHE:
        _CACHE['nc'] = build_program()
        _CACHE['consts'] = _build_consts()
    nc = _CACHE['nc']
    cst, cmf, zsel, dlt = _CACHE['consts']
    x_prompt, x_sample = f(x_prompt), f(x_sample)
    p_prompt, p_sample = f(p_prompt), f(p_sample)
    state_hgrn, state_delta, state_conv = f(state_hgrn), f(state_delta), f(state_conv)
    keysT = np.ascontiguousarray(np.transpose(f(peer_keys)[0], (0, 1, 3, 2)).reshape(16, 128, 128))
    shared = dict(
        lbp=f(lb_param), gmix=f(g_mix), w_in=f(w_in)[0], convw=f(conv_w)[0], alog=f(a_log), dtb=f(dt_bias),
        gna=f(g_norm_a), gnb=f(g_norm_b), wbra=f(w_br_a)[0], wbrb=f(w_br_b)[0], wout=f(w_out)[0], gffn=f(g_ffn),
        wq=f(peer_wq)[0], keysT=keysT, eu=f(expert_u)[0], ev=f(expert_v)[0], gple=f(g_ple), wple=f(w_ple)[0],
        wpg=f(w_ple_gate)[0], gfin=f(g_final).reshape(1, 1024), cst=cst, cmf=cmf, zsel=zsel, dlt=dlt)
    in_maps = []
    for b in range(8):
        m = dict(shared)
        m['x'] = np.ascontiguousarray(np.concatenate([x_prompt[b], x_sample[16 * b:16 * b + 16].reshape(64, 1024)], 0))
        m['p'] = np.ascontiguousarray(np.concatenate([p_prompt[0, b], p_sample[0, 16 * b:16 * b + 16].reshape(64, 256)], 0))
        m['sh'] = np.ascontiguousarray(state_hgrn[0, 16 * b:16 * b + 16])
        m['sd'] = np.ascontiguousarray(state_delta[0, 16 * b:16 * b + 16])
        m['scv'] = np.ascontiguousarray(state_conv[0, 16 * b:16 * b + 16].reshape(48, 1536))
        in_maps.append(m)
    res = run_bass_kernel_spmd(nc, in_maps, core_ids=list(range(8)))
    R = res.results
    y_prompt = np.stack([R[b]['y'][0:2048] for b in range(8)], 0)
    y_sample = np.concatenate([R[b]['y'][2048:2112].reshape(16, 4, 1024) for b in range(8)], 0)
    hp = np.stack([R[b]['hp'] for b in range(8)], 0)[None]
    dp = np.stack([R[b]['dp'] for b in range(8)], 0)[None]
    cp = np.stack([R[b]['cp'] for b in range(8)], 0)[None]
    hs = np.concatenate([R[b]['hs'] for b in range(8)], 0)[None]
    ds = np.concatenate([R[b]['ds'] for b in range(8)], 0)[None]
    cs = np.concatenate([R[b]['cs'].reshape(16, 3, 1536) for b in range(8)], 0)[None]
    _CACHE['dbg'] = R
    return tuple(np.ascontiguousarray(a.astype(np.float32)) for a in (y_prompt, y_sample, hp, dp, cp, hs, ds, cs))
```
